# Optimizing a Trainium2 kernel written in Bass

```python
import math
import jax
import jax.numpy as jnp
from jax import lax
import numpy as np

D_MODEL = 2048
BATCH = 4
SEQ = 2048
DEPTH = 1
DEC_BATCH = 8
DEC_SEQ = 8
PAST_LEN = 16384
PAGE_SIZE = 128

MIX_WIDTH = D_MODEL
M_HEADS = 8
M_V_DIM = (MIX_WIDTH // 2) // M_HEADS
M_QK_DIM = M_V_DIM // 2
M_CHUNK = 64
A_HEADS = 8
A_HEAD_DIM = (MIX_WIDTH - M_HEADS * M_V_DIM) // A_HEADS
MOBA_BLOCK = 256
MOBA_TOP_K = 3
MOBA_Q_BLOCK = 16
NUM_BUCKETS = 32
MAX_EXACT = NUM_BUCKETS // 2
MAX_DISTANCE = 128
D_FF = 4 * D_MODEL
PLE_DIM = 256
EPS = 1e-6
SPLITS = (M_HEADS * M_QK_DIM, M_HEADS * M_QK_DIM, M_HEADS * M_V_DIM, M_HEADS * M_V_DIM,
          M_HEADS, M_HEADS, A_HEADS * A_HEAD_DIM, A_HEADS * A_HEAD_DIM, A_HEADS * A_HEAD_DIM)
D_IN = sum(SPLITS)

kernel_name = "hymba_mlstm_moba_decoder_step"


def rms_norm(x, g):
    x32 = x.astype(jnp.float32)
    y = x32 * lax.rsqrt(jnp.mean(x32 * x32, axis=-1, keepdims=True) + EPS)
    return (y * g.astype(jnp.float32)).astype(x.dtype)


def split_cols(z):
    offsets = [int(o) for o in np.cumsum(SPLITS)[:-1]]
    return jnp.split(z, offsets, axis=-1)


def t5_bucket(rel):
    n = jnp.maximum(rel, 0)
    nf = jnp.maximum(n, 1).astype(jnp.float32)
    large = MAX_EXACT + (jnp.log(nf / MAX_EXACT) / math.log(MAX_DISTANCE / MAX_EXACT)
                         * (NUM_BUCKETS - MAX_EXACT)).astype(jnp.int32)
    large = jnp.minimum(large, NUM_BUCKETS - 1)
    return jnp.where(n < MAX_EXACT, n, large)


def mlstm_chunkwise(q, k, v, li, lf, C0, n0, m0):
    B, H, T, dk = q.shape
    dv = v.shape[-1]
    L = math.gcd(T, M_CHUNK)
    NC = T // L

    def to_chunks(a):
        return jnp.moveaxis(a.reshape((B, H, NC, L) + a.shape[3:]), 2, 0)

    causal = jnp.tril(jnp.ones((L, L), dtype=bool))

    def step(carry, inp):
        C, n, m = carry
        qc, kc, vc, lic, lfc = inp
        b = jnp.cumsum(lfc, axis=-1)
        a = b + m[..., None]
        D = b[..., :, None] - b[..., None, :] + lic[..., None, :]
        D = jnp.where(causal, D, -jnp.inf)
        m_t = jnp.maximum(a, jnp.max(D, axis=-1))
        S = jnp.einsum('bhtd,bhsd->bhts', qc, kc) * jnp.exp(D - m_t[..., None])
        inter = jnp.exp(a - m_t)
        num = inter[..., None] * jnp.einsum('bhtd,bhdv->bhtv', qc, C) + jnp.einsum('bhts,bhsv->bhtv', S, vc)
        den = inter * jnp.einsum('bhtd,bhd->bht', qc, n) + jnp.sum(S, axis=-1)
        h = num / jnp.maximum(jnp.abs(den), jnp.exp(-m_t))[..., None]
        m_L = m_t[..., -1]
        w_prev = jnp.exp(b[..., -1] + m - m_L)
        ws = jnp.exp(b[..., -1:] - b + lic - m_L[..., None])
        C_new = w_prev[..., None, None] * C + jnp.einsum('bhs,bhsd,bhsv->bhdv', ws, kc, vc)
        n_new = w_prev[..., None] * n + jnp.einsum('bhs,bhsd->bhd', ws, kc)
        return (C_new, n_new, m_L), h

    (C, n, m), hs = lax.scan(step, (C0, n0, m0),
                             (to_chunks(q), to_chunks(k), to_chunks(v), to_chunks(li), to_chunks(lf)))
    h = jnp.moveaxis(hs, 0, 2).reshape(B, H, T, dv)
    return h, C, n, m


def moba_attention(q, k_all, v_all, q_pos, rel_table):
    B, H, Tq, dh = q.shape
    NB = k_all.shape[2] // MOBA_BLOCK
    kb = k_all.reshape(B, H, NB, MOBA_BLOCK, dh)
    vb = v_all.reshape(B, H, NB, MOBA_BLOCK, dh)
    kmean = jnp.mean(kb, axis=3)
    QB = math.gcd(Tq, MOBA_Q_BLOCK)
    NQ = Tq // QB
    qs = jnp.moveaxis(q.reshape(B, H, NQ, QB, dh), 2, 0)
    ps = q_pos.reshape(NQ, QB)
    bi = jnp.arange(B)[:, None, None, None]
    hi = jnp.arange(H)[None, :, None, None]
    bias_hb = rel_table.T.astype(jnp.float32)
    offs = jnp.arange(MOBA_BLOCK, dtype=jnp.int32)
    k_top = min(MOBA_TOP_K, NB)

    def attend(args):
        qb, pb = args
        own = pb // MOBA_BLOCK
        gate = jnp.einsum('bhqd,bhnd->bhqn', qb, kmean)
        past_ok = jnp.arange(NB)[None, :] < own[:, None]
        gate = jnp.where(past_ok, gate, -jnp.inf)
        _, top_idx = lax.top_k(gate, k_top)
        sel_ok = top_idx < own[:, None]
        idx = jnp.concatenate([top_idx, jnp.broadcast_to(own[:, None], (B, H, QB, 1))], axis=-1)
        ok = jnp.concatenate([sel_ok, jnp.ones((B, H, QB, 1), dtype=bool)], axis=-1)
        kg = kb[bi, hi, idx]
        vg = vb[bi, hi, idx]
        kpos = idx[..., None] * MOBA_BLOCK + offs
        rel = pb[:, None, None] - kpos
        mask = ok[..., None] & (rel >= 0)
        logits = jnp.einsum('bhqd,bhqnkd->bhqnk', qb, kg) + bias_hb[hi[..., None], t5_bucket(rel)]
        logits = jnp.where(mask, logits, -jnp.inf)
        w = jax.nn.softmax(logits.reshape(B, H, QB, -1), axis=-1).reshape(logits.shape)
        return jnp.einsum('bhqnk,bhqnkd->bhqd', w, vg)

    out = lax.map(attend, (qs, ps))
    return jnp.moveaxis(out, 0, 2).reshape(B, H, Tq, dh)


def decoder_layer(h, pe, pos0, past_k, past_v, C0, n0, m0, rel_table, w_in, b_igate, b_fgate,
                  g_mix, g_mhead, w_out, g_ffn, w_up, w_down, g_ple, w_ple_gate, w_ple_proj):
    B, T, _ = h.shape
    dt = h.dtype
    f32 = jnp.float32
    z = rms_norm(h, g_mix) @ w_in
    mq, mk, mv, mo, mi, mf, aq, ak, av = split_cols(z)

    def heads(a, nh):
        return a.reshape(B, T, nh, -1).transpose(0, 2, 1, 3).astype(f32)

    q_m = heads(mq, M_HEADS)
    k_m = heads(mk, M_HEADS) * (M_QK_DIM ** -0.5)
    v_m = heads(mv, M_HEADS)
    li = (mi + b_igate).astype(f32).transpose(0, 2, 1)
    lf = jax.nn.log_sigmoid((mf + b_fgate).astype(f32)).transpose(0, 2, 1)
    h_m, C, n, m = mlstm_chunkwise(q_m, k_m, v_m, li, lf, C0.astype(f32), n0.astype(f32), m0.astype(f32))
    h_m = rms_norm(h_m.transpose(0, 2, 1, 3), g_mhead)
    out_m = (jax.nn.sigmoid(mo.astype(f32).reshape(B, T, M_HEADS, M_V_DIM)) * h_m).reshape(B, T, -1)

    k_new = ak.reshape(B, T, A_HEADS, A_HEAD_DIM)
    v_new = av.reshape(B, T, A_HEADS, A_HEAD_DIM)
    k_all = jnp.concatenate([past_k, k_new.astype(past_k.dtype)], axis=1).astype(f32)
    v_all = jnp.concatenate([past_v, v_new.astype(past_v.dtype)], axis=1).astype(f32)
    pad = (-k_all.shape[1]) % MOBA_BLOCK
    k_all = jnp.pad(k_all, ((0, 0), (0, pad), (0, 0), (0, 0))).transpose(0, 2, 1, 3)
    v_all = jnp.pad(v_all, ((0, 0), (0, pad), (0, 0), (0, 0))).transpose(0, 2, 1, 3)
    q_a = heads(aq, A_HEADS) * (A_HEAD_DIM ** -0.5)
    q_pos = pos0 + jnp.arange(T, dtype=jnp.int32)
    out_a = moba_attention(q_a, k_all, v_all, q_pos, rel_table).transpose(0, 2, 1, 3).reshape(B, T, -1)

    h = h + jnp.concatenate([out_m, out_a], axis=-1).astype(dt) @ w_out
    u = rms_norm(h, g_ffn) @ w_up
    h = h + jnp.square(jax.nn.relu(u)) @ w_down
    gate = jax.nn.sigmoid((rms_norm(h, g_ple) @ w_ple_gate).astype(f32))
    h = h + (gate * (pe @ w_ple_proj).astype(f32)).astype(dt)
    return h, k_new, v_new, C, n, m


def setup_inputs(seed: int = 0) -> dict:
    key = jax.random.key(seed)
    ks = jax.random.split(key, 32)
    f32 = jnp.float32

    def nrm(k, shape, scale):
        return jax.random.normal(k, shape, f32) * scale

    def gain(k, shape):
        return 1.0 + 0.02 * jax.random.normal(k, shape, f32)

    n_pages = PAST_LEN // PAGE_SIZE
    n_used = DEC_BATCH * n_pages
    n_phys = (5 * n_used + 3) // 4
    page_table = jax.random.permutation(ks[0], n_phys)[:n_used].reshape(DEC_BATCH, n_pages).astype(jnp.int32)
    b_f = jnp.broadcast_to(jnp.linspace(3.0, 6.0, M_HEADS, dtype=f32), (DEPTH, M_HEADS)) + nrm(ks[1], (DEPTH, M_HEADS), 0.1)
    return {
        "x_prompt": nrm(ks[2], (BATCH, SEQ, D_MODEL), 1.0),
        "x_sample": nrm(ks[3], (DEC_BATCH, DEC_SEQ, D_MODEL), 1.0),
        "cache_k": nrm(ks[4], (DEPTH, n_phys, PAGE_SIZE, A_HEADS, A_HEAD_DIM), 1.0),
        "cache_v": nrm(ks[5], (DEPTH, n_phys, PAGE_SIZE, A_HEADS, A_HEAD_DIM), 1.0),
        "state_C": nrm(ks[6], (DEPTH, DEC_BATCH, M_HEADS, M_QK_DIM, M_V_DIM), 0.5),
        "state_n": jnp.abs(nrm(ks[7], (DEPTH, DEC_BATCH, M_HEADS, M_QK_DIM), 1.0)),
        "state_m": nrm(ks[8], (DEPTH, DEC_BATCH, M_HEADS), 1.0),
        "page_table": page_table,
        "p_prompt": nrm(ks[9], (DEPTH, BATCH, SEQ, PLE_DIM), 1.0),
        "p_sample": nrm(ks[10], (DEPTH, DEC_BATCH, DEC_SEQ, PLE_DIM), 1.0),
        "rel_bias_table": nrm(ks[11], (NUM_BUCKETS, A_HEADS), 0.5),
        "w_in": nrm(ks[12], (DEPTH, D_MODEL, D_IN), D_MODEL ** -0.5),
        "b_igate": nrm(ks[13], (DEPTH, M_HEADS), 0.1),
        "b_fgate": b_f,
        "g_mix": gain(ks[14], (DEPTH, D_MODEL)),
        "g_mhead": gain(ks[15], (DEPTH, M_HEADS, M_V_DIM)),
        "w_out": nrm(ks[16], (DEPTH, MIX_WIDTH, D_MODEL), MIX_WIDTH ** -0.5),
        "g_ffn": gain(ks[17], (DEPTH, D_MODEL)),
        "w_up": nrm(ks[18], (DEPTH, D_MODEL, D_FF), D_MODEL ** -0.5),
        "w_down": nrm(ks[19], (DEPTH, D_FF, D_MODEL), D_FF ** -0.5),
        "g_ple": gain(ks[20], (DEPTH, D_MODEL)),
        "w_ple_gate": nrm(ks[21], (DEPTH, D_MODEL, D_MODEL), D_MODEL ** -0.5),
        "w_ple_proj": nrm(ks[22], (DEPTH, PLE_DIM, D_MODEL), PLE_DIM ** -0.5),
        "g_final": gain(ks[23], (D_MODEL,)),
    }


def reference(x_prompt, x_sample, cache_k, cache_v, state_C, state_n, state_m, page_table,
              p_prompt, p_sample, rel_bias_table, w_in, b_igate, b_fgate, g_mix, g_mhead, w_out,
              g_ffn, w_up, w_down, g_ple, w_ple_gate, w_ple_proj, g_final):
    Bp = x_prompt.shape[0]
    Bs, _, _ = x_sample.shape
    past_len = page_table.shape[1] * PAGE_SIZE
    h_p = x_prompt
    h_s = x_sample
    kp_l, vp_l, Cp_l, np_l, mp_l = [], [], [], [], []
    ks_l, vs_l, Cs_l, ns_l, ms_l = [], [], [], [], []
    for l in range(DEPTH):
        w = (rel_bias_table, w_in[l], b_igate[l], b_fgate[l], g_mix[l], g_mhead[l], w_out[l],
             g_ffn[l], w_up[l], w_down[l], g_ple[l], w_ple_gate[l], w_ple_proj[l])
        empty = jnp.zeros((Bp, 0, A_HEADS, A_HEAD_DIM), cache_k.dtype)
        C0 = jnp.zeros((Bp, M_HEADS, M_QK_DIM, M_V_DIM), jnp.float32)
        n0 = jnp.zeros((Bp, M_HEADS, M_QK_DIM), jnp.float32)
        m0 = jnp.zeros((Bp, M_HEADS), jnp.float32)
        h_p, kp, vp, Cp, np_, mp = decoder_layer(h_p, p_prompt[l], 0, empty, empty, C0, n0, m0, *w)
        past_k = cache_k[l, page_table].reshape(Bs, past_len, A_HEADS, A_HEAD_DIM)
        past_v = cache_v[l, page_table].reshape(Bs, past_len, A_HEADS, A_HEAD_DIM)
        h_s, ks_, vs_, Cs, ns, ms = decoder_layer(h_s, p_sample[l], past_len, past_k, past_v,
                                                  state_C[l], state_n[l], state_m[l], *w)
        kp_l.append(kp.astype(cache_k.dtype)); vp_l.append(vp.astype(cache_v.dtype))
        Cp_l.append(Cp.astype(state_C.dtype)); np_l.append(np_.astype(state_n.dtype)); mp_l.append(mp.astype(state_m.dtype))
        ks_l.append(ks_.astype(cache_k.dtype)); vs_l.append(vs_.astype(cache_v.dtype))
        Cs_l.append(Cs.astype(state_C.dtype)); ns_l.append(ns.astype(state_n.dtype)); ms_l.append(ms.astype(state_m.dtype))
    y_prompt = rms_norm(h_p, g_final)
    y_sample = rms_norm(h_s, g_final)
    return (y_prompt, y_sample,
            jnp.stack(kp_l), jnp.stack(vp_l), jnp.stack(Cp_l), jnp.stack(np_l), jnp.stack(mp_l),
            jnp.stack(ks_l), jnp.stack(vs_l), jnp.stack(Cs_l), jnp.stack(ns_l), jnp.stack(ms_l))
```

```python
import numpy as np
import concourse.bass as bass
import concourse.mybir as mybir

F32 = mybir.dt.float32
BF16 = mybir.dt.bfloat16
I32 = mybir.dt.int32
AF = mybir.ActivationFunctionType
ALU = mybir.AluOpType
AX = mybir.AxisListType

ENGS = ("pe", "act", "dve", "pool", "sp")


class Res:
    __slots__ = ("name", "parent", "kids", "lw", "rd", "excl")

    def __init__(self, name, parent=None):
        self.excl = False
        self.name = name
        self.parent = parent
        self.kids = {}
        self.lw = None
        self.rd = []

    def k(self, key):
        r = self.kids.get(key)
        if r is None:
            r = Res(f"{self.name}.{key}", self)
            self.kids[key] = r
        return r


class Op:
    __slots__ = ("eng", "fn", "deps", "idx", "sig", "cnt", "chan", "isdma", "dw", "ny")

    def __init__(self, eng, fn):
        self.eng = eng
        self.fn = fn
        self.deps = set()
        self.dw = {}
        self.sig = False
        self.cnt = 0
        self.chan = None
        self.isdma = False


class Chan:
    def __init__(self, name):
        self.name = name
        self.res = Res("chan_" + name)
        self.n = 0
        self.sem = None


class Prog:
    def __init__(self, nc, same_engine_sync=True):
        self.nc = nc
        self.ops = []
        self.chans = {}
        self.same_engine_sync = same_engine_sync

    def chan(self, name):
        c = self.chans.get(name)
        if c is None:
            c = Chan(name)
            self.chans[name] = c
        return c

    def _desc(self, r, out):
        for kk in r.kids.values():
            out.append(kk)
            if kk.kids:
                self._desc(kk, out)

    def _related(self, r):
        out = [r]
        p = r.parent
        while p is not None:
            out.append(p)
            p = p.parent
        if r.kids:
            self._desc(r, out)
        return out

    def _add(self, op, reads, writes):
        for r in reads:
            if r.excl and r not in writes:
                writes = writes + [r]
        deps = op.deps
        for r in reads:
            for x in self._related(r):
                if x.lw is not None:
                    deps.add(x.lw)
        for w in writes:
            for x in self._related(w):
                if x.lw is not None:
                    deps.add(x.lw)
                for o in x.rd:
                    deps.add(o)
        for d in list(deps):
            if d.isdma:
                c = d.chan
                if op.dw.get(c, 0) < 16 * c.n:
                    op.dw[c] = 16 * c.n
                if not (op.isdma and op.chan is c):
                    c.res.rd.append(op)
                deps.discard(d)
        for r in reads:
            r.rd.append(op)
        for w in writes:
            w.lw = op
            w.rd = []
            if w.kids:
                dd = []
                self._desc(w, dd)
                for kk in dd:
                    kk.lw = op
                    kk.rd = []
        deps.discard(op)
        op.idx = len(self.ops)
        self.ops.append(op)

    frozen = False

    def op(self, eng, fn, reads=(), writes=()):
        if self.frozen:
            return None
        o = Op(eng, fn)
        self._add(o, list(reads), list(writes))
        return o

    def dma(self, eng, chan, fn, reads=(), writes=()):
        if self.frozen:
            return None
        c = self.chan(chan) if isinstance(chan, str) else chan
        o = Op(eng, fn)
        o.isdma = True
        o.chan = c
        assert getattr(c, "eng", eng) == eng
        c.eng = eng
        for x in c.res.rd:
            o.deps.add(x)
        c.res.rd = []
        self._add(o, list(reads), list(writes))
        c.n += 1
        o.cnt = 16 * c.n
        return o

    def emit(self, final_wait_eng="sp"):
        nc = self.nc
        ops = self.ops
        for o in ops:
            for d in o.deps:
                if d.isdma:
                    continue
                if d.eng == o.eng:
                    if o.eng == "pe" or not self.same_engine_sync:
                        continue
                d.sig = True
        import inspect

        class _FI:
            def then_inc(self, *a, **k):
                return self

        class _FE:
            def __getattr__(self, n):
                return lambda *a, **k: _FI()
        counts = {e: 0 for e in ENGS}
        for o in ops:
            o.ny = 0
            if o.isdma:
                continue
            if inspect.isgeneratorfunction(o.fn):
                o.ny = sum(1 for _ in o.fn(_FE()))
                counts[o.eng] += o.ny
                o.cnt = counts[o.eng]
            elif o.sig:
                counts[o.eng] += 1
                o.cnt = counts[o.eng]
        import contextlib
        with contextlib.ExitStack() as st:
            esem = {e: st.enter_context(nc.semaphore("s_" + e)) for e in ENGS if e != "sp"}
            for c in self.chans.values():
                c.sem = st.enter_context(nc.semaphore("c_" + c.name))
            per = {e: [o for o in ops if o.eng == e] for e in ENGS}

            block = st.enter_context(nc.Block())

            def run(eng_name, eng):
                waited = {}
                for o in per[eng_name]:
                    need = {}
                    for c, v in o.dw.items():
                        need[id(c.sem)] = (c.sem, v)
                    for d in o.deps:
                        if d.eng == o.eng and (o.eng == "pe" or not self.same_engine_sync):
                            continue
                        s, v = esem[d.eng], d.cnt
                        key = id(s)
                        if need.get(key, (None, 0))[1] < v:
                            need[key] = (s, v)
                    for key, (s, v) in need.items():
                        if waited.get(key, 0) < v:
                            eng.wait_ge(s, v)
                            waited[key] = v
                    if o.ny:
                        gen = o.fn(eng)
                        base = o.cnt - o.ny
                        for gi, cur in enumerate(gen):
                            cur.then_inc(esem[eng_name], 1)
                            if gi < o.ny - 1:
                                eng.wait_ge(esem[eng_name], base + gi + 1)
                                waited[id(esem[eng_name])] = base + gi + 1
                        continue
                    ins = o.fn(eng)
                    if o.isdma:
                        ins.then_inc(o.chan.sem, 16)
                    elif o.sig:
                        ins.then_inc(esem[o.eng], 1)
                if eng_name == final_wait_eng:
                    for c in self.chans.values():
                        if c.n:
                            eng.wait_ge(c.sem, 16 * c.n)

            @block.tensor
            def _(e):
                run("pe", e)

            @block.scalar
            def _(e):
                run("act", e)

            @block.vector
            def _(e):
                run("dve", e)

            @block.gpsimd
            def _(e):
                run("pool", e)

            @block.sync
            def _(e):
                run("sp", e)

import math
import contextlib
from concourse.bass_utils import run_bass_kernel_spmd

BIG = 30000.0
EPS = 1e-6


class Cfg:
    def __init__(s, D=2048, NP=8, NO=8, MH=8, AH=8, DFF=8192, PLE=256, NPAGES=128, NPHYS=1280):
        s.D, s.NP, s.NO, s.MH, s.AH, s.DFF, s.PLE, s.NPAGES, s.NPHYS = D, NP, NO, MH, AH, DFF, PLE, NPAGES, NPHYS
        s.KC = D // 128
        s.DK, s.DV, s.DH, s.TS = 64, 128, 128, 8
        s.c_mq = 0
        s.c_mk = MH * 64
        s.c_mv = 2 * MH * 64
        s.c_mo = s.c_mv + MH * 128
        s.c_mi = s.c_mo + MH * 128
        s.c_mf = s.c_mi + MH
        s.c_aq = s.c_mf + MH
        s.c_ak = s.c_aq + AH * 128
        s.c_av = s.c_ak + AH * 128
        s.DIN = s.c_av + AH * 128
        s.MIXW = MH * 128 + AH * 128
        s.KM = s.MIXW // 128
        s.NTOK = 128 * (NP + NO) + 8
        s.NOW = 128 * NO + 8
        s.NPB, s.NOB = NP // 2, NO // 2
        s.NBLK = s.NPB + s.NOB
        s.NB = NPAGES // 2
        s.OHW = 768
        s.NSEL = 8 * AH * 6


class V:
    def __init__(s, ap, r):
        s.ap, s.r = ap, r

    def __getitem__(s, k):
        return s.ap[k]


def t5_bucket_np(rel):
    n = np.maximum(rel, 0)
    nf = np.maximum(n, 1).astype(np.float32)
    large = 16 + (np.log(nf / np.float32(16)) / np.float32(math.log(128 / 16)) * np.float32(16)).astype(np.int32)
    large = np.minimum(large, 31)
    return np.where(n < 16, n, large)


def host_consts(c):
    k = {}
    k["ident"] = np.eye(128, dtype=np.float32)
    s_ = np.arange(128)
    k["maskle"] = (s_[:, None] <= s_[None, :]).astype(np.float32)
    rel = np.arange(c.OHW) - 255
    b = t5_bucket_np(rel)
    oh = np.zeros((33, c.OHW), np.float32)
    for i in range(c.OHW):
        if rel[i] < 0:
            oh[32, i] = 1.0
        else:
            oh[b[i], i] = 1.0
    k["ohlong"] = oh
    bs = np.full((c.NOB, 8), -BIG, np.float32)
    for v in range(c.NOB):
        bs[v, : c.NPB + v] = 0.0
    k["bstruct"] = bs.reshape(1, c.NOB * 8)
    pm = np.zeros((1, 8), np.float32)
    pm[0, : c.NPB] = 1.0
    k["prefmask"] = pm
    nhp = c.MH // 2
    sel = np.zeros((c.MH, nhp * 128), np.float32)
    for hp in range(nhp):
        for p in range(128):
            sel[2 * hp + p // 64, hp * 128 + p] = 1.0
    k["sel"] = sel
    addc = np.zeros((128, c.NSEL), np.float32)
    col = 0
    for q in range(8):
        for h in range(c.AH):
            for s in range(3):
                for u in range(2):
                    addc[:, col] = np.arange(128) * c.AH + h
                    col += 1
    k["addc"] = addc
    addc2 = np.zeros((128, c.NSEL // 2), np.float32)
    col = 0
    for q in range(8):
        for h in range(c.AH):
            for s in range(3):
                addc2[:, col] = h * 64 + (np.arange(128) % 64)
                col += 1
    k["addc2"] = addc2
    k["iotap"] = np.arange(128, dtype=np.float32).reshape(128, 1)
    k["iotab"] = np.broadcast_to(np.arange(c.NB, dtype=np.float32), (8, c.NB)).copy()
    e8 = np.zeros((8, 8, c.AH * 6), np.float32)
    for q in range(8):
        e8[q, q, :] = 1.0
    k["eye8x"] = e8.reshape(8, 8 * c.AH * 6)
    return k


def build(c):
    import os as _os
    STOP = _os.environ.get("MK_STOP", "")

    def chk(name):
        if STOP == name:
            P.frozen = True
    nc = bass.Bass("TRN2", target_bir_lowering=False)
    P = Prog(nc)
    st = contextlib.ExitStack()
    D, KC, NP, NO, MH, AH, DFF, PLE = c.D, c.KC, c.NP, c.NO, c.MH, c.AH, c.DFF, c.PLE
    NTOK, NOW, KM = c.NTOK, c.NOW, c.KM
    NT = NP + NO + 1
    NB, NPG, NSEL = c.NB, c.NPAGES, c.NSEL
    NOT = NO + 1

    def din(name, shape, dt=F32):
        return nc.dram_tensor(name, list(shape), dt, kind="ExternalInput").ap()

    def dout(name, shape, dt=F32):
        return nc.dram_tensor(name, list(shape), dt, kind="ExternalOutput").ap()

    hc = host_consts(c)
    x_all = din("x_all", [NTOK, D])
    p_all = din("p_all", [NOW, PLE])
    w_in = din("w_in", [D, c.DIN])
    w_out = din("w_out", [c.MIXW, D])
    w_up = din("w_up", [D, DFF])
    w_down = din("w_down", [DFF, D])
    w_pg = din("w_pg", [D, D])
    w_pp = din("w_pp", [PLE, D])
    gvec = din("gvec", [4, D])
    g_mh = din("g_mh", [1, MH * 128])
    b_i = din("b_i", [MH, 1])
    b_f = din("b_f", [MH, 1])
    relt = din("relt", [33, AH])
    cache_k = din("cache_k", [c.NPHYS * 128, AH * 128])
    cache_kv = din("cache_kv", [c.NPHYS * AH * 128, 256])
    pt = din("pt", [1, c.NPAGES], I32)
    sC = din("sC", [MH * 64, 128])
    sn = din("sn", [MH * 64, 1])
    sm = din("sm", [MH, 1])
    flag = din("flag", [1, 1])
    cin = {k: din("c_" + k, v.shape) for k, v in hc.items()}

    y_o = dout("y_o", [NOW, D])
    k_o = dout("k_o", [NOW, AH * 128])
    v_o = dout("v_o", [NOW, AH * 128])
    Cp_o = dout("Cp_o", [MH * 64, 128])
    np_o = dout("np_o", [MH * 64, 1])
    mp_o = dout("mp_o", [MH, 1])
    Cs_o = dout("Cs_o", [MH * 64, 128])
    ns_o = dout("ns_o", [MH * 64, 1])
    ms_o = dout("ms_o", [MH, 1])

    _cnt = [0]

    def sbt(shape, dt=F32, name=None):
        _cnt[0] += 1
        nm = name or f"t{_cnt[0]}"
        t = st.enter_context(nc.sbuf_tensor(nm, list(shape), dt))
        return V(t[:], Res(nm))

    class Region:
        def __init__(s, nbytes, name):
            s.t = st.enter_context(nc.sbuf_tensor(name, [128, nbytes // 4], F32))
            s.r = Res(name)
            s.off = 0
            s.n = nbytes
            s.name = name

        def reset(s):
            s.off = 0

        def barrier(s):
            P.op("pool", lambda e: e.memset(s.t[0:1, 0:1], 0.0), writes=[s.r])
            s.off = 0

        def get(s, shape, dt=F32, name="v"):
            esz = 4 if dt in (F32, I32) else 2
            per = int(np.prod(shape[1:])) * esz
            per4 = (per + 3) // 4
            assert s.off + per4 * 4 <= s.n, (s.name, name, s.off, per4 * 4, s.n)
            ap = s.t[:, s.off // 4: s.off // 4 + per4]
            s.off += per4 * 4
            if dt != F32:
                ap = ap.bitcast(dt)
            n = int(np.prod(shape[1:]))
            ap = ap[:, 0:n]
            if len(shape) == 3:
                ap = ap.rearrange("p (a b) -> p a b", b=shape[2])
            elif len(shape) == 4:
                ap = ap.rearrange("p (a b c) -> p a b c", b=shape[2], c=shape[3])
            ap = ap[0:shape[0]]
            _cnt[0] += 1
            return V(ap, s.r.k(f"{name}{_cnt[0]}"))

    R1B = max(KC * NTOK * 2, NOT * D * 4)
    R2B = max(KM, KC) * NOW * 2
    R1 = Region(R1B, "R1")
    R2 = Region(R2B, "R2")
    R3B = 88 * 1024
    R3 = Region(R3B, "R3")

    psb = []
    for i in range(8):
        t = st.enter_context(nc.psum_tensor(f"ps{i}", [128, 512], F32))
        psb.append(V(t[:], Res(f"ps{i}")))
        psb[-1].r.excl = True
    _psi = [0]
    reserved = set()

    def nps():
        while True:
            i = _psi[0] % 8
            _psi[0] += 1
            if i not in reserved:
                return psb[i]

    def psbf(p):
        return p.ap.bitcast(BF16)

    ident = sbt([128, 128], F32, "ident")
    identb = sbt([128, 128], BF16, "identb")
    maskle = sbt([128, 128], F32, "maskle")
    reltt = sbt([33, AH], F32, "reltt")
    bstruct = sbt([128, c.NOB * 8], F32, "bstruct")
    prefmask = sbt([128, 8], F32, "prefmask")
    flagc = sbt([128, 1], F32, "flagc")
    iotap = sbt([128, 1], F32, "iotap")
    iotab = sbt([8, c.NB], F32, "iotab")
    onesf = sbt([128, 8], F32, "onesf")
    onesb = sbt([128, 8], BF16, "onesb")
    bi_t = sbt([MH, 1], F32, "bi_t")
    bf_t = sbt([MH, 1], F32, "bf_t")
    nbf_t = sbt([MH, 1], F32, "nbf_t")
    sm_t = sbt([MH, 1], F32, "sm_t")
    t31bc = sbt([128, AH], F32, "t31bc")
    bbv = sbt([128, c.NOB * 8], F32, "bbv")
    qTs = sbt([128, AH, 8], BF16, "qTs")
    kTs = sbt([128, AH, 8], BF16, "kTs")
    T256s = sbt([8, AH, 128], F32, "T256s")
    T0s = sbt([8, AH, 8], F32, "T0s")
    epT = sbt([128, NT, MH], F32, "epT")
    flT = sbt([128, NT, MH], F32, "flT")
    wpb = sbt([128, MH // 2, NT + 1], F32, "wpb")
    mouts = sbt([MH, 2], F32, "mouts")

    def cload(dst, src, eng="sp"):
        P.dma(eng, "const", lambda e, d=dst, s_=src: e.dma_start(out=d.ap, in_=s_), writes=[dst.r])

    cload(ident, cin["ident"])
    cload(maskle, cin["maskle"])
    cload(reltt, relt)
    cload(bstruct, cin["bstruct"].partition_broadcast(128))
    cload(prefmask, cin["prefmask"].partition_broadcast(128))
    cload(flagc, flag.partition_broadcast(128))
    cload(iotap, cin["iotap"])
    cload(iotab, cin["iotab"])
    cload(bi_t, b_i)
    cload(bf_t, b_f)
    cload(sm_t, sm)
    P.op("dve", lambda e: e.tensor_copy(out=identb.ap, in_=ident.ap), reads=[ident.r], writes=[identb.r])
    P.op("pool", lambda e: e.memset(onesf.ap, 1.0), writes=[onesf.r])
    epsD = sbt([128, 1], F32, "epsD")
    epsV = sbt([128, 1], F32, "epsV")
    one1 = sbt([128, 1], F32, "one1")
    P.op("pool", lambda e: e.memset(epsD.ap, float(D * EPS)), writes=[epsD.r])
    P.op("pool", lambda e: e.memset(epsV.ap, float(128 * EPS)), writes=[epsV.r])
    P.op("pool", lambda e: e.memset(one1.ap, 1.0), writes=[one1.r])
    P.op("pool", lambda e: e.memset(onesb.ap, 1.0), writes=[onesb.r])
    P.op("dve", lambda e: e.tensor_scalar(out=nbf_t.ap, in0=bf_t.ap, scalar1=-1.0, scalar2=None, op0=ALU.mult),
         reads=[bf_t.r], writes=[nbf_t.r])
    fm1 = sbt([128, 1], F32, "fm1")
    P.op("dve", lambda e: e.tensor_scalar(out=fm1.ap, in0=flagc.ap, scalar1=-1.0, scalar2=BIG, op0=ALU.add, op1=ALU.mult),
         reads=[flagc.r], writes=[fm1.r])
    for v in range(c.NOB):
        P.op("dve", lambda e, v=v: e.scalar_tensor_tensor(out=bbv[:, v * 8:(v + 1) * 8], in0=prefmask.ap, scalar=fm1[:, 0:1],
                                                          in1=bstruct[:, v * 8:(v + 1) * 8], op0=ALU.mult, op1=ALU.add),
             reads=[prefmask.r, fm1.r, bstruct.r], writes=[bbv.r])

    def load_g(slot, row):
        g = R3.get([128, D], F32, "gbc")
        P.dma("sp", f"g{slot}", lambda e: e.dma_start(out=g.ap, in_=gvec[row:row + 1, :].partition_broadcast(128)), writes=[g.r])
        P.op("pool", lambda e: e.tensor_scalar(out=g.ap, in0=g.ap, scalar1=float(math.sqrt(D)), scalar2=None, op0=ALU.mult),
             reads=[g.r], writes=[g.r])
        return g

    def evac_copy(eng, out_ap, in_ap, rd, wr):
        if eng == "act":
            P.op("act", lambda e: e.copy(out=out_ap, in_=in_ap), reads=rd, writes=wr)
        else:
            P.op(eng, lambda e: e.tensor_copy(out=out_ap, in_=in_ap), reads=rd, writes=wr)

    _alt = [0]

    def alt_eng():
        _alt[0] += 1
        return "act" if _alt[0] % 2 else "dve"

    def norm_tile(src_ap, src_r, r, g, xn_ap, xn_r, ssq, rstd, junk):
        P.op("act", lambda e: e.activation(out=junk[0:r, :], in_=src_ap, func=AF.Square, accum_out=ssq[0:r, 0:1]),
             reads=[src_r], writes=[junk.r, ssq.r])
        P.op("act", lambda e: e.activation(out=rstd[0:r, 0:1], in_=ssq[0:r, 0:1], func=AF.Ln, bias=epsD[0:r, 0:1], scale=1.0), reads=[ssq.r, epsD.r], writes=[rstd.r])
        P.op("act", lambda e: e.activation(out=rstd[0:r, 0:1], in_=rstd[0:r, 0:1], func=AF.Exp, scale=-0.5), reads=[rstd.r], writes=[rstd.r])
        P.op("dve", lambda e: e.scalar_tensor_tensor(out=xn_ap, in0=src_ap, scalar=rstd[0:r, 0:1], in1=g[0:r, :],
                                                     op0=ALU.mult, op1=ALU.mult), reads=[src_r, rstd.r, g.r], writes=[xn_r])

    def transpose_into(xn, r, nch, dstT, tok0, dst_r):
        j0 = 0
        while j0 < nch:
            n = min(8, nch - j0)
            p = nps()
            pb = psbf(p).rearrange("p (a b) -> p a b", b=128)

            def f(e, j0=j0, n=n, pb=pb):
                ins = None
                for j in range(n):
                    ins = e.transpose(out=pb[:, j, 0:r], in_=xn[0:r, (j0 + j) * 128:(j0 + j + 1) * 128], identity=identb[0:r, 0:r])
                return ins
            P.op("pe", f, reads=[xn.r, identb.r], writes=[p.r])
            evac_copy(alt_eng(), dstT[:, j0:j0 + n, tok0:tok0 + r], pb[:, 0:n, 0:r], [p.r], [dst_r])
            j0 += n

    wslots = {}

    def wload(pool_name, nslots, region, shape, pieces):
        key = pool_name
        if key not in wslots:
            wslots[key] = [[region.get(shape, BF16, name=f"w{pool_name}{i}") for i in range(nslots)], 0]
        sl = wslots[key]
        w = sl[0][sl[1] % nslots]
        ch = f"w{pool_name}{sl[1] % nslots}"
        sl[1] += 1
        for pi_, (c0, src) in enumerate(pieces):
            ncol = src.shape[1]
            nk_ = src.shape[0] // 128
            P.dma("pool", ch, lambda e, c0=c0, src=src, ncol=ncol, nk_=nk_: e.dma_start(
                out=w[:, 0:nk_, c0:c0 + ncol], in_=src.rearrange("(k p) c -> p k c", p=128)), writes=[w.r.k(pi_)] if len(pieces) > 1 else [w.r])
        return w

    def mm_tok(p, r, ncol, xT, tok0, nk, w, c0, rd):
        def f(e):
            ins = None
            for k in range(nk):
                ins = e.matmul(p[0:r, 0:ncol], lhsT=xT[:, k, tok0:tok0 + r], rhs=w[:, k, c0:c0 + ncol], start=(k == 0), stop=(k == nk - 1))
            return ins
        P.op("pe", f, reads=rd + [w.r], writes=[p.r])

    def mm_feat(p, m, n, w, c0, nk, xT, tok0, rd):
        def f(e):
            ins = None
            for k in range(nk):
                ins = e.matmul(p[0:m, 0:n], lhsT=w[:, k, c0:c0 + m], rhs=xT[:, k, tok0:tok0 + n], start=(k == 0), stop=(k == nk - 1))
            return ins
        P.op("pe", f, reads=rd + [w.r], writes=[p.r])

    tiles = [(128 * t, 128) for t in range(NP + NO)] + [(128 * (NP + NO), 8)]
    own_tiles = list(range(NP, NP + NO + 1))
    def chunks(t0, t1):
        out = []
        a = t0
        while a < t1:
            n = min(512, t1 - a)
            out.append((a, n))
            a += n
        return out
    TOK_P0, TOK_O0, TOK_S0 = 0, 128 * NP, 128 * (NP + NO)
    ch_all = chunks(0, TOK_O0) + chunks(TOK_O0, TOK_S0) + [(TOK_S0, 8)]
    ch_own = chunks(TOK_O0, TOK_S0) + [(TOK_S0, 8)]

    xnT = R1.get([128, KC, NTOK], BF16, "xnT")
    xnT_k = [xnT.r.k(t) for t in range(NT)]
    vsf = R1.get([8, AH, 128], F32, "vsf")
    g0 = load_g(0, 0)
    xs_ = [R3.get([128, D], F32, "xs") for _ in range(2)]
    xn_ = [R3.get([128, D], BF16, "xn") for _ in range(2)]
    junk = R3.get([128, D], BF16, "junk")
    ssq = sbt([128, 2], F32, "ssq")
    rstd = sbt([128, 2], F32, "rstd")
    for t, (tok0, r) in enumerate(tiles):
        xs, xn = xs_[t % 2], xn_[t % 2]
        P.dma("sp", f"xs{t % 2}", lambda e, xs=xs, tok0=tok0, r=r: e.dma_start(out=xs[0:r, :], in_=x_all[tok0:tok0 + r, :]), writes=[xs.r])
        norm_tile(xs[0:r, :], xs.r, r, g0, xn[0:r, :], xn.r, ssq, rstd, junk)
        transpose_into(xn, r, KC, xnT, tok0, xnT_k[t])

    R3.barrier()
    if STOP == "p0":
        P.emit()
        st.close()
        return nc, hc
    selc = R3.get([MH, (MH // 2) * 128], F32, "selc")
    cload(selc, cin["sel"])
    wg = wload("g", 1, R3, [128, KC, 2 * MH], [(0, w_in[:, c.c_mi:c.c_mi + 2 * MH])])
    NTK = NTOK
    li = R3.get([MH, NTK], F32, "li")
    nb = R3.get([MH, NTK], F32, "nb")
    nb2 = R3.get([MH, NTK], F32, "nb2")
    G = R3.get([MH, NTK], F32, "G")
    ep = R3.get([MH, NTK], F32, "ep")
    fl = R3.get([MH, NTK], F32, "fl")
    for (a, n) in ch_all:
        pi, pf = nps(), nps()
        mm_feat(pi, MH, n, wg, 0, KC, xnT, a, [xnT.r])
        mm_feat(pf, MH, n, wg, MH, KC, xnT, a, [xnT.r])
        P.op("act", lambda e, pi=pi, a=a, n=n: e.activation(out=li[:, a:a + n], in_=pi[0:MH, 0:n], func=AF.Identity, bias=bi_t[:, 0:1], scale=1.0),
             reads=[pi.r, bi_t.r], writes=[li.r])
        P.op("act", lambda e, pf=pf, a=a, n=n: e.activation(out=nb[:, a:a + n], in_=pf[0:MH, 0:n], func=AF.Exp, bias=nbf_t[:, 0:1], scale=-1.0),
             reads=[pf.r, nbf_t.r], writes=[nb.r])
    P.op("act", lambda e: e.activation(out=nb.ap, in_=nb.ap, func=AF.Ln, bias=one1[0:MH, 0:1], scale=1.0), reads=[nb.r, one1.r], writes=[nb.r])
    seqs = [(TOK_P0, 128 * NP), (TOK_O0, 128 * NO), (TOK_S0, 8)]
    cur, oth = nb, nb2
    kk = 1
    maxlen = max(128 * NP, 128 * NO)
    while kk < maxlen:
        def f(e, cur=cur, oth=oth, kk=kk):
            for (a, n) in seqs:
                if kk < n:
                    yield e.tensor_copy(out=oth[:, a:a + kk], in_=cur[:, a:a + kk])
                    yield e.tensor_tensor(out=oth[:, a + kk:a + n], in0=cur[:, a + kk:a + n], in1=cur[:, a:a + n - kk], op=ALU.add)
                else:
                    yield e.tensor_copy(out=oth[:, a:a + n], in_=cur[:, a:a + n])
        P.op("dve", f, reads=[cur.r], writes=[oth.r])
        cur, oth = oth, cur
        kk *= 2
    NBt = cur
    P.op("dve", lambda e: e.tensor_tensor(out=G.ap, in0=li.ap, in1=NBt.ap, op=ALU.add), reads=[li.r, NBt.r], writes=[G.r])
    cm = sbt([MH, NT], F32, "cm")
    Rext = sbt([MH, NT + 3], F32, "Rext")
    negR = sbt([MH, NT + 3], F32, "negR")
    negRl = sbt([MH, NT + 3], F32, "negRl")
    wprev = sbt([MH, NT + 1], F32, "wprev")
    seq_tiles = [(0, NP), (NP, NO), (NP + NO, 1)]
    rofs = [0, NP + 1, NP + NO + 2]
    for si, (t0, ntl) in enumerate(seq_tiles):
        a, n = seqs[si]
        if n >= 128:
            P.op("dve", lambda e, t0=t0, ntl=ntl, a=a, n=n: e.tensor_reduce(out=cm[:, t0:t0 + ntl], in_=G[:, a:a + n].rearrange("p (c l) -> p c l", l=128),
                                                                    axis=AX.X, op=ALU.max), reads=[G.r], writes=[cm.r])
        else:
            P.op("dve", lambda e, t0=t0, a=a, n=n: e.tensor_reduce(out=cm[:, t0:t0 + 1], in_=G[:, a:a + n], axis=AX.X, op=ALU.max),
                 reads=[G.r], writes=[cm.r])
        ro = rofs[si]
        if si == 0:
            P.op("dve", lambda e, ro=ro: e.memset(Rext[:, ro:ro + 1], 0.0), writes=[Rext.r])
        elif si == 1:
            def f(e, ro=ro):
                yield e.tensor_tensor(out=Rext[:, ro:ro + 1], in0=Rext[:, ro - 1:ro], in1=NBt[:, TOK_O0 - 1:TOK_O0], op=ALU.subtract)
                yield e.tensor_scalar(out=Rext[:, ro:ro + 1], in0=Rext[:, ro:ro + 1], scalar1=flagc[0:MH, 0:1], scalar2=None, op0=ALU.mult)
            P.op("dve", f, reads=[Rext.r, NBt.r, flagc.r], writes=[Rext.r])
        else:
            P.op("dve", lambda e, ro=ro: e.tensor_copy(out=Rext[:, ro:ro + 1], in_=sm_t.ap), reads=[sm_t.r], writes=[Rext.r])

        def f(e, ro=ro, t0=t0, ntl=ntl):
            for j in range(ntl):
                yield e.tensor_tensor(out=Rext[:, ro + 1 + j:ro + 2 + j], in0=Rext[:, ro + j:ro + 1 + j], in1=cm[:, t0 + j:t0 + j + 1], op=ALU.max)
        P.op("dve", f, reads=[Rext.r, cm.r], writes=[Rext.r])
        P.op("dve", lambda e, ro=ro, t0=t0, ntl=ntl: e.tensor_tensor(out=wprev[:, t0:t0 + ntl], in0=Rext[:, ro:ro + ntl], in1=Rext[:, ro + 1:ro + 1 + ntl], op=ALU.subtract),
             reads=[Rext.r], writes=[wprev.r])
    P.op("act", lambda e: e.activation(out=wprev[:, 0:NT], in_=wprev[:, 0:NT], func=AF.Exp), reads=[wprev.r], writes=[wprev.r])
    P.op("dve", lambda e: e.tensor_scalar(out=negR.ap, in0=Rext.ap, scalar1=-1.0, scalar2=None, op0=ALU.mult), reads=[Rext.r], writes=[negR.r])
    P.op("dve", lambda e: e.tensor_scalar(out=negRl.ap, in0=Rext.ap, scalar1=-1.0, scalar2=float(math.log(0.125)), op0=ALU.mult, op1=ALU.add),
         reads=[Rext.r], writes=[negRl.r])
    def f(e):
        yield e.tensor_tensor(out=mouts[:, 0:1], in0=Rext[:, rofs[1] + NO:rofs[1] + NO + 1], in1=NBt[:, TOK_S0 - 1:TOK_S0], op=ALU.subtract)
        yield e.tensor_tensor(out=mouts[:, 1:2], in0=Rext[:, rofs[2] + 1:rofs[2] + 2], in1=NBt[:, TOK_S0 + 7:TOK_S0 + 8], op=ALU.subtract)
    P.op("dve", f, reads=[Rext.r, NBt.r], writes=[mouts.r])
    P.dma("sp", "mo", lambda e: e.dma_start(out=mp_o, in_=mouts[:, 0:1]), reads=[mouts.r])
    P.dma("sp", "mo", lambda e: e.dma_start(out=ms_o, in_=mouts[:, 1:2]), reads=[mouts.r])
    for si, (t0, ntl) in enumerate(seq_tiles):
        ro = rofs[si]
        for j in range(ntl):
            tok0, r = tiles[t0 + j]
            P.op("act", lambda e, tok0=tok0, r=r, ro=ro, j=j: e.activation(out=ep[:, tok0:tok0 + r], in_=G[:, tok0:tok0 + r], func=AF.Exp,
                                                                        bias=negRl[:, ro + 1 + j:ro + 2 + j], scale=1.0), reads=[G.r, negRl.r], writes=[ep.r])
            P.op("act", lambda e, tok0=tok0, r=r, ro=ro, j=j: e.activation(out=fl[:, tok0:tok0 + r], in_=NBt[:, tok0:tok0 + r], func=AF.Exp,
                                                                        bias=negR[:, ro + 1 + j:ro + 2 + j], scale=1.0), reads=[NBt.r, negR.r], writes=[fl.r])
    for (src, dst) in ((ep, epT), (fl, flT)):
        p = nps()
        pv = p[:, 0:NT * MH].rearrange("p (t h) -> p t h", h=MH)

        def f(e, src=src, pv=pv):
            ins = None
            for t, (tok0, r) in enumerate(tiles):
                ins = e.transpose(out=pv[0:r, t, :], in_=src[:, tok0:tok0 + r], identity=ident[0:MH, 0:MH])
            return ins
        P.op("pe", f, reads=[src.r, ident.r], writes=[p.r])
        P.op("dve", lambda e, dst=dst, pv=pv: e.tensor_copy(out=dst[:, 0:NT - 1, :], in_=pv[:, 0:NT - 1, :]), reads=[p.r], writes=[dst.r])
        P.op("dve", lambda e, dst=dst, pv=pv: e.tensor_copy(out=dst[0:8, NT - 1, :], in_=pv[0:8, NT - 1, :]), reads=[p.r], writes=[dst.r])
    for hp in range(MH // 2):
        p = nps()
        P.op("pe", lambda e, p=p, hp=hp: e.matmul(p[:, 0:NT], lhsT=selc[:, hp * 128:(hp + 1) * 128], rhs=wprev[:, 0:NT], start=True, stop=True),
             reads=[selc.r, wprev.r], writes=[p.r])
        evac_copy("dve", wpb[:, hp, 0:NT], p[:, 0:NT], [p.r], [wpb.r])
    P.op("pool", lambda e: e.memset(wpb[:, :, NT:NT + 1], 1.0), writes=[wpb.r])

    if STOP == "gates":
        P.emit()
        st.close()
        return nc, hc
    mixT = R2.get([128, KM, NOW], BF16, "mixT")
    R3.barrier()
    wslots.clear()
    gmh = R3.get([128, MH * 128], F32, "gmh")
    P.dma("sp", "gmh", lambda e: e.dma_start(out=gmh.ap, in_=g_mh.partition_broadcast(128)), writes=[gmh.r])
    P.op("pool", lambda e: e.tensor_scalar(out=gmh.ap, in0=gmh.ap, scalar1=float(math.sqrt(128.0)), scalar2=None, op0=ALU.mult),
         reads=[gmh.r], writes=[gmh.r])
    qTm = R3.get([128, NOW], BF16, "qTm")
    kTz = R3.get([128, 2, NOW], BF16, "kTz")
    Kt = R3.get([128, NT, 128], BF16, "Kt")
    Va = R3.get([128, NT, 2, 129], BF16, "Va")
    gsig = R3.get([128, NOT, 256], F32, "gsig")
    Cf = R3.get([128, 129], F32, "Cf")
    Cbz = R3.get([128, 2, 129], BF16, "Cbz")
    StT = [R3.get([128, 2, 128], BF16, "StT") for _ in range(2)]
    omb = [R3.get([128, 256], BF16, "omb") for _ in range(2)]
    sml = [R3.get([128, 16], F32, "sml") for _ in range(2)]
    junk2 = R3.get([128, 128], F32, "junk2")
    Cin = R3.get([128, 129], F32, "Cin")
    P.op("pool", lambda e: e.memset(Va[:, :, :, 128:129], 1.0), writes=[Va.r])
    P.op("pool", lambda e: e.memset(kTz.ap, 0.0), writes=[kTz.r])
    P.op("pool", lambda e: e.memset(Cbz.ap, 0.0), writes=[Cbz.r])
    own_off = lambda t: 128 * (t - NP)

    ptb = sbt([128, NPG], I32, "ptb")
    ptf = sbt([128, NPG], F32, "ptf")
    idxp = sbt([128, NPG], I32, "idxp")
    kmsh = sbt([128, AH * NB], BF16, "kmsh")
    kmsl = sbt([128, AH * NB], BF16, "kmsl")
    kmst = R3.get([128, AH * NB], F32, "kmst")
    kpg = [R3.get([128, AH * 128], F32, "kpg") for _ in range(2)]

    def pass1():
        P.dma("sp", "ptl", lambda e: e.dma_start(out=ptb.ap, in_=pt.partition_broadcast(128)), writes=[ptb.r])

        def f(e):
            yield e.tensor_copy(out=ptf.ap, in_=ptb.ap)
            yield e.tensor_scalar(out=ptf.ap, in0=ptf.ap, scalar1=128.0, scalar2=iotap[:, 0:1], op0=ALU.mult, op1=ALU.add)
            yield e.tensor_copy(out=idxp.ap, in_=ptf.ap)
        P.op("dve", f, reads=[ptb.r, iotap.r], writes=[ptf.r, idxp.r])
        P.op("dve", lambda e: e.tensor_copy(out=ptf.ap, in_=ptb.ap), reads=[ptb.r, idxp.r], writes=[ptf.r])
        pKa, pKb = nps(), nps()
        rs_i = [psb.index(pKa), psb.index(pKb)]
        reserved.update(rs_i)
        HB = (AH + 1) // 2
        def page_dma(j):
            kp = kpg[j % 2]
            P.dma("pool", f"kpg{j % 2}", lambda e, kp=kp, j=j: e.indirect_dma_start(out=kp.ap, out_offset=None, in_=cache_k,
                                                                        in_offset=bass.IndirectOffsetOnAxis(ap=idxp[:, j:j + 1], axis=0)),
                  reads=[idxp.r], writes=[kp.r])
        page_dma(0)
        for j in range(NPG):
            kp = kpg[j % 2]
            if j + 1 < NPG:
                page_dma(j + 1)

            def f(e, kp=kp, j=j):
                ins = None
                for h in range(AH):
                    pk_ = pKa if h < HB else pKb
                    col = (h % HB) * NPG + j
                    ins = e.matmul(pk_[:, col:col + 1], lhsT=kp[:, 128 * h:128 * h + 128], rhs=onesf[:, 0:1], start=True, stop=True)
                return ins
            P.op("pe", f, reads=[kp.r, onesf.r], writes=[pKa.r, pKb.r])
            yield

        def f(e):
            for hb, pk_ in enumerate((pKa, pKb)):
                nh = min(HB, AH - hb * HB)
                if nh <= 0:
                    continue
                pv_ = pk_[:, 0:nh * NPG].rearrange("p (h n u) -> p h n u", n=NB, u=2)
                dst = kmst[:, hb * HB * NB:(hb * HB + nh) * NB].rearrange("p (h n) -> p h n", n=NB)
                yield e.tensor_copy(out=dst, in_=pv_[:, :, :, 0])
                yield e.tensor_tensor(out=dst, in0=dst, in1=pv_[:, :, :, 1], op=ALU.add)
            yield e.tensor_scalar(out=kmst.ap, in0=kmst.ap, scalar1=1.0 / 256.0, scalar2=None, op0=ALU.mult)
            yield e.tensor_copy(out=kmsh.ap, in_=kmst.ap)
            yield e.tensor_tensor(out=kmst.ap, in0=kmst.ap, in1=kmsh.ap, op=ALU.subtract)
            yield e.tensor_copy(out=kmsl.ap, in_=kmst.ap)
        P.op("dve", f, reads=[pKa.r, pKb.r], writes=[kmst.r, kmsh.r, kmsl.r])
        for x_ in rs_i:
            reserved.discard(x_)

    p1 = pass1()

    def p1step(n=1):
        for _ in range(n):
            try:
                next(p1)
            except StopIteration:
                return

    def do_pair(hp):
        wA = wload("m", 2, R3, [128, KC, 256], [(0, w_in[:, c.c_mq + 128 * hp:c.c_mq + 128 * hp + 128]),
                                                 (128, w_in[:, c.c_mk + 128 * hp:c.c_mk + 128 * hp + 128])])
        for (a, n) in ch_own:
            p = nps()
            mm_feat(p, 128, n, wA, 0, KC, xnT, a, [xnT.r])
            evac_copy(alt_eng(), qTm[:, a - TOK_O0:a - TOK_O0 + n], p[:, 0:n], [p.r], [qTm.r])
            p = nps()
            mm_feat(p, 128, n, wA, 128, KC, xnT, a, [xnT.r])
            evac_copy("act", kTz[0:64, 0, a - TOK_O0:a - TOK_O0 + n], p[0:64, 0:n], [p.r], [kTz.r])
            evac_copy("dve", kTz[64:128, 1, a - TOK_O0:a - TOK_O0 + n], p[64:128, 0:n], [p.r], [kTz.r])
        chk("a1")
        for t, (tok0, r) in enumerate(tiles):
            p = nps()
            mm_tok(p, r, 128, xnT, tok0, KC, wA, 128, [xnT_k[t]])
            p1step()
            P.op("dve", lambda e, p=p, t=t, r=r, hp=hp: e.tensor_tensor(
                out=Kt[0:r, t, :].rearrange("p (j d) -> p j d", d=64), in0=p[0:r, 0:128].rearrange("p (j d) -> p j d", d=64),
                in1=epT[0:r, t, 2 * hp:2 * hp + 2].unsqueeze(2).to_broadcast([r, 2, 64]), op=ALU.mult),
                reads=[p.r, epT.r], writes=[Kt.r])
        chk("a2")
        wB = wload("m", 2, R3, [128, KC, 256], [(0, w_in[:, c.c_mv + 256 * hp:c.c_mv + 256 * hp + 256])])
        for t, (tok0, r) in enumerate(tiles):
            p = nps()
            mm_tok(p, r, 256, xnT, tok0, KC, wB, 0, [xnT_k[t]])
            p1step()
            evac_copy(alt_eng(), Va[0:r, t, :, 0:128], p[0:r, 0:256].rearrange("p (j d) -> p j d", d=128), [p.r], [Va.r])
        chk("a3")
        wC = wload("m", 2, R3, [128, KC, 256], [(0, w_in[:, c.c_mo + 256 * hp:c.c_mo + 256 * hp + 256])])
        for t in own_tiles:
            tok0, r = tiles[t]
            p = nps()
            mm_tok(p, r, 256, xnT, tok0, KC, wC, 0, [xnT_k[t]])
            P.op("act", lambda e, p=p, t=t, r=r: e.activation(out=gsig[0:r, t - NP, :], in_=p[0:r, 0:256], func=AF.Sigmoid), reads=[p.r], writes=[gsig.r])
            P.op("pool", lambda e, t=t, r=r, hp=hp: e.tensor_tensor(out=gsig[0:r, t - NP, :], in0=gsig[0:r, t - NP, :], in1=gmh[0:r, 256 * hp:256 * hp + 256], op=ALU.mult),
                 reads=[gsig.r, gmh.r], writes=[gsig.r])

        chk("a4")

        def state_update(t, r, last):
            p = nps()

            def f(e, p=p, t=t, r=r):
                ins = None
                for j in range(2):
                    ins = e.matmul(p[64 * j:64 * j + 64, 0:129], lhsT=Kt[0:r, t, 64 * j:64 * j + 64], rhs=Va[0:r, t, j, :], start=True, stop=True)
                return ins
            P.op("pe", f, reads=[Kt.r, Va.r], writes=[p.r])
            P.op("dve", lambda e, p=p, t=t: e.scalar_tensor_tensor(out=Cf.ap, in0=Cf.ap, scalar=wpb[:, hp, t:t + 1], in1=p[:, 0:129], op0=ALU.mult, op1=ALU.add),
                 reads=[Cf.r, wpb.r, p.r], writes=[Cf.r])
            if not last:
                P.op("act", lambda e, t=t: e.activation(out=Cbz[0:64, 0, :], in_=Cf[0:64, :], func=AF.Identity, scale=wpb[0:64, hp, t + 1:t + 2]), reads=[Cf.r, wpb.r], writes=[Cbz.r])
                P.op("act", lambda e, t=t: e.activation(out=Cbz[64:128, 1, :], in_=Cf[64:128, :], func=AF.Identity, scale=wpb[64:128, hp, t + 1:t + 2]), reads=[Cf.r, wpb.r], writes=[Cbz.r])

        def chunk(t, r, ci):
            tok0 = tiles[t][0]
            oo = own_off(t)
            S, om, sm_ = StT[ci % 2], omb[ci % 2], sml[ci % 2]
            pS = nps()

            def f(e, pS=pS):
                ins = None
                for j in range(2):
                    ins = e.matmul(pS[0:r, j * 128:j * 128 + r], lhsT=kTz[:, j, oo:oo + r], rhs=qTm[:, oo:oo + r], start=True, stop=True)
                return ins
            P.op("pe", f, reads=[kTz.r, qTm.r], writes=[pS.r])
            chk("c1")
            for j in range(2):
                P.op("dve", lambda e, j=j, pS=pS, S=S: e.scalar_tensor_tensor(out=S[0:r, j, 0:r], in0=pS[0:r, j * 128:j * 128 + r], scalar=epT[0:r, t, 2 * hp + j:2 * hp + j + 1],
                                                                          in1=maskle[0:r, 0:r], op0=ALU.mult, op1=ALU.mult), reads=[pS.r, epT.r, maskle.r], writes=[S.r])
            chk("c2")
            pX = nps()
            pXv = pX[:, 0:512].rearrange("p (j d) -> p j d", d=256)

            def f(e, pXv=pXv, S=S):
                ins = None
                for j in range(2):
                    e.matmul(pXv[0:r, j, 0:129], lhsT=qTm[:, oo:oo + r], rhs=Cbz[:, j, :], start=True, stop=False)
                    ins = e.matmul(pXv[0:r, j, 0:129], lhsT=S[0:r, j, 0:r], rhs=Va[0:r, t, j, :], start=False, stop=True)
                return ins
            P.op("pe", f, reads=[qTm.r, Cbz.r, S.r, Va.r], writes=[pX.r])
            chk("c3")
            def f(e, pXv=pXv, sm_=sm_):
                yield e.tensor_scalar(out=sm_[0:r, 0:2], in0=pXv[0:r, :, 128], scalar1=-1.0, scalar2=None, op0=ALU.mult)
                yield e.tensor_tensor(out=sm_[0:r, 0:2], in0=sm_[0:r, 0:2], in1=pXv[0:r, :, 128], op=ALU.max)
                yield e.tensor_tensor(out=sm_[0:r, 0:2], in0=sm_[0:r, 0:2], in1=flT[0:r, t, 2 * hp:2 * hp + 2], op=ALU.max)
                yield e.reciprocal(out=sm_[0:r, 2:4], in_=sm_[0:r, 0:2])
            P.op("dve", f, reads=[pX.r, flT.r], writes=[sm_.r.k("a")])
            chk("c4")
            for j in range(2):
                P.op("act", lambda e, j=j, pXv=pXv, sm_=sm_: e.activation(out=junk2[0:r, :], in_=pXv[0:r, j, 0:128], func=AF.Square, scale=sm_[0:r, 2 + j:3 + j],
                                                                     accum_out=sm_[0:r, 4 + j:5 + j]), reads=[pX.r, sm_.r.k("a")], writes=[junk2.r, sm_.r.k(f"b{j}")])

            chk("c5")
            P.op("act", lambda e, sm_=sm_: e.activation(out=sm_[0:r, 6:8], in_=sm_[0:r, 4:6], func=AF.Ln, bias=epsV[0:r, 0:1], scale=1.0),
                 reads=[sm_.r.k("b0"), sm_.r.k("b1"), epsV.r], writes=[sm_.r.k("c0")])
            P.op("act", lambda e, sm_=sm_: e.activation(out=sm_[0:r, 6:8], in_=sm_[0:r, 6:8], func=AF.Exp, scale=-0.5),
                 reads=[sm_.r.k("c0")], writes=[sm_.r.k("c0")])
            P.op("dve", lambda e, sm_=sm_: e.tensor_tensor(out=sm_[0:r, 8:10], in0=sm_[0:r, 6:8], in1=sm_[0:r, 2:4], op=ALU.mult),
                 reads=[sm_.r.k("a"), sm_.r.k("c0")], writes=[sm_.r.k("c")])
            for j in range(2):
                P.op("dve", lambda e, j=j, pXv=pXv, sm_=sm_, om=om: e.scalar_tensor_tensor(out=om[0:r, j * 128:(j + 1) * 128], in0=pXv[0:r, j, 0:128], scalar=sm_[0:r, 8 + j:9 + j],
                                                                                in1=gsig[0:r, t - NP, j * 128:(j + 1) * 128], op0=ALU.mult, op1=ALU.mult),
                     reads=[pX.r, sm_.r.k("c"), gsig.r], writes=[om.r])
            chk("c7")
            pT = nps()
            pTb = psbf(pT).rearrange("p (a b) -> p a b", b=128)

            def f(e, pTb=pTb, om=om):
                ins = None
                for j in range(2):
                    ins = e.transpose(out=pTb[:, j, 0:r], in_=om[0:r, j * 128:(j + 1) * 128], identity=identb[0:r, 0:r])
                return ins
            P.op("pe", f, reads=[om.r, identb.r], writes=[pT.r])
            evac_copy("act", mixT[:, 2 * hp:2 * hp + 2, oo:oo + r], pTb[:, 0:2, 0:r], [pT.r], [mixT.r.k(2 * hp)])

        P.op("pool", lambda e: e.memset(Cf.ap, 0.0), writes=[Cf.r])
        for t in range(NP):
            state_update(t, 128, True)
        chk("a5")
        P.op("dve", lambda e: e.tensor_scalar(out=Cf.ap, in0=Cf.ap, scalar1=flagc[:, 0:1], scalar2=None, op0=ALU.mult), reads=[Cf.r, flagc.r], writes=[Cf.r])
        P.op("act", lambda e: e.activation(out=Cbz[0:64, 0, :], in_=Cf[0:64, :], func=AF.Identity, scale=wpb[0:64, hp, NP:NP + 1]), reads=[Cf.r, wpb.r], writes=[Cbz.r])
        P.op("act", lambda e: e.activation(out=Cbz[64:128, 1, :], in_=Cf[64:128, :], func=AF.Identity, scale=wpb[64:128, hp, NP:NP + 1]), reads=[Cf.r, wpb.r], writes=[Cbz.r])
        chk("a6")
        for i in range(NO):
            t = NP + i
            chunk(t, 128, i)
            if i == 0:
                chk("a7")
            state_update(t, 128, i == NO - 1)
            if i == 0:
                chk("a8")
        chk("a9")
        for j in range(2):
            h = 2 * hp + j
            P.dma("sp", "co", lambda e, j=j, h=h: e.dma_start(out=Cp_o[64 * h:64 * h + 64, :], in_=Cf[64 * j:64 * j + 64, 0:128]), reads=[Cf.r])
            P.dma("sp", "co", lambda e, j=j, h=h: e.dma_start(out=np_o[64 * h:64 * h + 64, :], in_=Cf[64 * j:64 * j + 64, 128:129]), reads=[Cf.r])
        chk("a10")
        P.dma("sp", "cin", lambda e: e.dma_start(out=Cin[:, 0:128], in_=sC[128 * hp:128 * hp + 128, :]), writes=[Cin.r])
        P.dma("sp", "cin", lambda e: e.dma_start(out=Cin[:, 128:129], in_=sn[128 * hp:128 * hp + 128, :]), writes=[Cin.r])
        P.op("dve", lambda e: e.tensor_copy(out=Cf.ap, in_=Cin.ap), reads=[Cin.r], writes=[Cf.r])
        ts_ = NP + NO
        P.op("act", lambda e: e.activation(out=Cbz[0:64, 0, :], in_=Cf[0:64, :], func=AF.Identity, scale=wpb[0:64, hp, ts_:ts_ + 1]), reads=[Cf.r, wpb.r], writes=[Cbz.r])
        P.op("act", lambda e: e.activation(out=Cbz[64:128, 1, :], in_=Cf[64:128, :], func=AF.Identity, scale=wpb[64:128, hp, ts_:ts_ + 1]), reads=[Cf.r, wpb.r], writes=[Cbz.r])
        chunk(ts_, 8, 0)
        state_update(ts_, 8, True)
        for j in range(2):
            h = 2 * hp + j
            P.dma("sp", "co", lambda e, j=j, h=h: e.dma_start(out=Cs_o[64 * h:64 * h + 64, :], in_=Cf[64 * j:64 * j + 64, 0:128]), reads=[Cf.r])
            P.dma("sp", "co", lambda e, j=j, h=h: e.dma_start(out=ns_o[64 * h:64 * h + 64, :], in_=Cf[64 * j:64 * j + 64, 128:129]), reads=[Cf.r])

    for hp in range(MH // 2):
        do_pair(hp)
    p1step(10 ** 6)

    if STOP == "p1a":
        P.emit()
        st.close()
        return nc, hc
    R3.barrier()
    wslots.clear()
    ohl = R3.get([33, c.OHW], F32, "ohl")
    cload(ohl, cin["ohlong"])
    Tb = {d: R3.get([128, AH, 256], F32, f"Tb{d}") for d in (0, 128, 256)}
    p = nps()
    P.op("pe", lambda e, p=p: e.matmul(p[:, 0:AH], lhsT=ohl[:, 600:728], rhs=reltt.ap, start=True, stop=True), reads=[ohl.r, reltt.r], writes=[p.r])
    evac_copy("dve", t31bc.ap, p[:, 0:AH], [p.r], [t31bc.r])
    P.op("pool", lambda e: e.memset(Tb[0][:, :, 128:256], -BIG), writes=[Tb[0].r])
    P.op("dve", lambda e: e.tensor_copy(out=Tb[256][:, :, 0:128], in_=t31bc.ap.unsqueeze(2).to_broadcast([128, AH, 128])), reads=[t31bc.r], writes=[Tb[256].r])
    for d in (0, 128, 256):
        for k0 in range(0, 256, 64):
            if (d == 0 and k0 >= 128) or (d == 256 and k0 < 128):
                continue
            p = nps()
            pv = p[:, 0:64 * AH].rearrange("p (k h) -> p k h", h=AH)

            def f(e, d=d, k0=k0, pv=pv):
                ins = None
                for kk in range(64):
                    stt = d - (k0 + kk) + 255
                    ins = e.matmul(pv[:, kk, :], lhsT=ohl[:, stt:stt + 128], rhs=reltt.ap, start=True, stop=True)
                return ins
            P.op("pe", f, reads=[ohl.r, reltt.r], writes=[p.r])
            evac_copy(alt_eng(), Tb[d][:, :, k0:k0 + 64], pv.rearrange("p k h -> p h k"), [p.r], [Tb[d].r])
    P.op("dve", lambda e: e.tensor_tensor(out=Tb[256].ap, in0=Tb[256].ap, in1=t31bc.ap.unsqueeze(2).to_broadcast([128, AH, 256]), op=ALU.subtract),
         reads=[Tb[256].r, t31bc.r], writes=[Tb[256].r])
    P.op("dve", lambda e: e.tensor_copy(out=T256s.ap, in_=Tb[256][0:8, :, 128:256]), reads=[Tb[256].r], writes=[T256s.r])
    P.op("dve", lambda e: e.tensor_copy(out=T0s.ap, in_=Tb[0][0:8, :, 0:8]), reads=[Tb[0].r], writes=[T0s.r])

    chk("b0")
    NBLK, NPB = c.NBLK, c.NPB
    qTa = R3.get([128, NOW], BF16, "qTa")
    kTa = R3.get([128, NTOK], BF16, "kTa")
    Vt = R3.get([128, NT, 128], BF16, "Vt")
    Lg2 = [R3.get([128, NBLK * 256], F32, "Lg") for _ in range(2)]
    Pb2 = [R3.get([128, NBLK * 256], BF16, "Pb") for _ in range(2)]
    PT2 = [R3.get([128, NBLK * 2, 128], BF16, "PT") for _ in range(2)]
    kst = [R3.get([128, 128], F32, "kst") for _ in range(2)]
    vst = [R3.get([128, 128], F32, "vst") for _ in range(2)]
    ksum = sbt([128, 8], F32, "ksum")
    kmh = sbt([128, 8], BF16, "kmh")
    kml = sbt([128, 8], BF16, "kml")
    kmt = sbt([128, 8], F32, "kmt")
    gm2 = [sbt([128, 8], F32, f"gm{i}") for i in range(2)]
    top82 = [sbt([128, 8], F32, f"top8{i}") for i in range(2)]
    fbt2 = [sbt([128, 8], F32, f"fbt{i}") for i in range(2)]
    cst = sbt([128, c.NOB * 8], F32, "cst")
    mxs2 = [sbt([128, 4], F32, f"mxs{i}") for i in range(2)]
    obf2 = [sbt([128, 128], BF16, f"obf{i}") for i in range(2)]
    SCL = float(128 ** -0.5)
    def do_head(h):
        wq = wload("a", 1, R3, [128, KC, 384], [(0, w_in[:, c.c_aq + 128 * h:c.c_aq + 128 * h + 128]),
                                                 (128, w_in[:, c.c_ak + 128 * h:c.c_ak + 128 * h + 128]),
                                                 (256, w_in[:, c.c_av + 128 * h:c.c_av + 128 * h + 128])])
        for (a, n) in ch_own:
            p = nps()
            mm_feat(p, 128, n, wq, 0, KC, xnT, a, [xnT.r])
            P.op("act", lambda e, p=p, a=a, n=n: e.activation(out=qTa[:, a - TOK_O0:a - TOK_O0 + n], in_=p[:, 0:n], func=AF.Identity, scale=SCL), reads=[p.r], writes=[qTa.r])
        P.op("dve", lambda e, h=h: e.tensor_copy(out=qTs[:, h, :], in_=qTa[:, 128 * NO:128 * NO + 8]), reads=[qTa.r], writes=[qTs.r])
        chk("b1a")
        P.op("pool", lambda e: e.memset(ksum.ap, 0.0), writes=[ksum.r])
        for (a, n) in ch_all:
            p = nps()
            mm_feat(p, 128, n, wq, 128, KC, xnT, a, [xnT.r])
            if n == 8:
                evac_copy("act", kTa[:, a:a + n], p[:, 0:n], [p.r], [kTa.r])
            else:
                for b0 in range(0, n, 256):
                    blk = (a + b0) // 256
                    P.op("act", lambda e, p=p, a=a, b0=b0, blk=blk: e.activation(out=kTa[:, a + b0:a + b0 + 256], in_=p[:, b0:b0 + 256], func=AF.Identity,
                                                                              accum_out=ksum[:, blk:blk + 1]), reads=[p.r], writes=[kTa.r, ksum.r])
        P.op("dve", lambda e, h=h: e.tensor_copy(out=kTs[:, h, :], in_=kTa[:, TOK_S0:TOK_S0 + 8]), reads=[kTa.r], writes=[kTs.r])
        chk("b1b")
        def f(e):
            yield e.tensor_scalar(out=kmt.ap, in0=ksum.ap, scalar1=1.0 / 256.0, scalar2=None, op0=ALU.mult)
            yield e.tensor_copy(out=kmh.ap, in_=kmt.ap)
            yield e.tensor_tensor(out=kmt.ap, in0=kmt.ap, in1=kmh.ap, op=ALU.subtract)
            yield e.tensor_copy(out=kml.ap, in_=kmt.ap)
        P.op("dve", f, reads=[ksum.r], writes=[kmt.r, kmh.r, kml.r])
        chk("b1c")
        for t, (tok0, r) in enumerate(tiles):
            p = nps()
            mm_tok(p, r, 128, xnT, tok0, KC, wq, 256, [xnT_k[t]])
            evac_copy("act", Vt[0:r, t, :], p[0:r, 0:128], [p.r], [Vt.r])
            if t == NP - 1:
                chk("b1d")
            if t >= NP:
                oo = own_off(t)
                vs_ = vst[t % 2]
                evac_copy("act", vs_[0:r, :], p[0:r, 0:128], [p.r], [vs_.r])
                if t == NP:
                    chk("b1e")
                P.dma("sp", f"vst{t % 2}", lambda e, vs_=vs_, oo=oo, r=r, h=h: e.dma_start(out=v_o[oo:oo + r, 128 * h:128 * h + 128], in_=vs_[0:r, :]), reads=[vs_.r])
                if t == NP:
                    chk("b1f")
                if r == 8:
                    P.op("dve", lambda e, p=p, h=h: e.tensor_copy(out=vsf[:, h, :], in_=p[0:8, 0:128]), reads=[p.r], writes=[vsf.r])
                p2 = nps()
                mm_tok(p2, r, 128, xnT, tok0, KC, wq, 128, [xnT_k[t]])
                ks_ = kst[t % 2]
                evac_copy("act", ks_[0:r, :], p2[0:r, 0:128], [p2.r], [ks_.r])
                P.dma("sp", f"kst{t % 2}", lambda e, ks_=ks_, oo=oo, r=r, h=h: e.dma_start(out=k_o[oo:oo + r, 128 * h:128 * h + 128], in_=ks_[0:r, :]), reads=[ks_.r])
                if t == NP:
                    chk("b1g")
                if t == NP + NO - 1:
                    chk("b1h")
        chk("b1")
        P.op("dve", lambda e, h=h: e.tensor_scalar(out=cst.ap, in0=bbv.ap, scalar1=t31bc[:, h:h + 1], scalar2=None, op0=ALU.add),
             reads=[bbv.r, t31bc.r], writes=[cst.r])
        def do_tile(i, Lg, Pb, PT, gm, top8, fbt, mxs, obf):
            v = i // 2
            ob = NPB + v
            oo = 128 * i
            nk = (ob + 1) * 256
            pg = nps()

            def f(e, pg=pg, oo=oo):
                e.matmul(pg[:, 0:8], lhsT=qTa[:, oo:oo + 128], rhs=kmh.ap, start=True, stop=False)
                return e.matmul(pg[:, 0:8], lhsT=qTa[:, oo:oo + 128], rhs=kml.ap, start=False, stop=True)
            P.op("pe", f, reads=[qTa.r, kmh.r, kml.r], writes=[pg.r])

            def f(e, pg=pg, v=v):
                yield e.tensor_tensor(out=gm.ap, in0=pg[:, 0:8], in1=bbv[:, v * 8:v * 8 + 8], op=ALU.add)
                yield e.max(out=top8.ap, in_=gm.ap)
                yield e.tensor_scalar(out=fbt.ap, in0=gm.ap, scalar1=top8[:, 2:3], scalar2=1.0, op0=ALU.is_ge, op1=ALU.subtract)
                yield e.scalar_tensor_tensor(out=fbt.ap, in0=fbt.ap, scalar=BIG, in1=cst[:, v * 8:v * 8 + 8], op0=ALU.mult, op1=ALU.add)
            P.op("dve", f, reads=[pg.r, bbv.r, cst.r], writes=[gm.r, top8.r, fbt.r])
            for c0 in range(0, nk, 512):
                n = min(512, nk - c0)
                pS = nps()
                P.op("pe", lambda e, pS=pS, c0=c0, n=n, oo=oo: e.matmul(pS[:, 0:n], lhsT=qTa[:, oo:oo + 128], rhs=kTa[:, c0:c0 + n], start=True, stop=True),
                     reads=[qTa.r, kTa.r], writes=[pS.r])
                for b0 in range(0, n, 256):
                    blk = (c0 + b0) // 256
                    if blk == ob:
                        dlt = 0 if i % 2 == 0 else 128
                        P.op("dve", lambda e, pS=pS, b0=b0, blk=blk, dlt=dlt, h=h: e.tensor_tensor(out=Lg[:, blk * 256:blk * 256 + 256], in0=pS[:, b0:b0 + 256],
                                                                                              in1=Tb[dlt][:, h, :], op=ALU.add), reads=[pS.r, Tb[dlt].r], writes=[Lg.r])
                    elif blk == ob - 1 and i % 2 == 0:
                        P.op("dve", lambda e, pS=pS, b0=b0, blk=blk, h=h: e.scalar_tensor_tensor(out=Lg[:, blk * 256:blk * 256 + 256], in0=pS[:, b0:b0 + 256], scalar=fbt[:, blk:blk + 1],
                                                                                            in1=Tb[256][:, h, :], op0=ALU.add, op1=ALU.add), reads=[pS.r, fbt.r, Tb[256].r], writes=[Lg.r])
                    else:
                        P.op("dve", lambda e, pS=pS, b0=b0, blk=blk: e.tensor_scalar(out=Lg[:, blk * 256:blk * 256 + 256], in0=pS[:, b0:b0 + 256], scalar1=fbt[:, blk:blk + 1], scalar2=None,
                                                                                 op0=ALU.add), reads=[pS.r, fbt.r], writes=[Lg.r])
            def f(e, nk=nk):
                yield e.reduce_max(out=mxs[:, 0:1], in_=Lg[:, 0:nk], axis=AX.X)
                yield e.tensor_scalar(out=mxs[:, 1:2], in0=mxs[:, 0:1], scalar1=-1.0, scalar2=None, op0=ALU.mult)
            P.op("dve", f, reads=[Lg.r], writes=[mxs.r.k("m")])
            P.op("act", lambda e, nk=nk: e.activation(out=Pb[:, 0:nk], in_=Lg[:, 0:nk], func=AF.Exp, bias=mxs[:, 1:2], scale=1.0, accum_out=mxs[:, 2:3]),
                 reads=[Lg.r, mxs.r.k("m")], writes=[Pb.r, mxs.r.k("s")])
            P.op("dve", lambda e: e.reciprocal(out=mxs[:, 3:4], in_=mxs[:, 2:3]), reads=[mxs.r.k("s")], writes=[mxs.r.k("r")])
            nch = nk // 128
            j0 = 0
            while j0 < nch:
                n = min(8, nch - j0)
                pT = nps()
                pTb = psbf(pT).rearrange("p (a b) -> p a b", b=128)

                def f(e, pTb=pTb, j0=j0, n=n):
                    ins = None
                    for j in range(n):
                        ins = e.transpose(out=pTb[:, j, :], in_=Pb[:, (j0 + j) * 128:(j0 + j + 1) * 128], identity=identb.ap)
                    return ins
                P.op("pe", f, reads=[Pb.r, identb.r], writes=[pT.r])
                evac_copy(alt_eng(), PT[:, j0:j0 + n, :], pTb[:, 0:n, :], [pT.r], [PT.r])
                j0 += n
            pO = nps()

            def f(e, pO=pO, nch=nch):
                ins = None
                for j in range(nch):
                    ins = e.matmul(pO[:, 0:128], lhsT=PT[:, j, :], rhs=Vt[:, j, :], start=(j == 0), stop=(j == nch - 1))
                return ins
            P.op("pe", f, reads=[PT.r, Vt.r], writes=[pO.r])
            P.op("act", lambda e, pO=pO: e.activation(out=obf.ap, in_=pO[:, 0:128], func=AF.Identity, scale=mxs[:, 3:4]), reads=[pO.r, mxs.r.k("r")], writes=[obf.r])
            pT = nps()
            pTb = psbf(pT).rearrange("p (a b) -> p a b", b=128)
            P.op("pe", lambda e, pTb=pTb: e.transpose(out=pTb[:, 0, :], in_=obf.ap, identity=identb.ap), reads=[obf.r, identb.r], writes=[pT.r])
            evac_copy("dve", mixT[:, MH + h, oo:oo + 128], pTb[:, 0, :], [pT.r], [mixT.r.k(MH + h)])
            chk("b2")
            if i == NO - 1:
                chk("b3")

        for i in range(NO):
            do_tile(i, Lg2[i % 2], Pb2[i % 2], PT2[i % 2], gm2[i % 2], top82[i % 2], fbt2[i % 2], mxs2[i % 2], obf2[i % 2])

    for h in range(AH):
        do_head(h)

    if STOP == "p1b":
        P.emit()
        st.close()
        return nc, hc
    R3.barrier()
    wslots.clear()
    kvsel = R3.get([128, 24, 2, 256], F32, "kvsel")
    kselT = R3.get([128, 48, 128], BF16, "kselT")
    gts = R3.get([8, AH, NB], F32, "gts")
    tp8 = R3.get([8, AH, 8], F32, "tp8")
    OH = R3.get([8, AH, NB], F32, "OH")
    OHt = R3.get([8, AH, NB], F32, "OHt")
    selp = R3.get([8, AH, 3, 2], F32, "selp")
    is63 = R3.get([8, AH, 3], F32, "is63")
    Xd = R3.get([8, 8, AH * 6], F32, "Xd")
    addc2 = R3.get([128, NSEL // 2], F32, "addc2")
    idxs = R3.get([128, NSEL // 2], I32, "idxs")
    sTs = R3.get([128, 48], F32, "sTs")
    Ls = R3.get([8, 784], F32, "Ls")
    tmpb = R3.get([8, 256], F32, "tmpb")
    pTs = R3.get([128, 6, 8], F32, "pTs")
    pTo = R3.get([8, 8], F32, "pTo")
    sms = R3.get([8, 4], F32, "sms")
    cload(addc2, cin["addc2"])
    eye8x = R3.get([8, 8 * AH * 6], F32, "eye8x")
    ones8 = R3.get([8, 128], F32, "ones8")
    cload(eye8x, cin["eye8x"])
    P.op("pool", lambda e: e.memset(ones8.ap, 1.0), writes=[ones8.r])
    pg = nps()

    def f(e):
        ins = None
        for h in range(AH):
            e.matmul(pg[0:8, h * NB:(h + 1) * NB], lhsT=qTs[:, h, :], rhs=kmsh[:, h * NB:(h + 1) * NB], start=True, stop=False)
            ins = e.matmul(pg[0:8, h * NB:(h + 1) * NB], lhsT=qTs[:, h, :], rhs=kmsl[:, h * NB:(h + 1) * NB], start=False, stop=True)
        return ins
    P.op("pe", f, reads=[qTs.r, kmsh.r, kmsl.r], writes=[pg.r])

    def f(e):
        yield e.tensor_copy(out=gts.ap, in_=pg[0:8, 0:AH * NB].rearrange("p (h n) -> p h n", n=NB))
        for h in range(AH):
            yield e.max(out=tp8[:, h, :], in_=gts[:, h, :])
    P.op("dve", f, reads=[pg.r], writes=[gts.r, tp8.r])
    ptv = ptf[0:8, :].rearrange("p (n u) -> p u n", u=2)
    for s_ in range(3):
        def f(e, s_=s_):
            yield e.tensor_tensor(out=OH.ap, in0=gts.ap, in1=tp8[:, :, s_:s_ + 1].to_broadcast([8, AH, NB]), op=ALU.is_equal)
            yield e.tensor_copy(out=is63[:, :, s_], in_=OH[:, :, NB - 1])
            for u in range(2):
                yield e.tensor_tensor(out=OHt.ap, in0=OH.ap, in1=ptv[:, u:u + 1, :].to_broadcast([8, AH, NB]), op=ALU.mult)
                yield e.tensor_reduce(out=selp[:, :, s_, u], in_=OHt.ap, axis=AX.X, op=ALU.add)
        P.op("dve", f, reads=[gts.r, tp8.r, ptf.r], writes=[OH.r, OHt.r, selp.r, is63.r])
    P.op("dve", lambda e: e.tensor_tensor(out=Xd.ap, in0=eye8x.ap.rearrange("p (q x) -> p q x", q=8),
                                          in1=selp.ap.rearrange("p h s u -> p (h s u)").unsqueeze(1).to_broadcast([8, 8, AH * 6]), op=ALU.mult),
         reads=[eye8x.r, selp.r], writes=[Xd.r])
    pB = nps()
    P.op("pe", lambda e, pB=pB: e.matmul(pB[:, 0:NSEL], lhsT=ones8.ap, rhs=Xd.ap.rearrange("p q x -> p (q x)"), start=True, stop=True),
         reads=[ones8.r, Xd.r], writes=[pB.r])

    pBv2 = pB[:, 0:NSEL].rearrange("p (x u) -> p x u", u=2)

    def f(e):
        yield e.scalar_tensor_tensor(out=addc2[0:64, :], in0=pBv2[0:64, :, 0], scalar=float(64 * AH), in1=addc2[0:64, :], op0=ALU.mult, op1=ALU.add)
        yield e.scalar_tensor_tensor(out=addc2[64:128, :], in0=pBv2[64:128, :, 1], scalar=float(64 * AH), in1=addc2[64:128, :], op0=ALU.mult, op1=ALU.add)
        yield e.tensor_copy(out=idxs.ap, in_=addc2.ap)
    P.op("dve", f, reads=[pB.r, addc2.r], writes=[addc2.r, idxs.r])
    ckv_rows = cache_kv.rearrange("(r two) x -> r (two x)", two=2)
    def do_shead(h):
        for q in range(8):
            for s_ in range(3):
                un = q * 3 + s_
                col = (q * AH + h) * 3 + s_
                P.dma("pool", "kvsel", lambda e, un=un, col=col: e.indirect_dma_start(out=kvsel[:, un, :, :].rearrange("p e x -> p (e x)"), out_offset=None, in_=ckv_rows,
                                                                                    in_offset=bass.IndirectOffsetOnAxis(ap=idxs[:, col:col + 1], axis=0)),
                      reads=[idxs.r], writes=[kvsel.r.k(un)])
        for u0 in range(0, 48, 4):
            pT = nps()
            pTv = pT.ap.rearrange("p (a b) -> p a b", b=128)

            def f(e, pTv=pTv, u0=u0):
                ins = None
                for j in range(4):
                    ins = e.transpose(out=pTv[:, j, :], in_=kvsel[:, (u0 + j) // 2, (u0 + j) % 2, 0:128], identity=ident.ap)
                return ins
            P.op("pe", f, reads=[kvsel.r, ident.r], writes=[pT.r])
            evac_copy(alt_eng(), kselT[:, u0:u0 + 4, :], pTv[:, 0:4, :], [pT.r], [kselT.r])
        pS = nps()

        def f(e, pS=pS, h=h):
            ins = None
            for q in range(8):
                for su in range(6):
                    un = q * 6 + su
                    ins = e.matmul(pS[:, un:un + 1], lhsT=kselT[:, un, :], rhs=qTs[:, h, q:q + 1], start=True, stop=True)
            return ins
        P.op("pe", f, reads=[kselT.r, qTs.r], writes=[pS.r])
        evac_copy("dve", sTs.ap, pS[:, 0:48], [pS.r], [sTs.r])
        sTv = sTs.ap.rearrange("p (q x) -> p x q", x=6)
        pA, pBk = nps(), nps()
        pAv = pA.ap.rearrange("p (a b) -> p a b", b=128)
        pBv = pBk.ap.rearrange("p (a b) -> p a b", b=128)

        def f(e, pAv=pAv, pBv=pBv, pBk=pBk, h=h):
            for su in range(4):
                e.transpose(out=pAv[0:8, su, :], in_=sTv[:, su, :], identity=ident.ap)
            for su in range(4, 6):
                e.transpose(out=pBv[0:8, su - 4, :], in_=sTv[:, su, :], identity=ident.ap)
            return e.matmul(pBk[0:8, 256:264], lhsT=qTs[:, h, :], rhs=kTs[:, h, :], start=True, stop=True)
        P.op("pe", f, reads=[sTs.r, ident.r, qTs.r, kTs.r], writes=[pA.r, pBk.r])

        def f(e, pAv=pAv, pBv=pBv, pBk=pBk, h=h):
            for s_ in range(3):
                yield e.memset(tmpb[:, 0:128], 0.0)
                yield e.tensor_scalar(out=tmpb[:, 128:256], in0=T256s[:, h, :], scalar1=is63[:, h, s_:s_ + 1], scalar2=None, op0=ALU.mult)
                yield e.tensor_scalar(out=tmpb.ap, in0=tmpb.ap, scalar1=t31bc[0:8, h:h + 1], scalar2=None, op0=ALU.add)
                src = pAv[0:8, 2 * s_:2 * s_ + 2, :] if s_ < 2 else pBv[0:8, 0:2, :]
                yield e.tensor_tensor(out=Ls[:, s_ * 256:(s_ + 1) * 256].rearrange("p (a u b) -> p a u b", u=2, b=64), in0=src.rearrange("p a (u b) -> p a u b", u=2),
                                in1=tmpb.ap.rearrange("q (u p e) -> q e u p", u=2, e=2), op=ALU.add)
            yield e.tensor_tensor(out=Ls[:, 768:776], in0=pBk[0:8, 256:264], in1=T0s[:, h, :], op=ALU.add)
            yield e.reduce_max(out=sms[:, 0:1], in_=Ls[:, 0:776], axis=AX.X)
            yield e.tensor_scalar(out=sms[:, 1:2], in0=sms[:, 0:1], scalar1=-1.0, scalar2=None, op0=ALU.mult)
        P.op("dve", f, reads=[pA.r, pBk.r, T256s.r, is63.r, t31bc.r, T0s.r], writes=[tmpb.r, Ls.r, sms.r.k("m")])
        P.op("act", lambda e: e.activation(out=Ls[:, 0:776], in_=Ls[:, 0:776], func=AF.Exp, bias=sms[:, 1:2], scale=1.0, accum_out=sms[:, 2:3]),
             reads=[Ls.r, sms.r.k("m")], writes=[Ls.r, sms.r.k("s")])

        def f(e):
            yield e.reciprocal(out=sms[:, 3:4], in_=sms[:, 2:3])
            yield e.tensor_scalar(out=Ls[:, 0:776], in0=Ls[:, 0:776], scalar1=sms[:, 3:4], scalar2=None, op0=ALU.mult)
        P.op("dve", f, reads=[Ls.r, sms.r.k("s")], writes=[Ls.r, sms.r.k("r")])
        pP = nps()
        pPv = pP[:, 0:48].rearrange("p (a b) -> p a b", b=8)

        def f(e, pP=pP, pPv=pPv):
            for su in range(6):
                e.transpose(out=pPv[:, su, :], in_=Ls[:, su * 128:(su + 1) * 128], identity=ident[0:8, 0:8])
            return e.transpose(out=pP[0:8, 64:72], in_=Ls[:, 768:776], identity=ident[0:8, 0:8])
        P.op("pe", f, reads=[Ls.r, ident.r], writes=[pP.r])

        def f(e, pP=pP, pPv=pPv):
            yield e.tensor_copy(out=pTs.ap, in_=pPv)
            yield e.tensor_copy(out=pTo.ap, in_=pP[0:8, 64:72])
        P.op("dve", f, reads=[pP.r], writes=[pTs.r, pTo.r])
        pO = nps()

        def f(e, pO=pO, h=h):
            ins = None
            for q in range(8):
                e.matmul(pO[:, q:q + 1], lhsT=vsf[:, h, :], rhs=pTo[:, q:q + 1], start=True, stop=False)
                for su in range(6):
                    ins = e.matmul(pO[:, q:q + 1], lhsT=kvsel[:, q * 3 + su // 2, su % 2, 128:256], rhs=pTs[:, su, q:q + 1], start=False, stop=(su == 5))
            return ins
        P.op("pe", f, reads=[vsf.r, pTo.r, kvsel.r, pTs.r], writes=[pO.r])
        evac_copy("act", mixT[:, MH + h, 128 * NO:128 * NO + 8], pO[:, 0:8], [pO.r], [mixT.r.k(MH + h)])

    for h in range(AH):
        do_shead(h)

    if STOP == "p1c":
        P.emit()
        st.close()
        return nc, hc
    R1.barrier()
    R3.barrier()
    wslots.clear()
    h1 = R1.get([128, NOT, D], F32, "h1")
    otl = [(128 * i, 128) for i in range(NO)] + [(128 * NO, 8)]
    for i, (o0, r) in enumerate(otl):
        P.dma("sp", "h1l", lambda e, i=i, o0=o0, r=r: e.dma_start(out=h1[0:r, i, :], in_=x_all[TOK_O0 + o0:TOK_O0 + o0 + r, :]), writes=[h1.r.k(i)])
    for cg in range(D // 512):
        w = wload("o", 2, R3, [128, max(KM, KC), 512], [(0, w_out[:, cg * 512:(cg + 1) * 512])])
        for i, (o0, r) in enumerate(otl):
            p = nps()
            mm_tok(p, r, 512, mixT, o0, KM, w, 0, [mixT.r])
            P.op("dve", lambda e, p=p, i=i, r=r, cg=cg: e.tensor_tensor(out=h1[0:r, i, cg * 512:(cg + 1) * 512], in0=p[0:r, :], in1=h1[0:r, i, cg * 512:(cg + 1) * 512], op=ALU.add),
                 reads=[p.r, h1.r.k(i)], writes=[h1.r.k(i)])
    def norm_stage(grow):
        R3.barrier()
        wslots.clear()
        g = load_g(0, grow)
        xnb = R3.get([128, D], BF16, "xnb")
        junk3 = R3.get([128, D], BF16, "junk3")
        R2.barrier()
        xT = R2.get([128, KC, NOW], BF16, "xT")
        for i, (o0, r) in enumerate(otl):
            norm_tile(h1[0:r, i, :], h1.r.k(i), r, g, xnb[0:r, :], xnb.r, ssq, rstd, junk3)
            transpose_into(xnb, r, KC, xT, o0, xT.r)
        R3.barrier()
        wslots.clear()
        return xT
    xT2 = norm_stage(1)
    aT = R3.get([128, 4, NOW], BF16, "aT")
    rl = [R3.get([128, 512], F32, "rl") for _ in range(2)]
    ch_o = chunks(0, 128 * NO) + [(128 * NO, 8)]
    nrl = 0
    for fg in range(DFF // 512):
        wu = wload("u", 2, R3, [128, KC, 512], [(0, w_up[:, fg * 512:(fg + 1) * 512])])
        wd = wload("d", 2, R3, [128, 4, D], [(0, w_down[fg * 512:(fg + 1) * 512, :])])
        for f_ in range(4):
            for (a, n) in ch_o:
                p = nps()
                mm_feat(p, 128, n, wu, f_ * 128, KC, xT2, a, [xT2.r])
                rt = rl[nrl % 2]
                nrl += 1
                P.op("act", lambda e, p=p, n=n, rt=rt: e.activation(out=rt[:, 0:n], in_=p[:, 0:n], func=AF.Relu), reads=[p.r], writes=[rt.r])
                P.op("pool", lambda e, rt=rt, f_=f_, a=a, n=n: e.tensor_tensor(out=aT[:, f_, a:a + n], in0=rt[:, 0:n], in1=rt[:, 0:n], op=ALU.mult), reads=[rt.r], writes=[aT.r])
        for i, (o0, r) in enumerate(otl):
            for cg in range(D // 512):
                p = nps()

                def f(e, p=p, o0=o0, r=r, cg=cg, wd=wd):
                    ins = None
                    for f_ in range(4):
                        ins = e.matmul(p[0:r, :], lhsT=aT[:, f_, o0:o0 + r], rhs=wd[:, f_, cg * 512:(cg + 1) * 512], start=(f_ == 0), stop=(f_ == 3))
                    return ins
                P.op("pe", f, reads=[aT.r, wd.r], writes=[p.r])
                P.op("dve", lambda e, p=p, i=i, r=r, cg=cg: e.tensor_tensor(out=h1[0:r, i, cg * 512:(cg + 1) * 512], in0=p[0:r, :], in1=h1[0:r, i, cg * 512:(cg + 1) * 512], op=ALU.add),
                     reads=[p.r, h1.r.k(i)], writes=[h1.r.k(i)])
    xT3 = norm_stage(2)
    KP = PLE // 128
    pT_ = R3.get([128, KP, NOW], BF16, "pT_")
    pst = R3.get([128, PLE], F32, "pst")
    pbf = R3.get([128, PLE], BF16, "pbf")
    for i, (o0, r) in enumerate(otl):
        P.dma("sp", "pst", lambda e, o0=o0, r=r: e.dma_start(out=pst[0:r, :], in_=p_all[o0:o0 + r, :]), writes=[pst.r])
        P.op("dve", lambda e, r=r: e.tensor_copy(out=pbf[0:r, :], in_=pst[0:r, :]), reads=[pst.r], writes=[pbf.r])
        transpose_into(pbf, r, KP, pT_, o0, pT_.r)
    sg = [R3.get([128, 512], F32, "sg") for _ in range(2)]
    for cg in range(D // 512):
        wgt = wload("u", 2, R3, [128, KC, 512], [(0, w_pg[:, cg * 512:(cg + 1) * 512])])
        wpp = wload("p", 2, R3, [128, KP, 512], [(0, w_pp[:, cg * 512:(cg + 1) * 512])])
        for i, (o0, r) in enumerate(otl):
            p1, p2 = nps(), nps()
            mm_tok(p1, r, 512, xT3, o0, KC, wgt, 0, [xT3.r])
            mm_tok(p2, r, 512, pT_, o0, KP, wpp, 0, [pT_.r])
            s_ = sg[i % 2]
            P.op("act", lambda e, p1=p1, r=r, s_=s_: e.activation(out=s_[0:r, :], in_=p1[0:r, :], func=AF.Sigmoid), reads=[p1.r], writes=[s_.r])
            P.op("dve", lambda e, p2=p2, r=r, s_=s_: e.tensor_tensor(out=s_[0:r, :], in0=s_[0:r, :], in1=p2[0:r, :], op=ALU.mult), reads=[p2.r, s_.r], writes=[s_.r])
            P.op("pool", lambda e, i=i, r=r, cg=cg, s_=s_: e.tensor_tensor(out=h1[0:r, i, cg * 512:(cg + 1) * 512], in0=h1[0:r, i, cg * 512:(cg + 1) * 512], in1=s_[0:r, :], op=ALU.add),
                 reads=[s_.r, h1.r.k(i)], writes=[h1.r.k(i)])
    R3.barrier()
    wslots.clear()
    g = load_g(0, 3)
    junk3 = R3.get([128, D], BF16, "junk3")
    yst = [R3.get([128, D], F32, "yst") for _ in range(2)]
    for i, (o0, r) in enumerate(otl):
        y_ = yst[i % 2]
        norm_tile(h1[0:r, i, :], h1.r.k(i), r, g, y_[0:r, :], y_.r, ssq, rstd, junk3)
        P.dma("sp", f"yst{i % 2}", lambda e, y_=y_, o0=o0, r=r: e.dma_start(out=y_o[o0:o0 + r, :], in_=y_[0:r, :]), reads=[y_.r])
    P.emit()
    st.close()
    return nc, hc


def make_in_maps(c, inp, ncores):
    MH, AH = c.MH, c.AH
    half_len = 128 * c.NO
    hc = host_consts(c)
    f32 = np.float32
    shared = {
        "w_in": np.ascontiguousarray(inp["w_in"][0]), "w_out": np.ascontiguousarray(inp["w_out"][0]),
        "w_up": np.ascontiguousarray(inp["w_up"][0]), "w_down": np.ascontiguousarray(inp["w_down"][0]),
        "w_pg": np.ascontiguousarray(inp["w_ple_gate"][0]), "w_pp": np.ascontiguousarray(inp["w_ple_proj"][0]),
        "gvec": np.ascontiguousarray(np.stack([inp["g_mix"][0], inp["g_ffn"][0], inp["g_ple"][0], inp["g_final"]]).astype(f32)),
        "g_mh": np.ascontiguousarray(inp["g_mhead"][0].reshape(1, MH * 128)),
        "b_i": np.ascontiguousarray(inp["b_igate"][0].reshape(MH, 1)), "b_f": np.ascontiguousarray(inp["b_fgate"][0].reshape(MH, 1)),
        "relt": np.ascontiguousarray(np.concatenate([inp["rel_bias_table"], np.full((1, AH), -BIG, f32)], 0).astype(f32)),
        "cache_k": np.ascontiguousarray(inp["cache_k"][0]).reshape(c.NPHYS * 128, AH * 128),
        "cache_kv": np.ascontiguousarray(np.stack([inp["cache_k"][0], inp["cache_v"][0]], axis=3).transpose(0, 2, 1, 3, 4)).reshape(c.NPHYS * AH * 128, 256),
    }
    for k, v in hc.items():
        shared["c_" + k] = v
    maps = []
    for cid in range(ncores):
        b, half = cid // 2, cid % 2
        xp = inp["x_prompt"][b]
        own = xp[half * half_len:(half + 1) * half_len]
        pre = xp[0:half_len] if half == 1 else np.zeros_like(own)
        m = dict(shared)
        m["x_all"] = np.ascontiguousarray(np.concatenate([pre, own, inp["x_sample"][cid]], 0))
        m["p_all"] = np.ascontiguousarray(np.concatenate([inp["p_prompt"][0, b, half * half_len:(half + 1) * half_len], inp["p_sample"][0, cid]], 0))
        m["pt"] = np.ascontiguousarray(inp["page_table"][cid:cid + 1]).astype(np.int32)
        m["sC"] = np.ascontiguousarray(inp["state_C"][0, cid]).reshape(MH * 64, 128)
        m["sn"] = np.ascontiguousarray(inp["state_n"][0, cid]).reshape(MH * 64, 1)
        m["sm"] = np.ascontiguousarray(inp["state_m"][0, cid]).reshape(MH, 1)
        m["flag"] = np.full((1, 1), float(half), f32)
        maps.append(m)
    return maps


def assemble(c, res, ncores):
    MH, AH = c.MH, c.AH
    B = ncores // 2
    hl = 128 * c.NO
    S = 2 * hl
    D = c.D
    f32 = np.float32
    y_p = np.zeros((B, S, D), f32)
    y_s = np.zeros((ncores, 8, D), f32)
    k_p = np.zeros((1, B, S, AH, 128), f32)
    v_p = np.zeros((1, B, S, AH, 128), f32)
    C_p = np.zeros((1, B, MH, 64, 128), f32)
    n_p = np.zeros((1, B, MH, 64), f32)
    m_p = np.zeros((1, B, MH), f32)
    k_s = np.zeros((1, ncores, 8, AH, 128), f32)
    v_s = np.zeros((1, ncores, 8, AH, 128), f32)
    C_s = np.zeros((1, ncores, MH, 64, 128), f32)
    n_s = np.zeros((1, ncores, MH, 64), f32)
    m_s = np.zeros((1, ncores, MH), f32)
    for cid in range(ncores):
        r = res[cid]
        b, half = cid // 2, cid % 2
        y_p[b, half * hl:(half + 1) * hl] = r["y_o"][0:hl]
        y_s[cid] = r["y_o"][hl:hl + 8]
        k_p[0, b, half * hl:(half + 1) * hl] = r["k_o"][0:hl].reshape(hl, AH, 128)
        v_p[0, b, half * hl:(half + 1) * hl] = r["v_o"][0:hl].reshape(hl, AH, 128)
        k_s[0, cid] = r["k_o"][hl:hl + 8].reshape(8, AH, 128)
        v_s[0, cid] = r["v_o"][hl:hl + 8].reshape(8, AH, 128)
        if half == 1:
            C_p[0, b] = r["Cp_o"].reshape(MH, 64, 128)
            n_p[0, b] = r["np_o"].reshape(MH, 64)
            m_p[0, b] = r["mp_o"].reshape(MH)
        C_s[0, cid] = r["Cs_o"].reshape(MH, 64, 128)
        n_s[0, cid] = r["ns_o"].reshape(MH, 64)
        m_s[0, cid] = r["ms_o"].reshape(MH)
    return (y_p, y_s, k_p, v_p, C_p, n_p, m_p, k_s, v_s, C_s, n_s, m_s)


def kernel(**inputs):
    c = Cfg()
    ncores = 8
    inp = {k: np.asarray(v) for k, v in inputs.items()}
    nc, _ = build(c)
    maps = make_in_maps(c, inp, ncores)
    res = run_bass_kernel_spmd(nc, maps, core_ids=list(range(ncores)))
    return assemble(c, res.results, ncores)
```

```python
import numpy as np
import concourse.bass as bass
import concourse.mybir as mybir

F32 = mybir.dt.float32
BF16 = mybir.dt.bfloat16
I32 = mybir.dt.int32
AF = mybir.ActivationFunctionType
ALU = mybir.AluOpType
AX = mybir.AxisListType

ENGS = ("pe", "act", "dve", "pool", "sp")


class Res:
    __slots__ = ("name", "parent", "kids", "lw", "rd", "excl")

    def __init__(self, name, parent=None):
        self.excl = False
        self.name = name
        self.parent = parent
        self.kids = {}
        self.lw = None
        self.rd = []

    def k(self, key):
        r = self.kids.get(key)
        if r is None:
            r = Res(f"{self.name}.{key}", self)
            self.kids[key] = r
        return r


class Op:
    __slots__ = ("eng", "fn", "deps", "idx", "sig", "cnt", "chan", "isdma", "dw", "ny")

    def __init__(self, eng, fn):
        self.eng = eng
        self.fn = fn
        self.deps = set()
        self.dw = {}
        self.sig = False
        self.cnt = 0
        self.chan = None
        self.isdma = False


class Chan:
    def __init__(self, name):
        self.name = name
        self.res = Res("chan_" + name)
        self.n = 0
        self.sem = None


class Prog:
    def __init__(self, nc, same_engine_sync=True):
        self.nc = nc
        self.ops = []
        self.chans = {}
        self.same_engine_sync = same_engine_sync

    def chan(self, name):
        c = self.chans.get(name)
        if c is None:
            c = Chan(name)
            self.chans[name] = c
        return c

    def _desc(self, r, out):
        for kk in r.kids.values():
            out.append(kk)
            if kk.kids:
                self._desc(kk, out)

    def _related(self, r):
        out = [r]
        p = r.parent
        while p is not None:
            out.append(p)
            p = p.parent
        if r.kids:
            self._desc(r, out)
        return out

    def _add(self, op, reads, writes):
        for r in reads:
            if r.excl and r not in writes:
                writes = writes + [r]
        deps = op.deps
        for r in reads:
            for x in self._related(r):
                if x.lw is not None:
                    deps.add(x.lw)
        for w in writes:
            for x in self._related(w):
                if x.lw is not None:
                    deps.add(x.lw)
                for o in x.rd:
                    deps.add(o)
        for d in list(deps):
            if d.isdma:
                c = d.chan
                if op.dw.get(c, 0) < 16 * c.n:
                    op.dw[c] = 16 * c.n
                if not (op.isdma and op.chan is c):
                    c.res.rd.append(op)
                deps.discard(d)
        for r in reads:
            r.rd.append(op)
        for w in writes:
            w.lw = op
            w.rd = []
            if w.kids:
                dd = []
                self._desc(w, dd)
                for kk in dd:
                    kk.lw = op
                    kk.rd = []
        deps.discard(op)
        op.idx = len(self.ops)
        self.ops.append(op)

    frozen = False

    def op(self, eng, fn, reads=(), writes=()):
        if self.frozen:
            return None
        o = Op(eng, fn)
        self._add(o, list(reads), list(writes))
        return o

    def dma(self, eng, chan, fn, reads=(), writes=()):
        if self.frozen:
            return None
        c = self.chan(chan) if isinstance(chan, str) else chan
        o = Op(eng, fn)
        o.isdma = True
        o.chan = c
        assert getattr(c, "eng", eng) == eng
        c.eng = eng
        for x in c.res.rd:
            o.deps.add(x)
        c.res.rd = []
        self._add(o, list(reads), list(writes))
        c.n += 1
        o.cnt = 16 * c.n
        return o

    def emit(self, final_wait_eng="sp"):
        nc = self.nc
        ops = self.ops
        for o in ops:
            for d in o.deps:
                if d.isdma:
                    continue
                if d.eng == o.eng:
                    if o.eng == "pe" or not self.same_engine_sync:
                        continue
                d.sig = True
        import inspect

        class _FI:
            def then_inc(self, *a, **k):
                return self

        class _FE:
            def __getattr__(self, n):
                return lambda *a, **k: _FI()
        counts = {e: 0 for e in ENGS}
        for o in ops:
            o.ny = 0
            if o.isdma:
                continue
            if inspect.isgeneratorfunction(o.fn):
                o.ny = sum(1 for _ in o.fn(_FE()))
                counts[o.eng] += o.ny
                o.cnt = counts[o.eng]
            elif o.sig:
                counts[o.eng] += 1
                o.cnt = counts[o.eng]
        import contextlib
        with contextlib.ExitStack() as st:
            esem = {e: st.enter_context(nc.semaphore("s_" + e)) for e in ENGS if e != "sp"}
            for c in self.chans.values():
                c.sem = st.enter_context(nc.semaphore("c_" + c.name))
            per = {e: [o for o in ops if o.eng == e] for e in ENGS}

            block = st.enter_context(nc.Block())

            def run(eng_name, eng):
                waited = {}
                for o in per[eng_name]:
                    need = {}
                    for c, v in o.dw.items():
                        need[id(c.sem)] = (c.sem, v)
                    for d in o.deps:
                        if d.eng == o.eng and (o.eng == "pe" or not self.same_engine_sync):
                            continue
                        s, v = esem[d.eng], d.cnt
                        key = id(s)
                        if need.get(key, (None, 0))[1] < v:
                            need[key] = (s, v)
                    for key, (s, v) in need.items():
                        if waited.get(key, 0) < v:
                            eng.wait_ge(s, v)
                            waited[key] = v
                    if o.ny:
                        gen = o.fn(eng)
                        base = o.cnt - o.ny
                        for gi, cur in enumerate(gen):
                            cur.then_inc(esem[eng_name], 1)
                            if gi < o.ny - 1:
                                eng.wait_ge(esem[eng_name], base + gi + 1)
                                waited[id(esem[eng_name])] = base + gi + 1
                        continue
                    ins = o.fn(eng)
                    if o.isdma:
                        ins.then_inc(o.chan.sem, 16)
                    elif o.sig:
                        ins.then_inc(esem[o.eng], 1)
                if eng_name == final_wait_eng:
                    for c in self.chans.values():
                        if c.n:
                            eng.wait_ge(c.sem, 16 * c.n)

            @block.tensor
            def _(e):
                run("pe", e)

            @block.scalar
            def _(e):
                run("act", e)

            @block.vector
            def _(e):
                run("dve", e)

            @block.gpsimd
            def _(e):
                run("pool", e)

            @block.sync
            def _(e):
                run("sp", e)

import math
import contextlib
from concourse.bass_utils import run_bass_kernel_spmd

BIG = 30000.0
EPS = 1e-6


class Cfg:
    def __init__(s, D=2048, NP=8, NO=8, MH=8, AH=8, DFF=8192, PLE=256, NPAGES=128, NPHYS=1280):
        s.D, s.NP, s.NO, s.MH, s.AH, s.DFF, s.PLE, s.NPAGES, s.NPHYS = D, NP, NO, MH, AH, DFF, PLE, NPAGES, NPHYS
        s.KC = D // 128
        s.DK, s.DV, s.DH, s.TS = 64, 128, 128, 8
        s.c_mq = 0
        s.c_mk = MH * 64
        s.c_mv = 2 * MH * 64
        s.c_mo = s.c_mv + MH * 128
        s.c_mi = s.c_mo + MH * 128
        s.c_mf = s.c_mi + MH
        s.c_aq = s.c_mf + MH
        s.c_ak = s.c_aq + AH * 128
        s.c_av = s.c_ak + AH * 128
        s.DIN = s.c_av + AH * 128
        s.MIXW = MH * 128 + AH * 128
        s.KM = s.MIXW // 128
        s.NTOK = 128 * (NP + NO) + 8
        s.NOW = 128 * NO + 8
        s.NPB, s.NOB = NP // 2, NO // 2
        s.NBLK = s.NPB + s.NOB
        s.NB = NPAGES // 2
        s.OHW = 768
        s.NSEL = 8 * AH * 6


class V:
    def __init__(s, ap, r):
        s.ap, s.r = ap, r

    def __getitem__(s, k):
        return s.ap[k]


def t5_bucket_np(rel):
    n = np.maximum(rel, 0)
    nf = np.maximum(n, 1).astype(np.float32)
    large = 16 + (np.log(nf / np.float32(16)) / np.float32(math.log(128 / 16)) * np.float32(16)).astype(np.int32)
    large = np.minimum(large, 31)
    return np.where(n < 16, n, large)


def host_consts(c):
    k = {}
    k["ident"] = np.eye(128, dtype=np.float32)
    s_ = np.arange(128)
    k["maskle"] = (s_[:, None] <= s_[None, :]).astype(np.float32)
    rel = np.arange(c.OHW) - 255
    b = t5_bucket_np(rel)
    oh = np.zeros((33, c.OHW), np.float32)
    for i in range(c.OHW):
        if rel[i] < 0:
            oh[32, i] = 1.0
        else:
            oh[b[i], i] = 1.0
    k["ohlong"] = oh
    bs = np.full((c.NOB, 8), -BIG, np.float32)
    for v in range(c.NOB):
        bs[v, : c.NPB + v] = 0.0
    k["bstruct"] = bs.reshape(1, c.NOB * 8)
    pm = np.zeros((1, 8), np.float32)
    pm[0, : c.NPB] = 1.0
    k["prefmask"] = pm
    nhp = c.MH // 2
    sel = np.zeros((c.MH, nhp * 128), np.float32)
    for hp in range(nhp):
        for p in range(128):
            sel[2 * hp + p // 64, hp * 128 + p] = 1.0
    k["sel"] = sel
    addc = np.zeros((128, c.NSEL), np.float32)
    col = 0
    for q in range(8):
        for h in range(c.AH):
            for s in range(3):
                for u in range(2):
                    addc[:, col] = np.arange(128) * c.AH + h
                    col += 1
    k["addc"] = addc
    ew = np.zeros((128, 2 * c.NB - 1), np.float32)
    ew[:, c.NB - 1] = 1.0
    k["ewin"] = ew
    addc2 = np.zeros((128, c.NSEL // 2), np.float32)
    col = 0
    for q in range(8):
        for h in range(c.AH):
            for s in range(3):
                addc2[:, col] = h * 64 + (np.arange(128) % 64)
                col += 1
    k["addc2"] = addc2
    k["iotap"] = np.arange(128, dtype=np.float32).reshape(128, 1)
    k["iotab"] = np.broadcast_to(np.arange(c.NB, dtype=np.float32), (8, c.NB)).copy()
    e8 = np.zeros((8, 8, c.AH * 6), np.float32)
    for q in range(8):
        e8[q, q, :] = 1.0
    k["eye8x"] = e8.reshape(8, 8 * c.AH * 6)
    return k


def build(c):
    import os as _os
    STOP = _os.environ.get("MK_STOP", "")

    def chk(name):
        if STOP == name:
            P.frozen = True
    nc = bass.Bass("TRN2", target_bir_lowering=False)
    P = Prog(nc)
    st = contextlib.ExitStack()
    D, KC, NP, NO, MH, AH, DFF, PLE = c.D, c.KC, c.NP, c.NO, c.MH, c.AH, c.DFF, c.PLE
    NTOK, NOW, KM = c.NTOK, c.NOW, c.KM
    NT = NP + NO + 1
    NB, NPG, NSEL = c.NB, c.NPAGES, c.NSEL
    NOT = NO + 1

    def din(name, shape, dt=F32):
        return nc.dram_tensor(name, list(shape), dt, kind="ExternalInput").ap()

    def dout(name, shape, dt=F32):
        return nc.dram_tensor(name, list(shape), dt, kind="ExternalOutput").ap()

    hc = host_consts(c)
    x_all = din("x_all", [NTOK, D])
    p_all = din("p_all", [NOW, PLE])
    w_in = din("w_in", [D, c.DIN])
    w_out = din("w_out", [c.MIXW, D])
    w_up = din("w_up", [D, DFF])
    w_down = din("w_down", [DFF, D])
    w_pg = din("w_pg", [D, D])
    w_pp = din("w_pp", [PLE, D])
    gvec = din("gvec", [4, D])
    g_mh = din("g_mh", [1, MH * 128])
    b_i = din("b_i", [MH, 1])
    b_f = din("b_f", [MH, 1])
    relt = din("relt", [33, AH])
    cache_k = din("cache_k", [c.NPHYS * 128, AH * 128])
    cache_kv = din("cache_kv", [c.NPHYS * AH * 128, 256])
    pt = din("pt", [1, c.NPAGES], I32)
    sC = din("sC", [MH * 64, 128])
    sn = din("sn", [MH * 64, 1])
    sm = din("sm", [MH, 1])
    flag = din("flag", [1, 1])
    cin = {k: din("c_" + k, v.shape) for k, v in hc.items()}

    y_o = dout("y_o", [NOW, D])
    k_o = dout("k_o", [NOW, AH * 128])
    v_o = dout("v_o", [NOW, AH * 128])
    Cp_o = dout("Cp_o", [MH * 64, 128])
    np_o = dout("np_o", [MH * 64, 1])
    mp_o = dout("mp_o", [MH, 1])
    Cs_o = dout("Cs_o", [MH * 64, 128])
    ns_o = dout("ns_o", [MH * 64, 1])
    ms_o = dout("ms_o", [MH, 1])

    _cnt = [0]

    def sbt(shape, dt=F32, name=None):
        _cnt[0] += 1
        nm = name or f"t{_cnt[0]}"
        t = st.enter_context(nc.sbuf_tensor(nm, list(shape), dt))
        return V(t[:], Res(nm))

    class Region:
        def __init__(s, nbytes, name):
            s.t = st.enter_context(nc.sbuf_tensor(name, [128, nbytes // 4], F32))
            s.r = Res(name)
            s.off = 0
            s.n = nbytes
            s.name = name

        def reset(s):
            s.off = 0

        def barrier(s):
            P.op("pool", lambda e: e.memset(s.t[0:1, 0:1], 0.0), writes=[s.r])
            s.off = 0

        def get(s, shape, dt=F32, name="v"):
            esz = 4 if dt in (F32, I32) else 2
            per = int(np.prod(shape[1:])) * esz
            per4 = (per + 3) // 4
            assert s.off + per4 * 4 <= s.n, (s.name, name, s.off, per4 * 4, s.n)
            ap = s.t[:, s.off // 4: s.off // 4 + per4]
            s.off += per4 * 4
            if dt != F32:
                ap = ap.bitcast(dt)
            n = int(np.prod(shape[1:]))
            ap = ap[:, 0:n]
            if len(shape) == 3:
                ap = ap.rearrange("p (a b) -> p a b", b=shape[2])
            elif len(shape) == 4:
                ap = ap.rearrange("p (a b c) -> p a b c", b=shape[2], c=shape[3])
            ap = ap[0:shape[0]]
            _cnt[0] += 1
            return V(ap, s.r.k(f"{name}{_cnt[0]}"))

    R1B = max(KC * NTOK * 2, NOT * D * 4)
    R2B = max(KM, KC) * NOW * 2
    R1 = Region(R1B, "R1")
    R2 = Region(R2B, "R2")
    R3B = 88 * 1024
    R3 = Region(R3B, "R3")

    psb = []
    for i in range(8):
        t = st.enter_context(nc.psum_tensor(f"ps{i}", [128, 512], F32))
        psb.append(V(t[:], Res(f"ps{i}")))
        psb[-1].r.excl = True
    _psi = [0]
    reserved = set()

    def nps():
        while True:
            i = _psi[0] % 8
            _psi[0] += 1
            if i not in reserved:
                return psb[i]

    def psbf(p):
        return p.ap.bitcast(BF16)

    ident = sbt([128, 128], F32, "ident")
    identb = sbt([128, 128], BF16, "identb")
    maskle = sbt([128, 128], F32, "maskle")
    reltt = sbt([33, AH], F32, "reltt")
    bstruct = sbt([128, c.NOB * 8], F32, "bstruct")
    prefmask = sbt([128, 8], F32, "prefmask")
    flagc = sbt([128, 1], F32, "flagc")
    iotap = sbt([128, 1], F32, "iotap")
    iotab = sbt([8, c.NB], F32, "iotab")
    onesf = sbt([128, 8], F32, "onesf")
    onesb = sbt([128, 8], BF16, "onesb")
    bi_t = sbt([MH, 1], F32, "bi_t")
    bf_t = sbt([MH, 1], F32, "bf_t")
    nbf_t = sbt([MH, 1], F32, "nbf_t")
    sm_t = sbt([MH, 1], F32, "sm_t")
    t31bc = sbt([128, AH], F32, "t31bc")
    bbv = sbt([128, c.NOB * 8], F32, "bbv")
    qTs = sbt([128, AH, 8], BF16, "qTs")
    kTs = sbt([128, AH, 8], BF16, "kTs")
    T256s = sbt([8, AH, 128], F32, "T256s")
    T0s = sbt([8, AH, 8], F32, "T0s")
    epT = sbt([128, NT, MH], F32, "epT")
    flT = sbt([128, NT, MH], F32, "flT")
    wpb = sbt([128, MH // 2, NT + 1], F32, "wpb")
    mouts = sbt([MH, 2], F32, "mouts")

    def cload(dst, src, eng="sp"):
        P.dma(eng, "const", lambda e, d=dst, s_=src: e.dma_start(out=d.ap, in_=s_), writes=[dst.r])

    cload(ident, cin["ident"])
    cload(maskle, cin["maskle"])
    cload(reltt, relt)
    cload(bstruct, cin["bstruct"].partition_broadcast(128))
    cload(prefmask, cin["prefmask"].partition_broadcast(128))
    cload(flagc, flag.partition_broadcast(128))
    cload(iotap, cin["iotap"])
    cload(iotab, cin["iotab"])
    cload(bi_t, b_i)
    cload(bf_t, b_f)
    cload(sm_t, sm)
    P.op("dve", lambda e: e.tensor_copy(out=identb.ap, in_=ident.ap), reads=[ident.r], writes=[identb.r])
    P.op("pool", lambda e: e.memset(onesf.ap, 1.0), writes=[onesf.r])
    epsD = sbt([128, 1], F32, "epsD")
    epsV = sbt([128, 1], F32, "epsV")
    one1 = sbt([128, 1], F32, "one1")
    P.op("pool", lambda e: e.memset(epsD.ap, float(D * EPS)), writes=[epsD.r])
    P.op("pool", lambda e: e.memset(epsV.ap, float(128 * EPS)), writes=[epsV.r])
    P.op("pool", lambda e: e.memset(one1.ap, 1.0), writes=[one1.r])
    P.op("pool", lambda e: e.memset(onesb.ap, 1.0), writes=[onesb.r])
    P.op("dve", lambda e: e.tensor_scalar(out=nbf_t.ap, in0=bf_t.ap, scalar1=-1.0, scalar2=None, op0=ALU.mult),
         reads=[bf_t.r], writes=[nbf_t.r])
    fm1 = sbt([128, 1], F32, "fm1")
    P.op("dve", lambda e: e.tensor_scalar(out=fm1.ap, in0=flagc.ap, scalar1=-1.0, scalar2=BIG, op0=ALU.add, op1=ALU.mult),
         reads=[flagc.r], writes=[fm1.r])
    for v in range(c.NOB):
        P.op("dve", lambda e, v=v: e.scalar_tensor_tensor(out=bbv[:, v * 8:(v + 1) * 8], in0=prefmask.ap, scalar=fm1[:, 0:1],
                                                          in1=bstruct[:, v * 8:(v + 1) * 8], op0=ALU.mult, op1=ALU.add),
             reads=[prefmask.r, fm1.r, bstruct.r], writes=[bbv.r])

    def load_g(slot, row):
        g = R3.get([128, D], F32, "gbc")
        P.dma("sp", f"g{slot}", lambda e: e.dma_start(out=g.ap, in_=gvec[row:row + 1, :].partition_broadcast(128)), writes=[g.r])
        P.op("pool", lambda e: e.tensor_scalar(out=g.ap, in0=g.ap, scalar1=float(math.sqrt(D)), scalar2=None, op0=ALU.mult),
             reads=[g.r], writes=[g.r])
        return g

    def evac_copy(eng, out_ap, in_ap, rd, wr):
        if eng == "act":
            P.op("act", lambda e: e.copy(out=out_ap, in_=in_ap), reads=rd, writes=wr)
        else:
            P.op(eng, lambda e: e.tensor_copy(out=out_ap, in_=in_ap), reads=rd, writes=wr)

    _alt = [0]

    def alt_eng():
        _alt[0] += 1
        return "act" if _alt[0] % 2 else "dve"

    def norm_tile(src_ap, src_r, r, g, xn_ap, xn_r, ssq, rstd, junk):
        P.op("act", lambda e: e.activation(out=junk[0:r, :], in_=src_ap, func=AF.Square, accum_out=ssq[0:r, 0:1]),
             reads=[src_r], writes=[junk.r, ssq.r])
        P.op("act", lambda e: e.activation(out=rstd[0:r, 0:1], in_=ssq[0:r, 0:1], func=AF.Ln, bias=epsD[0:r, 0:1], scale=1.0), reads=[ssq.r, epsD.r], writes=[rstd.r])
        P.op("act", lambda e: e.activation(out=rstd[0:r, 0:1], in_=rstd[0:r, 0:1], func=AF.Exp, scale=-0.5), reads=[rstd.r], writes=[rstd.r])
        P.op("dve", lambda e: e.scalar_tensor_tensor(out=xn_ap, in0=src_ap, scalar=rstd[0:r, 0:1], in1=g[0:r, :],
                                                     op0=ALU.mult, op1=ALU.mult), reads=[src_r, rstd.r, g.r], writes=[xn_r])

    def transpose_into(xn, r, nch, dstT, tok0, dst_r):
        j0 = 0
        while j0 < nch:
            n = min(8, nch - j0)
            p = nps()
            pb = psbf(p).rearrange("p (a b) -> p a b", b=128)

            def f(e, j0=j0, n=n, pb=pb):
                ins = None
                for j in range(n):
                    ins = e.transpose(out=pb[:, j, 0:r], in_=xn[0:r, (j0 + j) * 128:(j0 + j + 1) * 128], identity=identb[0:r, 0:r])
                return ins
            P.op("pe", f, reads=[xn.r, identb.r], writes=[p.r])
            evac_copy(alt_eng(), dstT[:, j0:j0 + n, tok0:tok0 + r], pb[:, 0:n, 0:r], [p.r], [dst_r])
            j0 += n

    wslots = {}

    def wload(pool_name, nslots, region, shape, pieces):
        key = pool_name
        if key not in wslots:
            wslots[key] = [[region.get(shape, BF16, name=f"w{pool_name}{i}") for i in range(nslots)], 0]
        sl = wslots[key]
        w = sl[0][sl[1] % nslots]
        ch = f"w{pool_name}{sl[1] % nslots}"
        sl[1] += 1
        for pi_, (c0, src) in enumerate(pieces):
            ncol = src.shape[1]
            nk_ = src.shape[0] // 128
            P.dma("pool", ch, lambda e, c0=c0, src=src, ncol=ncol, nk_=nk_: e.dma_start(
                out=w[:, 0:nk_, c0:c0 + ncol], in_=src.rearrange("(k p) c -> p k c", p=128)), writes=[w.r.k(pi_)] if len(pieces) > 1 else [w.r])
        return w

    def mm_tok(p, r, ncol, xT, tok0, nk, w, c0, rd):
        def f(e):
            ins = None
            for k in range(nk):
                ins = e.matmul(p[0:r, 0:ncol], lhsT=xT[:, k, tok0:tok0 + r], rhs=w[:, k, c0:c0 + ncol], start=(k == 0), stop=(k == nk - 1))
            return ins
        P.op("pe", f, reads=rd + [w.r], writes=[p.r])

    def mm_feat(p, m, n, w, c0, nk, xT, tok0, rd):
        def f(e):
            ins = None
            for k in range(nk):
                ins = e.matmul(p[0:m, 0:n], lhsT=w[:, k, c0:c0 + m], rhs=xT[:, k, tok0:tok0 + n], start=(k == 0), stop=(k == nk - 1))
            return ins
        P.op("pe", f, reads=rd + [w.r], writes=[p.r])

    tiles = [(128 * t, 128) for t in range(NP + NO)] + [(128 * (NP + NO), 8)]
    own_tiles = list(range(NP, NP + NO + 1))
    def chunks(t0, t1):
        out = []
        a = t0
        while a < t1:
            n = min(512, t1 - a)
            out.append((a, n))
            a += n
        return out
    TOK_P0, TOK_O0, TOK_S0 = 0, 128 * NP, 128 * (NP + NO)
    ch_all = chunks(0, TOK_O0) + chunks(TOK_O0, TOK_S0) + [(TOK_S0, 8)]
    ch_own = chunks(TOK_O0, TOK_S0) + [(TOK_S0, 8)]

    xnT = R1.get([128, KC, NTOK], BF16, "xnT")
    xnT_k = [xnT.r.k(t) for t in range(NT)]
    vsf = R1.get([8, AH, 128], F32, "vsf")
    g0 = load_g(0, 0)
    xs_ = [R3.get([128, D], F32, "xs") for _ in range(2)]
    xn_ = [R3.get([128, D], BF16, "xn") for _ in range(2)]
    junk = R3.get([128, D], BF16, "junk")
    ssq = sbt([128, 2], F32, "ssq")
    rstd = sbt([128, 2], F32, "rstd")
    for t, (tok0, r) in enumerate(tiles):
        xs, xn = xs_[t % 2], xn_[t % 2]
        P.dma("sp", f"xs{t % 2}", lambda e, xs=xs, tok0=tok0, r=r: e.dma_start(out=xs[0:r, :], in_=x_all[tok0:tok0 + r, :]), writes=[xs.r])
        norm_tile(xs[0:r, :], xs.r, r, g0, xn[0:r, :], xn.r, ssq, rstd, junk)
        transpose_into(xn, r, KC, xnT, tok0, xnT_k[t])

    R3.barrier()
    if STOP == "p0":
        P.emit()
        st.close()
        return nc, hc
    selc = R3.get([MH, (MH // 2) * 128], F32, "selc")
    cload(selc, cin["sel"])
    wg = wload("g", 1, R3, [128, KC, 2 * MH], [(0, w_in[:, c.c_mi:c.c_mi + 2 * MH])])
    NTK = NTOK
    li = R3.get([MH, NTK], F32, "li")
    nb = R3.get([MH, NTK], F32, "nb")
    nb2 = R3.get([MH, NTK], F32, "nb2")
    G = R3.get([MH, NTK], F32, "G")
    ep = R3.get([MH, NTK], F32, "ep")
    fl = R3.get([MH, NTK], F32, "fl")
    for (a, n) in ch_all:
        pi, pf = nps(), nps()
        mm_feat(pi, MH, n, wg, 0, KC, xnT, a, [xnT.r])
        mm_feat(pf, MH, n, wg, MH, KC, xnT, a, [xnT.r])
        P.op("act", lambda e, pi=pi, a=a, n=n: e.activation(out=li[:, a:a + n], in_=pi[0:MH, 0:n], func=AF.Identity, bias=bi_t[:, 0:1], scale=1.0),
             reads=[pi.r, bi_t.r], writes=[li.r])
        P.op("act", lambda e, pf=pf, a=a, n=n: e.activation(out=nb[:, a:a + n], in_=pf[0:MH, 0:n], func=AF.Exp, bias=nbf_t[:, 0:1], scale=-1.0),
             reads=[pf.r, nbf_t.r], writes=[nb.r])
    P.op("act", lambda e: e.activation(out=nb.ap, in_=nb.ap, func=AF.Ln, bias=one1[0:MH, 0:1], scale=1.0), reads=[nb.r, one1.r], writes=[nb.r])
    seqs = [(TOK_P0, 128 * NP), (TOK_O0, 128 * NO), (TOK_S0, 8)]
    cur, oth = nb, nb2
    kk = 1
    maxlen = max(128 * NP, 128 * NO)
    while kk < maxlen:
        def f(e, cur=cur, oth=oth, kk=kk):
            for (a, n) in seqs:
                if kk < n:
                    yield e.tensor_copy(out=oth[:, a:a + kk], in_=cur[:, a:a + kk])
                    yield e.tensor_tensor(out=oth[:, a + kk:a + n], in0=cur[:, a + kk:a + n], in1=cur[:, a:a + n - kk], op=ALU.add)
                else:
                    yield e.tensor_copy(out=oth[:, a:a + n], in_=cur[:, a:a + n])
        P.op("dve", f, reads=[cur.r], writes=[oth.r])
        cur, oth = oth, cur
        kk *= 2
    NBt = cur
    P.op("dve", lambda e: e.tensor_tensor(out=G.ap, in0=li.ap, in1=NBt.ap, op=ALU.add), reads=[li.r, NBt.r], writes=[G.r])
    cm = sbt([MH, NT], F32, "cm")
    Rext = sbt([MH, NT + 3], F32, "Rext")
    negR = sbt([MH, NT + 3], F32, "negR")
    negRl = sbt([MH, NT + 3], F32, "negRl")
    wprev = sbt([MH, NT + 1], F32, "wprev")
    seq_tiles = [(0, NP), (NP, NO), (NP + NO, 1)]
    rofs = [0, NP + 1, NP + NO + 2]
    for si, (t0, ntl) in enumerate(seq_tiles):
        a, n = seqs[si]
        if n >= 128:
            P.op("dve", lambda e, t0=t0, ntl=ntl, a=a, n=n: e.tensor_reduce(out=cm[:, t0:t0 + ntl], in_=G[:, a:a + n].rearrange("p (c l) -> p c l", l=128),
                                                                    axis=AX.X, op=ALU.max), reads=[G.r], writes=[cm.r])
        else:
            P.op("dve", lambda e, t0=t0, a=a, n=n: e.tensor_reduce(out=cm[:, t0:t0 + 1], in_=G[:, a:a + n], axis=AX.X, op=ALU.max),
                 reads=[G.r], writes=[cm.r])
        ro = rofs[si]
        if si == 0:
            P.op("dve", lambda e, ro=ro: e.memset(Rext[:, ro:ro + 1], 0.0), writes=[Rext.r])
        elif si == 1:
            def f(e, ro=ro):
                yield e.tensor_tensor(out=Rext[:, ro:ro + 1], in0=Rext[:, ro - 1:ro], in1=NBt[:, TOK_O0 - 1:TOK_O0], op=ALU.subtract)
                yield e.tensor_scalar(out=Rext[:, ro:ro + 1], in0=Rext[:, ro:ro + 1], scalar1=flagc[0:MH, 0:1], scalar2=None, op0=ALU.mult)
            P.op("dve", f, reads=[Rext.r, NBt.r, flagc.r], writes=[Rext.r])
        else:
            P.op("dve", lambda e, ro=ro: e.tensor_copy(out=Rext[:, ro:ro + 1], in_=sm_t.ap), reads=[sm_t.r], writes=[Rext.r])

        def f(e, ro=ro, t0=t0, ntl=ntl):
            for j in range(ntl):
                yield e.tensor_tensor(out=Rext[:, ro + 1 + j:ro + 2 + j], in0=Rext[:, ro + j:ro + 1 + j], in1=cm[:, t0 + j:t0 + j + 1], op=ALU.max)
        P.op("dve", f, reads=[Rext.r, cm.r], writes=[Rext.r])
        P.op("dve", lambda e, ro=ro, t0=t0, ntl=ntl: e.tensor_tensor(out=wprev[:, t0:t0 + ntl], in0=Rext[:, ro:ro + ntl], in1=Rext[:, ro + 1:ro + 1 + ntl], op=ALU.subtract),
             reads=[Rext.r], writes=[wprev.r])
    P.op("act", lambda e: e.activation(out=wprev[:, 0:NT], in_=wprev[:, 0:NT], func=AF.Exp), reads=[wprev.r], writes=[wprev.r])
    P.op("dve", lambda e: e.tensor_scalar(out=negR.ap, in0=Rext.ap, scalar1=-1.0, scalar2=None, op0=ALU.mult), reads=[Rext.r], writes=[negR.r])
    P.op("dve", lambda e: e.tensor_scalar(out=negRl.ap, in0=Rext.ap, scalar1=-1.0, scalar2=float(math.log(0.125)), op0=ALU.mult, op1=ALU.add),
         reads=[Rext.r], writes=[negRl.r])
    def f(e):
        yield e.tensor_tensor(out=mouts[:, 0:1], in0=Rext[:, rofs[1] + NO:rofs[1] + NO + 1], in1=NBt[:, TOK_S0 - 1:TOK_S0], op=ALU.subtract)
        yield e.tensor_tensor(out=mouts[:, 1:2], in0=Rext[:, rofs[2] + 1:rofs[2] + 2], in1=NBt[:, TOK_S0 + 7:TOK_S0 + 8], op=ALU.subtract)
    P.op("dve", f, reads=[Rext.r, NBt.r], writes=[mouts.r])
    P.dma("sp", "mo", lambda e: e.dma_start(out=mp_o, in_=mouts[:, 0:1]), reads=[mouts.r])
    P.dma("sp", "mo", lambda e: e.dma_start(out=ms_o, in_=mouts[:, 1:2]), reads=[mouts.r])
    for si, (t0, ntl) in enumerate(seq_tiles):
        ro = rofs[si]
        for j in range(ntl):
            tok0, r = tiles[t0 + j]
            P.op("act", lambda e, tok0=tok0, r=r, ro=ro, j=j: e.activation(out=ep[:, tok0:tok0 + r], in_=G[:, tok0:tok0 + r], func=AF.Exp,
                                                                        bias=negRl[:, ro + 1 + j:ro + 2 + j], scale=1.0), reads=[G.r, negRl.r], writes=[ep.r])
            P.op("act", lambda e, tok0=tok0, r=r, ro=ro, j=j: e.activation(out=fl[:, tok0:tok0 + r], in_=NBt[:, tok0:tok0 + r], func=AF.Exp,
                                                                        bias=negR[:, ro + 1 + j:ro + 2 + j], scale=1.0), reads=[NBt.r, negR.r], writes=[fl.r])
    for (src, dst) in ((ep, epT), (fl, flT)):
        p = nps()
        pv = p[:, 0:NT * MH].rearrange("p (t h) -> p t h", h=MH)

        def f(e, src=src, pv=pv):
            ins = None
            for t, (tok0, r) in enumerate(tiles):
                ins = e.transpose(out=pv[0:r, t, :], in_=src[:, tok0:tok0 + r], identity=ident[0:MH, 0:MH])
            return ins
        P.op("pe", f, reads=[src.r, ident.r], writes=[p.r])
        P.op("dve", lambda e, dst=dst, pv=pv: e.tensor_copy(out=dst[:, 0:NT - 1, :], in_=pv[:, 0:NT - 1, :]), reads=[p.r], writes=[dst.r])
        P.op("dve", lambda e, dst=dst, pv=pv: e.tensor_copy(out=dst[0:8, NT - 1, :], in_=pv[0:8, NT - 1, :]), reads=[p.r], writes=[dst.r])
    for hp in range(MH // 2):
        p = nps()
        P.op("pe", lambda e, p=p, hp=hp: e.matmul(p[:, 0:NT], lhsT=selc[:, hp * 128:(hp + 1) * 128], rhs=wprev[:, 0:NT], start=True, stop=True),
             reads=[selc.r, wprev.r], writes=[p.r])
        evac_copy("dve", wpb[:, hp, 0:NT], p[:, 0:NT], [p.r], [wpb.r])
    P.op("pool", lambda e: e.memset(wpb[:, :, NT:NT + 1], 1.0), writes=[wpb.r])

    if STOP == "gates":
        P.emit()
        st.close()
        return nc, hc
    mixT = R2.get([128, KM, NOW], BF16, "mixT")
    R3.barrier()
    wslots.clear()
    gmh = R3.get([128, MH * 128], F32, "gmh")
    P.dma("sp", "gmh", lambda e: e.dma_start(out=gmh.ap, in_=g_mh.partition_broadcast(128)), writes=[gmh.r])
    P.op("pool", lambda e: e.tensor_scalar(out=gmh.ap, in0=gmh.ap, scalar1=float(math.sqrt(128.0)), scalar2=None, op0=ALU.mult),
         reads=[gmh.r], writes=[gmh.r])
    qTm = R3.get([128, NOW], BF16, "qTm")
    kTz = R3.get([128, 2, NOW], BF16, "kTz")
    Kt = R3.get([128, NT, 128], BF16, "Kt")
    Va = R3.get([128, NT, 2, 129], BF16, "Va")
    gsig = R3.get([128, NOT, 256], F32, "gsig")
    Cf = R3.get([128, 129], F32, "Cf")
    Cbz = R3.get([128, 2, 129], BF16, "Cbz")
    StT = [R3.get([128, 2, 128], BF16, "StT") for _ in range(2)]
    omb = [R3.get([128, 256], BF16, "omb") for _ in range(2)]
    sml = [R3.get([128, 16], F32, "sml") for _ in range(2)]
    junk2 = R3.get([128, 128], F32, "junk2")
    Cin = R3.get([128, 129], F32, "Cin")
    P.op("pool", lambda e: e.memset(Va[:, :, :, 128:129], 1.0), writes=[Va.r])
    P.op("pool", lambda e: e.memset(kTz.ap, 0.0), writes=[kTz.r])
    P.op("pool", lambda e: e.memset(Cbz.ap, 0.0), writes=[Cbz.r])
    own_off = lambda t: 128 * (t - NP)

    ptb = sbt([128, NPG], I32, "ptb")
    ptf = sbt([128, NPG], F32, "ptf")
    idxp = sbt([128, NPG], I32, "idxp")
    kmsh = sbt([128, AH * NB], BF16, "kmsh")
    kmsl = sbt([128, AH * NB], BF16, "kmsl")
    kmst = R3.get([128, AH * NB], F32, "kmst")
    kpg = [R3.get([128, AH * 128], F32, "kpg") for _ in range(2)]
    kpb = [R3.get([128, AH * 128], BF16, "kpb") for _ in range(2)]
    kmrow = R3.get([NB, AH * 128], F32, "kmrow")
    ewf = R3.get([128, 2 * NB - 1], F32, "ewf")
    ewb = R3.get([128, 2 * NB - 1], BF16, "ewb")
    cload(ewf, cin["ewin"])

    def pass1():
        P.dma("sp", "ptl", lambda e: e.dma_start(out=ptb.ap, in_=pt.partition_broadcast(128)), writes=[ptb.r])

        def f(e):
            yield e.tensor_copy(out=ptf.ap, in_=ptb.ap)
            yield e.tensor_scalar(out=ptf.ap, in0=ptf.ap, scalar1=128.0, scalar2=iotap[:, 0:1], op0=ALU.mult, op1=ALU.add)
            yield e.tensor_copy(out=idxp.ap, in_=ptf.ap)
        P.op("dve", f, reads=[ptb.r, iotap.r], writes=[ptf.r, idxp.r])
        P.op("dve", lambda e: e.tensor_copy(out=ptf.ap, in_=ptb.ap), reads=[ptb.r, idxp.r], writes=[ptf.r])
        pKa, pKb = nps(), nps()
        rs_i = [psb.index(pKa), psb.index(pKb)]
        reserved.update(rs_i)
        HW_ = AH * 128
        H1 = min(512, HW_)
        def page_dma(j):
            kp = kpg[j % 2]
            P.dma("pool", f"kpg{j % 2}", lambda e, kp=kp, j=j: e.indirect_dma_start(out=kp.ap, out_offset=None, in_=cache_k,
                                                                        in_offset=bass.IndirectOffsetOnAxis(ap=idxp[:, j:j + 1], axis=0)),
                  reads=[idxp.r], writes=[kp.r])
        P.op("dve", lambda e: e.tensor_copy(out=ewb.ap, in_=ewf.ap), reads=[ewf.r], writes=[ewb.r])
        page_dma(0)
        for j in range(NPG):
            kp = kpg[j % 2]
            kb = kpb[j % 2]
            if j + 1 < NPG:
                page_dma(j + 1)
            P.op("pool", lambda e, kp=kp, kb=kb: e.tensor_copy(out=kb.ap, in_=kp.ap), reads=[kp.r], writes=[kb.r])
            b_ = j // 2

            def f(e, kb=kb, j=j, b_=b_):
                ins = e.matmul(pKa[0:NB, 0:H1], lhsT=ewb[:, NB - 1 - b_:2 * NB - 1 - b_], rhs=kb[:, 0:H1], start=(j == 0), stop=(j == NPG - 1))
                if HW_ > 512:
                    ins = e.matmul(pKb[0:NB, 0:HW_ - 512], lhsT=ewb[:, NB - 1 - b_:2 * NB - 1 - b_], rhs=kb[:, 512:HW_], start=(j == 0), stop=(j == NPG - 1))
                return ins
            P.op("pe", f, reads=[kb.r, ewb.r], writes=[pKa.r, pKb.r])
            yield

        P.op("dve", lambda e: e.tensor_copy(out=kmrow[:, 0:H1], in_=pKa[0:NB, 0:H1]), reads=[pKa.r], writes=[kmrow.r])
        if HW_ > 512:
            P.op("act", lambda e: e.copy(out=kmrow[:, 512:HW_], in_=pKb[0:NB, 0:HW_ - 512]), reads=[pKb.r], writes=[kmrow.r])
        pKt = nps()

        def f(e):
            ins = None
            for h in range(AH):
                ins = e.transpose(out=pKt[:, h * NB:(h + 1) * NB], in_=kmrow[:, 128 * h:128 * h + 128], identity=ident[0:NB, 0:NB])
            return ins
        P.op("pe", f, reads=[kmrow.r, ident.r], writes=[pKt.r])

        def f(e):
            yield e.tensor_scalar(out=kmst.ap, in0=pKt[:, 0:AH * NB], scalar1=1.0 / 256.0, scalar2=None, op0=ALU.mult)
            yield e.tensor_copy(out=kmsh.ap, in_=kmst.ap)
            yield e.tensor_tensor(out=kmst.ap, in0=kmst.ap, in1=kmsh.ap, op=ALU.subtract)
            yield e.tensor_copy(out=kmsl.ap, in_=kmst.ap)
        P.op("dve", f, reads=[pKt.r], writes=[kmst.r, kmsh.r, kmsl.r])
        for x_ in rs_i:
            reserved.discard(x_)

    p1 = pass1()

    def p1step(n=1):
        for _ in range(n):
            try:
                next(p1)
            except StopIteration:
                return

    def do_pair(hp):
        wA = wload("m", 2, R3, [128, KC, 256], [(0, w_in[:, c.c_mq + 128 * hp:c.c_mq + 128 * hp + 128]),
                                                 (128, w_in[:, c.c_mk + 128 * hp:c.c_mk + 128 * hp + 128])])
        for (a, n) in ch_own:
            p = nps()
            mm_feat(p, 128, n, wA, 0, KC, xnT, a, [xnT.r])
            evac_copy(alt_eng(), qTm[:, a - TOK_O0:a - TOK_O0 + n], p[:, 0:n], [p.r], [qTm.r])
            p = nps()
            mm_feat(p, 128, n, wA, 128, KC, xnT, a, [xnT.r])
            evac_copy("act", kTz[0:64, 0, a - TOK_O0:a - TOK_O0 + n], p[0:64, 0:n], [p.r], [kTz.r])
            evac_copy("dve", kTz[64:128, 1, a - TOK_O0:a - TOK_O0 + n], p[64:128, 0:n], [p.r], [kTz.r])
        chk("a1")
        for t, (tok0, r) in enumerate(tiles):
            p = nps()
            mm_tok(p, r, 128, xnT, tok0, KC, wA, 128, [xnT_k[t]])
            p1step()
            P.op("dve", lambda e, p=p, t=t, r=r, hp=hp: e.tensor_tensor(
                out=Kt[0:r, t, :].rearrange("p (j d) -> p j d", d=64), in0=p[0:r, 0:128].rearrange("p (j d) -> p j d", d=64),
                in1=epT[0:r, t, 2 * hp:2 * hp + 2].unsqueeze(2).to_broadcast([r, 2, 64]), op=ALU.mult),
                reads=[p.r, epT.r], writes=[Kt.r])
        chk("a2")
        wB = wload("m", 2, R3, [128, KC, 256], [(0, w_in[:, c.c_mv + 256 * hp:c.c_mv + 256 * hp + 256])])
        for t, (tok0, r) in enumerate(tiles):
            p = nps()
            mm_tok(p, r, 256, xnT, tok0, KC, wB, 0, [xnT_k[t]])
            p1step()
            evac_copy(alt_eng(), Va[0:r, t, :, 0:128], p[0:r, 0:256].rearrange("p (j d) -> p j d", d=128), [p.r], [Va.r])
        chk("a3")
        wC = wload("m", 2, R3, [128, KC, 256], [(0, w_in[:, c.c_mo + 256 * hp:c.c_mo + 256 * hp + 256])])
        for t in own_tiles:
            tok0, r = tiles[t]
            p = nps()
            mm_tok(p, r, 256, xnT, tok0, KC, wC, 0, [xnT_k[t]])
            P.op("act", lambda e, p=p, t=t, r=r: e.activation(out=gsig[0:r, t - NP, :], in_=p[0:r, 0:256], func=AF.Sigmoid), reads=[p.r], writes=[gsig.r])
            P.op("pool", lambda e, t=t, r=r, hp=hp: e.tensor_tensor(out=gsig[0:r, t - NP, :], in0=gsig[0:r, t - NP, :], in1=gmh[0:r, 256 * hp:256 * hp + 256], op=ALU.mult),
                 reads=[gsig.r, gmh.r], writes=[gsig.r])

        chk("a4")

        def state_update(t, r, last):
            p = nps()

            def f(e, p=p, t=t, r=r):
                ins = None
                for j in range(2):
                    ins = e.matmul(p[64 * j:64 * j + 64, 0:129], lhsT=Kt[0:r, t, 64 * j:64 * j + 64], rhs=Va[0:r, t, j, :], start=True, stop=True)
                return ins
            P.op("pe", f, reads=[Kt.r, Va.r], writes=[p.r])
            P.op("dve", lambda e, p=p, t=t: e.scalar_tensor_tensor(out=Cf.ap, in0=Cf.ap, scalar=wpb[:, hp, t:t + 1], in1=p[:, 0:129], op0=ALU.mult, op1=ALU.add),
                 reads=[Cf.r, wpb.r, p.r], writes=[Cf.r])
            if not last:
                P.op("act", lambda e, t=t: e.activation(out=Cbz[0:64, 0, :], in_=Cf[0:64, :], func=AF.Identity, scale=wpb[0:64, hp, t + 1:t + 2]), reads=[Cf.r, wpb.r], writes=[Cbz.r])
                P.op("act", lambda e, t=t: e.activation(out=Cbz[64:128, 1, :], in_=Cf[64:128, :], func=AF.Identity, scale=wpb[64:128, hp, t + 1:t + 2]), reads=[Cf.r, wpb.r], writes=[Cbz.r])

        def chunk(t, r, ci):
            tok0 = tiles[t][0]
            oo = own_off(t)
            S, om, sm_ = StT[ci % 2], omb[ci % 2], sml[ci % 2]
            pS = nps()

            def f(e, pS=pS):
                ins = None
                for j in range(2):
                    ins = e.matmul(pS[0:r, j * 128:j * 128 + r], lhsT=kTz[:, j, oo:oo + r], rhs=qTm[:, oo:oo + r], start=True, stop=True)
                return ins
            P.op("pe", f, reads=[kTz.r, qTm.r], writes=[pS.r])
            chk("c1")
            for j in range(2):
                P.op("dve", lambda e, j=j, pS=pS, S=S: e.scalar_tensor_tensor(out=S[0:r, j, 0:r], in0=pS[0:r, j * 128:j * 128 + r], scalar=epT[0:r, t, 2 * hp + j:2 * hp + j + 1],
                                                                          in1=maskle[0:r, 0:r], op0=ALU.mult, op1=ALU.mult), reads=[pS.r, epT.r, maskle.r], writes=[S.r])
            chk("c2")
            pX = nps()
            pXv = pX[:, 0:512].rearrange("p (j d) -> p j d", d=256)

            def f(e, pXv=pXv, S=S):
                ins = None
                for j in range(2):
                    e.matmul(pXv[0:r, j, 0:129], lhsT=qTm[:, oo:oo + r], rhs=Cbz[:, j, :], start=True, stop=False)
                    ins = e.matmul(pXv[0:r, j, 0:129], lhsT=S[0:r, j, 0:r], rhs=Va[0:r, t, j, :], start=False, stop=True)
                return ins
            P.op("pe", f, reads=[qTm.r, Cbz.r, S.r, Va.r], writes=[pX.r])
            chk("c3")
            def f(e, pXv=pXv, sm_=sm_):
                yield e.tensor_scalar(out=sm_[0:r, 0:2], in0=pXv[0:r, :, 128], scalar1=-1.0, scalar2=None, op0=ALU.mult)
                yield e.tensor_tensor(out=sm_[0:r, 0:2], in0=sm_[0:r, 0:2], in1=pXv[0:r, :, 128], op=ALU.max)
                yield e.tensor_tensor(out=sm_[0:r, 0:2], in0=sm_[0:r, 0:2], in1=flT[0:r, t, 2 * hp:2 * hp + 2], op=ALU.max)
                yield e.reciprocal(out=sm_[0:r, 2:4], in_=sm_[0:r, 0:2])
            P.op("dve", f, reads=[pX.r, flT.r], writes=[sm_.r.k("a")])
            chk("c4")
            for j in range(2):
                P.op("act", lambda e, j=j, pXv=pXv, sm_=sm_: e.activation(out=junk2[0:r, :], in_=pXv[0:r, j, 0:128], func=AF.Square, scale=sm_[0:r, 2 + j:3 + j],
                                                                     accum_out=sm_[0:r, 4 + j:5 + j]), reads=[pX.r, sm_.r.k("a")], writes=[junk2.r, sm_.r.k(f"b{j}")])

            chk("c5")
            P.op("act", lambda e, sm_=sm_: e.activation(out=sm_[0:r, 6:8], in_=sm_[0:r, 4:6], func=AF.Ln, bias=epsV[0:r, 0:1], scale=1.0),
                 reads=[sm_.r.k("b0"), sm_.r.k("b1"), epsV.r], writes=[sm_.r.k("c0")])
            P.op("act", lambda e, sm_=sm_: e.activation(out=sm_[0:r, 6:8], in_=sm_[0:r, 6:8], func=AF.Exp, scale=-0.5),
                 reads=[sm_.r.k("c0")], writes=[sm_.r.k("c0")])
            P.op("dve", lambda e, sm_=sm_: e.tensor_tensor(out=sm_[0:r, 8:10], in0=sm_[0:r, 6:8], in1=sm_[0:r, 2:4], op=ALU.mult),
                 reads=[sm_.r.k("a"), sm_.r.k("c0")], writes=[sm_.r.k("c")])
            for j in range(2):
                P.op("dve", lambda e, j=j, pXv=pXv, sm_=sm_, om=om: e.scalar_tensor_tensor(out=om[0:r, j * 128:(j + 1) * 128], in0=pXv[0:r, j, 0:128], scalar=sm_[0:r, 8 + j:9 + j],
                                                                                in1=gsig[0:r, t - NP, j * 128:(j + 1) * 128], op0=ALU.mult, op1=ALU.mult),
                     reads=[pX.r, sm_.r.k("c"), gsig.r], writes=[om.r])
            chk("c7")
            pT = nps()
            pTb = psbf(pT).rearrange("p (a b) -> p a b", b=128)

            def f(e, pTb=pTb, om=om):
                ins = None
                for j in range(2):
                    ins = e.transpose(out=pTb[:, j, 0:r], in_=om[0:r, j * 128:(j + 1) * 128], identity=identb[0:r, 0:r])
                return ins
            P.op("pe", f, reads=[om.r, identb.r], writes=[pT.r])
            evac_copy("act", mixT[:, 2 * hp:2 * hp + 2, oo:oo + r], pTb[:, 0:2, 0:r], [pT.r], [mixT.r.k(2 * hp)])

        P.op("pool", lambda e: e.memset(Cf.ap, 0.0), writes=[Cf.r])
        for t in range(NP):
            state_update(t, 128, True)
        chk("a5")
        P.op("dve", lambda e: e.tensor_scalar(out=Cf.ap, in0=Cf.ap, scalar1=flagc[:, 0:1], scalar2=None, op0=ALU.mult), reads=[Cf.r, flagc.r], writes=[Cf.r])
        P.op("act", lambda e: e.activation(out=Cbz[0:64, 0, :], in_=Cf[0:64, :], func=AF.Identity, scale=wpb[0:64, hp, NP:NP + 1]), reads=[Cf.r, wpb.r], writes=[Cbz.r])
        P.op("act", lambda e: e.activation(out=Cbz[64:128, 1, :], in_=Cf[64:128, :], func=AF.Identity, scale=wpb[64:128, hp, NP:NP + 1]), reads=[Cf.r, wpb.r], writes=[Cbz.r])
        chk("a6")
        for i in range(NO):
            t = NP + i
            chunk(t, 128, i)
            if i == 0:
                chk("a7")
            state_update(t, 128, i == NO - 1)
            if i == 0:
                chk("a8")
        chk("a9")
        for j in range(2):
            h = 2 * hp + j
            P.dma("sp", "co", lambda e, j=j, h=h: e.dma_start(out=Cp_o[64 * h:64 * h + 64, :], in_=Cf[64 * j:64 * j + 64, 0:128]), reads=[Cf.r])
            P.dma("sp", "co", lambda e, j=j, h=h: e.dma_start(out=np_o[64 * h:64 * h + 64, :], in_=Cf[64 * j:64 * j + 64, 128:129]), reads=[Cf.r])
        chk("a10")
        P.dma("sp", "cin", lambda e: e.dma_start(out=Cin[:, 0:128], in_=sC[128 * hp:128 * hp + 128, :]), writes=[Cin.r])
        P.dma("sp", "cin", lambda e: e.dma_start(out=Cin[:, 128:129], in_=sn[128 * hp:128 * hp + 128, :]), writes=[Cin.r])
        P.op("dve", lambda e: e.tensor_copy(out=Cf.ap, in_=Cin.ap), reads=[Cin.r], writes=[Cf.r])
        ts_ = NP + NO
        P.op("act", lambda e: e.activation(out=Cbz[0:64, 0, :], in_=Cf[0:64, :], func=AF.Identity, scale=wpb[0:64, hp, ts_:ts_ + 1]), reads=[Cf.r, wpb.r], writes=[Cbz.r])
        P.op("act", lambda e: e.activation(out=Cbz[64:128, 1, :], in_=Cf[64:128, :], func=AF.Identity, scale=wpb[64:128, hp, ts_:ts_ + 1]), reads=[Cf.r, wpb.r], writes=[Cbz.r])
        chunk(ts_, 8, 0)
        state_update(ts_, 8, True)
        for j in range(2):
            h = 2 * hp + j
            P.dma("sp", "co", lambda e, j=j, h=h: e.dma_start(out=Cs_o[64 * h:64 * h + 64, :], in_=Cf[64 * j:64 * j + 64, 0:128]), reads=[Cf.r])
            P.dma("sp", "co", lambda e, j=j, h=h: e.dma_start(out=ns_o[64 * h:64 * h + 64, :], in_=Cf[64 * j:64 * j + 64, 128:129]), reads=[Cf.r])

    for hp in range(MH // 2):
        do_pair(hp)
    p1step(10 ** 6)

    if STOP == "p1a":
        P.emit()
        st.close()
        return nc, hc
    R3.barrier()
    wslots.clear()
    ohl = R3.get([33, c.OHW], F32, "ohl")
    cload(ohl, cin["ohlong"])
    Tb = {d: R3.get([128, AH, 256], F32, f"Tb{d}") for d in (0, 128, 256)}
    p = nps()
    P.op("pe", lambda e, p=p: e.matmul(p[:, 0:AH], lhsT=ohl[:, 600:728], rhs=reltt.ap, start=True, stop=True), reads=[ohl.r, reltt.r], writes=[p.r])
    evac_copy("dve", t31bc.ap, p[:, 0:AH], [p.r], [t31bc.r])
    P.op("pool", lambda e: e.memset(Tb[0][:, :, 128:256], -BIG), writes=[Tb[0].r])
    P.op("dve", lambda e: e.tensor_copy(out=Tb[256][:, :, 0:128], in_=t31bc.ap.unsqueeze(2).to_broadcast([128, AH, 128])), reads=[t31bc.r], writes=[Tb[256].r])
    for d in (0, 128, 256):
        for k0 in range(0, 256, 64):
            if (d == 0 and k0 >= 128) or (d == 256 and k0 < 128):
                continue
            p = nps()
            pv = p[:, 0:64 * AH].rearrange("p (k h) -> p k h", h=AH)

            def f(e, d=d, k0=k0, pv=pv):
                ins = None
                for kk in range(64):
                    stt = d - (k0 + kk) + 255
                    ins = e.matmul(pv[:, kk, :], lhsT=ohl[:, stt:stt + 128], rhs=reltt.ap, start=True, stop=True)
                return ins
            P.op("pe", f, reads=[ohl.r, reltt.r], writes=[p.r])
            evac_copy(alt_eng(), Tb[d][:, :, k0:k0 + 64], pv.rearrange("p k h -> p h k"), [p.r], [Tb[d].r])
    P.op("dve", lambda e: e.tensor_tensor(out=Tb[256].ap, in0=Tb[256].ap, in1=t31bc.ap.unsqueeze(2).to_broadcast([128, AH, 256]), op=ALU.subtract),
         reads=[Tb[256].r, t31bc.r], writes=[Tb[256].r])
    P.op("dve", lambda e: e.tensor_copy(out=T256s.ap, in_=Tb[256][0:8, :, 128:256]), reads=[Tb[256].r], writes=[T256s.r])
    P.op("dve", lambda e: e.tensor_copy(out=T0s.ap, in_=Tb[0][0:8, :, 0:8]), reads=[Tb[0].r], writes=[T0s.r])

    chk("b0")
    NBLK, NPB = c.NBLK, c.NPB
    qTa = R3.get([128, NOW], BF16, "qTa")
    kTa = R3.get([128, NTOK], BF16, "kTa")
    Vt = R3.get([128, NT, 128], BF16, "Vt")
    Lg2 = [R3.get([128, NBLK * 256], F32, "Lg") for _ in range(2)]
    Pb2 = [R3.get([128, NBLK * 256], BF16, "Pb") for _ in range(2)]
    PT2 = [R3.get([128, NBLK * 2, 128], BF16, "PT") for _ in range(2)]
    kst = [R3.get([128, 128], F32, "kst") for _ in range(2)]
    vst = [R3.get([128, 128], F32, "vst") for _ in range(2)]
    ksum = sbt([128, 8], F32, "ksum")
    kmh = sbt([128, 8], BF16, "kmh")
    kml = sbt([128, 8], BF16, "kml")
    kmt = sbt([128, 8], F32, "kmt")
    gm2 = [sbt([128, 8], F32, f"gm{i}") for i in range(2)]
    top82 = [sbt([128, 8], F32, f"top8{i}") for i in range(2)]
    fbt2 = [sbt([128, 8], F32, f"fbt{i}") for i in range(2)]
    cst = sbt([128, c.NOB * 8], F32, "cst")
    mxs2 = [sbt([128, 4], F32, f"mxs{i}") for i in range(2)]
    obf2 = [sbt([128, 128], BF16, f"obf{i}") for i in range(2)]
    SCL = float(128 ** -0.5)
    def do_head(h):
        wq = wload("a", 1, R3, [128, KC, 384], [(0, w_in[:, c.c_aq + 128 * h:c.c_aq + 128 * h + 128]),
                                                 (128, w_in[:, c.c_ak + 128 * h:c.c_ak + 128 * h + 128]),
                                                 (256, w_in[:, c.c_av + 128 * h:c.c_av + 128 * h + 128])])
        for (a, n) in ch_own:
            p = nps()
            mm_feat(p, 128, n, wq, 0, KC, xnT, a, [xnT.r])
            P.op("act", lambda e, p=p, a=a, n=n: e.activation(out=qTa[:, a - TOK_O0:a - TOK_O0 + n], in_=p[:, 0:n], func=AF.Identity, scale=SCL), reads=[p.r], writes=[qTa.r])
        P.op("dve", lambda e, h=h: e.tensor_copy(out=qTs[:, h, :], in_=qTa[:, 128 * NO:128 * NO + 8]), reads=[qTa.r], writes=[qTs.r])
        chk("b1a")
        P.op("pool", lambda e: e.memset(ksum.ap, 0.0), writes=[ksum.r])
        for (a, n) in ch_all:
            p = nps()
            mm_feat(p, 128, n, wq, 128, KC, xnT, a, [xnT.r])
            if n == 8:
                evac_copy("act", kTa[:, a:a + n], p[:, 0:n], [p.r], [kTa.r])
            else:
                for b0 in range(0, n, 256):
                    blk = (a + b0) // 256
                    P.op("act", lambda e, p=p, a=a, b0=b0, blk=blk: e.activation(out=kTa[:, a + b0:a + b0 + 256], in_=p[:, b0:b0 + 256], func=AF.Identity,
                                                                              accum_out=ksum[:, blk:blk + 1]), reads=[p.r], writes=[kTa.r, ksum.r])
        P.op("dve", lambda e, h=h: e.tensor_copy(out=kTs[:, h, :], in_=kTa[:, TOK_S0:TOK_S0 + 8]), reads=[kTa.r], writes=[kTs.r])
        chk("b1b")
        def f(e):
            yield e.tensor_scalar(out=kmt.ap, in0=ksum.ap, scalar1=1.0 / 256.0, scalar2=None, op0=ALU.mult)
            yield e.tensor_copy(out=kmh.ap, in_=kmt.ap)
            yield e.tensor_tensor(out=kmt.ap, in0=kmt.ap, in1=kmh.ap, op=ALU.subtract)
            yield e.tensor_copy(out=kml.ap, in_=kmt.ap)
        P.op("dve", f, reads=[ksum.r], writes=[kmt.r, kmh.r, kml.r])
        chk("b1c")
        for t, (tok0, r) in enumerate(tiles):
            p = nps()
            mm_tok(p, r, 128, xnT, tok0, KC, wq, 256, [xnT_k[t]])
            evac_copy("act", Vt[0:r, t, :], p[0:r, 0:128], [p.r], [Vt.r])
            if t == NP - 1:
                chk("b1d")
            if t >= NP:
                oo = own_off(t)
                vs_ = vst[t % 2]
                evac_copy("act", vs_[0:r, :], p[0:r, 0:128], [p.r], [vs_.r])
                if t == NP:
                    chk("b1e")
                P.dma("sp", f"vst{t % 2}", lambda e, vs_=vs_, oo=oo, r=r, h=h: e.dma_start(out=v_o[oo:oo + r, 128 * h:128 * h + 128], in_=vs_[0:r, :]), reads=[vs_.r])
                if t == NP:
                    chk("b1f")
                if r == 8:
                    P.op("dve", lambda e, p=p, h=h: e.tensor_copy(out=vsf[:, h, :], in_=p[0:8, 0:128]), reads=[p.r], writes=[vsf.r])
                p2 = nps()
                mm_tok(p2, r, 128, xnT, tok0, KC, wq, 128, [xnT_k[t]])
                ks_ = kst[t % 2]
                evac_copy("act", ks_[0:r, :], p2[0:r, 0:128], [p2.r], [ks_.r])
                P.dma("sp", f"kst{t % 2}", lambda e, ks_=ks_, oo=oo, r=r, h=h: e.dma_start(out=k_o[oo:oo + r, 128 * h:128 * h + 128], in_=ks_[0:r, :]), reads=[ks_.r])
                if t == NP:
                    chk("b1g")
                if t == NP + NO - 1:
                    chk("b1h")
        chk("b1")
        P.op("dve", lambda e, h=h: e.tensor_scalar(out=cst.ap, in0=bbv.ap, scalar1=t31bc[:, h:h + 1], scalar2=None, op0=ALU.add),
             reads=[bbv.r, t31bc.r], writes=[cst.r])
        def do_tileA(i, Lg, Pb, PT, gm, top8, fbt, mxs, obf):
            v = i // 2
            ob = NPB + v
            oo = 128 * i
            nk = (ob + 1) * 256
            pg = nps()

            def f(e, pg=pg, oo=oo):
                e.matmul(pg[:, 0:8], lhsT=qTa[:, oo:oo + 128], rhs=kmh.ap, start=True, stop=False)
                return e.matmul(pg[:, 0:8], lhsT=qTa[:, oo:oo + 128], rhs=kml.ap, start=False, stop=True)
            P.op("pe", f, reads=[qTa.r, kmh.r, kml.r], writes=[pg.r])

            def f(e, pg=pg, v=v):
                yield e.tensor_tensor(out=gm.ap, in0=pg[:, 0:8], in1=bbv[:, v * 8:v * 8 + 8], op=ALU.add)
                yield e.max(out=top8.ap, in_=gm.ap)
                yield e.tensor_scalar(out=fbt.ap, in0=gm.ap, scalar1=top8[:, 2:3], scalar2=1.0, op0=ALU.is_ge, op1=ALU.subtract)
                yield e.scalar_tensor_tensor(out=fbt.ap, in0=fbt.ap, scalar=BIG, in1=cst[:, v * 8:v * 8 + 8], op0=ALU.mult, op1=ALU.add)
            P.op("dve", f, reads=[pg.r, bbv.r, cst.r], writes=[gm.r, top8.r, fbt.r])
            for c0 in range(0, nk, 512):
                n = min(512, nk - c0)
                pS = nps()
                P.op("pe", lambda e, pS=pS, c0=c0, n=n, oo=oo: e.matmul(pS[:, 0:n], lhsT=qTa[:, oo:oo + 128], rhs=kTa[:, c0:c0 + n], start=True, stop=True),
                     reads=[qTa.r, kTa.r], writes=[pS.r])
                for b0 in range(0, n, 256):
                    blk = (c0 + b0) // 256
                    if blk == ob:
                        dlt = 0 if i % 2 == 0 else 128
                        P.op("dve", lambda e, pS=pS, b0=b0, blk=blk, dlt=dlt, h=h: e.tensor_tensor(out=Lg[:, blk * 256:blk * 256 + 256], in0=pS[:, b0:b0 + 256],
                                                                                              in1=Tb[dlt][:, h, :], op=ALU.add), reads=[pS.r, Tb[dlt].r], writes=[Lg.r.k(blk)])
                    elif blk == ob - 1 and i % 2 == 0:
                        P.op("dve", lambda e, pS=pS, b0=b0, blk=blk, h=h: e.scalar_tensor_tensor(out=Lg[:, blk * 256:blk * 256 + 256], in0=pS[:, b0:b0 + 256], scalar=fbt[:, blk:blk + 1],
                                                                                            in1=Tb[256][:, h, :], op0=ALU.add, op1=ALU.add), reads=[pS.r, fbt.r, Tb[256].r], writes=[Lg.r.k(blk)])
                    else:
                        P.op("act", lambda e, pS=pS, b0=b0, blk=blk: e.activation(out=Lg[:, blk * 256:blk * 256 + 256], in_=pS[:, b0:b0 + 256], func=AF.Identity,
                                                                              bias=fbt[:, blk:blk + 1], scale=1.0), reads=[pS.r, fbt.r], writes=[Lg.r.k(blk)])
            def f(e, nk=nk):
                yield e.reduce_max(out=mxs[:, 0:1], in_=Lg[:, 0:nk], axis=AX.X)
                yield e.tensor_scalar(out=mxs[:, 1:2], in0=mxs[:, 0:1], scalar1=-1.0, scalar2=None, op0=ALU.mult)
            P.op("dve", f, reads=[Lg.r], writes=[mxs.r.k("m")])
            P.op("act", lambda e, nk=nk: e.activation(out=Pb[:, 0:nk], in_=Lg[:, 0:nk], func=AF.Exp, bias=mxs[:, 1:2], scale=1.0, accum_out=mxs[:, 2:3]),
                 reads=[Lg.r, mxs.r.k("m")], writes=[Pb.r, mxs.r.k("s")])
            P.op("dve", lambda e: e.reciprocal(out=mxs[:, 3:4], in_=mxs[:, 2:3]), reads=[mxs.r.k("s")], writes=[mxs.r.k("r")])

        def do_tileB(i, Lg, Pb, PT, gm, top8, fbt, mxs, obf):
            v = i // 2
            ob = NPB + v
            oo = 128 * i
            nk = (ob + 1) * 256
            nch = nk // 128
            j0 = 0
            while j0 < nch:
                n = min(8, nch - j0)
                pT = nps()
                pTb = psbf(pT).rearrange("p (a b) -> p a b", b=128)

                def f(e, pTb=pTb, j0=j0, n=n):
                    ins = None
                    for j in range(n):
                        ins = e.transpose(out=pTb[:, j, :], in_=Pb[:, (j0 + j) * 128:(j0 + j + 1) * 128], identity=identb.ap)
                    return ins
                P.op("pe", f, reads=[Pb.r, identb.r], writes=[pT.r])
                evac_copy(alt_eng(), PT[:, j0:j0 + n, :], pTb[:, 0:n, :], [pT.r], [PT.r])
                j0 += n
            pO = nps()

            def f(e, pO=pO, nch=nch):
                ins = None
                for j in range(nch):
                    ins = e.matmul(pO[:, 0:128], lhsT=PT[:, j, :], rhs=Vt[:, j, :], start=(j == 0), stop=(j == nch - 1))
                return ins
            P.op("pe", f, reads=[PT.r, Vt.r], writes=[pO.r])
            P.op("act", lambda e, pO=pO: e.activation(out=obf.ap, in_=pO[:, 0:128], func=AF.Identity, scale=mxs[:, 3:4]), reads=[pO.r, mxs.r.k("r")], writes=[obf.r])
            pT = nps()
            pTb = psbf(pT).rearrange("p (a b) -> p a b", b=128)
            P.op("pe", lambda e, pTb=pTb: e.transpose(out=pTb[:, 0, :], in_=obf.ap, identity=identb.ap), reads=[obf.r, identb.r], writes=[pT.r])
            evac_copy("dve", mixT[:, MH + h, oo:oo + 128], pTb[:, 0, :], [pT.r], [mixT.r.k(MH + h)])
            chk("b2")
            if i == NO - 1:
                chk("b3")

        def bufs(i):
            return (Lg2[i % 2], Pb2[i % 2], PT2[i % 2], gm2[i % 2], top82[i % 2], fbt2[i % 2], mxs2[i % 2], obf2[i % 2])
        do_tileA(0, *bufs(0))
        for i in range(1, NO):
            do_tileA(i, *bufs(i))
            do_tileB(i - 1, *bufs(i - 1))
        do_tileB(NO - 1, *bufs(NO - 1))

    for h in range(AH):
        do_head(h)

    if STOP == "p1b":
        P.emit()
        st.close()
        return nc, hc
    R3.barrier()
    wslots.clear()
    kvsel = R3.get([128, 24, 2, 256], F32, "kvsel")
    kselT = R3.get([128, 48, 128], BF16, "kselT")
    gts = R3.get([8, AH, NB], F32, "gts")
    tp8 = R3.get([8, AH, 8], F32, "tp8")
    OH = R3.get([8, AH, NB], F32, "OH")
    OHt = R3.get([8, AH, NB], F32, "OHt")
    selp = R3.get([8, AH, 3, 2], F32, "selp")
    is63 = R3.get([8, AH, 3], F32, "is63")
    Xd = R3.get([8, 8, AH * 6], F32, "Xd")
    addc2 = R3.get([128, NSEL // 2], F32, "addc2")
    idxs = R3.get([128, NSEL // 2], I32, "idxs")
    sTs = R3.get([128, 48], F32, "sTs")
    Ls = R3.get([8, 784], F32, "Ls")
    tmpb = R3.get([8, 256], F32, "tmpb")
    pTs = R3.get([128, 6, 8], F32, "pTs")
    pTo = R3.get([8, 8], F32, "pTo")
    sms = R3.get([8, 4], F32, "sms")
    cload(addc2, cin["addc2"])
    eye8x = R3.get([8, 8 * AH * 6], F32, "eye8x")
    ones8 = R3.get([8, 128], F32, "ones8")
    cload(eye8x, cin["eye8x"])
    P.op("pool", lambda e: e.memset(ones8.ap, 1.0), writes=[ones8.r])
    pg = nps()

    def f(e):
        ins = None
        for h in range(AH):
            e.matmul(pg[0:8, h * NB:(h + 1) * NB], lhsT=qTs[:, h, :], rhs=kmsh[:, h * NB:(h + 1) * NB], start=True, stop=False)
            ins = e.matmul(pg[0:8, h * NB:(h + 1) * NB], lhsT=qTs[:, h, :], rhs=kmsl[:, h * NB:(h + 1) * NB], start=False, stop=True)
        return ins
    P.op("pe", f, reads=[qTs.r, kmsh.r, kmsl.r], writes=[pg.r])

    def f(e):
        yield e.tensor_copy(out=gts.ap, in_=pg[0:8, 0:AH * NB].rearrange("p (h n) -> p h n", n=NB))
        for h in range(AH):
            yield e.max(out=tp8[:, h, :], in_=gts[:, h, :])
    P.op("dve", f, reads=[pg.r], writes=[gts.r, tp8.r])
    ptv = ptf[0:8, :].rearrange("p (n u) -> p u n", u=2)
    for s_ in range(3):
        def f(e, s_=s_):
            yield e.tensor_tensor(out=OH.ap, in0=gts.ap, in1=tp8[:, :, s_:s_ + 1].to_broadcast([8, AH, NB]), op=ALU.is_equal)
            yield e.tensor_copy(out=is63[:, :, s_], in_=OH[:, :, NB - 1])
            for u in range(2):
                yield e.tensor_tensor(out=OHt.ap, in0=OH.ap, in1=ptv[:, u:u + 1, :].to_broadcast([8, AH, NB]), op=ALU.mult)
                yield e.tensor_reduce(out=selp[:, :, s_, u], in_=OHt.ap, axis=AX.X, op=ALU.add)
        P.op("dve", f, reads=[gts.r, tp8.r, ptf.r], writes=[OH.r, OHt.r, selp.r, is63.r])
    P.op("dve", lambda e: e.tensor_tensor(out=Xd.ap, in0=eye8x.ap.rearrange("p (q x) -> p q x", q=8),
                                          in1=selp.ap.rearrange("p h s u -> p (h s u)").unsqueeze(1).to_broadcast([8, 8, AH * 6]), op=ALU.mult),
         reads=[eye8x.r, selp.r], writes=[Xd.r])
    pB = nps()
    P.op("pe", lambda e, pB=pB: e.matmul(pB[:, 0:NSEL], lhsT=ones8.ap, rhs=Xd.ap.rearrange("p q x -> p (q x)"), start=True, stop=True),
         reads=[ones8.r, Xd.r], writes=[pB.r])

    pBv2 = pB[:, 0:NSEL].rearrange("p (x u) -> p x u", u=2)

    def f(e):
        yield e.scalar_tensor_tensor(out=addc2[0:64, :], in0=pBv2[0:64, :, 0], scalar=float(64 * AH), in1=addc2[0:64, :], op0=ALU.mult, op1=ALU.add)
        yield e.scalar_tensor_tensor(out=addc2[64:128, :], in0=pBv2[64:128, :, 1], scalar=float(64 * AH), in1=addc2[64:128, :], op0=ALU.mult, op1=ALU.add)
        yield e.tensor_copy(out=idxs.ap, in_=addc2.ap)
    P.op("dve", f, reads=[pB.r, addc2.r], writes=[addc2.r, idxs.r])
    ckv_rows = cache_kv.rearrange("(r two) x -> r (two x)", two=2)
    def do_shead(h):
        for q in range(8):
            for s_ in range(3):
                un = q * 3 + s_
                col = (q * AH + h) * 3 + s_
                P.dma("pool", "kvsel", lambda e, un=un, col=col: e.indirect_dma_start(out=kvsel[:, un, :, :].rearrange("p e x -> p (e x)"), out_offset=None, in_=ckv_rows,
                                                                                    in_offset=bass.IndirectOffsetOnAxis(ap=idxs[:, col:col + 1], axis=0)),
                      reads=[idxs.r], writes=[kvsel.r.k(un)])
        for u0 in range(0, 48, 4):
            pT = nps()
            pTv = pT.ap.rearrange("p (a b) -> p a b", b=128)

            def f(e, pTv=pTv, u0=u0):
                ins = None
                for j in range(4):
                    ins = e.transpose(out=pTv[:, j, :], in_=kvsel[:, (u0 + j) // 2, (u0 + j) % 2, 0:128], identity=ident.ap)
                return ins
            P.op("pe", f, reads=[kvsel.r, ident.r], writes=[pT.r])
            evac_copy(alt_eng(), kselT[:, u0:u0 + 4, :], pTv[:, 0:4, :], [pT.r], [kselT.r])
        pS = nps()

        def f(e, pS=pS, h=h):
            ins = None
            for q in range(8):
                for su in range(6):
                    un = q * 6 + su
                    ins = e.matmul(pS[:, un:un + 1], lhsT=kselT[:, un, :], rhs=qTs[:, h, q:q + 1], start=True, stop=True)
            return ins
        P.op("pe", f, reads=[kselT.r, qTs.r], writes=[pS.r])
        evac_copy("dve", sTs.ap, pS[:, 0:48], [pS.r], [sTs.r])
        sTv = sTs.ap.rearrange("p (q x) -> p x q", x=6)
        pA, pBk = nps(), nps()
        pAv = pA.ap.rearrange("p (a b) -> p a b", b=128)
        pBv = pBk.ap.rearrange("p (a b) -> p a b", b=128)

        def f(e, pAv=pAv, pBv=pBv, pBk=pBk, h=h):
            for su in range(4):
                e.transpose(out=pAv[0:8, su, :], in_=sTv[:, su, :], identity=ident.ap)
            for su in range(4, 6):
                e.transpose(out=pBv[0:8, su - 4, :], in_=sTv[:, su, :], identity=ident.ap)
            return e.matmul(pBk[0:8, 256:264], lhsT=qTs[:, h, :], rhs=kTs[:, h, :], start=True, stop=True)
        P.op("pe", f, reads=[sTs.r, ident.r, qTs.r, kTs.r], writes=[pA.r, pBk.r])

        def f(e, pAv=pAv, pBv=pBv, pBk=pBk, h=h):
            for s_ in range(3):
                yield e.memset(tmpb[:, 0:128], 0.0)
                yield e.tensor_scalar(out=tmpb[:, 128:256], in0=T256s[:, h, :], scalar1=is63[:, h, s_:s_ + 1], scalar2=None, op0=ALU.mult)
                yield e.tensor_scalar(out=tmpb.ap, in0=tmpb.ap, scalar1=t31bc[0:8, h:h + 1], scalar2=None, op0=ALU.add)
                src = pAv[0:8, 2 * s_:2 * s_ + 2, :] if s_ < 2 else pBv[0:8, 0:2, :]
                yield e.tensor_tensor(out=Ls[:, s_ * 256:(s_ + 1) * 256].rearrange("p (a u b) -> p a u b", u=2, b=64), in0=src.rearrange("p a (u b) -> p a u b", u=2),
                                in1=tmpb.ap.rearrange("q (u p e) -> q e u p", u=2, e=2), op=ALU.add)
            yield e.tensor_tensor(out=Ls[:, 768:776], in0=pBk[0:8, 256:264], in1=T0s[:, h, :], op=ALU.add)
            yield e.reduce_max(out=sms[:, 0:1], in_=Ls[:, 0:776], axis=AX.X)
            yield e.tensor_scalar(out=sms[:, 1:2], in0=sms[:, 0:1], scalar1=-1.0, scalar2=None, op0=ALU.mult)
        P.op("dve", f, reads=[pA.r, pBk.r, T256s.r, is63.r, t31bc.r, T0s.r], writes=[tmpb.r, Ls.r, sms.r.k("m")])
        P.op("act", lambda e: e.activation(out=Ls[:, 0:776], in_=Ls[:, 0:776], func=AF.Exp, bias=sms[:, 1:2], scale=1.0, accum_out=sms[:, 2:3]),
             reads=[Ls.r, sms.r.k("m")], writes=[Ls.r, sms.r.k("s")])

        def f(e):
            yield e.reciprocal(out=sms[:, 3:4], in_=sms[:, 2:3])
            yield e.tensor_scalar(out=Ls[:, 0:776], in0=Ls[:, 0:776], scalar1=sms[:, 3:4], scalar2=None, op0=ALU.mult)
        P.op("dve", f, reads=[Ls.r, sms.r.k("s")], writes=[Ls.r, sms.r.k("r")])
        pP = nps()
        pPv = pP[:, 0:48].rearrange("p (a b) -> p a b", b=8)

        def f(e, pP=pP, pPv=pPv):
            for su in range(6):
                e.transpose(out=pPv[:, su, :], in_=Ls[:, su * 128:(su + 1) * 128], identity=ident[0:8, 0:8])
            return e.transpose(out=pP[0:8, 64:72], in_=Ls[:, 768:776], identity=ident[0:8, 0:8])
        P.op("pe", f, reads=[Ls.r, ident.r], writes=[pP.r])

        def f(e, pP=pP, pPv=pPv):
            yield e.tensor_copy(out=pTs.ap, in_=pPv)
            yield e.tensor_copy(out=pTo.ap, in_=pP[0:8, 64:72])
        P.op("dve", f, reads=[pP.r], writes=[pTs.r, pTo.r])
        pO = nps()

        def f(e, pO=pO, h=h):
            ins = None
            for q in range(8):
                e.matmul(pO[:, q:q + 1], lhsT=vsf[:, h, :], rhs=pTo[:, q:q + 1], start=True, stop=False)
                for su in range(6):
                    ins = e.matmul(pO[:, q:q + 1], lhsT=kvsel[:, q * 3 + su // 2, su % 2, 128:256], rhs=pTs[:, su, q:q + 1], start=False, stop=(su == 5))
            return ins
        P.op("pe", f, reads=[vsf.r, pTo.r, kvsel.r, pTs.r], writes=[pO.r])
        evac_copy("act", mixT[:, MH + h, 128 * NO:128 * NO + 8], pO[:, 0:8], [pO.r], [mixT.r.k(MH + h)])

    for h in range(AH):
        do_shead(h)

    if STOP == "p1c":
        P.emit()
        st.close()
        return nc, hc
    R1.barrier()
    R3.barrier()
    wslots.clear()
    h1 = R1.get([128, NOT, D], F32, "h1")
    otl = [(128 * i, 128) for i in range(NO)] + [(128 * NO, 8)]
    for i, (o0, r) in enumerate(otl):
        P.dma("sp", "h1l", lambda e, i=i, o0=o0, r=r: e.dma_start(out=h1[0:r, i, :], in_=x_all[TOK_O0 + o0:TOK_O0 + o0 + r, :]), writes=[h1.r.k(i)])
    for cg in range(D // 512):
        w = wload("o", 2, R3, [128, max(KM, KC), 512], [(0, w_out[:, cg * 512:(cg + 1) * 512])])
        for i, (o0, r) in enumerate(otl):
            p = nps()
            mm_tok(p, r, 512, mixT, o0, KM, w, 0, [mixT.r])
            P.op("dve", lambda e, p=p, i=i, r=r, cg=cg: e.tensor_tensor(out=h1[0:r, i, cg * 512:(cg + 1) * 512], in0=p[0:r, :], in1=h1[0:r, i, cg * 512:(cg + 1) * 512], op=ALU.add),
                 reads=[p.r, h1.r.k(i)], writes=[h1.r.k(i)])
    def norm_stage(grow):
        R3.barrier()
        wslots.clear()
        g = load_g(0, grow)
        xnb = R3.get([128, D], BF16, "xnb")
        junk3 = R3.get([128, D], BF16, "junk3")
        R2.barrier()
        xT = R2.get([128, KC, NOW], BF16, "xT")
        for i, (o0, r) in enumerate(otl):
            norm_tile(h1[0:r, i, :], h1.r.k(i), r, g, xnb[0:r, :], xnb.r, ssq, rstd, junk3)
            transpose_into(xnb, r, KC, xT, o0, xT.r)
        R3.barrier()
        wslots.clear()
        return xT
    xT2 = norm_stage(1)
    aT = R3.get([128, 4, NOW], BF16, "aT")
    rl = [R3.get([128, 512], F32, "rl") for _ in range(2)]
    ch_o = chunks(0, 128 * NO) + [(128 * NO, 8)]
    nrl = 0
    for fg in range(DFF // 512):
        wu = wload("u", 2, R3, [128, KC, 512], [(0, w_up[:, fg * 512:(fg + 1) * 512])])
        wd = wload("d", 2, R3, [128, 4, D], [(0, w_down[fg * 512:(fg + 1) * 512, :])])
        for f_ in range(4):
            for (a, n) in ch_o:
                p = nps()
                mm_feat(p, 128, n, wu, f_ * 128, KC, xT2, a, [xT2.r])
                rt = rl[nrl % 2]
                nrl += 1
                P.op("act", lambda e, p=p, n=n, rt=rt: e.activation(out=rt[:, 0:n], in_=p[:, 0:n], func=AF.Relu), reads=[p.r], writes=[rt.r])
                P.op("pool", lambda e, rt=rt, f_=f_, a=a, n=n: e.tensor_tensor(out=aT[:, f_, a:a + n], in0=rt[:, 0:n], in1=rt[:, 0:n], op=ALU.mult), reads=[rt.r], writes=[aT.r])
        for i, (o0, r) in enumerate(otl):
            for cg in range(D // 512):
                p = nps()

                def f(e, p=p, o0=o0, r=r, cg=cg, wd=wd):
                    ins = None
                    for f_ in range(4):
                        ins = e.matmul(p[0:r, :], lhsT=aT[:, f_, o0:o0 + r], rhs=wd[:, f_, cg * 512:(cg + 1) * 512], start=(f_ == 0), stop=(f_ == 3))
                    return ins
                P.op("pe", f, reads=[aT.r, wd.r], writes=[p.r])
                P.op("dve", lambda e, p=p, i=i, r=r, cg=cg: e.tensor_tensor(out=h1[0:r, i, cg * 512:(cg + 1) * 512], in0=p[0:r, :], in1=h1[0:r, i, cg * 512:(cg + 1) * 512], op=ALU.add),
                     reads=[p.r, h1.r.k(i)], writes=[h1.r.k(i)])
    xT3 = norm_stage(2)
    KP = PLE // 128
    pT_ = R3.get([128, KP, NOW], BF16, "pT_")
    pst = R3.get([128, PLE], F32, "pst")
    pbf = R3.get([128, PLE], BF16, "pbf")
    for i, (o0, r) in enumerate(otl):
        P.dma("sp", "pst", lambda e, o0=o0, r=r: e.dma_start(out=pst[0:r, :], in_=p_all[o0:o0 + r, :]), writes=[pst.r])
        P.op("dve", lambda e, r=r: e.tensor_copy(out=pbf[0:r, :], in_=pst[0:r, :]), reads=[pst.r], writes=[pbf.r])
        transpose_into(pbf, r, KP, pT_, o0, pT_.r)
    sg = [R3.get([128, 512], F32, "sg") for _ in range(2)]
    for cg in range(D // 512):
        wgt = wload("u", 2, R3, [128, KC, 512], [(0, w_pg[:, cg * 512:(cg + 1) * 512])])
        wpp = wload("p", 2, R3, [128, KP, 512], [(0, w_pp[:, cg * 512:(cg + 1) * 512])])
        for i, (o0, r) in enumerate(otl):
            p1, p2 = nps(), nps()
            mm_tok(p1, r, 512, xT3, o0, KC, wgt, 0, [xT3.r])
            mm_tok(p2, r, 512, pT_, o0, KP, wpp, 0, [pT_.r])
            s_ = sg[i % 2]
            P.op("act", lambda e, p1=p1, r=r, s_=s_: e.activation(out=s_[0:r, :], in_=p1[0:r, :], func=AF.Sigmoid), reads=[p1.r], writes=[s_.r])
            P.op("dve", lambda e, p2=p2, r=r, s_=s_: e.tensor_tensor(out=s_[0:r, :], in0=s_[0:r, :], in1=p2[0:r, :], op=ALU.mult), reads=[p2.r, s_.r], writes=[s_.r])
            P.op("pool", lambda e, i=i, r=r, cg=cg, s_=s_: e.tensor_tensor(out=h1[0:r, i, cg * 512:(cg + 1) * 512], in0=h1[0:r, i, cg * 512:(cg + 1) * 512], in1=s_[0:r, :], op=ALU.add),
                 reads=[s_.r, h1.r.k(i)], writes=[h1.r.k(i)])
    R3.barrier()
    wslots.clear()
    g = load_g(0, 3)
    junk3 = R3.get([128, D], BF16, "junk3")
    yst = [R3.get([128, D], F32, "yst") for _ in range(2)]
    for i, (o0, r) in enumerate(otl):
        y_ = yst[i % 2]
        norm_tile(h1[0:r, i, :], h1.r.k(i), r, g, y_[0:r, :], y_.r, ssq, rstd, junk3)
        P.dma("sp", f"yst{i % 2}", lambda e, y_=y_, o0=o0, r=r: e.dma_start(out=y_o[o0:o0 + r, :], in_=y_[0:r, :]), reads=[y_.r])
    P.emit()
    st.close()
    return nc, hc


def make_in_maps(c, inp, ncores):
    MH, AH = c.MH, c.AH
    half_len = 128 * c.NO
    hc = host_consts(c)
    f32 = np.float32
    shared = {
        "w_in": np.ascontiguousarray(inp["w_in"][0]), "w_out": np.ascontiguousarray(inp["w_out"][0]),
        "w_up": np.ascontiguousarray(inp["w_up"][0]), "w_down": np.ascontiguousarray(inp["w_down"][0]),
        "w_pg": np.ascontiguousarray(inp["w_ple_gate"][0]), "w_pp": np.ascontiguousarray(inp["w_ple_proj"][0]),
        "gvec": np.ascontiguousarray(np.stack([inp["g_mix"][0], inp["g_ffn"][0], inp["g_ple"][0], inp["g_final"]]).astype(f32)),
        "g_mh": np.ascontiguousarray(inp["g_mhead"][0].reshape(1, MH * 128)),
        "b_i": np.ascontiguousarray(inp["b_igate"][0].reshape(MH, 1)), "b_f": np.ascontiguousarray(inp["b_fgate"][0].reshape(MH, 1)),
        "relt": np.ascontiguousarray(np.concatenate([inp["rel_bias_table"], np.full((1, AH), -BIG, f32)], 0).astype(f32)),
        "cache_k": np.ascontiguousarray(inp["cache_k"][0]).reshape(c.NPHYS * 128, AH * 128),
        "cache_kv": np.ascontiguousarray(np.stack([inp["cache_k"][0], inp["cache_v"][0]], axis=3).transpose(0, 2, 1, 3, 4)).reshape(c.NPHYS * AH * 128, 256),
    }
    for k, v in hc.items():
        shared["c_" + k] = v
    maps = []
    for cid in range(ncores):
        b, half = cid // 2, cid % 2
        xp = inp["x_prompt"][b]
        own = xp[half * half_len:(half + 1) * half_len]
        pre = xp[0:half_len] if half == 1 else np.zeros_like(own)
        m = dict(shared)
        m["x_all"] = np.ascontiguousarray(np.concatenate([pre, own, inp["x_sample"][cid]], 0))
        m["p_all"] = np.ascontiguousarray(np.concatenate([inp["p_prompt"][0, b, half * half_len:(half + 1) * half_len], inp["p_sample"][0, cid]], 0))
        m["pt"] = np.ascontiguousarray(inp["page_table"][cid:cid + 1]).astype(np.int32)
        m["sC"] = np.ascontiguousarray(inp["state_C"][0, cid]).reshape(MH * 64, 128)
        m["sn"] = np.ascontiguousarray(inp["state_n"][0, cid]).reshape(MH * 64, 1)
        m["sm"] = np.ascontiguousarray(inp["state_m"][0, cid]).reshape(MH, 1)
        m["flag"] = np.full((1, 1), float(half), f32)
        maps.append(m)
    return maps


def assemble(c, res, ncores):
    MH, AH = c.MH, c.AH
    B = ncores // 2
    hl = 128 * c.NO
    S = 2 * hl
    D = c.D
    f32 = np.float32
    y_p = np.zeros((B, S, D), f32)
    y_s = np.zeros((ncores, 8, D), f32)
    k_p = np.zeros((1, B, S, AH, 128), f32)
    v_p = np.zeros((1, B, S, AH, 128), f32)
    C_p = np.zeros((1, B, MH, 64, 128), f32)
    n_p = np.zeros((1, B, MH, 64), f32)
    m_p = np.zeros((1, B, MH), f32)
    k_s = np.zeros((1, ncores, 8, AH, 128), f32)
    v_s = np.zeros((1, ncores, 8, AH, 128), f32)
    C_s = np.zeros((1, ncores, MH, 64, 128), f32)
    n_s = np.zeros((1, ncores, MH, 64), f32)
    m_s = np.zeros((1, ncores, MH), f32)
    for cid in range(ncores):
        r = res[cid]
        b, half = cid // 2, cid % 2
        y_p[b, half * hl:(half + 1) * hl] = r["y_o"][0:hl]
        y_s[cid] = r["y_o"][hl:hl + 8]
        k_p[0, b, half * hl:(half + 1) * hl] = r["k_o"][0:hl].reshape(hl, AH, 128)
        v_p[0, b, half * hl:(half + 1) * hl] = r["v_o"][0:hl].reshape(hl, AH, 128)
        k_s[0, cid] = r["k_o"][hl:hl + 8].reshape(8, AH, 128)
        v_s[0, cid] = r["v_o"][hl:hl + 8].reshape(8, AH, 128)
        if half == 1:
            C_p[0, b] = r["Cp_o"].reshape(MH, 64, 128)
            n_p[0, b] = r["np_o"].reshape(MH, 64)
            m_p[0, b] = r["mp_o"].reshape(MH)
        C_s[0, cid] = r["Cs_o"].reshape(MH, 64, 128)
        n_s[0, cid] = r["ns_o"].reshape(MH, 64)
        m_s[0, cid] = r["ms_o"].reshape(MH)
    return (y_p, y_s, k_p, v_p, C_p, n_p, m_p, k_s, v_s, C_s, n_s, m_s)


def kernel(**inputs):
    c = Cfg()
    ncores = 8
    inp = {k: np.asarray(v) for k, v in inputs.items()}
    nc, _ = build(c)
    maps = make_in_maps(c, inp, ncores)
    res = run_bass_kernel_spmd(nc, maps, core_ids=list(range(ncores)))
    return assemble(c, res.results, ncores)
```

```python
import numpy as np
import concourse.bass as bass
import concourse.mybir as mybir

F32 = mybir.dt.float32
BF16 = mybir.dt.bfloat16
I32 = mybir.dt.int32
AF = mybir.ActivationFunctionType
ALU = mybir.AluOpType
AX = mybir.AxisListType

ENGS = ("pe", "act", "dve", "pool", "sp")


class Res:
    __slots__ = ("name", "parent", "kids", "lw", "rd", "excl")

    def __init__(self, name, parent=None):
        self.excl = False
        self.name = name
        self.parent = parent
        self.kids = {}
        self.lw = None
        self.rd = []

    def k(self, key):
        r = self.kids.get(key)
        if r is None:
            r = Res(f"{self.name}.{key}", self)
            self.kids[key] = r
        return r


class Op:
    __slots__ = ("eng", "fn", "deps", "idx", "sig", "cnt", "chan", "isdma", "dw", "ny")

    def __init__(self, eng, fn):
        self.eng = eng
        self.fn = fn
        self.deps = set()
        self.dw = {}
        self.sig = False
        self.cnt = 0
        self.chan = None
        self.isdma = False


class Chan:
    def __init__(self, name):
        self.name = name
        self.res = Res("chan_" + name)
        self.n = 0
        self.sem = None


class Prog:
    def __init__(self, nc, same_engine_sync=True):
        self.nc = nc
        self.ops = []
        self.chans = {}
        self.same_engine_sync = same_engine_sync

    def chan(self, name):
        c = self.chans.get(name)
        if c is None:
            c = Chan(name)
            self.chans[name] = c
        return c

    def _desc(self, r, out):
        for kk in r.kids.values():
            out.append(kk)
            if kk.kids:
                self._desc(kk, out)

    def _related(self, r):
        out = [r]
        p = r.parent
        while p is not None:
            out.append(p)
            p = p.parent
        if r.kids:
            self._desc(r, out)
        return out

    def _add(self, op, reads, writes):
        for r in reads:
            if r.excl and r not in writes:
                writes = writes + [r]
        deps = op.deps
        for r in reads:
            for x in self._related(r):
                if x.lw is not None:
                    deps.add(x.lw)
        for w in writes:
            for x in self._related(w):
                if x.lw is not None:
                    deps.add(x.lw)
                for o in x.rd:
                    deps.add(o)
        for d in list(deps):
            if d.isdma:
                c = d.chan
                if op.dw.get(c, 0) < 16 * c.n:
                    op.dw[c] = 16 * c.n
                if not (op.isdma and op.chan is c):
                    c.res.rd.append(op)
                deps.discard(d)
        for r in reads:
            r.rd.append(op)
        for w in writes:
            w.lw = op
            w.rd = []
            if w.kids:
                dd = []
                self._desc(w, dd)
                for kk in dd:
                    kk.lw = op
                    kk.rd = []
        deps.discard(op)
        op.idx = len(self.ops)
        self.ops.append(op)

    frozen = False

    def op(self, eng, fn, reads=(), writes=()):
        if self.frozen:
            return None
        o = Op(eng, fn)
        self._add(o, list(reads), list(writes))
        return o

    def dma(self, eng, chan, fn, reads=(), writes=()):
        if self.frozen:
            return None
        c = self.chan(chan) if isinstance(chan, str) else chan
        o = Op(eng, fn)
        o.isdma = True
        o.chan = c
        assert getattr(c, "eng", eng) == eng
        c.eng = eng
        for x in c.res.rd:
            o.deps.add(x)
        c.res.rd = []
        self._add(o, list(reads), list(writes))
        c.n += 1
        o.cnt = 16 * c.n
        return o

    def emit(self, final_wait_eng="sp"):
        nc = self.nc
        ops = self.ops
        for o in ops:
            for d in o.deps:
                if d.isdma:
                    continue
                if d.eng == o.eng:
                    if o.eng == "pe" or not self.same_engine_sync:
                        continue
                d.sig = True
        import inspect

        class _FI:
            def then_inc(self, *a, **k):
                return self

        class _FE:
            def __getattr__(self, n):
                return lambda *a, **k: _FI()
        counts = {e: 0 for e in ENGS}
        for o in ops:
            o.ny = 0
            if o.isdma:
                continue
            if inspect.isgeneratorfunction(o.fn):
                o.ny = sum(1 for _ in o.fn(_FE()))
                counts[o.eng] += o.ny
                o.cnt = counts[o.eng]
            elif o.sig:
                counts[o.eng] += 1
                o.cnt = counts[o.eng]
        import contextlib
        with contextlib.ExitStack() as st:
            esem = {e: st.enter_context(nc.semaphore("s_" + e)) for e in ENGS if e != "sp"}
            for c in self.chans.values():
                c.sem = st.enter_context(nc.semaphore("c_" + c.name))
            per = {e: [o for o in ops if o.eng == e] for e in ENGS}

            block = st.enter_context(nc.Block())

            def run(eng_name, eng):
                waited = {}
                for o in per[eng_name]:
                    need = {}
                    for c, v in o.dw.items():
                        need[id(c.sem)] = (c.sem, v)
                    for d in o.deps:
                        if d.eng == o.eng and (o.eng == "pe" or not self.same_engine_sync):
                            continue
                        s, v = esem[d.eng], d.cnt
                        key = id(s)
                        if need.get(key, (None, 0))[1] < v:
                            need[key] = (s, v)
                    for key, (s, v) in need.items():
                        if waited.get(key, 0) < v:
                            eng.wait_ge(s, v)
                            waited[key] = v
                    if o.ny:
                        gen = o.fn(eng)
                        base = o.cnt - o.ny
                        for gi, cur in enumerate(gen):
                            cur.then_inc(esem[eng_name], 1)
                            if gi < o.ny - 1:
                                eng.wait_ge(esem[eng_name], base + gi + 1)
                                waited[id(esem[eng_name])] = base + gi + 1
                        continue
                    ins = o.fn(eng)
                    if o.isdma:
                        ins.then_inc(o.chan.sem, 16)
                    elif o.sig:
                        ins.then_inc(esem[o.eng], 1)
                if eng_name == final_wait_eng:
                    for c in self.chans.values():
                        if c.n:
                            eng.wait_ge(c.sem, 16 * c.n)

            @block.tensor
            def _(e):
                run("pe", e)

            @block.scalar
            def _(e):
                run("act", e)

            @block.vector
            def _(e):
                run("dve", e)

            @block.gpsimd
            def _(e):
                run("pool", e)

            @block.sync
            def _(e):
                run("sp", e)

import math
import contextlib
from concourse.bass_utils import run_bass_kernel_spmd

BIG = 30000.0
EPS = 1e-6


class Cfg:
    def __init__(s, D=2048, NP=8, NO=8, MH=8, AH=8, DFF=8192, PLE=256, NPAGES=128, NPHYS=1280):
        s.D, s.NP, s.NO, s.MH, s.AH, s.DFF, s.PLE, s.NPAGES, s.NPHYS = D, NP, NO, MH, AH, DFF, PLE, NPAGES, NPHYS
        s.KC = D // 128
        s.DK, s.DV, s.DH, s.TS = 64, 128, 128, 8
        s.c_mq = 0
        s.c_mk = MH * 64
        s.c_mv = 2 * MH * 64
        s.c_mo = s.c_mv + MH * 128
        s.c_mi = s.c_mo + MH * 128
        s.c_mf = s.c_mi + MH
        s.c_aq = s.c_mf + MH
        s.c_ak = s.c_aq + AH * 128
        s.c_av = s.c_ak + AH * 128
        s.DIN = s.c_av + AH * 128
        s.MIXW = MH * 128 + AH * 128
        s.KM = s.MIXW // 128
        s.NTOK = 128 * (NP + NO) + 8
        s.NOW = 128 * NO + 8
        s.NPB, s.NOB = NP // 2, NO // 2
        s.NBLK = s.NPB + s.NOB
        s.NB = NPAGES // 2
        s.OHW = 768
        s.NSEL = 8 * AH * 6


class V:
    def __init__(s, ap, r):
        s.ap, s.r = ap, r

    def __getitem__(s, k):
        return s.ap[k]


def t5_bucket_np(rel):
    n = np.maximum(rel, 0)
    nf = np.maximum(n, 1).astype(np.float32)
    large = 16 + (np.log(nf / np.float32(16)) / np.float32(math.log(128 / 16)) * np.float32(16)).astype(np.int32)
    large = np.minimum(large, 31)
    return np.where(n < 16, n, large)


def host_consts(c):
    k = {}
    k["ident"] = np.eye(128, dtype=np.float32)
    s_ = np.arange(128)
    k["maskle"] = (s_[:, None] <= s_[None, :]).astype(np.float32)
    rel = np.arange(c.OHW) - 255
    b = t5_bucket_np(rel)
    oh = np.zeros((33, c.OHW), np.float32)
    for i in range(c.OHW):
        if rel[i] < 0:
            oh[32, i] = 1.0
        else:
            oh[b[i], i] = 1.0
    k["ohlong"] = oh
    bs = np.full((c.NOB, 8), -BIG, np.float32)
    for v in range(c.NOB):
        bs[v, : c.NPB + v] = 0.0
    k["bstruct"] = bs.reshape(1, c.NOB * 8)
    pm = np.zeros((1, 8), np.float32)
    pm[0, : c.NPB] = 1.0
    k["prefmask"] = pm
    nhp = c.MH // 2
    sel = np.zeros((c.MH, nhp * 128), np.float32)
    for hp in range(nhp):
        for p in range(128):
            sel[2 * hp + p // 64, hp * 128 + p] = 1.0
    k["sel"] = sel
    addc = np.zeros((128, c.NSEL), np.float32)
    col = 0
    for q in range(8):
        for h in range(c.AH):
            for s in range(3):
                for u in range(2):
                    addc[:, col] = np.arange(128) * c.AH + h
                    col += 1
    k["addc"] = addc
    ew = np.zeros((128, 2 * c.NB - 1), np.float32)
    ew[:, c.NB - 1] = 1.0
    k["ewin"] = ew
    addc2 = np.zeros((128, c.NSEL // 2), np.float32)
    col = 0
    for q in range(8):
        for h in range(c.AH):
            for s in range(3):
                addc2[:, col] = h * 64 + (np.arange(128) % 64)
                col += 1
    k["addc2"] = addc2
    k["iotap"] = np.arange(128, dtype=np.float32).reshape(128, 1)
    k["iotab"] = np.broadcast_to(np.arange(c.NB, dtype=np.float32), (8, c.NB)).copy()
    e8 = np.zeros((8, 8, c.AH * 6), np.float32)
    for q in range(8):
        e8[q, q, :] = 1.0
    k["eye8x"] = e8.reshape(8, 8 * c.AH * 6)
    return k


def build(c):
    import os as _os
    STOP = _os.environ.get("MK_STOP", "")

    def chk(name):
        if STOP == name:
            P.frozen = True
    nc = bass.Bass("TRN2", target_bir_lowering=False)
    P = Prog(nc)
    st = contextlib.ExitStack()
    D, KC, NP, NO, MH, AH, DFF, PLE = c.D, c.KC, c.NP, c.NO, c.MH, c.AH, c.DFF, c.PLE
    NTOK, NOW, KM = c.NTOK, c.NOW, c.KM
    NT = NP + NO + 1
    NB, NPG, NSEL = c.NB, c.NPAGES, c.NSEL
    NOT = NO + 1

    def din(name, shape, dt=F32):
        return nc.dram_tensor(name, list(shape), dt, kind="ExternalInput").ap()

    def dout(name, shape, dt=F32):
        return nc.dram_tensor(name, list(shape), dt, kind="ExternalOutput").ap()

    hc = host_consts(c)
    x_all = din("x_all", [NTOK, D])
    p_all = din("p_all", [NOW, PLE])
    w_in = din("w_in", [D, c.DIN])
    w_out = din("w_out", [c.MIXW, D])
    w_up = din("w_up", [D, DFF])
    w_down = din("w_down", [DFF, D])
    w_pg = din("w_pg", [D, D])
    w_pp = din("w_pp", [PLE, D])
    gvec = din("gvec", [4, D])
    g_mh = din("g_mh", [1, MH * 128])
    b_i = din("b_i", [MH, 1])
    b_f = din("b_f", [MH, 1])
    relt = din("relt", [33, AH])
    cache_k = din("cache_k", [c.NPHYS * 128, AH * 128])
    cache_kv = din("cache_kv", [c.NPHYS * AH * 128, 256])
    pt = din("pt", [1, c.NPAGES], I32)
    sC = din("sC", [MH * 64, 128])
    sn = din("sn", [MH * 64, 1])
    sm = din("sm", [MH, 1])
    flag = din("flag", [1, 1])
    cin = {k: din("c_" + k, v.shape) for k, v in hc.items()}

    y_o = dout("y_o", [NOW, D])
    k_o = dout("k_o", [NOW, AH * 128])
    v_o = dout("v_o", [NOW, AH * 128])
    Cp_o = dout("Cp_o", [MH * 64, 128])
    np_o = dout("np_o", [MH * 64, 1])
    mp_o = dout("mp_o", [MH, 1])
    Cs_o = dout("Cs_o", [MH * 64, 128])
    ns_o = dout("ns_o", [MH * 64, 1])
    ms_o = dout("ms_o", [MH, 1])

    _cnt = [0]

    def sbt(shape, dt=F32, name=None):
        _cnt[0] += 1
        nm = name or f"t{_cnt[0]}"
        t = st.enter_context(nc.sbuf_tensor(nm, list(shape), dt))
        return V(t[:], Res(nm))

    class Region:
        def __init__(s, nbytes, name):
            s.t = st.enter_context(nc.sbuf_tensor(name, [128, nbytes // 4], F32))
            s.r = Res(name)
            s.off = 0
            s.n = nbytes
            s.name = name

        def reset(s):
            s.off = 0

        def barrier(s):
            P.op("pool", lambda e: e.memset(s.t[0:1, 0:1], 0.0), writes=[s.r])
            s.off = 0

        def get(s, shape, dt=F32, name="v"):
            esz = 4 if dt in (F32, I32) else 2
            per = int(np.prod(shape[1:])) * esz
            per4 = (per + 3) // 4
            assert s.off + per4 * 4 <= s.n, (s.name, name, s.off, per4 * 4, s.n)
            ap = s.t[:, s.off // 4: s.off // 4 + per4]
            s.off += per4 * 4
            if dt != F32:
                ap = ap.bitcast(dt)
            n = int(np.prod(shape[1:]))
            ap = ap[:, 0:n]
            if len(shape) == 3:
                ap = ap.rearrange("p (a b) -> p a b", b=shape[2])
            elif len(shape) == 4:
                ap = ap.rearrange("p (a b c) -> p a b c", b=shape[2], c=shape[3])
            ap = ap[0:shape[0]]
            _cnt[0] += 1
            return V(ap, s.r.k(f"{name}{_cnt[0]}"))

    R1B = max(KC * NTOK * 2, NOT * D * 4)
    R2B = max(KM, KC) * NOW * 2
    R1 = Region(R1B, "R1")
    R2 = Region(R2B, "R2")
    R3B = 88 * 1024
    R3 = Region(R3B, "R3")

    psb = []
    for i in range(8):
        t = st.enter_context(nc.psum_tensor(f"ps{i}", [128, 512], F32))
        psb.append(V(t[:], Res(f"ps{i}")))
        psb[-1].r.excl = True
    _psi = [0]
    reserved = set()

    def nps():
        while True:
            i = _psi[0] % 8
            _psi[0] += 1
            if i not in reserved:
                return psb[i]

    def psbf(p):
        return p.ap.bitcast(BF16)

    ident = sbt([128, 128], F32, "ident")
    identb = sbt([128, 128], BF16, "identb")
    maskle = sbt([128, 128], F32, "maskle")
    reltt = sbt([33, AH], F32, "reltt")
    bstruct = sbt([128, c.NOB * 8], F32, "bstruct")
    prefmask = sbt([128, 8], F32, "prefmask")
    flagc = sbt([128, 1], F32, "flagc")
    iotap = sbt([128, 1], F32, "iotap")
    iotab = sbt([8, c.NB], F32, "iotab")
    onesf = sbt([128, 8], F32, "onesf")
    onesb = sbt([128, 8], BF16, "onesb")
    bi_t = sbt([MH, 1], F32, "bi_t")
    bf_t = sbt([MH, 1], F32, "bf_t")
    nbf_t = sbt([MH, 1], F32, "nbf_t")
    sm_t = sbt([MH, 1], F32, "sm_t")
    t31bc = sbt([128, AH], F32, "t31bc")
    bbv = sbt([128, c.NOB * 8], F32, "bbv")
    qTs = sbt([128, AH, 8], BF16, "qTs")
    kTs = sbt([128, AH, 8], BF16, "kTs")
    T256s = sbt([8, AH, 128], F32, "T256s")
    T0s = sbt([8, AH, 8], F32, "T0s")
    epT = sbt([128, NT, MH], F32, "epT")
    flT = sbt([128, NT, MH], F32, "flT")
    wpb = sbt([128, MH // 2, NT + 1], F32, "wpb")
    mouts = sbt([MH, 2], F32, "mouts")

    def cload(dst, src, eng="sp"):
        P.dma(eng, "const", lambda e, d=dst, s_=src: e.dma_start(out=d.ap, in_=s_), writes=[dst.r])

    cload(ident, cin["ident"])
    cload(maskle, cin["maskle"])
    cload(reltt, relt)
    cload(bstruct, cin["bstruct"].partition_broadcast(128))
    cload(prefmask, cin["prefmask"].partition_broadcast(128))
    cload(flagc, flag.partition_broadcast(128))
    cload(iotap, cin["iotap"])
    cload(iotab, cin["iotab"])
    cload(bi_t, b_i)
    cload(bf_t, b_f)
    cload(sm_t, sm)
    P.op("dve", lambda e: e.tensor_copy(out=identb.ap, in_=ident.ap), reads=[ident.r], writes=[identb.r])
    P.op("pool", lambda e: e.memset(onesf.ap, 1.0), writes=[onesf.r])
    epsD = sbt([128, 1], F32, "epsD")
    epsV = sbt([128, 1], F32, "epsV")
    one1 = sbt([128, 1], F32, "one1")
    P.op("pool", lambda e: e.memset(epsD.ap, float(D * EPS)), writes=[epsD.r])
    P.op("pool", lambda e: e.memset(epsV.ap, float(128 * EPS)), writes=[epsV.r])
    P.op("pool", lambda e: e.memset(one1.ap, 1.0), writes=[one1.r])
    P.op("pool", lambda e: e.memset(onesb.ap, 1.0), writes=[onesb.r])
    P.op("dve", lambda e: e.tensor_scalar(out=nbf_t.ap, in0=bf_t.ap, scalar1=-1.0, scalar2=None, op0=ALU.mult),
         reads=[bf_t.r], writes=[nbf_t.r])
    fm1 = sbt([128, 1], F32, "fm1")
    P.op("dve", lambda e: e.tensor_scalar(out=fm1.ap, in0=flagc.ap, scalar1=-1.0, scalar2=BIG, op0=ALU.add, op1=ALU.mult),
         reads=[flagc.r], writes=[fm1.r])
    for v in range(c.NOB):
        P.op("dve", lambda e, v=v: e.scalar_tensor_tensor(out=bbv[:, v * 8:(v + 1) * 8], in0=prefmask.ap, scalar=fm1[:, 0:1],
                                                          in1=bstruct[:, v * 8:(v + 1) * 8], op0=ALU.mult, op1=ALU.add),
             reads=[prefmask.r, fm1.r, bstruct.r], writes=[bbv.r])

    def load_g(slot, row):
        g = R3.get([128, D], F32, "gbc")
        P.dma("sp", f"g{slot}", lambda e: e.dma_start(out=g.ap, in_=gvec[row:row + 1, :].partition_broadcast(128)), writes=[g.r])
        P.op("pool", lambda e: e.tensor_scalar(out=g.ap, in0=g.ap, scalar1=float(math.sqrt(D)), scalar2=None, op0=ALU.mult),
             reads=[g.r], writes=[g.r])
        return g

    def evac_copy(eng, out_ap, in_ap, rd, wr):
        if eng == "act":
            P.op("act", lambda e: e.copy(out=out_ap, in_=in_ap), reads=rd, writes=wr)
        else:
            P.op(eng, lambda e: e.tensor_copy(out=out_ap, in_=in_ap), reads=rd, writes=wr)

    _alt = [0]

    def alt_eng():
        _alt[0] += 1
        return "act" if _alt[0] % 2 else "dve"

    def norm_tile(src_ap, src_r, r, g, xn_ap, xn_r, ssq, rstd, junk):
        P.op("act", lambda e: e.activation(out=junk[0:r, :], in_=src_ap, func=AF.Square, accum_out=ssq[0:r, 0:1]),
             reads=[src_r], writes=[junk.r, ssq.r])
        P.op("act", lambda e: e.activation(out=rstd[0:r, 0:1], in_=ssq[0:r, 0:1], func=AF.Ln, bias=epsD[0:r, 0:1], scale=1.0), reads=[ssq.r, epsD.r], writes=[rstd.r])
        P.op("act", lambda e: e.activation(out=rstd[0:r, 0:1], in_=rstd[0:r, 0:1], func=AF.Exp, scale=-0.5), reads=[rstd.r], writes=[rstd.r])
        P.op("dve", lambda e: e.scalar_tensor_tensor(out=xn_ap, in0=src_ap, scalar=rstd[0:r, 0:1], in1=g[0:r, :],
                                                     op0=ALU.mult, op1=ALU.mult), reads=[src_r, rstd.r, g.r], writes=[xn_r])

    def transpose_into(xn, r, nch, dstT, tok0, dst_r):
        j0 = 0
        while j0 < nch:
            n = min(8, nch - j0)
            p = nps()
            pb = psbf(p).rearrange("p (a b) -> p a b", b=128)

            def f(e, j0=j0, n=n, pb=pb):
                ins = None
                for j in range(n):
                    ins = e.transpose(out=pb[:, j, 0:r], in_=xn[0:r, (j0 + j) * 128:(j0 + j + 1) * 128], identity=identb[0:r, 0:r])
                return ins
            P.op("pe", f, reads=[xn.r, identb.r], writes=[p.r])
            evac_copy(alt_eng(), dstT[:, j0:j0 + n, tok0:tok0 + r], pb[:, 0:n, 0:r], [p.r], [dst_r])
            j0 += n

    wslots = {}

    def wload(pool_name, nslots, region, shape, pieces):
        key = pool_name
        if key not in wslots:
            wslots[key] = [[region.get(shape, BF16, name=f"w{pool_name}{i}") for i in range(nslots)], 0]
        sl = wslots[key]
        w = sl[0][sl[1] % nslots]
        ch = f"w{pool_name}{sl[1] % nslots}"
        sl[1] += 1
        for pi_, (c0, src) in enumerate(pieces):
            ncol = src.shape[1]
            nk_ = src.shape[0] // 128
            P.dma("pool", ch, lambda e, c0=c0, src=src, ncol=ncol, nk_=nk_: e.dma_start(
                out=w[:, 0:nk_, c0:c0 + ncol], in_=src.rearrange("(k p) c -> p k c", p=128)), writes=[w.r.k(pi_)] if len(pieces) > 1 else [w.r])
        return w

    def mm_tok(p, r, ncol, xT, tok0, nk, w, c0, rd):
        def f(e):
            ins = None
            for k in range(nk):
                ins = e.matmul(p[0:r, 0:ncol], lhsT=xT[:, k, tok0:tok0 + r], rhs=w[:, k, c0:c0 + ncol], start=(k == 0), stop=(k == nk - 1))
            return ins
        P.op("pe", f, reads=rd + [w.r], writes=[p.r])

    def mm_feat(p, m, n, w, c0, nk, xT, tok0, rd):
        def f(e):
            ins = None
            for k in range(nk):
                ins = e.matmul(p[0:m, 0:n], lhsT=w[:, k, c0:c0 + m], rhs=xT[:, k, tok0:tok0 + n], start=(k == 0), stop=(k == nk - 1))
            return ins
        P.op("pe", f, reads=rd + [w.r], writes=[p.r])

    tiles = [(128 * t, 128) for t in range(NP + NO)] + [(128 * (NP + NO), 8)]
    own_tiles = list(range(NP, NP + NO + 1))
    def chunks(t0, t1):
        out = []
        a = t0
        while a < t1:
            n = min(512, t1 - a)
            out.append((a, n))
            a += n
        return out
    TOK_P0, TOK_O0, TOK_S0 = 0, 128 * NP, 128 * (NP + NO)
    ch_all = chunks(0, TOK_O0) + chunks(TOK_O0, TOK_S0) + [(TOK_S0, 8)]
    ch_own = chunks(TOK_O0, TOK_S0) + [(TOK_S0, 8)]

    xnT = R1.get([128, KC, NTOK], BF16, "xnT")
    xnT_k = [xnT.r.k(t) for t in range(NT)]
    vsf = R1.get([8, AH, 128], F32, "vsf")
    g0 = load_g(0, 0)
    xs_ = [R3.get([128, D], F32, "xs") for _ in range(2)]
    xn_ = [R3.get([128, D], BF16, "xn") for _ in range(2)]
    junk = R3.get([128, D], BF16, "junk")
    ssq = sbt([128, 2], F32, "ssq")
    rstd = sbt([128, 2], F32, "rstd")
    for t, (tok0, r) in enumerate(tiles):
        xs, xn = xs_[t % 2], xn_[t % 2]
        P.dma("sp", f"xs{t % 2}", lambda e, xs=xs, tok0=tok0, r=r: e.dma_start(out=xs[0:r, :], in_=x_all[tok0:tok0 + r, :]), writes=[xs.r])
        norm_tile(xs[0:r, :], xs.r, r, g0, xn[0:r, :], xn.r, ssq, rstd, junk)
        transpose_into(xn, r, KC, xnT, tok0, xnT_k[t])

    R3.barrier()
    if STOP == "p0":
        P.emit()
        st.close()
        return nc, hc
    selc = R3.get([MH, (MH // 2) * 128], F32, "selc")
    cload(selc, cin["sel"])
    wg = wload("g", 1, R3, [128, KC, 2 * MH], [(0, w_in[:, c.c_mi:c.c_mi + 2 * MH])])
    NTK = NTOK
    li = R3.get([MH, NTK], F32, "li")
    nb = R3.get([MH, NTK], F32, "nb")
    nb2 = R3.get([MH, NTK], F32, "nb2")
    G = R3.get([MH, NTK], F32, "G")
    ep = R3.get([MH, NTK], F32, "ep")
    fl = R3.get([MH, NTK], F32, "fl")
    for (a, n) in ch_all:
        pi, pf = nps(), nps()
        mm_feat(pi, MH, n, wg, 0, KC, xnT, a, [xnT.r])
        mm_feat(pf, MH, n, wg, MH, KC, xnT, a, [xnT.r])
        P.op("act", lambda e, pi=pi, a=a, n=n: e.activation(out=li[:, a:a + n], in_=pi[0:MH, 0:n], func=AF.Identity, bias=bi_t[:, 0:1], scale=1.0),
             reads=[pi.r, bi_t.r], writes=[li.r])
        P.op("act", lambda e, pf=pf, a=a, n=n: e.activation(out=nb[:, a:a + n], in_=pf[0:MH, 0:n], func=AF.Exp, bias=nbf_t[:, 0:1], scale=-1.0),
             reads=[pf.r, nbf_t.r], writes=[nb.r])
    P.op("act", lambda e: e.activation(out=nb.ap, in_=nb.ap, func=AF.Ln, bias=one1[0:MH, 0:1], scale=1.0), reads=[nb.r, one1.r], writes=[nb.r])
    seqs = [(TOK_P0, 128 * NP), (TOK_O0, 128 * NO), (TOK_S0, 8)]
    cur, oth = nb, nb2
    kk = 1
    maxlen = max(128 * NP, 128 * NO)
    while kk < maxlen:
        def f(e, cur=cur, oth=oth, kk=kk):
            for (a, n) in seqs:
                if kk < n:
                    yield e.tensor_copy(out=oth[:, a:a + kk], in_=cur[:, a:a + kk])
                    yield e.tensor_tensor(out=oth[:, a + kk:a + n], in0=cur[:, a + kk:a + n], in1=cur[:, a:a + n - kk], op=ALU.add)
                else:
                    yield e.tensor_copy(out=oth[:, a:a + n], in_=cur[:, a:a + n])
        P.op("dve", f, reads=[cur.r], writes=[oth.r])
        cur, oth = oth, cur
        kk *= 2
    NBt = cur
    P.op("dve", lambda e: e.tensor_tensor(out=G.ap, in0=li.ap, in1=NBt.ap, op=ALU.add), reads=[li.r, NBt.r], writes=[G.r])
    cm = sbt([MH, NT], F32, "cm")
    Rext = sbt([MH, NT + 3], F32, "Rext")
    negR = sbt([MH, NT + 3], F32, "negR")
    negRl = sbt([MH, NT + 3], F32, "negRl")
    wprev = sbt([MH, NT + 1], F32, "wprev")
    seq_tiles = [(0, NP), (NP, NO), (NP + NO, 1)]
    rofs = [0, NP + 1, NP + NO + 2]
    for si, (t0, ntl) in enumerate(seq_tiles):
        a, n = seqs[si]
        if n >= 128:
            P.op("dve", lambda e, t0=t0, ntl=ntl, a=a, n=n: e.tensor_reduce(out=cm[:, t0:t0 + ntl], in_=G[:, a:a + n].rearrange("p (c l) -> p c l", l=128),
                                                                    axis=AX.X, op=ALU.max), reads=[G.r], writes=[cm.r])
        else:
            P.op("dve", lambda e, t0=t0, a=a, n=n: e.tensor_reduce(out=cm[:, t0:t0 + 1], in_=G[:, a:a + n], axis=AX.X, op=ALU.max),
                 reads=[G.r], writes=[cm.r])
        ro = rofs[si]
        if si == 0:
            P.op("dve", lambda e, ro=ro: e.memset(Rext[:, ro:ro + 1], 0.0), writes=[Rext.r])
        elif si == 1:
            def f(e, ro=ro):
                yield e.tensor_tensor(out=Rext[:, ro:ro + 1], in0=Rext[:, ro - 1:ro], in1=NBt[:, TOK_O0 - 1:TOK_O0], op=ALU.subtract)
                yield e.tensor_scalar(out=Rext[:, ro:ro + 1], in0=Rext[:, ro:ro + 1], scalar1=flagc[0:MH, 0:1], scalar2=None, op0=ALU.mult)
            P.op("dve", f, reads=[Rext.r, NBt.r, flagc.r], writes=[Rext.r])
        else:
            P.op("dve", lambda e, ro=ro: e.tensor_copy(out=Rext[:, ro:ro + 1], in_=sm_t.ap), reads=[sm_t.r], writes=[Rext.r])

        def f(e, ro=ro, t0=t0, ntl=ntl):
            for j in range(ntl):
                yield e.tensor_tensor(out=Rext[:, ro + 1 + j:ro + 2 + j], in0=Rext[:, ro + j:ro + 1 + j], in1=cm[:, t0 + j:t0 + j + 1], op=ALU.max)
        P.op("dve", f, reads=[Rext.r, cm.r], writes=[Rext.r])
        P.op("dve", lambda e, ro=ro, t0=t0, ntl=ntl: e.tensor_tensor(out=wprev[:, t0:t0 + ntl], in0=Rext[:, ro:ro + ntl], in1=Rext[:, ro + 1:ro + 1 + ntl], op=ALU.subtract),
             reads=[Rext.r], writes=[wprev.r])
    P.op("act", lambda e: e.activation(out=wprev[:, 0:NT], in_=wprev[:, 0:NT], func=AF.Exp), reads=[wprev.r], writes=[wprev.r])
    P.op("dve", lambda e: e.tensor_scalar(out=negR.ap, in0=Rext.ap, scalar1=-1.0, scalar2=None, op0=ALU.mult), reads=[Rext.r], writes=[negR.r])
    P.op("dve", lambda e: e.tensor_scalar(out=negRl.ap, in0=Rext.ap, scalar1=-1.0, scalar2=float(math.log(0.125)), op0=ALU.mult, op1=ALU.add),
         reads=[Rext.r], writes=[negRl.r])
    def f(e):
        yield e.tensor_tensor(out=mouts[:, 0:1], in0=Rext[:, rofs[1] + NO:rofs[1] + NO + 1], in1=NBt[:, TOK_S0 - 1:TOK_S0], op=ALU.subtract)
        yield e.tensor_tensor(out=mouts[:, 1:2], in0=Rext[:, rofs[2] + 1:rofs[2] + 2], in1=NBt[:, TOK_S0 + 7:TOK_S0 + 8], op=ALU.subtract)
    P.op("dve", f, reads=[Rext.r, NBt.r], writes=[mouts.r])
    P.dma("sp", "mo", lambda e: e.dma_start(out=mp_o, in_=mouts[:, 0:1]), reads=[mouts.r])
    P.dma("sp", "mo", lambda e: e.dma_start(out=ms_o, in_=mouts[:, 1:2]), reads=[mouts.r])
    for si, (t0, ntl) in enumerate(seq_tiles):
        ro = rofs[si]
        for j in range(ntl):
            tok0, r = tiles[t0 + j]
            P.op("act", lambda e, tok0=tok0, r=r, ro=ro, j=j: e.activation(out=ep[:, tok0:tok0 + r], in_=G[:, tok0:tok0 + r], func=AF.Exp,
                                                                        bias=negRl[:, ro + 1 + j:ro + 2 + j], scale=1.0), reads=[G.r, negRl.r], writes=[ep.r])
            P.op("act", lambda e, tok0=tok0, r=r, ro=ro, j=j: e.activation(out=fl[:, tok0:tok0 + r], in_=NBt[:, tok0:tok0 + r], func=AF.Exp,
                                                                        bias=negR[:, ro + 1 + j:ro + 2 + j], scale=1.0), reads=[NBt.r, negR.r], writes=[fl.r])
    for (src, dst) in ((ep, epT), (fl, flT)):
        p = nps()
        pv = p[:, 0:NT * MH].rearrange("p (t h) -> p t h", h=MH)

        def f(e, src=src, pv=pv):
            ins = None
            for t, (tok0, r) in enumerate(tiles):
                ins = e.transpose(out=pv[0:r, t, :], in_=src[:, tok0:tok0 + r], identity=ident[0:MH, 0:MH])
            return ins
        P.op("pe", f, reads=[src.r, ident.r], writes=[p.r])
        P.op("dve", lambda e, dst=dst, pv=pv: e.tensor_copy(out=dst[:, 0:NT - 1, :], in_=pv[:, 0:NT - 1, :]), reads=[p.r], writes=[dst.r])
        P.op("dve", lambda e, dst=dst, pv=pv: e.tensor_copy(out=dst[0:8, NT - 1, :], in_=pv[0:8, NT - 1, :]), reads=[p.r], writes=[dst.r])
    for hp in range(MH // 2):
        p = nps()
        P.op("pe", lambda e, p=p, hp=hp: e.matmul(p[:, 0:NT], lhsT=selc[:, hp * 128:(hp + 1) * 128], rhs=wprev[:, 0:NT], start=True, stop=True),
             reads=[selc.r, wprev.r], writes=[p.r])
        evac_copy("dve", wpb[:, hp, 0:NT], p[:, 0:NT], [p.r], [wpb.r])
    P.op("pool", lambda e: e.memset(wpb[:, :, NT:NT + 1], 1.0), writes=[wpb.r])

    if STOP == "gates":
        P.emit()
        st.close()
        return nc, hc
    mixT = R2.get([128, KM, NOW], BF16, "mixT")
    R3.barrier()
    wslots.clear()
    gmh = R3.get([128, MH * 128], F32, "gmh")
    P.dma("sp", "gmh", lambda e: e.dma_start(out=gmh.ap, in_=g_mh.partition_broadcast(128)), writes=[gmh.r])
    P.op("pool", lambda e: e.tensor_scalar(out=gmh.ap, in0=gmh.ap, scalar1=float(math.sqrt(128.0)), scalar2=None, op0=ALU.mult),
         reads=[gmh.r], writes=[gmh.r])
    qTm = R3.get([128, NOW], BF16, "qTm")
    kTz = R3.get([128, 2, NOW], BF16, "kTz")
    Kt = R3.get([128, NT, 128], BF16, "Kt")
    Va = R3.get([128, NT, 2, 129], BF16, "Va")
    gsig = R3.get([128, NOT, 256], F32, "gsig")
    Cf = R3.get([128, 129], F32, "Cf")
    Cbz = R3.get([128, 2, 129], BF16, "Cbz")
    StT = [R3.get([128, 2, 128], BF16, "StT") for _ in range(2)]
    omb = [R3.get([128, 256], BF16, "omb") for _ in range(2)]
    sml = [R3.get([128, 16], F32, "sml") for _ in range(2)]
    junk2 = R3.get([128, 128], F32, "junk2")
    Cin = R3.get([128, 129], F32, "Cin")
    P.op("pool", lambda e: e.memset(Va[:, :, :, 128:129], 1.0), writes=[Va.r])
    P.op("pool", lambda e: e.memset(kTz.ap, 0.0), writes=[kTz.r])
    P.op("pool", lambda e: e.memset(Cbz.ap, 0.0), writes=[Cbz.r])
    own_off = lambda t: 128 * (t - NP)

    ptb = sbt([128, NPG], I32, "ptb")
    ptf = sbt([128, NPG], F32, "ptf")
    idxp = sbt([128, NPG], I32, "idxp")
    kmsh = sbt([128, AH * NB], BF16, "kmsh")
    kmsl = sbt([128, AH * NB], BF16, "kmsl")
    kmst = R3.get([128, AH * NB], F32, "kmst")
    kpg = [R3.get([128, AH * 128], F32, "kpg") for _ in range(2)]
    kpb = [R3.get([128, AH * 128], BF16, "kpb") for _ in range(2)]
    kmrow = R3.get([NB, AH * 128], F32, "kmrow")
    ewf = R3.get([128, 2 * NB - 1], F32, "ewf")
    ewb = R3.get([128, 2 * NB - 1], BF16, "ewb")
    cload(ewf, cin["ewin"])

    def pass1():
        P.dma("sp", "ptl", lambda e: e.dma_start(out=ptb.ap, in_=pt.partition_broadcast(128)), writes=[ptb.r])

        def f(e):
            yield e.tensor_copy(out=ptf.ap, in_=ptb.ap)
            yield e.tensor_scalar(out=ptf.ap, in0=ptf.ap, scalar1=128.0, scalar2=iotap[:, 0:1], op0=ALU.mult, op1=ALU.add)
            yield e.tensor_copy(out=idxp.ap, in_=ptf.ap)
        P.op("dve", f, reads=[ptb.r, iotap.r], writes=[ptf.r, idxp.r])
        P.op("dve", lambda e: e.tensor_copy(out=ptf.ap, in_=ptb.ap), reads=[ptb.r, idxp.r], writes=[ptf.r])
        pKa, pKb = nps(), nps()
        rs_i = [psb.index(pKa), psb.index(pKb)]
        reserved.update(rs_i)
        HW_ = AH * 128
        H1 = min(512, HW_)
        def page_dma(j):
            kp = kpg[j % 2]
            P.dma("pool", f"kpg{j % 2}", lambda e, kp=kp, j=j: e.indirect_dma_start(out=kp.ap, out_offset=None, in_=cache_k,
                                                                        in_offset=bass.IndirectOffsetOnAxis(ap=idxp[:, j:j + 1], axis=0)),
                  reads=[idxp.r], writes=[kp.r])
        P.op("dve", lambda e: e.tensor_copy(out=ewb.ap, in_=ewf.ap), reads=[ewf.r], writes=[ewb.r])
        page_dma(0)
        for j in range(NPG):
            kp = kpg[j % 2]
            kb = kpb[j % 2]
            if j + 1 < NPG:
                page_dma(j + 1)
            P.op("pool", lambda e, kp=kp, kb=kb: e.tensor_copy(out=kb.ap, in_=kp.ap), reads=[kp.r], writes=[kb.r])
            b_ = j // 2

            def f(e, kb=kb, j=j, b_=b_):
                ins = e.matmul(pKa[0:NB, 0:H1], lhsT=ewb[:, NB - 1 - b_:2 * NB - 1 - b_], rhs=kb[:, 0:H1], start=(j == 0), stop=(j == NPG - 1))
                if HW_ > 512:
                    ins = e.matmul(pKb[0:NB, 0:HW_ - 512], lhsT=ewb[:, NB - 1 - b_:2 * NB - 1 - b_], rhs=kb[:, 512:HW_], start=(j == 0), stop=(j == NPG - 1))
                return ins
            P.op("pe", f, reads=[kb.r, ewb.r], writes=[pKa.r, pKb.r])
            yield

        P.op("dve", lambda e: e.tensor_copy(out=kmrow[:, 0:H1], in_=pKa[0:NB, 0:H1]), reads=[pKa.r], writes=[kmrow.r])
        if HW_ > 512:
            P.op("act", lambda e: e.copy(out=kmrow[:, 512:HW_], in_=pKb[0:NB, 0:HW_ - 512]), reads=[pKb.r], writes=[kmrow.r])
        pKt = nps()

        def f(e):
            ins = None
            for h in range(AH):
                ins = e.transpose(out=pKt[:, h * NB:(h + 1) * NB], in_=kmrow[:, 128 * h:128 * h + 128], identity=ident[0:NB, 0:NB])
            return ins
        P.op("pe", f, reads=[kmrow.r, ident.r], writes=[pKt.r])

        def f(e):
            yield e.tensor_scalar(out=kmst.ap, in0=pKt[:, 0:AH * NB], scalar1=1.0 / 256.0, scalar2=None, op0=ALU.mult)
            yield e.tensor_copy(out=kmsh.ap, in_=kmst.ap)
            yield e.tensor_tensor(out=kmst.ap, in0=kmst.ap, in1=kmsh.ap, op=ALU.subtract)
            yield e.tensor_copy(out=kmsl.ap, in_=kmst.ap)
        P.op("dve", f, reads=[pKt.r], writes=[kmst.r, kmsh.r, kmsl.r])
        for x_ in rs_i:
            reserved.discard(x_)

    p1 = pass1()

    def p1step(n=1):
        for _ in range(n):
            try:
                next(p1)
            except StopIteration:
                return

    def do_pair(hp):
        wA = wload("m", 2, R3, [128, KC, 256], [(0, w_in[:, c.c_mq + 128 * hp:c.c_mq + 128 * hp + 128]),
                                                 (128, w_in[:, c.c_mk + 128 * hp:c.c_mk + 128 * hp + 128])])
        for (a, n) in ch_own:
            p = nps()
            mm_feat(p, 128, n, wA, 0, KC, xnT, a, [xnT.r])
            evac_copy(alt_eng(), qTm[:, a - TOK_O0:a - TOK_O0 + n], p[:, 0:n], [p.r], [qTm.r])
            p = nps()
            mm_feat(p, 128, n, wA, 128, KC, xnT, a, [xnT.r])
            evac_copy("act", kTz[0:64, 0, a - TOK_O0:a - TOK_O0 + n], p[0:64, 0:n], [p.r], [kTz.r])
            evac_copy("dve", kTz[64:128, 1, a - TOK_O0:a - TOK_O0 + n], p[64:128, 0:n], [p.r], [kTz.r])
        chk("a1")
        for t, (tok0, r) in enumerate(tiles):
            p = nps()
            mm_tok(p, r, 128, xnT, tok0, KC, wA, 128, [xnT_k[t]])
            p1step()
            P.op("dve", lambda e, p=p, t=t, r=r, hp=hp: e.tensor_tensor(
                out=Kt[0:r, t, :].rearrange("p (j d) -> p j d", d=64), in0=p[0:r, 0:128].rearrange("p (j d) -> p j d", d=64),
                in1=epT[0:r, t, 2 * hp:2 * hp + 2].unsqueeze(2).to_broadcast([r, 2, 64]), op=ALU.mult),
                reads=[p.r, epT.r], writes=[Kt.r])
        chk("a2")
        wB = wload("m", 2, R3, [128, KC, 256], [(0, w_in[:, c.c_mv + 256 * hp:c.c_mv + 256 * hp + 256])])
        for t, (tok0, r) in enumerate(tiles):
            p = nps()
            mm_tok(p, r, 256, xnT, tok0, KC, wB, 0, [xnT_k[t]])
            p1step()
            evac_copy(alt_eng(), Va[0:r, t, :, 0:128], p[0:r, 0:256].rearrange("p (j d) -> p j d", d=128), [p.r], [Va.r])
        chk("a3")
        wC = wload("m", 2, R3, [128, KC, 256], [(0, w_in[:, c.c_mo + 256 * hp:c.c_mo + 256 * hp + 256])])
        for t in own_tiles:
            tok0, r = tiles[t]
            p = nps()
            mm_tok(p, r, 256, xnT, tok0, KC, wC, 0, [xnT_k[t]])
            P.op("act", lambda e, p=p, t=t, r=r: e.activation(out=gsig[0:r, t - NP, :], in_=p[0:r, 0:256], func=AF.Sigmoid), reads=[p.r], writes=[gsig.r])
            P.op("pool", lambda e, t=t, r=r, hp=hp: e.tensor_tensor(out=gsig[0:r, t - NP, :], in0=gsig[0:r, t - NP, :], in1=gmh[0:r, 256 * hp:256 * hp + 256], op=ALU.mult),
                 reads=[gsig.r, gmh.r], writes=[gsig.r])

        chk("a4")

        def state_update(t, r, last):
            p = nps()

            def f(e, p=p, t=t, r=r):
                ins = None
                for j in range(2):
                    ins = e.matmul(p[64 * j:64 * j + 64, 0:129], lhsT=Kt[0:r, t, 64 * j:64 * j + 64], rhs=Va[0:r, t, j, :], start=True, stop=True)
                return ins
            P.op("pe", f, reads=[Kt.r, Va.r], writes=[p.r])
            P.op("dve", lambda e, p=p, t=t: e.scalar_tensor_tensor(out=Cf.ap, in0=Cf.ap, scalar=wpb[:, hp, t:t + 1], in1=p[:, 0:129], op0=ALU.mult, op1=ALU.add),
                 reads=[Cf.r, wpb.r, p.r], writes=[Cf.r])
            if not last:
                P.op("act", lambda e, t=t: e.activation(out=Cbz[0:64, 0, :], in_=Cf[0:64, :], func=AF.Identity, scale=wpb[0:64, hp, t + 1:t + 2]), reads=[Cf.r, wpb.r], writes=[Cbz.r])
                P.op("act", lambda e, t=t: e.activation(out=Cbz[64:128, 1, :], in_=Cf[64:128, :], func=AF.Identity, scale=wpb[64:128, hp, t + 1:t + 2]), reads=[Cf.r, wpb.r], writes=[Cbz.r])

        def chunk(t, r, ci):
            tok0 = tiles[t][0]
            oo = own_off(t)
            S, om, sm_ = StT[ci % 2], omb[ci % 2], sml[ci % 2]
            pS = nps()

            def f(e, pS=pS):
                ins = None
                for j in range(2):
                    ins = e.matmul(pS[0:r, j * 128:j * 128 + r], lhsT=kTz[:, j, oo:oo + r], rhs=qTm[:, oo:oo + r], start=True, stop=True)
                return ins
            P.op("pe", f, reads=[kTz.r, qTm.r], writes=[pS.r])
            chk("c1")
            for j in range(2):
                P.op("dve", lambda e, j=j, pS=pS, S=S: e.scalar_tensor_tensor(out=S[0:r, j, 0:r], in0=pS[0:r, j * 128:j * 128 + r], scalar=epT[0:r, t, 2 * hp + j:2 * hp + j + 1],
                                                                          in1=maskle[0:r, 0:r], op0=ALU.mult, op1=ALU.mult), reads=[pS.r, epT.r, maskle.r], writes=[S.r])
            chk("c2")
            pX = nps()
            pXv = pX[:, 0:512].rearrange("p (j d) -> p j d", d=256)

            def f(e, pXv=pXv, S=S):
                ins = None
                for j in range(2):
                    e.matmul(pXv[0:r, j, 0:129], lhsT=qTm[:, oo:oo + r], rhs=Cbz[:, j, :], start=True, stop=False)
                    ins = e.matmul(pXv[0:r, j, 0:129], lhsT=S[0:r, j, 0:r], rhs=Va[0:r, t, j, :], start=False, stop=True)
                return ins
            P.op("pe", f, reads=[qTm.r, Cbz.r, S.r, Va.r], writes=[pX.r])
            chk("c3")
            def f(e, pXv=pXv, sm_=sm_):
                yield e.tensor_scalar(out=sm_[0:r, 0:2], in0=pXv[0:r, :, 128], scalar1=-1.0, scalar2=None, op0=ALU.mult)
                yield e.tensor_tensor(out=sm_[0:r, 0:2], in0=sm_[0:r, 0:2], in1=pXv[0:r, :, 128], op=ALU.max)
                yield e.tensor_tensor(out=sm_[0:r, 0:2], in0=sm_[0:r, 0:2], in1=flT[0:r, t, 2 * hp:2 * hp + 2], op=ALU.max)
                yield e.reciprocal(out=sm_[0:r, 2:4], in_=sm_[0:r, 0:2])
            P.op("dve", f, reads=[pX.r, flT.r], writes=[sm_.r.k("a")])
            chk("c4")
            for j in range(2):
                P.op("act", lambda e, j=j, pXv=pXv, sm_=sm_: e.activation(out=junk2[0:r, :], in_=pXv[0:r, j, 0:128], func=AF.Square, scale=sm_[0:r, 2 + j:3 + j],
                                                                     accum_out=sm_[0:r, 4 + j:5 + j]), reads=[pX.r, sm_.r.k("a")], writes=[junk2.r, sm_.r.k(f"b{j}")])

            chk("c5")
            P.op("act", lambda e, sm_=sm_: e.activation(out=sm_[0:r, 6:8], in_=sm_[0:r, 4:6], func=AF.Ln, bias=epsV[0:r, 0:1], scale=1.0),
                 reads=[sm_.r.k("b0"), sm_.r.k("b1"), epsV.r], writes=[sm_.r.k("c0")])
            P.op("act", lambda e, sm_=sm_: e.activation(out=sm_[0:r, 6:8], in_=sm_[0:r, 6:8], func=AF.Exp, scale=-0.5),
                 reads=[sm_.r.k("c0")], writes=[sm_.r.k("c0")])
            P.op("dve", lambda e, sm_=sm_: e.tensor_tensor(out=sm_[0:r, 8:10], in0=sm_[0:r, 6:8], in1=sm_[0:r, 2:4], op=ALU.mult),
                 reads=[sm_.r.k("a"), sm_.r.k("c0")], writes=[sm_.r.k("c")])
            for j in range(2):
                P.op("dve", lambda e, j=j, pXv=pXv, sm_=sm_, om=om: e.scalar_tensor_tensor(out=om[0:r, j * 128:(j + 1) * 128], in0=pXv[0:r, j, 0:128], scalar=sm_[0:r, 8 + j:9 + j],
                                                                                in1=gsig[0:r, t - NP, j * 128:(j + 1) * 128], op0=ALU.mult, op1=ALU.mult),
                     reads=[pX.r, sm_.r.k("c"), gsig.r], writes=[om.r])
            chk("c7")
            pT = nps()
            pTb = psbf(pT).rearrange("p (a b) -> p a b", b=128)

            def f(e, pTb=pTb, om=om):
                ins = None
                for j in range(2):
                    ins = e.transpose(out=pTb[:, j, 0:r], in_=om[0:r, j * 128:(j + 1) * 128], identity=identb[0:r, 0:r])
                return ins
            P.op("pe", f, reads=[om.r, identb.r], writes=[pT.r])
            evac_copy("act", mixT[:, 2 * hp:2 * hp + 2, oo:oo + r], pTb[:, 0:2, 0:r], [pT.r], [mixT.r.k(2 * hp)])

        P.op("pool", lambda e: e.memset(Cf.ap, 0.0), writes=[Cf.r])
        for t in range(NP):
            state_update(t, 128, True)
        chk("a5")
        P.op("dve", lambda e: e.tensor_scalar(out=Cf.ap, in0=Cf.ap, scalar1=flagc[:, 0:1], scalar2=None, op0=ALU.mult), reads=[Cf.r, flagc.r], writes=[Cf.r])
        P.op("act", lambda e: e.activation(out=Cbz[0:64, 0, :], in_=Cf[0:64, :], func=AF.Identity, scale=wpb[0:64, hp, NP:NP + 1]), reads=[Cf.r, wpb.r], writes=[Cbz.r])
        P.op("act", lambda e: e.activation(out=Cbz[64:128, 1, :], in_=Cf[64:128, :], func=AF.Identity, scale=wpb[64:128, hp, NP:NP + 1]), reads=[Cf.r, wpb.r], writes=[Cbz.r])
        chk("a6")
        for i in range(NO):
            t = NP + i
            chunk(t, 128, i)
            if i == 0:
                chk("a7")
            state_update(t, 128, i == NO - 1)
            if i == 0:
                chk("a8")
        chk("a9")
        for j in range(2):
            h = 2 * hp + j
            P.dma("sp", "co", lambda e, j=j, h=h: e.dma_start(out=Cp_o[64 * h:64 * h + 64, :], in_=Cf[64 * j:64 * j + 64, 0:128]), reads=[Cf.r])
            P.dma("sp", "co", lambda e, j=j, h=h: e.dma_start(out=np_o[64 * h:64 * h + 64, :], in_=Cf[64 * j:64 * j + 64, 128:129]), reads=[Cf.r])
        chk("a10")
        P.dma("sp", "cin", lambda e: e.dma_start(out=Cin[:, 0:128], in_=sC[128 * hp:128 * hp + 128, :]), writes=[Cin.r])
        P.dma("sp", "cin", lambda e: e.dma_start(out=Cin[:, 128:129], in_=sn[128 * hp:128 * hp + 128, :]), writes=[Cin.r])
        P.op("dve", lambda e: e.tensor_copy(out=Cf.ap, in_=Cin.ap), reads=[Cin.r], writes=[Cf.r])
        ts_ = NP + NO
        P.op("act", lambda e: e.activation(out=Cbz[0:64, 0, :], in_=Cf[0:64, :], func=AF.Identity, scale=wpb[0:64, hp, ts_:ts_ + 1]), reads=[Cf.r, wpb.r], writes=[Cbz.r])
        P.op("act", lambda e: e.activation(out=Cbz[64:128, 1, :], in_=Cf[64:128, :], func=AF.Identity, scale=wpb[64:128, hp, ts_:ts_ + 1]), reads=[Cf.r, wpb.r], writes=[Cbz.r])
        chunk(ts_, 8, 0)
        state_update(ts_, 8, True)
        for j in range(2):
            h = 2 * hp + j
            P.dma("sp", "co", lambda e, j=j, h=h: e.dma_start(out=Cs_o[64 * h:64 * h + 64, :], in_=Cf[64 * j:64 * j + 64, 0:128]), reads=[Cf.r])
            P.dma("sp", "co", lambda e, j=j, h=h: e.dma_start(out=ns_o[64 * h:64 * h + 64, :], in_=Cf[64 * j:64 * j + 64, 128:129]), reads=[Cf.r])

    for hp in range(MH // 2):
        do_pair(hp)
    p1step(10 ** 6)

    if STOP == "p1a":
        P.emit()
        st.close()
        return nc, hc
    R3.barrier()
    wslots.clear()
    ohl = R3.get([33, c.OHW], F32, "ohl")
    cload(ohl, cin["ohlong"])
    Tb = {d: R3.get([128, AH, 256], F32, f"Tb{d}") for d in (0, 128, 256)}
    ohb = R3.get([33, c.OHW], BF16, "ohb")
    r2 = sbt([33, 2 * AH], BF16, "r2")
    rtmp = sbt([33, AH], F32, "rtmp")
    P.op("dve", lambda e: e.tensor_copy(out=ohb.ap, in_=ohl.ap), reads=[ohl.r], writes=[ohb.r])

    def f(e):
        yield e.tensor_copy(out=r2[:, 0:AH], in_=reltt.ap)
        yield e.tensor_tensor(out=rtmp.ap, in0=reltt.ap, in1=r2[:, 0:AH], op=ALU.subtract)
        yield e.tensor_copy(out=r2[:, AH:2 * AH], in_=rtmp.ap)
    P.op("dve", f, reads=[reltt.r], writes=[r2.r, rtmp.r])
    p = nps()
    P.op("pe", lambda e, p=p: e.matmul(p[:, 0:2 * AH], lhsT=ohb[:, 600:728], rhs=r2.ap, start=True, stop=True), reads=[ohb.r, r2.r], writes=[p.r])

    def f(e, p=p):
        yield e.tensor_copy(out=t31bc.ap, in_=p[:, 0:AH])
        yield e.tensor_tensor(out=t31bc.ap, in0=t31bc.ap, in1=p[:, AH:2 * AH], op=ALU.add)
    P.op("dve", f, reads=[p.r], writes=[t31bc.r])
    P.op("pool", lambda e: e.memset(Tb[0][:, :, 128:256], -BIG), writes=[Tb[0].r])
    P.op("dve", lambda e: e.tensor_copy(out=Tb[256][:, :, 0:128], in_=t31bc.ap.unsqueeze(2).to_broadcast([128, AH, 128])), reads=[t31bc.r], writes=[Tb[256].r])
    for d in (0, 128, 256):
        for k0 in range(0, 256, 32):
            if (d == 0 and k0 >= 128) or (d == 256 and k0 < 128):
                continue
            p = nps()
            pv = p[:, 0:32 * 2 * AH].rearrange("p (k h) -> p k h", h=2 * AH)

            def f(e, d=d, k0=k0, pv=pv):
                ins = None
                for kk in range(32):
                    stt = d - (k0 + kk) + 255
                    ins = e.matmul(pv[:, kk, :], lhsT=ohb[:, stt:stt + 128], rhs=r2.ap, start=True, stop=True)
                return ins
            P.op("pe", f, reads=[ohb.r, r2.r], writes=[p.r])
            dst = Tb[d][:, :, k0:k0 + 32]

            def f(e, pv=pv, dst=dst):
                yield e.tensor_copy(out=dst, in_=pv[:, :, 0:AH].rearrange("p k h -> p h k"))
                yield e.tensor_tensor(out=dst, in0=dst, in1=pv[:, :, AH:2 * AH].rearrange("p k h -> p h k"), op=ALU.add)
            P.op("dve", f, reads=[p.r], writes=[Tb[d].r.k(k0)])
    P.op("dve", lambda e: e.tensor_tensor(out=Tb[256].ap, in0=Tb[256].ap, in1=t31bc.ap.unsqueeze(2).to_broadcast([128, AH, 256]), op=ALU.subtract),
         reads=[Tb[256].r, t31bc.r], writes=[Tb[256].r])
    P.op("dve", lambda e: e.tensor_copy(out=T256s.ap, in_=Tb[256][0:8, :, 128:256]), reads=[Tb[256].r], writes=[T256s.r])
    P.op("dve", lambda e: e.tensor_copy(out=T0s.ap, in_=Tb[0][0:8, :, 0:8]), reads=[Tb[0].r], writes=[T0s.r])

    chk("b0")
    NBLK, NPB = c.NBLK, c.NPB
    qTa = R3.get([128, NOW], BF16, "qTa")
    kTa = R3.get([128, NTOK], BF16, "kTa")
    Vt = R3.get([128, NT, 128], BF16, "Vt")
    Lg2 = [R3.get([128, NBLK * 256], F32, "Lg") for _ in range(2)]
    Pb2 = [R3.get([128, NBLK * 256], BF16, "Pb") for _ in range(2)]
    PT2 = [R3.get([128, NBLK * 2, 128], BF16, "PT") for _ in range(2)]
    kst = [R3.get([128, 128], F32, "kst") for _ in range(2)]
    vst = [R3.get([128, 128], F32, "vst") for _ in range(2)]
    ksum = sbt([128, 8], F32, "ksum")
    kmh = sbt([128, 8], BF16, "kmh")
    kml = sbt([128, 8], BF16, "kml")
    kmt = sbt([128, 8], F32, "kmt")
    gm2 = [sbt([128, 8], F32, f"gm{i}") for i in range(2)]
    top82 = [sbt([128, 8], F32, f"top8{i}") for i in range(2)]
    fbt2 = [sbt([128, 8], F32, f"fbt{i}") for i in range(2)]
    cst = sbt([128, c.NOB * 8], F32, "cst")
    mxs2 = [sbt([128, 4], F32, f"mxs{i}") for i in range(2)]
    obf2 = [sbt([128, 128], BF16, f"obf{i}") for i in range(2)]
    SCL = float(128 ** -0.5)
    def do_head(h):
        wq = wload("a", 1, R3, [128, KC, 384], [(0, w_in[:, c.c_aq + 128 * h:c.c_aq + 128 * h + 128]),
                                                 (128, w_in[:, c.c_ak + 128 * h:c.c_ak + 128 * h + 128]),
                                                 (256, w_in[:, c.c_av + 128 * h:c.c_av + 128 * h + 128])])
        for (a, n) in ch_own:
            p = nps()
            mm_feat(p, 128, n, wq, 0, KC, xnT, a, [xnT.r])
            P.op("act", lambda e, p=p, a=a, n=n: e.activation(out=qTa[:, a - TOK_O0:a - TOK_O0 + n], in_=p[:, 0:n], func=AF.Identity, scale=SCL), reads=[p.r], writes=[qTa.r])
        P.op("dve", lambda e, h=h: e.tensor_copy(out=qTs[:, h, :], in_=qTa[:, 128 * NO:128 * NO + 8]), reads=[qTa.r], writes=[qTs.r])
        chk("b1a")
        P.op("pool", lambda e: e.memset(ksum.ap, 0.0), writes=[ksum.r])
        for (a, n) in ch_all:
            p = nps()
            mm_feat(p, 128, n, wq, 128, KC, xnT, a, [xnT.r])
            if n == 8:
                evac_copy("act", kTa[:, a:a + n], p[:, 0:n], [p.r], [kTa.r])
            else:
                for b0 in range(0, n, 256):
                    blk = (a + b0) // 256
                    P.op("act", lambda e, p=p, a=a, b0=b0, blk=blk: e.activation(out=kTa[:, a + b0:a + b0 + 256], in_=p[:, b0:b0 + 256], func=AF.Identity,
                                                                              accum_out=ksum[:, blk:blk + 1]), reads=[p.r], writes=[kTa.r, ksum.r])
        P.op("dve", lambda e, h=h: e.tensor_copy(out=kTs[:, h, :], in_=kTa[:, TOK_S0:TOK_S0 + 8]), reads=[kTa.r], writes=[kTs.r])
        chk("b1b")
        def f(e):
            yield e.tensor_scalar(out=kmt.ap, in0=ksum.ap, scalar1=1.0 / 256.0, scalar2=None, op0=ALU.mult)
            yield e.tensor_copy(out=kmh.ap, in_=kmt.ap)
            yield e.tensor_tensor(out=kmt.ap, in0=kmt.ap, in1=kmh.ap, op=ALU.subtract)
            yield e.tensor_copy(out=kml.ap, in_=kmt.ap)
        P.op("dve", f, reads=[ksum.r], writes=[kmt.r, kmh.r, kml.r])
        chk("b1c")
        for t, (tok0, r) in enumerate(tiles):
            p = nps()
            mm_tok(p, r, 128, xnT, tok0, KC, wq, 256, [xnT_k[t]])
            evac_copy("act", Vt[0:r, t, :], p[0:r, 0:128], [p.r], [Vt.r])
            if t == NP - 1:
                chk("b1d")
            if t >= NP:
                oo = own_off(t)
                vs_ = vst[t % 2]
                evac_copy("act", vs_[0:r, :], p[0:r, 0:128], [p.r], [vs_.r])
                if t == NP:
                    chk("b1e")
                P.dma("sp", f"vst{t % 2}", lambda e, vs_=vs_, oo=oo, r=r, h=h: e.dma_start(out=v_o[oo:oo + r, 128 * h:128 * h + 128], in_=vs_[0:r, :]), reads=[vs_.r])
                if t == NP:
                    chk("b1f")
                if r == 8:
                    P.op("dve", lambda e, p=p, h=h: e.tensor_copy(out=vsf[:, h, :], in_=p[0:8, 0:128]), reads=[p.r], writes=[vsf.r])
                p2 = nps()
                mm_tok(p2, r, 128, xnT, tok0, KC, wq, 128, [xnT_k[t]])
                ks_ = kst[t % 2]
                evac_copy("act", ks_[0:r, :], p2[0:r, 0:128], [p2.r], [ks_.r])
                P.dma("sp", f"kst{t % 2}", lambda e, ks_=ks_, oo=oo, r=r, h=h: e.dma_start(out=k_o[oo:oo + r, 128 * h:128 * h + 128], in_=ks_[0:r, :]), reads=[ks_.r])
                if t == NP:
                    chk("b1g")
                if t == NP + NO - 1:
                    chk("b1h")
        chk("b1")
        P.op("dve", lambda e, h=h: e.tensor_scalar(out=cst.ap, in0=bbv.ap, scalar1=t31bc[:, h:h + 1], scalar2=None, op0=ALU.add),
             reads=[bbv.r, t31bc.r], writes=[cst.r])
        def do_tileA(i, Lg, Pb, PT, gm, top8, fbt, mxs, obf):
            v = i // 2
            ob = NPB + v
            oo = 128 * i
            nk = (ob + 1) * 256
            pg = nps()

            def f(e, pg=pg, oo=oo):
                e.matmul(pg[:, 0:8], lhsT=qTa[:, oo:oo + 128], rhs=kmh.ap, start=True, stop=False)
                return e.matmul(pg[:, 0:8], lhsT=qTa[:, oo:oo + 128], rhs=kml.ap, start=False, stop=True)
            P.op("pe", f, reads=[qTa.r, kmh.r, kml.r], writes=[pg.r])

            def f(e, pg=pg, v=v):
                yield e.tensor_tensor(out=gm.ap, in0=pg[:, 0:8], in1=bbv[:, v * 8:v * 8 + 8], op=ALU.add)
                yield e.max(out=top8.ap, in_=gm.ap)
                yield e.tensor_scalar(out=fbt.ap, in0=gm.ap, scalar1=top8[:, 2:3], scalar2=1.0, op0=ALU.is_ge, op1=ALU.subtract)
                yield e.scalar_tensor_tensor(out=fbt.ap, in0=fbt.ap, scalar=BIG, in1=cst[:, v * 8:v * 8 + 8], op0=ALU.mult, op1=ALU.add)
            P.op("dve", f, reads=[pg.r, bbv.r, cst.r], writes=[gm.r, top8.r, fbt.r])
            for c0 in range(0, nk, 512):
                n = min(512, nk - c0)
                pS = nps()
                P.op("pe", lambda e, pS=pS, c0=c0, n=n, oo=oo: e.matmul(pS[:, 0:n], lhsT=qTa[:, oo:oo + 128], rhs=kTa[:, c0:c0 + n], start=True, stop=True),
                     reads=[qTa.r, kTa.r], writes=[pS.r])
                for b0 in range(0, n, 256):
                    blk = (c0 + b0) // 256
                    if blk == ob:
                        dlt = 0 if i % 2 == 0 else 128
                        P.op("dve", lambda e, pS=pS, b0=b0, blk=blk, dlt=dlt, h=h: e.tensor_tensor(out=Lg[:, blk * 256:blk * 256 + 256], in0=pS[:, b0:b0 + 256],
                                                                                              in1=Tb[dlt][:, h, :], op=ALU.add), reads=[pS.r, Tb[dlt].r], writes=[Lg.r.k(blk)])
                    elif blk == ob - 1 and i % 2 == 0:
                        P.op("dve", lambda e, pS=pS, b0=b0, blk=blk, h=h: e.scalar_tensor_tensor(out=Lg[:, blk * 256:blk * 256 + 256], in0=pS[:, b0:b0 + 256], scalar=fbt[:, blk:blk + 1],
                                                                                            in1=Tb[256][:, h, :], op0=ALU.add, op1=ALU.add), reads=[pS.r, fbt.r, Tb[256].r], writes=[Lg.r.k(blk)])
                    else:
                        P.op("act", lambda e, pS=pS, b0=b0, blk=blk: e.activation(out=Lg[:, blk * 256:blk * 256 + 256], in_=pS[:, b0:b0 + 256], func=AF.Identity,
                                                                              bias=fbt[:, blk:blk + 1], scale=1.0), reads=[pS.r, fbt.r], writes=[Lg.r.k(blk)])
            def f(e, nk=nk):
                yield e.reduce_max(out=mxs[:, 0:1], in_=Lg[:, 0:nk], axis=AX.X)
                yield e.tensor_scalar(out=mxs[:, 1:2], in0=mxs[:, 0:1], scalar1=-1.0, scalar2=None, op0=ALU.mult)
            P.op("dve", f, reads=[Lg.r], writes=[mxs.r.k("m")])

        def do_tileA2(i, Lg, Pb, PT, gm, top8, fbt, mxs, obf):
            v = i // 2
            ob = NPB + v
            nk = (ob + 1) * 256
            P.op("act", lambda e, nk=nk: e.activation(out=Pb[:, 0:nk], in_=Lg[:, 0:nk], func=AF.Exp, bias=mxs[:, 1:2], scale=1.0, accum_out=mxs[:, 2:3]),
                 reads=[Lg.r, mxs.r.k("m")], writes=[Pb.r, mxs.r.k("s")])
            P.op("dve", lambda e: e.reciprocal(out=mxs[:, 3:4], in_=mxs[:, 2:3]), reads=[mxs.r.k("s")], writes=[mxs.r.k("r")])

        def do_tileB(i, Lg, Pb, PT, gm, top8, fbt, mxs, obf):
            v = i // 2
            ob = NPB + v
            oo = 128 * i
            nk = (ob + 1) * 256
            nch = nk // 128
            j0 = 0
            while j0 < nch:
                n = min(8, nch - j0)
                pT = nps()
                pTb = psbf(pT).rearrange("p (a b) -> p a b", b=128)

                def f(e, pTb=pTb, j0=j0, n=n):
                    ins = None
                    for j in range(n):
                        ins = e.transpose(out=pTb[:, j, :], in_=Pb[:, (j0 + j) * 128:(j0 + j + 1) * 128], identity=identb.ap)
                    return ins
                P.op("pe", f, reads=[Pb.r, identb.r], writes=[pT.r])
                evac_copy(alt_eng(), PT[:, j0:j0 + n, :], pTb[:, 0:n, :], [pT.r], [PT.r])
                j0 += n
            pO = nps()

            def f(e, pO=pO, nch=nch):
                ins = None
                for j in range(nch):
                    ins = e.matmul(pO[:, 0:128], lhsT=PT[:, j, :], rhs=Vt[:, j, :], start=(j == 0), stop=(j == nch - 1))
                return ins
            P.op("pe", f, reads=[PT.r, Vt.r], writes=[pO.r])
            P.op("act", lambda e, pO=pO: e.activation(out=obf.ap, in_=pO[:, 0:128], func=AF.Identity, scale=mxs[:, 3:4]), reads=[pO.r, mxs.r.k("r")], writes=[obf.r])
            pT = nps()
            pTb = psbf(pT).rearrange("p (a b) -> p a b", b=128)
            P.op("pe", lambda e, pTb=pTb: e.transpose(out=pTb[:, 0, :], in_=obf.ap, identity=identb.ap), reads=[obf.r, identb.r], writes=[pT.r])
            evac_copy("dve", mixT[:, MH + h, oo:oo + 128], pTb[:, 0, :], [pT.r], [mixT.r.k(MH + h)])
            chk("b2")
            if i == NO - 1:
                chk("b3")

        def bufs(i):
            return (Lg2[i % 2], Pb2[i % 2], PT2[i % 2], gm2[i % 2], top82[i % 2], fbt2[i % 2], mxs2[i % 2], obf2[i % 2])
        do_tileA(0, *bufs(0))
        do_tileA2(0, *bufs(0))
        for i in range(1, NO):
            do_tileA(i, *bufs(i))
            do_tileB(i - 1, *bufs(i - 1))
            do_tileA2(i, *bufs(i))
        do_tileB(NO - 1, *bufs(NO - 1))

    for h in range(AH):
        do_head(h)

    if STOP == "p1b":
        P.emit()
        st.close()
        return nc, hc
    R3.barrier()
    wslots.clear()
    kvsel = R3.get([128, 24, 2, 256], F32, "kvsel")
    kselT = R3.get([128, 48, 128], BF16, "kselT")
    gts = R3.get([8, AH, NB], F32, "gts")
    tp8 = R3.get([8, AH, 8], F32, "tp8")
    OH = R3.get([8, AH, NB], F32, "OH")
    OHt = R3.get([8, AH, NB], F32, "OHt")
    selp = R3.get([8, AH, 3, 2], F32, "selp")
    is63 = R3.get([8, AH, 3], F32, "is63")
    Xd = R3.get([8, 8, AH * 6], F32, "Xd")
    addc2 = R3.get([128, NSEL // 2], F32, "addc2")
    idxs = R3.get([128, NSEL // 2], I32, "idxs")
    sTs = R3.get([128, 48], F32, "sTs")
    Ls = R3.get([8, 784], F32, "Ls")
    tmpb = R3.get([8, 256], F32, "tmpb")
    pTs = R3.get([128, 6, 8], F32, "pTs")
    pTo = R3.get([8, 8], F32, "pTo")
    sms = R3.get([8, 4], F32, "sms")
    cload(addc2, cin["addc2"])
    eye8x = R3.get([8, 8 * AH * 6], F32, "eye8x")
    ones8 = R3.get([8, 128], F32, "ones8")
    cload(eye8x, cin["eye8x"])
    P.op("pool", lambda e: e.memset(ones8.ap, 1.0), writes=[ones8.r])
    pg = nps()

    def f(e):
        ins = None
        for h in range(AH):
            e.matmul(pg[0:8, h * NB:(h + 1) * NB], lhsT=qTs[:, h, :], rhs=kmsh[:, h * NB:(h + 1) * NB], start=True, stop=False)
            ins = e.matmul(pg[0:8, h * NB:(h + 1) * NB], lhsT=qTs[:, h, :], rhs=kmsl[:, h * NB:(h + 1) * NB], start=False, stop=True)
        return ins
    P.op("pe", f, reads=[qTs.r, kmsh.r, kmsl.r], writes=[pg.r])

    def f(e):
        yield e.tensor_copy(out=gts.ap, in_=pg[0:8, 0:AH * NB].rearrange("p (h n) -> p h n", n=NB))
        for h in range(AH):
            yield e.max(out=tp8[:, h, :], in_=gts[:, h, :])
    P.op("dve", f, reads=[pg.r], writes=[gts.r, tp8.r])
    ptv = ptf[0:8, :].rearrange("p (n u) -> p u n", u=2)
    for s_ in range(3):
        def f(e, s_=s_):
            yield e.tensor_tensor(out=OH.ap, in0=gts.ap, in1=tp8[:, :, s_:s_ + 1].to_broadcast([8, AH, NB]), op=ALU.is_equal)
            yield e.tensor_copy(out=is63[:, :, s_], in_=OH[:, :, NB - 1])
            for u in range(2):
                yield e.tensor_tensor(out=OHt.ap, in0=OH.ap, in1=ptv[:, u:u + 1, :].to_broadcast([8, AH, NB]), op=ALU.mult)
                yield e.tensor_reduce(out=selp[:, :, s_, u], in_=OHt.ap, axis=AX.X, op=ALU.add)
        P.op("dve", f, reads=[gts.r, tp8.r, ptf.r], writes=[OH.r, OHt.r, selp.r, is63.r])
    P.op("dve", lambda e: e.tensor_tensor(out=Xd.ap, in0=eye8x.ap.rearrange("p (q x) -> p q x", q=8),
                                          in1=selp.ap.rearrange("p h s u -> p (h s u)").unsqueeze(1).to_broadcast([8, 8, AH * 6]), op=ALU.mult),
         reads=[eye8x.r, selp.r], writes=[Xd.r])
    pB = nps()
    P.op("pe", lambda e, pB=pB: e.matmul(pB[:, 0:NSEL], lhsT=ones8.ap, rhs=Xd.ap.rearrange("p q x -> p (q x)"), start=True, stop=True),
         reads=[ones8.r, Xd.r], writes=[pB.r])

    pBv2 = pB[:, 0:NSEL].rearrange("p (x u) -> p x u", u=2)

    def f(e):
        yield e.scalar_tensor_tensor(out=addc2[0:64, :], in0=pBv2[0:64, :, 0], scalar=float(64 * AH), in1=addc2[0:64, :], op0=ALU.mult, op1=ALU.add)
        yield e.scalar_tensor_tensor(out=addc2[64:128, :], in0=pBv2[64:128, :, 1], scalar=float(64 * AH), in1=addc2[64:128, :], op0=ALU.mult, op1=ALU.add)
        yield e.tensor_copy(out=idxs.ap, in_=addc2.ap)
    P.op("dve", f, reads=[pB.r, addc2.r], writes=[addc2.r, idxs.r])
    ckv_rows = cache_kv.rearrange("(r two) x -> r (two x)", two=2)
    def do_shead(h):
        for q in range(8):
            for s_ in range(3):
                un = q * 3 + s_
                col = (q * AH + h) * 3 + s_
                P.dma("pool", "kvsel", lambda e, un=un, col=col: e.indirect_dma_start(out=kvsel[:, un, :, :].rearrange("p e x -> p (e x)"), out_offset=None, in_=ckv_rows,
                                                                                    in_offset=bass.IndirectOffsetOnAxis(ap=idxs[:, col:col + 1], axis=0)),
                      reads=[idxs.r], writes=[kvsel.r.k(un)])
        for u0 in range(0, 48, 4):
            pT = nps()
            pTv = pT.ap.rearrange("p (a b) -> p a b", b=128)

            def f(e, pTv=pTv, u0=u0):
                ins = None
                for j in range(4):
                    ins = e.transpose(out=pTv[:, j, :], in_=kvsel[:, (u0 + j) // 2, (u0 + j) % 2, 0:128], identity=ident.ap)
                return ins
            P.op("pe", f, reads=[kvsel.r, ident.r], writes=[pT.r])
            evac_copy(alt_eng(), kselT[:, u0:u0 + 4, :], pTv[:, 0:4, :], [pT.r], [kselT.r])
        pS = nps()

        def f(e, pS=pS, h=h):
            ins = None
            for q in range(8):
                for su in range(6):
                    un = q * 6 + su
                    ins = e.matmul(pS[:, un:un + 1], lhsT=kselT[:, un, :], rhs=qTs[:, h, q:q + 1], start=True, stop=True)
            return ins
        P.op("pe", f, reads=[kselT.r, qTs.r], writes=[pS.r])
        evac_copy("dve", sTs.ap, pS[:, 0:48], [pS.r], [sTs.r])
        sTv = sTs.ap.rearrange("p (q x) -> p x q", x=6)
        pA, pBk = nps(), nps()
        pAv = pA.ap.rearrange("p (a b) -> p a b", b=128)
        pBv = pBk.ap.rearrange("p (a b) -> p a b", b=128)

        def f(e, pAv=pAv, pBv=pBv, pBk=pBk, h=h):
            for su in range(4):
                e.transpose(out=pAv[0:8, su, :], in_=sTv[:, su, :], identity=ident.ap)
            for su in range(4, 6):
                e.transpose(out=pBv[0:8, su - 4, :], in_=sTv[:, su, :], identity=ident.ap)
            return e.matmul(pBk[0:8, 256:264], lhsT=qTs[:, h, :], rhs=kTs[:, h, :], start=True, stop=True)
        P.op("pe", f, reads=[sTs.r, ident.r, qTs.r, kTs.r], writes=[pA.r, pBk.r])

        def f(e, pAv=pAv, pBv=pBv, pBk=pBk, h=h):
            for s_ in range(3):
                yield e.memset(tmpb[:, 0:128], 0.0)
                yield e.tensor_scalar(out=tmpb[:, 128:256], in0=T256s[:, h, :], scalar1=is63[:, h, s_:s_ + 1], scalar2=None, op0=ALU.mult)
                yield e.tensor_scalar(out=tmpb.ap, in0=tmpb.ap, scalar1=t31bc[0:8, h:h + 1], scalar2=None, op0=ALU.add)
                src = pAv[0:8, 2 * s_:2 * s_ + 2, :] if s_ < 2 else pBv[0:8, 0:2, :]
                yield e.tensor_tensor(out=Ls[:, s_ * 256:(s_ + 1) * 256].rearrange("p (a u b) -> p a u b", u=2, b=64), in0=src.rearrange("p a (u b) -> p a u b", u=2),
                                in1=tmpb.ap.rearrange("q (u p e) -> q e u p", u=2, e=2), op=ALU.add)
            yield e.tensor_tensor(out=Ls[:, 768:776], in0=pBk[0:8, 256:264], in1=T0s[:, h, :], op=ALU.add)
            yield e.reduce_max(out=sms[:, 0:1], in_=Ls[:, 0:776], axis=AX.X)
            yield e.tensor_scalar(out=sms[:, 1:2], in0=sms[:, 0:1], scalar1=-1.0, scalar2=None, op0=ALU.mult)
        P.op("dve", f, reads=[pA.r, pBk.r, T256s.r, is63.r, t31bc.r, T0s.r], writes=[tmpb.r, Ls.r, sms.r.k("m")])
        P.op("act", lambda e: e.activation(out=Ls[:, 0:776], in_=Ls[:, 0:776], func=AF.Exp, bias=sms[:, 1:2], scale=1.0, accum_out=sms[:, 2:3]),
             reads=[Ls.r, sms.r.k("m")], writes=[Ls.r, sms.r.k("s")])

        def f(e):
            yield e.reciprocal(out=sms[:, 3:4], in_=sms[:, 2:3])
            yield e.tensor_scalar(out=Ls[:, 0:776], in0=Ls[:, 0:776], scalar1=sms[:, 3:4], scalar2=None, op0=ALU.mult)
        P.op("dve", f, reads=[Ls.r, sms.r.k("s")], writes=[Ls.r, sms.r.k("r")])
        pP = nps()
        pPv = pP[:, 0:48].rearrange("p (a b) -> p a b", b=8)

        def f(e, pP=pP, pPv=pPv):
            for su in range(6):
                e.transpose(out=pPv[:, su, :], in_=Ls[:, su * 128:(su + 1) * 128], identity=ident[0:8, 0:8])
            return e.transpose(out=pP[0:8, 64:72], in_=Ls[:, 768:776], identity=ident[0:8, 0:8])
        P.op("pe", f, reads=[Ls.r, ident.r], writes=[pP.r])

        def f(e, pP=pP, pPv=pPv):
            yield e.tensor_copy(out=pTs.ap, in_=pPv)
            yield e.tensor_copy(out=pTo.ap, in_=pP[0:8, 64:72])
        P.op("dve", f, reads=[pP.r], writes=[pTs.r, pTo.r])
        pO = nps()

        def f(e, pO=pO, h=h):
            ins = None
            for q in range(8):
                e.matmul(pO[:, q:q + 1], lhsT=vsf[:, h, :], rhs=pTo[:, q:q + 1], start=True, stop=False)
                for su in range(6):
                    ins = e.matmul(pO[:, q:q + 1], lhsT=kvsel[:, q * 3 + su // 2, su % 2, 128:256], rhs=pTs[:, su, q:q + 1], start=False, stop=(su == 5))
            return ins
        P.op("pe", f, reads=[vsf.r, pTo.r, kvsel.r, pTs.r], writes=[pO.r])
        evac_copy("act", mixT[:, MH + h, 128 * NO:128 * NO + 8], pO[:, 0:8], [pO.r], [mixT.r.k(MH + h)])

    for h in range(AH):
        do_shead(h)

    if STOP == "p1c":
        P.emit()
        st.close()
        return nc, hc
    R1.barrier()
    R3.barrier()
    wslots.clear()
    h1 = R1.get([128, NOT, D], F32, "h1")
    otl = [(128 * i, 128) for i in range(NO)] + [(128 * NO, 8)]
    for i, (o0, r) in enumerate(otl):
        P.dma("sp", "h1l", lambda e, i=i, o0=o0, r=r: e.dma_start(out=h1[0:r, i, :], in_=x_all[TOK_O0 + o0:TOK_O0 + o0 + r, :]), writes=[h1.r.k(i)])
    for cg in range(D // 512):
        w = wload("o", 2, R3, [128, max(KM, KC), 512], [(0, w_out[:, cg * 512:(cg + 1) * 512])])
        for i, (o0, r) in enumerate(otl):
            p = nps()
            mm_tok(p, r, 512, mixT, o0, KM, w, 0, [mixT.r])
            P.op("dve", lambda e, p=p, i=i, r=r, cg=cg: e.tensor_tensor(out=h1[0:r, i, cg * 512:(cg + 1) * 512], in0=p[0:r, :], in1=h1[0:r, i, cg * 512:(cg + 1) * 512], op=ALU.add),
                 reads=[p.r, h1.r.k(i)], writes=[h1.r.k(i)])
    def norm_stage(grow):
        R3.barrier()
        wslots.clear()
        g = load_g(0, grow)
        xnb = R3.get([128, D], BF16, "xnb")
        junk3 = R3.get([128, D], BF16, "junk3")
        R2.barrier()
        xT = R2.get([128, KC, NOW], BF16, "xT")
        for i, (o0, r) in enumerate(otl):
            norm_tile(h1[0:r, i, :], h1.r.k(i), r, g, xnb[0:r, :], xnb.r, ssq, rstd, junk3)
            transpose_into(xnb, r, KC, xT, o0, xT.r)
        R3.barrier()
        wslots.clear()
        return xT
    xT2 = norm_stage(1)
    aT = R3.get([128, 4, NOW], BF16, "aT")
    rl = [R3.get([128, 512], F32, "rl") for _ in range(2)]
    ch_o = chunks(0, 128 * NO) + [(128 * NO, 8)]
    nrl = 0
    for fg in range(DFF // 512):
        wu = wload("u", 2, R3, [128, KC, 512], [(0, w_up[:, fg * 512:(fg + 1) * 512])])
        wd = wload("d", 2, R3, [128, 4, D], [(0, w_down[fg * 512:(fg + 1) * 512, :])])
        for f_ in range(4):
            for (a, n) in ch_o:
                p = nps()
                mm_feat(p, 128, n, wu, f_ * 128, KC, xT2, a, [xT2.r])
                rt = rl[nrl % 2]
                nrl += 1
                P.op("act", lambda e, p=p, n=n, rt=rt: e.activation(out=rt[:, 0:n], in_=p[:, 0:n], func=AF.Relu), reads=[p.r], writes=[rt.r])
                P.op("pool", lambda e, rt=rt, f_=f_, a=a, n=n: e.tensor_tensor(out=aT[:, f_, a:a + n], in0=rt[:, 0:n], in1=rt[:, 0:n], op=ALU.mult), reads=[rt.r], writes=[aT.r])
        for i, (o0, r) in enumerate(otl):
            for cg in range(D // 512):
                p = nps()

                def f(e, p=p, o0=o0, r=r, cg=cg, wd=wd):
                    ins = None
                    for f_ in range(4):
                        ins = e.matmul(p[0:r, :], lhsT=aT[:, f_, o0:o0 + r], rhs=wd[:, f_, cg * 512:(cg + 1) * 512], start=(f_ == 0), stop=(f_ == 3))
                    return ins
                P.op("pe", f, reads=[aT.r, wd.r], writes=[p.r])
                P.op("dve", lambda e, p=p, i=i, r=r, cg=cg: e.tensor_tensor(out=h1[0:r, i, cg * 512:(cg + 1) * 512], in0=p[0:r, :], in1=h1[0:r, i, cg * 512:(cg + 1) * 512], op=ALU.add),
                     reads=[p.r, h1.r.k(i)], writes=[h1.r.k(i)])
    xT3 = norm_stage(2)
    KP = PLE // 128
    pT_ = R3.get([128, KP, NOW], BF16, "pT_")
    pst = R3.get([128, PLE], F32, "pst")
    pbf = R3.get([128, PLE], BF16, "pbf")
    for i, (o0, r) in enumerate(otl):
        P.dma("sp", "pst", lambda e, o0=o0, r=r: e.dma_start(out=pst[0:r, :], in_=p_all[o0:o0 + r, :]), writes=[pst.r])
        P.op("dve", lambda e, r=r: e.tensor_copy(out=pbf[0:r, :], in_=pst[0:r, :]), reads=[pst.r], writes=[pbf.r])
        transpose_into(pbf, r, KP, pT_, o0, pT_.r)
    sg = [R3.get([128, 512], F32, "sg") for _ in range(2)]
    for cg in range(D // 512):
        wgt = wload("u", 2, R3, [128, KC, 512], [(0, w_pg[:, cg * 512:(cg + 1) * 512])])
        wpp = wload("p", 2, R3, [128, KP, 512], [(0, w_pp[:, cg * 512:(cg + 1) * 512])])
        for i, (o0, r) in enumerate(otl):
            p1, p2 = nps(), nps()
            mm_tok(p1, r, 512, xT3, o0, KC, wgt, 0, [xT3.r])
            mm_tok(p2, r, 512, pT_, o0, KP, wpp, 0, [pT_.r])
            s_ = sg[i % 2]
            P.op("act", lambda e, p1=p1, r=r, s_=s_: e.activation(out=s_[0:r, :], in_=p1[0:r, :], func=AF.Sigmoid), reads=[p1.r], writes=[s_.r])
            P.op("dve", lambda e, p2=p2, r=r, s_=s_: e.tensor_tensor(out=s_[0:r, :], in0=s_[0:r, :], in1=p2[0:r, :], op=ALU.mult), reads=[p2.r, s_.r], writes=[s_.r])
            P.op("pool", lambda e, i=i, r=r, cg=cg, s_=s_: e.tensor_tensor(out=h1[0:r, i, cg * 512:(cg + 1) * 512], in0=h1[0:r, i, cg * 512:(cg + 1) * 512], in1=s_[0:r, :], op=ALU.add),
                 reads=[s_.r, h1.r.k(i)], writes=[h1.r.k(i)])
    R3.barrier()
    wslots.clear()
    g = load_g(0, 3)
    junk3 = R3.get([128, D], BF16, "junk3")
    yst = [R3.get([128, D], F32, "yst") for _ in range(2)]
    for i, (o0, r) in enumerate(otl):
        y_ = yst[i % 2]
        norm_tile(h1[0:r, i, :], h1.r.k(i), r, g, y_[0:r, :], y_.r, ssq, rstd, junk3)
        P.dma("sp", f"yst{i % 2}", lambda e, y_=y_, o0=o0, r=r: e.dma_start(out=y_o[o0:o0 + r, :], in_=y_[0:r, :]), reads=[y_.r])
    P.emit()
    st.close()
    return nc, hc


def make_in_maps(c, inp, ncores):
    MH, AH = c.MH, c.AH
    half_len = 128 * c.NO
    hc = host_consts(c)
    f32 = np.float32
    shared = {
        "w_in": np.ascontiguousarray(inp["w_in"][0]), "w_out": np.ascontiguousarray(inp["w_out"][0]),
        "w_up": np.ascontiguousarray(inp["w_up"][0]), "w_down": np.ascontiguousarray(inp["w_down"][0]),
        "w_pg": np.ascontiguousarray(inp["w_ple_gate"][0]), "w_pp": np.ascontiguousarray(inp["w_ple_proj"][0]),
        "gvec": np.ascontiguousarray(np.stack([inp["g_mix"][0], inp["g_ffn"][0], inp["g_ple"][0], inp["g_final"]]).astype(f32)),
        "g_mh": np.ascontiguousarray(inp["g_mhead"][0].reshape(1, MH * 128)),
        "b_i": np.ascontiguousarray(inp["b_igate"][0].reshape(MH, 1)), "b_f": np.ascontiguousarray(inp["b_fgate"][0].reshape(MH, 1)),
        "relt": np.ascontiguousarray(np.concatenate([inp["rel_bias_table"], np.full((1, AH), -BIG, f32)], 0).astype(f32)),
        "cache_k": np.ascontiguousarray(inp["cache_k"][0]).reshape(c.NPHYS * 128, AH * 128),
        "cache_kv": np.ascontiguousarray(np.stack([inp["cache_k"][0], inp["cache_v"][0]], axis=3).transpose(0, 2, 1, 3, 4)).reshape(c.NPHYS * AH * 128, 256),
    }
    for k, v in hc.items():
        shared["c_" + k] = v
    maps = []
    for cid in range(ncores):
        b, half = cid // 2, cid % 2
        xp = inp["x_prompt"][b]
        own = xp[half * half_len:(half + 1) * half_len]
        pre = xp[0:half_len] if half == 1 else np.zeros_like(own)
        m = dict(shared)
        m["x_all"] = np.ascontiguousarray(np.concatenate([pre, own, inp["x_sample"][cid]], 0))
        m["p_all"] = np.ascontiguousarray(np.concatenate([inp["p_prompt"][0, b, half * half_len:(half + 1) * half_len], inp["p_sample"][0, cid]], 0))
        m["pt"] = np.ascontiguousarray(inp["page_table"][cid:cid + 1]).astype(np.int32)
        m["sC"] = np.ascontiguousarray(inp["state_C"][0, cid]).reshape(MH * 64, 128)
        m["sn"] = np.ascontiguousarray(inp["state_n"][0, cid]).reshape(MH * 64, 1)
        m["sm"] = np.ascontiguousarray(inp["state_m"][0, cid]).reshape(MH, 1)
        m["flag"] = np.full((1, 1), float(half), f32)
        maps.append(m)
    return maps


def assemble(c, res, ncores):
    MH, AH = c.MH, c.AH
    B = ncores // 2
    hl = 128 * c.NO
    S = 2 * hl
    D = c.D
    f32 = np.float32
    y_p = np.zeros((B, S, D), f32)
    y_s = np.zeros((ncores, 8, D), f32)
    k_p = np.zeros((1, B, S, AH, 128), f32)
    v_p = np.zeros((1, B, S, AH, 128), f32)
    C_p = np.zeros((1, B, MH, 64, 128), f32)
    n_p = np.zeros((1, B, MH, 64), f32)
    m_p = np.zeros((1, B, MH), f32)
    k_s = np.zeros((1, ncores, 8, AH, 128), f32)
    v_s = np.zeros((1, ncores, 8, AH, 128), f32)
    C_s = np.zeros((1, ncores, MH, 64, 128), f32)
    n_s = np.zeros((1, ncores, MH, 64), f32)
    m_s = np.zeros((1, ncores, MH), f32)
    for cid in range(ncores):
        r = res[cid]
        b, half = cid // 2, cid % 2
        y_p[b, half * hl:(half + 1) * hl] = r["y_o"][0:hl]
        y_s[cid] = r["y_o"][hl:hl + 8]
        k_p[0, b, half * hl:(half + 1) * hl] = r["k_o"][0:hl].reshape(hl, AH, 128)
        v_p[0, b, half * hl:(half + 1) * hl] = r["v_o"][0:hl].reshape(hl, AH, 128)
        k_s[0, cid] = r["k_o"][hl:hl + 8].reshape(8, AH, 128)
        v_s[0, cid] = r["v_o"][hl:hl + 8].reshape(8, AH, 128)
        if half == 1:
            C_p[0, b] = r["Cp_o"].reshape(MH, 64, 128)
            n_p[0, b] = r["np_o"].reshape(MH, 64)
            m_p[0, b] = r["mp_o"].reshape(MH)
        C_s[0, cid] = r["Cs_o"].reshape(MH, 64, 128)
        n_s[0, cid] = r["ns_o"].reshape(MH, 64)
        m_s[0, cid] = r["ms_o"].reshape(MH)
    return (y_p, y_s, k_p, v_p, C_p, n_p, m_p, k_s, v_s, C_s, n_s, m_s)


def kernel(**inputs):
    c = Cfg()
    ncores = 8
    inp = {k: np.asarray(v) for k, v in inputs.items()}
    nc, _ = build(c)
    maps = make_in_maps(c, inp, ncores)
    res = run_bass_kernel_spmd(nc, maps, core_ids=list(range(ncores)))
    return assemble(c, res.results, ncores)
```

```python
import numpy as np
import concourse.bass as bass
import concourse.mybir as mybir

F32 = mybir.dt.float32
BF16 = mybir.dt.bfloat16
I32 = mybir.dt.int32
AF = mybir.ActivationFunctionType
ALU = mybir.AluOpType
AX = mybir.AxisListType

ENGS = ("pe", "act", "dve", "pool", "sp")


class Res:
    __slots__ = ("name", "parent", "kids", "lw", "rd", "excl")

    def __init__(self, name, parent=None):
        self.excl = False
        self.name = name
        self.parent = parent
        self.kids = {}
        self.lw = None
        self.rd = []

    def k(self, key):
        r = self.kids.get(key)
        if r is None:
            r = Res(f"{self.name}.{key}", self)
            self.kids[key] = r
        return r


class Op:
    __slots__ = ("eng", "fn", "deps", "idx", "sig", "cnt", "chan", "isdma", "dw", "ny")

    def __init__(self, eng, fn):
        self.eng = eng
        self.fn = fn
        self.deps = set()
        self.dw = {}
        self.sig = False
        self.cnt = 0
        self.chan = None
        self.isdma = False


class Chan:
    def __init__(self, name):
        self.name = name
        self.res = Res("chan_" + name)
        self.n = 0
        self.sem = None


class Prog:
    def __init__(self, nc, same_engine_sync=True):
        self.nc = nc
        self.ops = []
        self.chans = {}
        self.same_engine_sync = same_engine_sync

    def chan(self, name):
        c = self.chans.get(name)
        if c is None:
            c = Chan(name)
            self.chans[name] = c
        return c

    def _desc(self, r, out):
        for kk in r.kids.values():
            out.append(kk)
            if kk.kids:
                self._desc(kk, out)

    def _related(self, r):
        out = [r]
        p = r.parent
        while p is not None:
            out.append(p)
            p = p.parent
        if r.kids:
            self._desc(r, out)
        return out

    def _add(self, op, reads, writes):
        for r in reads:
            if r.excl and r not in writes:
                writes = writes + [r]
        deps = op.deps
        for r in reads:
            for x in self._related(r):
                if x.lw is not None:
                    deps.add(x.lw)
        for w in writes:
            for x in self._related(w):
                if x.lw is not None:
                    deps.add(x.lw)
                for o in x.rd:
                    deps.add(o)
        for d in list(deps):
            if d.isdma:
                c = d.chan
                if op.dw.get(c, 0) < 16 * c.n:
                    op.dw[c] = 16 * c.n
                if not (op.isdma and op.chan is c):
                    c.res.rd.append(op)
                deps.discard(d)
        for r in reads:
            r.rd.append(op)
        for w in writes:
            w.lw = op
            w.rd = []
            if w.kids:
                dd = []
                self._desc(w, dd)
                for kk in dd:
                    kk.lw = op
                    kk.rd = []
        deps.discard(op)
        op.idx = len(self.ops)
        self.ops.append(op)

    frozen = False

    def op(self, eng, fn, reads=(), writes=()):
        if self.frozen:
            return None
        o = Op(eng, fn)
        self._add(o, list(reads), list(writes))
        return o

    def dma(self, eng, chan, fn, reads=(), writes=()):
        if self.frozen:
            return None
        c = self.chan(chan) if isinstance(chan, str) else chan
        o = Op(eng, fn)
        o.isdma = True
        o.chan = c
        assert getattr(c, "eng", eng) == eng
        c.eng = eng
        for x in c.res.rd:
            o.deps.add(x)
        c.res.rd = []
        self._add(o, list(reads), list(writes))
        c.n += 1
        o.cnt = 16 * c.n
        return o

    def emit(self, final_wait_eng="sp"):
        nc = self.nc
        ops = self.ops
        for o in ops:
            for d in o.deps:
                if d.isdma:
                    continue
                if d.eng == o.eng:
                    if o.eng == "pe" or not self.same_engine_sync:
                        continue
                d.sig = True
        import inspect

        class _FI:
            def then_inc(self, *a, **k):
                return self

        class _FE:
            def __getattr__(self, n):
                return lambda *a, **k: _FI()
        counts = {e: 0 for e in ENGS}
        for o in ops:
            o.ny = 0
            if o.isdma:
                continue
            if inspect.isgeneratorfunction(o.fn):
                o.ny = sum(1 for _ in o.fn(_FE()))
                counts[o.eng] += o.ny
                o.cnt = counts[o.eng]
            elif o.sig:
                counts[o.eng] += 1
                o.cnt = counts[o.eng]
        import contextlib
        with contextlib.ExitStack() as st:
            esem = {e: st.enter_context(nc.semaphore("s_" + e)) for e in ENGS if e != "sp"}
            for c in self.chans.values():
                c.sem = st.enter_context(nc.semaphore("c_" + c.name))
            per = {e: [o for o in ops if o.eng == e] for e in ENGS}

            block = st.enter_context(nc.Block())

            def run(eng_name, eng):
                waited = {}
                for o in per[eng_name]:
                    need = {}
                    for c, v in o.dw.items():
                        need[id(c.sem)] = (c.sem, v)
                    for d in o.deps:
                        if d.eng == o.eng and (o.eng == "pe" or not self.same_engine_sync):
                            continue
                        s, v = esem[d.eng], d.cnt
                        key = id(s)
                        if need.get(key, (None, 0))[1] < v:
                            need[key] = (s, v)
                    for key, (s, v) in need.items():
                        if waited.get(key, 0) < v:
                            eng.wait_ge(s, v)
                            waited[key] = v
                    if o.ny:
                        gen = o.fn(eng)
                        base = o.cnt - o.ny
                        for gi, cur in enumerate(gen):
                            cur.then_inc(esem[eng_name], 1)
                            if gi < o.ny - 1:
                                eng.wait_ge(esem[eng_name], base + gi + 1)
                                waited[id(esem[eng_name])] = base + gi + 1
                        continue
                    ins = o.fn(eng)
                    if o.isdma:
                        ins.then_inc(o.chan.sem, 16)
                    elif o.sig:
                        ins.then_inc(esem[o.eng], 1)
                if eng_name == final_wait_eng:
                    for c in self.chans.values():
                        if c.n:
                            eng.wait_ge(c.sem, 16 * c.n)

            @block.tensor
            def _(e):
                run("pe", e)

            @block.scalar
            def _(e):
                run("act", e)

            @block.vector
            def _(e):
                run("dve", e)

            @block.gpsimd
            def _(e):
                run("pool", e)

            @block.sync
            def _(e):
                run("sp", e)

import math
import contextlib
from concourse.bass_utils import run_bass_kernel_spmd

BIG = 30000.0
EPS = 1e-6


class Cfg:
    def __init__(s, D=2048, NP=8, NO=8, MH=8, AH=8, DFF=8192, PLE=256, NPAGES=128, NPHYS=1280):
        s.D, s.NP, s.NO, s.MH, s.AH, s.DFF, s.PLE, s.NPAGES, s.NPHYS = D, NP, NO, MH, AH, DFF, PLE, NPAGES, NPHYS
        s.KC = D // 128
        s.DK, s.DV, s.DH, s.TS = 64, 128, 128, 8
        s.c_mq = 0
        s.c_mk = MH * 64
        s.c_mv = 2 * MH * 64
        s.c_mo = s.c_mv + MH * 128
        s.c_mi = s.c_mo + MH * 128
        s.c_mf = s.c_mi + MH
        s.c_aq = s.c_mf + MH
        s.c_ak = s.c_aq + AH * 128
        s.c_av = s.c_ak + AH * 128
        s.DIN = s.c_av + AH * 128
        s.MIXW = MH * 128 + AH * 128
        s.KM = s.MIXW // 128
        s.NTOK = 128 * (NP + NO) + 8
        s.NOW = 128 * NO + 8
        s.NPB, s.NOB = NP // 2, NO // 2
        s.NBLK = s.NPB + s.NOB
        s.NB = NPAGES // 2
        s.OHW = 768
        s.NSEL = 8 * AH * 6


class V:
    def __init__(s, ap, r):
        s.ap, s.r = ap, r

    def __getitem__(s, k):
        return s.ap[k]


def t5_bucket_np(rel):
    n = np.maximum(rel, 0)
    nf = np.maximum(n, 1).astype(np.float32)
    large = 16 + (np.log(nf / np.float32(16)) / np.float32(math.log(128 / 16)) * np.float32(16)).astype(np.int32)
    large = np.minimum(large, 31)
    return np.where(n < 16, n, large)


def host_consts(c):
    k = {}
    k["ident"] = np.eye(128, dtype=np.float32)
    s_ = np.arange(128)
    k["maskle"] = (s_[:, None] <= s_[None, :]).astype(np.float32)
    rel = np.arange(c.OHW) - 255
    b = t5_bucket_np(rel)
    oh = np.zeros((33, c.OHW), np.float32)
    for i in range(c.OHW):
        if rel[i] < 0:
            oh[32, i] = 1.0
        else:
            oh[b[i], i] = 1.0
    k["ohlong"] = oh
    bs = np.full((c.NOB, 8), -BIG, np.float32)
    for v in range(c.NOB):
        bs[v, : c.NPB + v] = 0.0
    k["bstruct"] = bs.reshape(1, c.NOB * 8)
    pm = np.zeros((1, 8), np.float32)
    pm[0, : c.NPB] = 1.0
    k["prefmask"] = pm
    nhp = c.MH // 2
    sel = np.zeros((c.MH, nhp * 128), np.float32)
    for hp in range(nhp):
        for p in range(128):
            sel[2 * hp + p // 64, hp * 128 + p] = 1.0
    k["sel"] = sel
    addc = np.zeros((128, c.NSEL), np.float32)
    col = 0
    for q in range(8):
        for h in range(c.AH):
            for s in range(3):
                for u in range(2):
                    addc[:, col] = np.arange(128) * c.AH + h
                    col += 1
    k["addc"] = addc
    ew = np.zeros((128, 2 * c.NB - 1), np.float32)
    ew[:, c.NB - 1] = 1.0
    k["ewin"] = ew
    addc2 = np.zeros((128, c.NSEL // 2), np.float32)
    col = 0
    for q in range(8):
        for h in range(c.AH):
            for s in range(3):
                addc2[:, col] = h * 64 + (np.arange(128) % 64)
                col += 1
    k["addc2"] = addc2
    k["iotap"] = np.arange(128, dtype=np.float32).reshape(128, 1)
    k["iotab"] = np.broadcast_to(np.arange(c.NB, dtype=np.float32), (8, c.NB)).copy()
    e8 = np.zeros((8, 8, c.AH * 6), np.float32)
    for q in range(8):
        e8[q, q, :] = 1.0
    k["eye8x"] = e8.reshape(8, 8 * c.AH * 6)
    return k


def build(c):
    import os as _os
    STOP = _os.environ.get("MK_STOP", "")

    def chk(name):
        if STOP == name:
            P.frozen = True
    nc = bass.Bass("TRN2", target_bir_lowering=False)
    P = Prog(nc)
    st = contextlib.ExitStack()
    D, KC, NP, NO, MH, AH, DFF, PLE = c.D, c.KC, c.NP, c.NO, c.MH, c.AH, c.DFF, c.PLE
    NTOK, NOW, KM = c.NTOK, c.NOW, c.KM
    NT = NP + NO + 1
    NB, NPG, NSEL = c.NB, c.NPAGES, c.NSEL
    NOT = NO + 1

    def din(name, shape, dt=F32):
        return nc.dram_tensor(name, list(shape), dt, kind="ExternalInput").ap()

    def dout(name, shape, dt=F32):
        return nc.dram_tensor(name, list(shape), dt, kind="ExternalOutput").ap()

    hc = host_consts(c)
    x_all = din("x_all", [NTOK, D])
    p_all = din("p_all", [NOW, PLE])
    w_in = din("w_in", [D, c.DIN])
    w_out = din("w_out", [c.MIXW, D])
    w_up = din("w_up", [D, DFF])
    w_down = din("w_down", [DFF, D])
    w_pg = din("w_pg", [D, D])
    w_pp = din("w_pp", [PLE, D])
    gvec = din("gvec", [4, D])
    g_mh = din("g_mh", [1, MH * 128])
    b_i = din("b_i", [MH, 1])
    b_f = din("b_f", [MH, 1])
    relt = din("relt", [33, AH])
    cache_k = din("cache_k", [c.NPHYS * 128, AH * 128])
    cache_kv = din("cache_kv", [c.NPHYS * AH * 128, 256])
    pt = din("pt", [1, c.NPAGES], I32)
    sC = din("sC", [MH * 64, 128])
    sn = din("sn", [MH * 64, 1])
    sm = din("sm", [MH, 1])
    flag = din("flag", [1, 1])
    cin = {k: din("c_" + k, v.shape) for k, v in hc.items()}

    y_o = dout("y_o", [NOW, D])
    k_o = dout("k_o", [NOW, AH * 128])
    v_o = dout("v_o", [NOW, AH * 128])
    Cp_o = dout("Cp_o", [MH * 64, 128])
    np_o = dout("np_o", [MH * 64, 1])
    mp_o = dout("mp_o", [MH, 1])
    Cs_o = dout("Cs_o", [MH * 64, 128])
    ns_o = dout("ns_o", [MH * 64, 1])
    ms_o = dout("ms_o", [MH, 1])

    _cnt = [0]

    def sbt(shape, dt=F32, name=None):
        _cnt[0] += 1
        nm = name or f"t{_cnt[0]}"
        t = st.enter_context(nc.sbuf_tensor(nm, list(shape), dt))
        return V(t[:], Res(nm))

    class Region:
        def __init__(s, nbytes, name):
            s.t = st.enter_context(nc.sbuf_tensor(name, [128, nbytes // 4], F32))
            s.r = Res(name)
            s.off = 0
            s.n = nbytes
            s.name = name

        def reset(s):
            s.off = 0

        def barrier(s):
            P.op("pool", lambda e: e.memset(s.t[0:1, 0:1], 0.0), writes=[s.r])
            s.off = 0

        def get(s, shape, dt=F32, name="v"):
            esz = 4 if dt in (F32, I32) else 2
            per = int(np.prod(shape[1:])) * esz
            per4 = (per + 3) // 4
            assert s.off + per4 * 4 <= s.n, (s.name, name, s.off, per4 * 4, s.n)
            ap = s.t[:, s.off // 4: s.off // 4 + per4]
            s.off += per4 * 4
            if dt != F32:
                ap = ap.bitcast(dt)
            n = int(np.prod(shape[1:]))
            ap = ap[:, 0:n]
            if len(shape) == 3:
                ap = ap.rearrange("p (a b) -> p a b", b=shape[2])
            elif len(shape) == 4:
                ap = ap.rearrange("p (a b c) -> p a b c", b=shape[2], c=shape[3])
            ap = ap[0:shape[0]]
            _cnt[0] += 1
            return V(ap, s.r.k(f"{name}{_cnt[0]}"))

    R1B = max(KC * NTOK * 2, NOT * D * 4)
    R2B = max(KM, KC) * NOW * 2
    R1 = Region(R1B, "R1")
    R2 = Region(R2B, "R2")
    R3B = 88 * 1024
    R3 = Region(R3B, "R3")

    psb = []
    for i in range(8):
        t = st.enter_context(nc.psum_tensor(f"ps{i}", [128, 512], F32))
        psb.append(V(t[:], Res(f"ps{i}")))
        psb[-1].r.excl = True
    _psi = [0]
    reserved = set()

    def nps():
        while True:
            i = _psi[0] % 8
            _psi[0] += 1
            if i not in reserved:
                return psb[i]

    def psbf(p):
        return p.ap.bitcast(BF16)

    ident = sbt([128, 128], F32, "ident")
    identb = sbt([128, 128], BF16, "identb")
    maskle = sbt([128, 128], F32, "maskle")
    reltt = sbt([33, AH], F32, "reltt")
    bstruct = sbt([128, c.NOB * 8], F32, "bstruct")
    prefmask = sbt([128, 8], F32, "prefmask")
    flagc = sbt([128, 1], F32, "flagc")
    iotap = sbt([128, 1], F32, "iotap")
    iotab = sbt([8, c.NB], F32, "iotab")
    onesf = sbt([128, 8], F32, "onesf")
    onesb = sbt([128, 8], BF16, "onesb")
    bi_t = sbt([MH, 1], F32, "bi_t")
    bf_t = sbt([MH, 1], F32, "bf_t")
    nbf_t = sbt([MH, 1], F32, "nbf_t")
    sm_t = sbt([MH, 1], F32, "sm_t")
    t31bc = sbt([128, AH], F32, "t31bc")
    bbv = sbt([128, c.NOB * 8], F32, "bbv")
    qTs = sbt([128, AH, 8], BF16, "qTs")
    kTs = sbt([128, AH, 8], BF16, "kTs")
    T256s = sbt([8, AH, 128], F32, "T256s")
    T0s = sbt([8, AH, 8], F32, "T0s")
    epT = sbt([128, NT, MH], F32, "epT")
    flT = sbt([128, NT, MH], F32, "flT")
    wpb = sbt([128, MH // 2, NT + 1], F32, "wpb")
    mouts = sbt([MH, 2], F32, "mouts")

    def cload(dst, src, eng="sp"):
        P.dma(eng, "const", lambda e, d=dst, s_=src: e.dma_start(out=d.ap, in_=s_), writes=[dst.r])

    cload(ident, cin["ident"])
    cload(maskle, cin["maskle"])
    cload(reltt, relt)
    cload(bstruct, cin["bstruct"].partition_broadcast(128))
    cload(prefmask, cin["prefmask"].partition_broadcast(128))
    cload(flagc, flag.partition_broadcast(128))
    cload(iotap, cin["iotap"])
    cload(iotab, cin["iotab"])
    cload(bi_t, b_i)
    cload(bf_t, b_f)
    cload(sm_t, sm)
    P.op("dve", lambda e: e.tensor_copy(out=identb.ap, in_=ident.ap), reads=[ident.r], writes=[identb.r])
    P.op("pool", lambda e: e.memset(onesf.ap, 1.0), writes=[onesf.r])
    epsD = sbt([128, 1], F32, "epsD")
    epsV = sbt([128, 1], F32, "epsV")
    one1 = sbt([128, 1], F32, "one1")
    P.op("pool", lambda e: e.memset(epsD.ap, float(D * EPS)), writes=[epsD.r])
    P.op("pool", lambda e: e.memset(epsV.ap, float(128 * EPS)), writes=[epsV.r])
    P.op("pool", lambda e: e.memset(one1.ap, 1.0), writes=[one1.r])
    P.op("pool", lambda e: e.memset(onesb.ap, 1.0), writes=[onesb.r])
    P.op("dve", lambda e: e.tensor_scalar(out=nbf_t.ap, in0=bf_t.ap, scalar1=-1.0, scalar2=None, op0=ALU.mult),
         reads=[bf_t.r], writes=[nbf_t.r])
    fm1 = sbt([128, 1], F32, "fm1")
    P.op("dve", lambda e: e.tensor_scalar(out=fm1.ap, in0=flagc.ap, scalar1=-1.0, scalar2=BIG, op0=ALU.add, op1=ALU.mult),
         reads=[flagc.r], writes=[fm1.r])
    for v in range(c.NOB):
        P.op("dve", lambda e, v=v: e.scalar_tensor_tensor(out=bbv[:, v * 8:(v + 1) * 8], in0=prefmask.ap, scalar=fm1[:, 0:1],
                                                          in1=bstruct[:, v * 8:(v + 1) * 8], op0=ALU.mult, op1=ALU.add),
             reads=[prefmask.r, fm1.r, bstruct.r], writes=[bbv.r])

    def load_g(slot, row):
        g = R3.get([128, D], F32, "gbc")
        P.dma("sp", f"g{slot}", lambda e: e.dma_start(out=g.ap, in_=gvec[row:row + 1, :].partition_broadcast(128)), writes=[g.r])
        P.op("pool", lambda e: e.tensor_scalar(out=g.ap, in0=g.ap, scalar1=float(math.sqrt(D)), scalar2=None, op0=ALU.mult),
             reads=[g.r], writes=[g.r])
        return g

    def evac_copy(eng, out_ap, in_ap, rd, wr):
        if eng == "act":
            P.op("act", lambda e: e.copy(out=out_ap, in_=in_ap), reads=rd, writes=wr)
        else:
            P.op(eng, lambda e: e.tensor_copy(out=out_ap, in_=in_ap), reads=rd, writes=wr)

    _alt = [0]

    def alt_eng():
        _alt[0] += 1
        return "act" if _alt[0] % 2 else "dve"

    def norm_tile(src_ap, src_r, r, g, xn_ap, xn_r, ssq, rstd, junk):
        P.op("act", lambda e: e.activation(out=junk[0:r, :], in_=src_ap, func=AF.Square, accum_out=ssq[0:r, 0:1]),
             reads=[src_r], writes=[junk.r, ssq.r])
        P.op("act", lambda e: e.activation(out=rstd[0:r, 0:1], in_=ssq[0:r, 0:1], func=AF.Ln, bias=epsD[0:r, 0:1], scale=1.0), reads=[ssq.r, epsD.r], writes=[rstd.r])
        P.op("act", lambda e: e.activation(out=rstd[0:r, 0:1], in_=rstd[0:r, 0:1], func=AF.Exp, scale=-0.5), reads=[rstd.r], writes=[rstd.r])
        P.op("dve", lambda e: e.scalar_tensor_tensor(out=xn_ap, in0=src_ap, scalar=rstd[0:r, 0:1], in1=g[0:r, :],
                                                     op0=ALU.mult, op1=ALU.mult), reads=[src_r, rstd.r, g.r], writes=[xn_r])

    def transpose_into(xn, r, nch, dstT, tok0, dst_r):
        j0 = 0
        while j0 < nch:
            n = min(8, nch - j0)
            p = nps()
            pb = psbf(p).rearrange("p (a b) -> p a b", b=128)

            def f(e, j0=j0, n=n, pb=pb):
                ins = None
                for j in range(n):
                    ins = e.transpose(out=pb[:, j, 0:r], in_=xn[0:r, (j0 + j) * 128:(j0 + j + 1) * 128], identity=identb[0:r, 0:r])
                return ins
            P.op("pe", f, reads=[xn.r, identb.r], writes=[p.r])
            evac_copy(alt_eng(), dstT[:, j0:j0 + n, tok0:tok0 + r], pb[:, 0:n, 0:r], [p.r], [dst_r])
            j0 += n

    wslots = {}

    def wload(pool_name, nslots, region, shape, pieces):
        key = pool_name
        if key not in wslots:
            wslots[key] = [[region.get(shape, BF16, name=f"w{pool_name}{i}") for i in range(nslots)], 0]
        sl = wslots[key]
        w = sl[0][sl[1] % nslots]
        ch = f"w{pool_name}{sl[1] % nslots}"
        sl[1] += 1
        for pi_, (c0, src) in enumerate(pieces):
            ncol = src.shape[1]
            nk_ = src.shape[0] // 128
            P.dma("pool", ch, lambda e, c0=c0, src=src, ncol=ncol, nk_=nk_: e.dma_start(
                out=w[:, 0:nk_, c0:c0 + ncol], in_=src.rearrange("(k p) c -> p k c", p=128)), writes=[w.r.k(pi_)] if len(pieces) > 1 else [w.r])
        return w

    def mm_tok(p, r, ncol, xT, tok0, nk, w, c0, rd):
        def f(e):
            ins = None
            for k in range(nk):
                ins = e.matmul(p[0:r, 0:ncol], lhsT=xT[:, k, tok0:tok0 + r], rhs=w[:, k, c0:c0 + ncol], start=(k == 0), stop=(k == nk - 1))
            return ins
        P.op("pe", f, reads=rd + [w.r], writes=[p.r])

    def mm_feat(p, m, n, w, c0, nk, xT, tok0, rd):
        def f(e):
            ins = None
            for k in range(nk):
                ins = e.matmul(p[0:m, 0:n], lhsT=w[:, k, c0:c0 + m], rhs=xT[:, k, tok0:tok0 + n], start=(k == 0), stop=(k == nk - 1))
            return ins
        P.op("pe", f, reads=rd + [w.r], writes=[p.r])

    tiles = [(128 * t, 128) for t in range(NP + NO)] + [(128 * (NP + NO), 8)]
    own_tiles = list(range(NP, NP + NO + 1))
    def chunks(t0, t1):
        out = []
        a = t0
        while a < t1:
            n = min(512, t1 - a)
            out.append((a, n))
            a += n
        return out
    TOK_P0, TOK_O0, TOK_S0 = 0, 128 * NP, 128 * (NP + NO)
    ch_all = chunks(0, TOK_O0) + chunks(TOK_O0, TOK_S0) + [(TOK_S0, 8)]
    ch_own = chunks(TOK_O0, TOK_S0) + [(TOK_S0, 8)]

    xnT = R1.get([128, KC, NTOK], BF16, "xnT")
    xnT_k = [xnT.r.k(t) for t in range(NT)]
    vsf = R1.get([8, AH, 128], F32, "vsf")
    g0 = load_g(0, 0)
    xs_ = [R3.get([128, D], F32, "xs") for _ in range(2)]
    xn_ = [R3.get([128, D], BF16, "xn") for _ in range(2)]
    junk = R3.get([128, D], BF16, "junk")
    ssq = sbt([128, 2], F32, "ssq")
    rstd = sbt([128, 2], F32, "rstd")
    for t, (tok0, r) in enumerate(tiles):
        xs, xn = xs_[t % 2], xn_[t % 2]
        P.dma("sp", f"xs{t % 2}", lambda e, xs=xs, tok0=tok0, r=r: e.dma_start(out=xs[0:r, :], in_=x_all[tok0:tok0 + r, :]), writes=[xs.r])
        norm_tile(xs[0:r, :], xs.r, r, g0, xn[0:r, :], xn.r, ssq, rstd, junk)
        transpose_into(xn, r, KC, xnT, tok0, xnT_k[t])

    R3.barrier()
    if STOP == "p0":
        P.emit()
        st.close()
        return nc, hc
    selc = R3.get([MH, (MH // 2) * 128], F32, "selc")
    cload(selc, cin["sel"])
    wg = wload("g", 1, R3, [128, KC, 2 * MH], [(0, w_in[:, c.c_mi:c.c_mi + 2 * MH])])
    NTK = NTOK
    li = R3.get([MH, NTK], F32, "li")
    nb = R3.get([MH, NTK], F32, "nb")
    nb2 = R3.get([MH, NTK], F32, "nb2")
    G = R3.get([MH, NTK], F32, "G")
    ep = R3.get([MH, NTK], F32, "ep")
    fl = R3.get([MH, NTK], F32, "fl")
    for (a, n) in ch_all:
        pi, pf = nps(), nps()
        mm_feat(pi, MH, n, wg, 0, KC, xnT, a, [xnT.r])
        mm_feat(pf, MH, n, wg, MH, KC, xnT, a, [xnT.r])
        P.op("act", lambda e, pi=pi, a=a, n=n: e.activation(out=li[:, a:a + n], in_=pi[0:MH, 0:n], func=AF.Identity, bias=bi_t[:, 0:1], scale=1.0),
             reads=[pi.r, bi_t.r], writes=[li.r])
        P.op("act", lambda e, pf=pf, a=a, n=n: e.activation(out=nb[:, a:a + n], in_=pf[0:MH, 0:n], func=AF.Exp, bias=nbf_t[:, 0:1], scale=-1.0),
             reads=[pf.r, nbf_t.r], writes=[nb.r])
    P.op("act", lambda e: e.activation(out=nb.ap, in_=nb.ap, func=AF.Ln, bias=one1[0:MH, 0:1], scale=1.0), reads=[nb.r, one1.r], writes=[nb.r])
    seqs = [(TOK_P0, 128 * NP), (TOK_O0, 128 * NO), (TOK_S0, 8)]
    cur, oth = nb, nb2
    kk = 1
    maxlen = max(128 * NP, 128 * NO)
    while kk < maxlen:
        def f(e, cur=cur, oth=oth, kk=kk):
            for (a, n) in seqs:
                if kk < n:
                    yield e.tensor_copy(out=oth[:, a:a + kk], in_=cur[:, a:a + kk])
                    yield e.tensor_tensor(out=oth[:, a + kk:a + n], in0=cur[:, a + kk:a + n], in1=cur[:, a:a + n - kk], op=ALU.add)
                else:
                    yield e.tensor_copy(out=oth[:, a:a + n], in_=cur[:, a:a + n])
        P.op("dve", f, reads=[cur.r], writes=[oth.r])
        cur, oth = oth, cur
        kk *= 2
    NBt = cur
    P.op("dve", lambda e: e.tensor_tensor(out=G.ap, in0=li.ap, in1=NBt.ap, op=ALU.add), reads=[li.r, NBt.r], writes=[G.r])
    cm = sbt([MH, NT], F32, "cm")
    Rext = sbt([MH, NT + 3], F32, "Rext")
    negR = sbt([MH, NT + 3], F32, "negR")
    negRl = sbt([MH, NT + 3], F32, "negRl")
    wprev = sbt([MH, NT + 1], F32, "wprev")
    seq_tiles = [(0, NP), (NP, NO), (NP + NO, 1)]
    rofs = [0, NP + 1, NP + NO + 2]
    for si, (t0, ntl) in enumerate(seq_tiles):
        a, n = seqs[si]
        if n >= 128:
            P.op("dve", lambda e, t0=t0, ntl=ntl, a=a, n=n: e.tensor_reduce(out=cm[:, t0:t0 + ntl], in_=G[:, a:a + n].rearrange("p (c l) -> p c l", l=128),
                                                                    axis=AX.X, op=ALU.max), reads=[G.r], writes=[cm.r])
        else:
            P.op("dve", lambda e, t0=t0, a=a, n=n: e.tensor_reduce(out=cm[:, t0:t0 + 1], in_=G[:, a:a + n], axis=AX.X, op=ALU.max),
                 reads=[G.r], writes=[cm.r])
        ro = rofs[si]
        if si == 0:
            P.op("dve", lambda e, ro=ro: e.memset(Rext[:, ro:ro + 1], 0.0), writes=[Rext.r])
        elif si == 1:
            def f(e, ro=ro):
                yield e.tensor_tensor(out=Rext[:, ro:ro + 1], in0=Rext[:, ro - 1:ro], in1=NBt[:, TOK_O0 - 1:TOK_O0], op=ALU.subtract)
                yield e.tensor_scalar(out=Rext[:, ro:ro + 1], in0=Rext[:, ro:ro + 1], scalar1=flagc[0:MH, 0:1], scalar2=None, op0=ALU.mult)
            P.op("dve", f, reads=[Rext.r, NBt.r, flagc.r], writes=[Rext.r])
        else:
            P.op("dve", lambda e, ro=ro: e.tensor_copy(out=Rext[:, ro:ro + 1], in_=sm_t.ap), reads=[sm_t.r], writes=[Rext.r])

        def f(e, ro=ro, t0=t0, ntl=ntl):
            for j in range(ntl):
                yield e.tensor_tensor(out=Rext[:, ro + 1 + j:ro + 2 + j], in0=Rext[:, ro + j:ro + 1 + j], in1=cm[:, t0 + j:t0 + j + 1], op=ALU.max)
        P.op("dve", f, reads=[Rext.r, cm.r], writes=[Rext.r])
        P.op("dve", lambda e, ro=ro, t0=t0, ntl=ntl: e.tensor_tensor(out=wprev[:, t0:t0 + ntl], in0=Rext[:, ro:ro + ntl], in1=Rext[:, ro + 1:ro + 1 + ntl], op=ALU.subtract),
             reads=[Rext.r], writes=[wprev.r])
    P.op("act", lambda e: e.activation(out=wprev[:, 0:NT], in_=wprev[:, 0:NT], func=AF.Exp), reads=[wprev.r], writes=[wprev.r])
    P.op("dve", lambda e: e.tensor_scalar(out=negR.ap, in0=Rext.ap, scalar1=-1.0, scalar2=None, op0=ALU.mult), reads=[Rext.r], writes=[negR.r])
    P.op("dve", lambda e: e.tensor_scalar(out=negRl.ap, in0=Rext.ap, scalar1=-1.0, scalar2=float(math.log(0.125)), op0=ALU.mult, op1=ALU.add),
         reads=[Rext.r], writes=[negRl.r])
    def f(e):
        yield e.tensor_tensor(out=mouts[:, 0:1], in0=Rext[:, rofs[1] + NO:rofs[1] + NO + 1], in1=NBt[:, TOK_S0 - 1:TOK_S0], op=ALU.subtract)
        yield e.tensor_tensor(out=mouts[:, 1:2], in0=Rext[:, rofs[2] + 1:rofs[2] + 2], in1=NBt[:, TOK_S0 + 7:TOK_S0 + 8], op=ALU.subtract)
    P.op("dve", f, reads=[Rext.r, NBt.r], writes=[mouts.r])
    P.dma("sp", "mo", lambda e: e.dma_start(out=mp_o, in_=mouts[:, 0:1]), reads=[mouts.r])
    P.dma("sp", "mo", lambda e: e.dma_start(out=ms_o, in_=mouts[:, 1:2]), reads=[mouts.r])
    for si, (t0, ntl) in enumerate(seq_tiles):
        ro = rofs[si]
        for j in range(ntl):
            tok0, r = tiles[t0 + j]
            P.op("act", lambda e, tok0=tok0, r=r, ro=ro, j=j: e.activation(out=ep[:, tok0:tok0 + r], in_=G[:, tok0:tok0 + r], func=AF.Exp,
                                                                        bias=negRl[:, ro + 1 + j:ro + 2 + j], scale=1.0), reads=[G.r, negRl.r], writes=[ep.r])
            P.op("act", lambda e, tok0=tok0, r=r, ro=ro, j=j: e.activation(out=fl[:, tok0:tok0 + r], in_=NBt[:, tok0:tok0 + r], func=AF.Exp,
                                                                        bias=negR[:, ro + 1 + j:ro + 2 + j], scale=1.0), reads=[NBt.r, negR.r], writes=[fl.r])
    for (src, dst) in ((ep, epT), (fl, flT)):
        p = nps()
        pv = p[:, 0:NT * MH].rearrange("p (t h) -> p t h", h=MH)

        def f(e, src=src, pv=pv):
            ins = None
            for t, (tok0, r) in enumerate(tiles):
                ins = e.transpose(out=pv[0:r, t, :], in_=src[:, tok0:tok0 + r], identity=ident[0:MH, 0:MH])
            return ins
        P.op("pe", f, reads=[src.r, ident.r], writes=[p.r])
        P.op("dve", lambda e, dst=dst, pv=pv: e.tensor_copy(out=dst[:, 0:NT - 1, :], in_=pv[:, 0:NT - 1, :]), reads=[p.r], writes=[dst.r])
        P.op("dve", lambda e, dst=dst, pv=pv: e.tensor_copy(out=dst[0:8, NT - 1, :], in_=pv[0:8, NT - 1, :]), reads=[p.r], writes=[dst.r])
    for hp in range(MH // 2):
        p = nps()
        P.op("pe", lambda e, p=p, hp=hp: e.matmul(p[:, 0:NT], lhsT=selc[:, hp * 128:(hp + 1) * 128], rhs=wprev[:, 0:NT], start=True, stop=True),
             reads=[selc.r, wprev.r], writes=[p.r])
        evac_copy("dve", wpb[:, hp, 0:NT], p[:, 0:NT], [p.r], [wpb.r])
    P.op("pool", lambda e: e.memset(wpb[:, :, NT:NT + 1], 1.0), writes=[wpb.r])

    if STOP == "gates":
        P.emit()
        st.close()
        return nc, hc
    mixT = R2.get([128, KM, NOW], BF16, "mixT")
    R3.barrier()
    wslots.clear()
    gmh = R3.get([128, MH * 128], F32, "gmh")
    P.dma("sp", "gmh", lambda e: e.dma_start(out=gmh.ap, in_=g_mh.partition_broadcast(128)), writes=[gmh.r])
    P.op("pool", lambda e: e.tensor_scalar(out=gmh.ap, in0=gmh.ap, scalar1=float(math.sqrt(128.0)), scalar2=None, op0=ALU.mult),
         reads=[gmh.r], writes=[gmh.r])
    qTm = R3.get([128, NOW], BF16, "qTm")
    kTz = R3.get([128, 2, NOW], BF16, "kTz")
    Kt = R3.get([128, NT, 128], BF16, "Kt")
    Va = R3.get([128, NT, 2, 129], BF16, "Va")
    gsig = R3.get([128, NOT, 256], F32, "gsig")
    Cf = R3.get([128, 129], F32, "Cf")
    Cbz = R3.get([128, 2, 129], BF16, "Cbz")
    StT = [R3.get([128, 2, 128], BF16, "StT") for _ in range(2)]
    omb = [R3.get([128, 256], BF16, "omb") for _ in range(2)]
    sml = [R3.get([128, 16], F32, "sml") for _ in range(2)]
    junk2 = R3.get([128, 128], F32, "junk2")
    Cin = R3.get([128, 129], F32, "Cin")
    P.op("pool", lambda e: e.memset(Va[:, :, :, 128:129], 1.0), writes=[Va.r])
    P.op("pool", lambda e: e.memset(kTz.ap, 0.0), writes=[kTz.r])
    P.op("pool", lambda e: e.memset(Cbz.ap, 0.0), writes=[Cbz.r])
    own_off = lambda t: 128 * (t - NP)

    ptb = sbt([128, NPG], I32, "ptb")
    ptf = sbt([128, NPG], F32, "ptf")
    idxp = sbt([128, NPG], I32, "idxp")
    kmsh = sbt([128, AH * NB], BF16, "kmsh")
    kmsl = sbt([128, AH * NB], BF16, "kmsl")
    kmst = R3.get([128, AH * NB], F32, "kmst")
    kpg = [R3.get([128, AH * 128], F32, "kpg") for _ in range(2)]
    kpb = [R3.get([128, AH * 128], BF16, "kpb") for _ in range(2)]
    kmrow = R3.get([NB, AH * 128], F32, "kmrow")
    ewf = R3.get([128, 2 * NB - 1], F32, "ewf")
    ewb = R3.get([128, 2 * NB - 1], BF16, "ewb")
    cload(ewf, cin["ewin"])

    def pass1():
        P.dma("sp", "ptl", lambda e: e.dma_start(out=ptb.ap, in_=pt.partition_broadcast(128)), writes=[ptb.r])

        def f(e):
            yield e.tensor_copy(out=ptf.ap, in_=ptb.ap)
            yield e.tensor_scalar(out=ptf.ap, in0=ptf.ap, scalar1=128.0, scalar2=iotap[:, 0:1], op0=ALU.mult, op1=ALU.add)
            yield e.tensor_copy(out=idxp.ap, in_=ptf.ap)
        P.op("dve", f, reads=[ptb.r, iotap.r], writes=[ptf.r, idxp.r])
        P.op("dve", lambda e: e.tensor_copy(out=ptf.ap, in_=ptb.ap), reads=[ptb.r, idxp.r], writes=[ptf.r])
        pKa, pKb = nps(), nps()
        rs_i = [psb.index(pKa), psb.index(pKb)]
        reserved.update(rs_i)
        HW_ = AH * 128
        H1 = min(512, HW_)
        def page_dma(j):
            kp = kpg[j % 2]
            P.dma("pool", f"kpg{j % 2}", lambda e, kp=kp, j=j: e.indirect_dma_start(out=kp.ap, out_offset=None, in_=cache_k,
                                                                        in_offset=bass.IndirectOffsetOnAxis(ap=idxp[:, j:j + 1], axis=0)),
                  reads=[idxp.r], writes=[kp.r])
        P.op("dve", lambda e: e.tensor_copy(out=ewb.ap, in_=ewf.ap), reads=[ewf.r], writes=[ewb.r])
        page_dma(0)
        for j in range(NPG):
            kp = kpg[j % 2]
            kb = kpb[j % 2]
            if j + 1 < NPG:
                page_dma(j + 1)
            if j % 2 == 0:
                P.op("act", lambda e, kp=kp, kb=kb: e.copy(out=kb.ap, in_=kp.ap), reads=[kp.r], writes=[kb.r])
            else:
                P.op("dve", lambda e, kp=kp, kb=kb: e.tensor_copy(out=kb.ap, in_=kp.ap), reads=[kp.r], writes=[kb.r])
            b_ = j // 2

            def f(e, kb=kb, j=j, b_=b_):
                ins = e.matmul(pKa[0:NB, 0:H1], lhsT=ewb[:, NB - 1 - b_:2 * NB - 1 - b_], rhs=kb[:, 0:H1], start=(j == 0), stop=(j == NPG - 1))
                if HW_ > 512:
                    ins = e.matmul(pKb[0:NB, 0:HW_ - 512], lhsT=ewb[:, NB - 1 - b_:2 * NB - 1 - b_], rhs=kb[:, 512:HW_], start=(j == 0), stop=(j == NPG - 1))
                return ins
            P.op("pe", f, reads=[kb.r, ewb.r], writes=[pKa.r, pKb.r])
            yield

        P.op("dve", lambda e: e.tensor_copy(out=kmrow[:, 0:H1], in_=pKa[0:NB, 0:H1]), reads=[pKa.r], writes=[kmrow.r])
        if HW_ > 512:
            P.op("act", lambda e: e.copy(out=kmrow[:, 512:HW_], in_=pKb[0:NB, 0:HW_ - 512]), reads=[pKb.r], writes=[kmrow.r])
        pKt = nps()

        def f(e):
            ins = None
            for h in range(AH):
                ins = e.transpose(out=pKt[:, h * NB:(h + 1) * NB], in_=kmrow[:, 128 * h:128 * h + 128], identity=ident[0:NB, 0:NB])
            return ins
        P.op("pe", f, reads=[kmrow.r, ident.r], writes=[pKt.r])

        def f(e):
            yield e.tensor_scalar(out=kmst.ap, in0=pKt[:, 0:AH * NB], scalar1=1.0 / 256.0, scalar2=None, op0=ALU.mult)
            yield e.tensor_copy(out=kmsh.ap, in_=kmst.ap)
            yield e.tensor_tensor(out=kmst.ap, in0=kmst.ap, in1=kmsh.ap, op=ALU.subtract)
            yield e.tensor_copy(out=kmsl.ap, in_=kmst.ap)
        P.op("dve", f, reads=[pKt.r], writes=[kmst.r, kmsh.r, kmsl.r])
        for x_ in rs_i:
            reserved.discard(x_)

    p1 = pass1()

    _p1c = [0]

    def p1step(n=1):
        if n == 1:
            _p1c[0] += 1
            if _p1c[0] % 2:
                return
        for _ in range(n):
            try:
                next(p1)
            except StopIteration:
                return

    def do_pair(hp):
        wA = wload("m", 2, R3, [128, KC, 256], [(0, w_in[:, c.c_mq + 128 * hp:c.c_mq + 128 * hp + 128]),
                                                 (128, w_in[:, c.c_mk + 128 * hp:c.c_mk + 128 * hp + 128])])
        for (a, n) in ch_own:
            p = nps()
            mm_feat(p, 128, n, wA, 0, KC, xnT, a, [xnT.r])
            evac_copy(alt_eng(), qTm[:, a - TOK_O0:a - TOK_O0 + n], p[:, 0:n], [p.r], [qTm.r])
            p = nps()
            mm_feat(p, 128, n, wA, 128, KC, xnT, a, [xnT.r])
            evac_copy("act", kTz[0:64, 0, a - TOK_O0:a - TOK_O0 + n], p[0:64, 0:n], [p.r], [kTz.r])
            evac_copy("dve", kTz[64:128, 1, a - TOK_O0:a - TOK_O0 + n], p[64:128, 0:n], [p.r], [kTz.r])
        chk("a1")
        for t, (tok0, r) in enumerate(tiles):
            p = nps()
            mm_tok(p, r, 128, xnT, tok0, KC, wA, 128, [xnT_k[t]])
            p1step()
            P.op("dve", lambda e, p=p, t=t, r=r, hp=hp: e.tensor_tensor(
                out=Kt[0:r, t, :].rearrange("p (j d) -> p j d", d=64), in0=p[0:r, 0:128].rearrange("p (j d) -> p j d", d=64),
                in1=epT[0:r, t, 2 * hp:2 * hp + 2].unsqueeze(2).to_broadcast([r, 2, 64]), op=ALU.mult),
                reads=[p.r, epT.r], writes=[Kt.r])
        chk("a2")
        wB = wload("m", 2, R3, [128, KC, 256], [(0, w_in[:, c.c_mv + 256 * hp:c.c_mv + 256 * hp + 256])])
        for t, (tok0, r) in enumerate(tiles):
            p = nps()
            mm_tok(p, r, 256, xnT, tok0, KC, wB, 0, [xnT_k[t]])
            p1step()
            evac_copy(alt_eng(), Va[0:r, t, :, 0:128], p[0:r, 0:256].rearrange("p (j d) -> p j d", d=128), [p.r], [Va.r])
        chk("a3")
        wC = wload("m", 2, R3, [128, KC, 256], [(0, w_in[:, c.c_mo + 256 * hp:c.c_mo + 256 * hp + 256])])
        for t in own_tiles:
            tok0, r = tiles[t]
            p = nps()
            mm_tok(p, r, 256, xnT, tok0, KC, wC, 0, [xnT_k[t]])
            P.op("act", lambda e, p=p, t=t, r=r: e.activation(out=gsig[0:r, t - NP, :], in_=p[0:r, 0:256], func=AF.Sigmoid), reads=[p.r], writes=[gsig.r])
            P.op("pool", lambda e, t=t, r=r, hp=hp: e.tensor_tensor(out=gsig[0:r, t - NP, :], in0=gsig[0:r, t - NP, :], in1=gmh[0:r, 256 * hp:256 * hp + 256], op=ALU.mult),
                 reads=[gsig.r, gmh.r], writes=[gsig.r])
            p1step()

        chk("a4")

        def state_update(t, r, last):
            p = nps()

            def f(e, p=p, t=t, r=r):
                ins = None
                for j in range(2):
                    ins = e.matmul(p[64 * j:64 * j + 64, 0:129], lhsT=Kt[0:r, t, 64 * j:64 * j + 64], rhs=Va[0:r, t, j, :], start=True, stop=True)
                return ins
            P.op("pe", f, reads=[Kt.r, Va.r], writes=[p.r])
            P.op("dve", lambda e, p=p, t=t: e.scalar_tensor_tensor(out=Cf.ap, in0=Cf.ap, scalar=wpb[:, hp, t:t + 1], in1=p[:, 0:129], op0=ALU.mult, op1=ALU.add),
                 reads=[Cf.r, wpb.r, p.r], writes=[Cf.r])
            if not last:
                P.op("act", lambda e, t=t: e.activation(out=Cbz[0:64, 0, :], in_=Cf[0:64, :], func=AF.Identity, scale=wpb[0:64, hp, t + 1:t + 2]), reads=[Cf.r, wpb.r], writes=[Cbz.r])
                P.op("act", lambda e, t=t: e.activation(out=Cbz[64:128, 1, :], in_=Cf[64:128, :], func=AF.Identity, scale=wpb[64:128, hp, t + 1:t + 2]), reads=[Cf.r, wpb.r], writes=[Cbz.r])

        def chunk(t, r, ci):
            tok0 = tiles[t][0]
            oo = own_off(t)
            S, om, sm_ = StT[ci % 2], omb[ci % 2], sml[ci % 2]
            pS = nps()

            def f(e, pS=pS):
                ins = None
                for j in range(2):
                    ins = e.matmul(pS[0:r, j * 128:j * 128 + r], lhsT=kTz[:, j, oo:oo + r], rhs=qTm[:, oo:oo + r], start=True, stop=True)
                return ins
            P.op("pe", f, reads=[kTz.r, qTm.r], writes=[pS.r])
            chk("c1")
            for j in range(2):
                P.op("dve", lambda e, j=j, pS=pS, S=S: e.scalar_tensor_tensor(out=S[0:r, j, 0:r], in0=pS[0:r, j * 128:j * 128 + r], scalar=epT[0:r, t, 2 * hp + j:2 * hp + j + 1],
                                                                          in1=maskle[0:r, 0:r], op0=ALU.mult, op1=ALU.mult), reads=[pS.r, epT.r, maskle.r], writes=[S.r])
            chk("c2")
            pX = nps()
            pXv = pX[:, 0:512].rearrange("p (j d) -> p j d", d=256)

            def f(e, pXv=pXv, S=S):
                ins = None
                for j in range(2):
                    e.matmul(pXv[0:r, j, 0:129], lhsT=qTm[:, oo:oo + r], rhs=Cbz[:, j, :], start=True, stop=False)
                    ins = e.matmul(pXv[0:r, j, 0:129], lhsT=S[0:r, j, 0:r], rhs=Va[0:r, t, j, :], start=False, stop=True)
                return ins
            P.op("pe", f, reads=[qTm.r, Cbz.r, S.r, Va.r], writes=[pX.r])
            chk("c3")
            def f(e, pXv=pXv, sm_=sm_):
                yield e.tensor_scalar(out=sm_[0:r, 0:2], in0=pXv[0:r, :, 128], scalar1=-1.0, scalar2=None, op0=ALU.mult)
                yield e.tensor_tensor(out=sm_[0:r, 0:2], in0=sm_[0:r, 0:2], in1=pXv[0:r, :, 128], op=ALU.max)
                yield e.tensor_tensor(out=sm_[0:r, 0:2], in0=sm_[0:r, 0:2], in1=flT[0:r, t, 2 * hp:2 * hp + 2], op=ALU.max)
                yield e.reciprocal(out=sm_[0:r, 2:4], in_=sm_[0:r, 0:2])
            P.op("dve", f, reads=[pX.r, flT.r], writes=[sm_.r.k("a")])
            chk("c4")
            for j in range(2):
                P.op("act", lambda e, j=j, pXv=pXv, sm_=sm_: e.activation(out=junk2[0:r, :], in_=pXv[0:r, j, 0:128], func=AF.Square, scale=sm_[0:r, 2 + j:3 + j],
                                                                     accum_out=sm_[0:r, 4 + j:5 + j]), reads=[pX.r, sm_.r.k("a")], writes=[junk2.r, sm_.r.k(f"b{j}")])

            chk("c5")
            P.op("act", lambda e, sm_=sm_: e.activation(out=sm_[0:r, 6:8], in_=sm_[0:r, 4:6], func=AF.Ln, bias=epsV[0:r, 0:1], scale=1.0),
                 reads=[sm_.r.k("b0"), sm_.r.k("b1"), epsV.r], writes=[sm_.r.k("c0")])
            P.op("act", lambda e, sm_=sm_: e.activation(out=sm_[0:r, 6:8], in_=sm_[0:r, 6:8], func=AF.Exp, scale=-0.5),
                 reads=[sm_.r.k("c0")], writes=[sm_.r.k("c0")])
            P.op("dve", lambda e, sm_=sm_: e.tensor_tensor(out=sm_[0:r, 8:10], in0=sm_[0:r, 6:8], in1=sm_[0:r, 2:4], op=ALU.mult),
                 reads=[sm_.r.k("a"), sm_.r.k("c0")], writes=[sm_.r.k("c")])
            for j in range(2):
                P.op("dve", lambda e, j=j, pXv=pXv, sm_=sm_, om=om: e.scalar_tensor_tensor(out=om[0:r, j * 128:(j + 1) * 128], in0=pXv[0:r, j, 0:128], scalar=sm_[0:r, 8 + j:9 + j],
                                                                                in1=gsig[0:r, t - NP, j * 128:(j + 1) * 128], op0=ALU.mult, op1=ALU.mult),
                     reads=[pX.r, sm_.r.k("c"), gsig.r], writes=[om.r])
            chk("c7")
            pT = nps()
            pTb = psbf(pT).rearrange("p (a b) -> p a b", b=128)

            def f(e, pTb=pTb, om=om):
                ins = None
                for j in range(2):
                    ins = e.transpose(out=pTb[:, j, 0:r], in_=om[0:r, j * 128:(j + 1) * 128], identity=identb[0:r, 0:r])
                return ins
            P.op("pe", f, reads=[om.r, identb.r], writes=[pT.r])
            evac_copy("act", mixT[:, 2 * hp:2 * hp + 2, oo:oo + r], pTb[:, 0:2, 0:r], [pT.r], [mixT.r.k(2 * hp)])

        P.op("pool", lambda e: e.memset(Cf.ap, 0.0), writes=[Cf.r])
        for t in range(NP):
            state_update(t, 128, True)
            p1step()
        chk("a5")
        P.op("dve", lambda e: e.tensor_scalar(out=Cf.ap, in0=Cf.ap, scalar1=flagc[:, 0:1], scalar2=None, op0=ALU.mult), reads=[Cf.r, flagc.r], writes=[Cf.r])
        P.op("act", lambda e: e.activation(out=Cbz[0:64, 0, :], in_=Cf[0:64, :], func=AF.Identity, scale=wpb[0:64, hp, NP:NP + 1]), reads=[Cf.r, wpb.r], writes=[Cbz.r])
        P.op("act", lambda e: e.activation(out=Cbz[64:128, 1, :], in_=Cf[64:128, :], func=AF.Identity, scale=wpb[64:128, hp, NP:NP + 1]), reads=[Cf.r, wpb.r], writes=[Cbz.r])
        chk("a6")
        for i in range(NO):
            t = NP + i
            chunk(t, 128, i)
            if i == 0:
                chk("a7")
            state_update(t, 128, i == NO - 1)
            p1step()
            if i == 0:
                chk("a8")
        chk("a9")
        for j in range(2):
            h = 2 * hp + j
            P.dma("sp", "co", lambda e, j=j, h=h: e.dma_start(out=Cp_o[64 * h:64 * h + 64, :], in_=Cf[64 * j:64 * j + 64, 0:128]), reads=[Cf.r])
            P.dma("sp", "co", lambda e, j=j, h=h: e.dma_start(out=np_o[64 * h:64 * h + 64, :], in_=Cf[64 * j:64 * j + 64, 128:129]), reads=[Cf.r])
        chk("a10")
        P.dma("sp", "cin", lambda e: e.dma_start(out=Cin[:, 0:128], in_=sC[128 * hp:128 * hp + 128, :]), writes=[Cin.r])
        P.dma("sp", "cin", lambda e: e.dma_start(out=Cin[:, 128:129], in_=sn[128 * hp:128 * hp + 128, :]), writes=[Cin.r])
        P.op("dve", lambda e: e.tensor_copy(out=Cf.ap, in_=Cin.ap), reads=[Cin.r], writes=[Cf.r])
        ts_ = NP + NO
        P.op("act", lambda e: e.activation(out=Cbz[0:64, 0, :], in_=Cf[0:64, :], func=AF.Identity, scale=wpb[0:64, hp, ts_:ts_ + 1]), reads=[Cf.r, wpb.r], writes=[Cbz.r])
        P.op("act", lambda e: e.activation(out=Cbz[64:128, 1, :], in_=Cf[64:128, :], func=AF.Identity, scale=wpb[64:128, hp, ts_:ts_ + 1]), reads=[Cf.r, wpb.r], writes=[Cbz.r])
        chunk(ts_, 8, 0)
        state_update(ts_, 8, True)
        for j in range(2):
            h = 2 * hp + j
            P.dma("sp", "co", lambda e, j=j, h=h: e.dma_start(out=Cs_o[64 * h:64 * h + 64, :], in_=Cf[64 * j:64 * j + 64, 0:128]), reads=[Cf.r])
            P.dma("sp", "co", lambda e, j=j, h=h: e.dma_start(out=ns_o[64 * h:64 * h + 64, :], in_=Cf[64 * j:64 * j + 64, 128:129]), reads=[Cf.r])

    for hp in range(MH // 2):
        do_pair(hp)
    p1step(10 ** 6)

    if STOP == "p1a":
        P.emit()
        st.close()
        return nc, hc
    R3.barrier()
    wslots.clear()
    ohl = R3.get([33, c.OHW], F32, "ohl")
    cload(ohl, cin["ohlong"])
    Tb = {d: R3.get([128, AH, 256], F32, f"Tb{d}") for d in (0, 128, 256)}
    ohb = R3.get([33, c.OHW], BF16, "ohb")
    r2 = sbt([33, 2 * AH], BF16, "r2")
    rtmp = sbt([33, AH], F32, "rtmp")
    P.op("dve", lambda e: e.tensor_copy(out=ohb.ap, in_=ohl.ap), reads=[ohl.r], writes=[ohb.r])

    def f(e):
        yield e.tensor_copy(out=r2[:, 0:AH], in_=reltt.ap)
        yield e.tensor_tensor(out=rtmp.ap, in0=reltt.ap, in1=r2[:, 0:AH], op=ALU.subtract)
        yield e.tensor_copy(out=r2[:, AH:2 * AH], in_=rtmp.ap)
    P.op("dve", f, reads=[reltt.r], writes=[r2.r, rtmp.r])
    p = nps()
    P.op("pe", lambda e, p=p: e.matmul(p[:, 0:2 * AH], lhsT=ohb[:, 600:728], rhs=r2.ap, start=True, stop=True), reads=[ohb.r, r2.r], writes=[p.r])

    def f(e, p=p):
        yield e.tensor_copy(out=t31bc.ap, in_=p[:, 0:AH])
        yield e.tensor_tensor(out=t31bc.ap, in0=t31bc.ap, in1=p[:, AH:2 * AH], op=ALU.add)
    P.op("dve", f, reads=[p.r], writes=[t31bc.r])
    P.op("pool", lambda e: e.memset(Tb[0][:, :, 128:256], -BIG), writes=[Tb[0].r])
    P.op("dve", lambda e: e.tensor_copy(out=Tb[256][:, :, 0:128], in_=t31bc.ap.unsqueeze(2).to_broadcast([128, AH, 128])), reads=[t31bc.r], writes=[Tb[256].r])
    for d in (0, 128, 256):
        for k0 in range(0, 256, 32):
            if (d == 0 and k0 >= 128) or (d == 256 and k0 < 128):
                continue
            p = nps()
            pv = p[:, 0:32 * 2 * AH].rearrange("p (k h) -> p k h", h=2 * AH)

            def f(e, d=d, k0=k0, pv=pv):
                ins = None
                for kk in range(32):
                    stt = d - (k0 + kk) + 255
                    ins = e.matmul(pv[:, kk, :], lhsT=ohb[:, stt:stt + 128], rhs=r2.ap, start=True, stop=True)
                return ins
            P.op("pe", f, reads=[ohb.r, r2.r], writes=[p.r])
            dst = Tb[d][:, :, k0:k0 + 32]

            def f(e, pv=pv, dst=dst):
                yield e.tensor_copy(out=dst, in_=pv[:, :, 0:AH].rearrange("p k h -> p h k"))
                yield e.tensor_tensor(out=dst, in0=dst, in1=pv[:, :, AH:2 * AH].rearrange("p k h -> p h k"), op=ALU.add)
            P.op("dve", f, reads=[p.r], writes=[Tb[d].r.k(k0)])
    P.op("dve", lambda e: e.tensor_tensor(out=Tb[256].ap, in0=Tb[256].ap, in1=t31bc.ap.unsqueeze(2).to_broadcast([128, AH, 256]), op=ALU.subtract),
         reads=[Tb[256].r, t31bc.r], writes=[Tb[256].r])
    P.op("dve", lambda e: e.tensor_copy(out=T256s.ap, in_=Tb[256][0:8, :, 128:256]), reads=[Tb[256].r], writes=[T256s.r])
    P.op("dve", lambda e: e.tensor_copy(out=T0s.ap, in_=Tb[0][0:8, :, 0:8]), reads=[Tb[0].r], writes=[T0s.r])

    chk("b0")
    NBLK, NPB = c.NBLK, c.NPB
    qTa = R3.get([128, NOW], BF16, "qTa")
    kTa = R3.get([128, NTOK], BF16, "kTa")
    Vt = R3.get([128, NT, 128], BF16, "Vt")
    Lg2 = [R3.get([128, NBLK * 256], F32, "Lg") for _ in range(2)]
    Pb2 = [R3.get([128, NBLK * 256], BF16, "Pb") for _ in range(2)]
    PT2 = [R3.get([128, NBLK * 2, 128], BF16, "PT") for _ in range(2)]
    kst = [R3.get([128, 128], F32, "kst") for _ in range(2)]
    vst = [R3.get([128, 128], F32, "vst") for _ in range(2)]
    ksum = sbt([128, 8], F32, "ksum")
    kmh = sbt([128, 8], BF16, "kmh")
    kml = sbt([128, 8], BF16, "kml")
    kmt = sbt([128, 8], F32, "kmt")
    gm2 = [sbt([128, 8], F32, f"gm{i}") for i in range(2)]
    top82 = [sbt([128, 8], F32, f"top8{i}") for i in range(2)]
    fbt2 = [sbt([128, 8], F32, f"fbt{i}") for i in range(2)]
    cst = sbt([128, c.NOB * 8], F32, "cst")
    mxs2 = [sbt([128, 4], F32, f"mxs{i}") for i in range(2)]
    obf2 = [sbt([128, 128], BF16, f"obf{i}") for i in range(2)]
    SCL = float(128 ** -0.5)
    def do_head(h):
        wq = wload("a", 1, R3, [128, KC, 384], [(0, w_in[:, c.c_aq + 128 * h:c.c_aq + 128 * h + 128]),
                                                 (128, w_in[:, c.c_ak + 128 * h:c.c_ak + 128 * h + 128]),
                                                 (256, w_in[:, c.c_av + 128 * h:c.c_av + 128 * h + 128])])
        for (a, n) in ch_own:
            p = nps()
            mm_feat(p, 128, n, wq, 0, KC, xnT, a, [xnT.r])
            P.op("act", lambda e, p=p, a=a, n=n: e.activation(out=qTa[:, a - TOK_O0:a - TOK_O0 + n], in_=p[:, 0:n], func=AF.Identity, scale=SCL), reads=[p.r], writes=[qTa.r])
        P.op("dve", lambda e, h=h: e.tensor_copy(out=qTs[:, h, :], in_=qTa[:, 128 * NO:128 * NO + 8]), reads=[qTa.r], writes=[qTs.r])
        chk("b1a")
        P.op("pool", lambda e: e.memset(ksum.ap, 0.0), writes=[ksum.r])
        for (a, n) in ch_all:
            p = nps()
            mm_feat(p, 128, n, wq, 128, KC, xnT, a, [xnT.r])
            if n == 8:
                evac_copy("act", kTa[:, a:a + n], p[:, 0:n], [p.r], [kTa.r])
            else:
                for b0 in range(0, n, 256):
                    blk = (a + b0) // 256
                    P.op("act", lambda e, p=p, a=a, b0=b0, blk=blk: e.activation(out=kTa[:, a + b0:a + b0 + 256], in_=p[:, b0:b0 + 256], func=AF.Identity,
                                                                              accum_out=ksum[:, blk:blk + 1]), reads=[p.r], writes=[kTa.r, ksum.r])
        P.op("dve", lambda e, h=h: e.tensor_copy(out=kTs[:, h, :], in_=kTa[:, TOK_S0:TOK_S0 + 8]), reads=[kTa.r], writes=[kTs.r])
        chk("b1b")
        def f(e):
            yield e.tensor_scalar(out=kmt.ap, in0=ksum.ap, scalar1=1.0 / 256.0, scalar2=None, op0=ALU.mult)
            yield e.tensor_copy(out=kmh.ap, in_=kmt.ap)
            yield e.tensor_tensor(out=kmt.ap, in0=kmt.ap, in1=kmh.ap, op=ALU.subtract)
            yield e.tensor_copy(out=kml.ap, in_=kmt.ap)
        P.op("dve", f, reads=[ksum.r], writes=[kmt.r, kmh.r, kml.r])
        chk("b1c")
        for t, (tok0, r) in enumerate(tiles):
            p = nps()
            mm_tok(p, r, 128, xnT, tok0, KC, wq, 256, [xnT_k[t]])
            evac_copy("act", Vt[0:r, t, :], p[0:r, 0:128], [p.r], [Vt.r])
            if t == NP - 1:
                chk("b1d")
            if t >= NP:
                oo = own_off(t)
                vs_ = vst[t % 2]
                evac_copy("act", vs_[0:r, :], p[0:r, 0:128], [p.r], [vs_.r])
                if t == NP:
                    chk("b1e")
                P.dma("sp", f"vst{t % 2}", lambda e, vs_=vs_, oo=oo, r=r, h=h: e.dma_start(out=v_o[oo:oo + r, 128 * h:128 * h + 128], in_=vs_[0:r, :]), reads=[vs_.r])
                if t == NP:
                    chk("b1f")
                if r == 8:
                    P.op("dve", lambda e, p=p, h=h: e.tensor_copy(out=vsf[:, h, :], in_=p[0:8, 0:128]), reads=[p.r], writes=[vsf.r])
                p2 = nps()
                mm_tok(p2, r, 128, xnT, tok0, KC, wq, 128, [xnT_k[t]])
                ks_ = kst[t % 2]
                evac_copy("act", ks_[0:r, :], p2[0:r, 0:128], [p2.r], [ks_.r])
                P.dma("sp", f"kst{t % 2}", lambda e, ks_=ks_, oo=oo, r=r, h=h: e.dma_start(out=k_o[oo:oo + r, 128 * h:128 * h + 128], in_=ks_[0:r, :]), reads=[ks_.r])
                if t == NP:
                    chk("b1g")
                if t == NP + NO - 1:
                    chk("b1h")
        chk("b1")
        P.op("dve", lambda e, h=h: e.tensor_scalar(out=cst.ap, in0=bbv.ap, scalar1=t31bc[:, h:h + 1], scalar2=None, op0=ALU.add),
             reads=[bbv.r, t31bc.r], writes=[cst.r])
        def do_tileA(i, Lg, Pb, PT, gm, top8, fbt, mxs, obf):
            v = i // 2
            ob = NPB + v
            oo = 128 * i
            nk = (ob + 1) * 256
            pg = nps()

            def f(e, pg=pg, oo=oo):
                e.matmul(pg[:, 0:8], lhsT=qTa[:, oo:oo + 128], rhs=kmh.ap, start=True, stop=False)
                return e.matmul(pg[:, 0:8], lhsT=qTa[:, oo:oo + 128], rhs=kml.ap, start=False, stop=True)
            P.op("pe", f, reads=[qTa.r, kmh.r, kml.r], writes=[pg.r])

            def f(e, pg=pg, v=v):
                yield e.tensor_tensor(out=gm.ap, in0=pg[:, 0:8], in1=bbv[:, v * 8:v * 8 + 8], op=ALU.add)
                yield e.max(out=top8.ap, in_=gm.ap)
                yield e.tensor_scalar(out=fbt.ap, in0=gm.ap, scalar1=top8[:, 2:3], scalar2=1.0, op0=ALU.is_ge, op1=ALU.subtract)
                yield e.scalar_tensor_tensor(out=fbt.ap, in0=fbt.ap, scalar=BIG, in1=cst[:, v * 8:v * 8 + 8], op0=ALU.mult, op1=ALU.add)
            P.op("dve", f, reads=[pg.r, bbv.r, cst.r], writes=[gm.r, top8.r, fbt.r])
            for c0 in range(0, nk, 512):
                n = min(512, nk - c0)
                pS = nps()
                P.op("pe", lambda e, pS=pS, c0=c0, n=n, oo=oo: e.matmul(pS[:, 0:n], lhsT=qTa[:, oo:oo + 128], rhs=kTa[:, c0:c0 + n], start=True, stop=True),
                     reads=[qTa.r, kTa.r], writes=[pS.r])
                for b0 in range(0, n, 256):
                    blk = (c0 + b0) // 256
                    if blk == ob:
                        dlt = 0 if i % 2 == 0 else 128
                        P.op("dve", lambda e, pS=pS, b0=b0, blk=blk, dlt=dlt, h=h: e.tensor_tensor(out=Lg[:, blk * 256:blk * 256 + 256], in0=pS[:, b0:b0 + 256],
                                                                                              in1=Tb[dlt][:, h, :], op=ALU.add), reads=[pS.r, Tb[dlt].r], writes=[Lg.r.k(blk)])
                    elif blk == ob - 1 and i % 2 == 0:
                        P.op("dve", lambda e, pS=pS, b0=b0, blk=blk, h=h: e.scalar_tensor_tensor(out=Lg[:, blk * 256:blk * 256 + 256], in0=pS[:, b0:b0 + 256], scalar=fbt[:, blk:blk + 1],
                                                                                            in1=Tb[256][:, h, :], op0=ALU.add, op1=ALU.add), reads=[pS.r, fbt.r, Tb[256].r], writes=[Lg.r.k(blk)])
                    else:
                        P.op("act", lambda e, pS=pS, b0=b0, blk=blk: e.activation(out=Lg[:, blk * 256:blk * 256 + 256], in_=pS[:, b0:b0 + 256], func=AF.Identity,
                                                                              bias=fbt[:, blk:blk + 1], scale=1.0), reads=[pS.r, fbt.r], writes=[Lg.r.k(blk)])
            def f(e, nk=nk):
                yield e.reduce_max(out=mxs[:, 0:1], in_=Lg[:, 0:nk], axis=AX.X)
                yield e.tensor_scalar(out=mxs[:, 1:2], in0=mxs[:, 0:1], scalar1=-1.0, scalar2=None, op0=ALU.mult)
            P.op("dve", f, reads=[Lg.r], writes=[mxs.r.k("m")])

        def do_tileA2(i, Lg, Pb, PT, gm, top8, fbt, mxs, obf):
            v = i // 2
            ob = NPB + v
            nk = (ob + 1) * 256
            P.op("act", lambda e, nk=nk: e.activation(out=Pb[:, 0:nk], in_=Lg[:, 0:nk], func=AF.Exp, bias=mxs[:, 1:2], scale=1.0, accum_out=mxs[:, 2:3]),
                 reads=[Lg.r, mxs.r.k("m")], writes=[Pb.r, mxs.r.k("s")])
            P.op("dve", lambda e: e.reciprocal(out=mxs[:, 3:4], in_=mxs[:, 2:3]), reads=[mxs.r.k("s")], writes=[mxs.r.k("r")])

        def do_tileB(i, Lg, Pb, PT, gm, top8, fbt, mxs, obf):
            v = i // 2
            ob = NPB + v
            oo = 128 * i
            nk = (ob + 1) * 256
            nch = nk // 128
            j0 = 0
            while j0 < nch:
                n = min(8, nch - j0)
                pT = nps()
                pTb = psbf(pT).rearrange("p (a b) -> p a b", b=128)

                def f(e, pTb=pTb, j0=j0, n=n):
                    ins = None
                    for j in range(n):
                        ins = e.transpose(out=pTb[:, j, :], in_=Pb[:, (j0 + j) * 128:(j0 + j + 1) * 128], identity=identb.ap)
                    return ins
                P.op("pe", f, reads=[Pb.r, identb.r], writes=[pT.r])
                evac_copy(alt_eng(), PT[:, j0:j0 + n, :], pTb[:, 0:n, :], [pT.r], [PT.r])
                j0 += n
            pO = nps()

            def f(e, pO=pO, nch=nch):
                ins = None
                for j in range(nch):
                    ins = e.matmul(pO[:, 0:128], lhsT=PT[:, j, :], rhs=Vt[:, j, :], start=(j == 0), stop=(j == nch - 1))
                return ins
            P.op("pe", f, reads=[PT.r, Vt.r], writes=[pO.r])
            P.op("act", lambda e, pO=pO: e.activation(out=obf.ap, in_=pO[:, 0:128], func=AF.Identity, scale=mxs[:, 3:4]), reads=[pO.r, mxs.r.k("r")], writes=[obf.r])
            pT = nps()
            pTb = psbf(pT).rearrange("p (a b) -> p a b", b=128)
            P.op("pe", lambda e, pTb=pTb: e.transpose(out=pTb[:, 0, :], in_=obf.ap, identity=identb.ap), reads=[obf.r, identb.r], writes=[pT.r])
            evac_copy("dve", mixT[:, MH + h, oo:oo + 128], pTb[:, 0, :], [pT.r], [mixT.r.k(MH + h)])
            chk("b2")
            if i == NO - 1:
                chk("b3")

        def bufs(i):
            return (Lg2[i % 2], Pb2[i % 2], PT2[i % 2], gm2[i % 2], top82[i % 2], fbt2[i % 2], mxs2[i % 2], obf2[i % 2])
        do_tileA(0, *bufs(0))
        do_tileA2(0, *bufs(0))
        for i in range(1, NO):
            do_tileA(i, *bufs(i))
            do_tileB(i - 1, *bufs(i - 1))
            do_tileA2(i, *bufs(i))
        do_tileB(NO - 1, *bufs(NO - 1))

    for h in range(AH):
        do_head(h)

    if STOP == "p1b":
        P.emit()
        st.close()
        return nc, hc
    R3.barrier()
    wslots.clear()
    kvsel = R3.get([128, 24, 2, 256], F32, "kvsel")
    kselT = R3.get([128, 48, 128], BF16, "kselT")
    gts = R3.get([8, AH, NB], F32, "gts")
    tp8 = R3.get([8, AH, 8], F32, "tp8")
    OH = R3.get([8, AH, NB], F32, "OH")
    OHt = R3.get([8, AH, NB], F32, "OHt")
    selp = R3.get([8, AH, 3, 2], F32, "selp")
    is63 = R3.get([8, AH, 3], F32, "is63")
    Xd = R3.get([8, 8, AH * 6], F32, "Xd")
    addc2 = R3.get([128, NSEL // 2], F32, "addc2")
    idxs = R3.get([128, NSEL // 2], I32, "idxs")
    sTs = R3.get([128, 48], F32, "sTs")
    Ls = R3.get([8, 784], F32, "Ls")
    tmpb = R3.get([8, 256], F32, "tmpb")
    pTs = R3.get([128, 6, 8], F32, "pTs")
    pTo = R3.get([8, 8], F32, "pTo")
    sms = R3.get([8, 4], F32, "sms")
    cload(addc2, cin["addc2"])
    eye8x = R3.get([8, 8 * AH * 6], F32, "eye8x")
    ones8 = R3.get([8, 128], F32, "ones8")
    cload(eye8x, cin["eye8x"])
    P.op("pool", lambda e: e.memset(ones8.ap, 1.0), writes=[ones8.r])
    pg = nps()

    def f(e):
        ins = None
        for h in range(AH):
            e.matmul(pg[0:8, h * NB:(h + 1) * NB], lhsT=qTs[:, h, :], rhs=kmsh[:, h * NB:(h + 1) * NB], start=True, stop=False)
            ins = e.matmul(pg[0:8, h * NB:(h + 1) * NB], lhsT=qTs[:, h, :], rhs=kmsl[:, h * NB:(h + 1) * NB], start=False, stop=True)
        return ins
    P.op("pe", f, reads=[qTs.r, kmsh.r, kmsl.r], writes=[pg.r])

    def f(e):
        yield e.tensor_copy(out=gts.ap, in_=pg[0:8, 0:AH * NB].rearrange("p (h n) -> p h n", n=NB))
        for h in range(AH):
            yield e.max(out=tp8[:, h, :], in_=gts[:, h, :])
    P.op("dve", f, reads=[pg.r], writes=[gts.r, tp8.r])
    ptv = ptf[0:8, :].rearrange("p (n u) -> p u n", u=2)
    for s_ in range(3):
        def f(e, s_=s_):
            yield e.tensor_tensor(out=OH.ap, in0=gts.ap, in1=tp8[:, :, s_:s_ + 1].to_broadcast([8, AH, NB]), op=ALU.is_equal)
            yield e.tensor_copy(out=is63[:, :, s_], in_=OH[:, :, NB - 1])
            for u in range(2):
                yield e.tensor_tensor(out=OHt.ap, in0=OH.ap, in1=ptv[:, u:u + 1, :].to_broadcast([8, AH, NB]), op=ALU.mult)
                yield e.tensor_reduce(out=selp[:, :, s_, u], in_=OHt.ap, axis=AX.X, op=ALU.add)
        P.op("dve", f, reads=[gts.r, tp8.r, ptf.r], writes=[OH.r, OHt.r, selp.r, is63.r])
    P.op("dve", lambda e: e.tensor_tensor(out=Xd.ap, in0=eye8x.ap.rearrange("p (q x) -> p q x", q=8),
                                          in1=selp.ap.rearrange("p h s u -> p (h s u)").unsqueeze(1).to_broadcast([8, 8, AH * 6]), op=ALU.mult),
         reads=[eye8x.r, selp.r], writes=[Xd.r])
    pB = nps()
    P.op("pe", lambda e, pB=pB: e.matmul(pB[:, 0:NSEL], lhsT=ones8.ap, rhs=Xd.ap.rearrange("p q x -> p (q x)"), start=True, stop=True),
         reads=[ones8.r, Xd.r], writes=[pB.r])

    pBv2 = pB[:, 0:NSEL].rearrange("p (x u) -> p x u", u=2)

    def f(e):
        yield e.scalar_tensor_tensor(out=addc2[0:64, :], in0=pBv2[0:64, :, 0], scalar=float(64 * AH), in1=addc2[0:64, :], op0=ALU.mult, op1=ALU.add)
        yield e.scalar_tensor_tensor(out=addc2[64:128, :], in0=pBv2[64:128, :, 1], scalar=float(64 * AH), in1=addc2[64:128, :], op0=ALU.mult, op1=ALU.add)
        yield e.tensor_copy(out=idxs.ap, in_=addc2.ap)
    P.op("dve", f, reads=[pB.r, addc2.r], writes=[addc2.r, idxs.r])
    ckv_rows = cache_kv.rearrange("(r two) x -> r (two x)", two=2)
    def do_shead(h):
        for q in range(8):
            for s_ in range(3):
                un = q * 3 + s_
                col = (q * AH + h) * 3 + s_
                P.dma("pool", "kvsel", lambda e, un=un, col=col: e.indirect_dma_start(out=kvsel[:, un, :, :].rearrange("p e x -> p (e x)"), out_offset=None, in_=ckv_rows,
                                                                                    in_offset=bass.IndirectOffsetOnAxis(ap=idxs[:, col:col + 1], axis=0)),
                      reads=[idxs.r], writes=[kvsel.r.k(un)])
        for u0 in range(0, 48, 4):
            pT = nps()
            pTv = pT.ap.rearrange("p (a b) -> p a b", b=128)

            def f(e, pTv=pTv, u0=u0):
                ins = None
                for j in range(4):
                    ins = e.transpose(out=pTv[:, j, :], in_=kvsel[:, (u0 + j) // 2, (u0 + j) % 2, 0:128], identity=ident.ap)
                return ins
            P.op("pe", f, reads=[kvsel.r, ident.r], writes=[pT.r])
            evac_copy(alt_eng(), kselT[:, u0:u0 + 4, :], pTv[:, 0:4, :], [pT.r], [kselT.r])
        pS = nps()

        def f(e, pS=pS, h=h):
            ins = None
            for q in range(8):
                for su in range(6):
                    un = q * 6 + su
                    ins = e.matmul(pS[:, un:un + 1], lhsT=kselT[:, un, :], rhs=qTs[:, h, q:q + 1], start=True, stop=True)
            return ins
        P.op("pe", f, reads=[kselT.r, qTs.r], writes=[pS.r])
        evac_copy("dve", sTs.ap, pS[:, 0:48], [pS.r], [sTs.r])
        sTv = sTs.ap.rearrange("p (q x) -> p x q", x=6)
        pA, pBk = nps(), nps()
        pAv = pA.ap.rearrange("p (a b) -> p a b", b=128)
        pBv = pBk.ap.rearrange("p (a b) -> p a b", b=128)

        def f(e, pAv=pAv, pBv=pBv, pBk=pBk, h=h):
            for su in range(4):
                e.transpose(out=pAv[0:8, su, :], in_=sTv[:, su, :], identity=ident.ap)
            for su in range(4, 6):
                e.transpose(out=pBv[0:8, su - 4, :], in_=sTv[:, su, :], identity=ident.ap)
            return e.matmul(pBk[0:8, 256:264], lhsT=qTs[:, h, :], rhs=kTs[:, h, :], start=True, stop=True)
        P.op("pe", f, reads=[sTs.r, ident.r, qTs.r, kTs.r], writes=[pA.r, pBk.r])

        def f(e, pAv=pAv, pBv=pBv, pBk=pBk, h=h):
            for s_ in range(3):
                yield e.memset(tmpb[:, 0:128], 0.0)
                yield e.tensor_scalar(out=tmpb[:, 128:256], in0=T256s[:, h, :], scalar1=is63[:, h, s_:s_ + 1], scalar2=None, op0=ALU.mult)
                yield e.tensor_scalar(out=tmpb.ap, in0=tmpb.ap, scalar1=t31bc[0:8, h:h + 1], scalar2=None, op0=ALU.add)
                src = pAv[0:8, 2 * s_:2 * s_ + 2, :] if s_ < 2 else pBv[0:8, 0:2, :]
                yield e.tensor_tensor(out=Ls[:, s_ * 256:(s_ + 1) * 256].rearrange("p (a u b) -> p a u b", u=2, b=64), in0=src.rearrange("p a (u b) -> p a u b", u=2),
                                in1=tmpb.ap.rearrange("q (u p e) -> q e u p", u=2, e=2), op=ALU.add)
            yield e.tensor_tensor(out=Ls[:, 768:776], in0=pBk[0:8, 256:264], in1=T0s[:, h, :], op=ALU.add)
            yield e.reduce_max(out=sms[:, 0:1], in_=Ls[:, 0:776], axis=AX.X)
            yield e.tensor_scalar(out=sms[:, 1:2], in0=sms[:, 0:1], scalar1=-1.0, scalar2=None, op0=ALU.mult)
        P.op("dve", f, reads=[pA.r, pBk.r, T256s.r, is63.r, t31bc.r, T0s.r], writes=[tmpb.r, Ls.r, sms.r.k("m")])
        P.op("act", lambda e: e.activation(out=Ls[:, 0:776], in_=Ls[:, 0:776], func=AF.Exp, bias=sms[:, 1:2], scale=1.0, accum_out=sms[:, 2:3]),
             reads=[Ls.r, sms.r.k("m")], writes=[Ls.r, sms.r.k("s")])

        def f(e):
            yield e.reciprocal(out=sms[:, 3:4], in_=sms[:, 2:3])
            yield e.tensor_scalar(out=Ls[:, 0:776], in0=Ls[:, 0:776], scalar1=sms[:, 3:4], scalar2=None, op0=ALU.mult)
        P.op("dve", f, reads=[Ls.r, sms.r.k("s")], writes=[Ls.r, sms.r.k("r")])
        pP = nps()
        pPv = pP[:, 0:48].rearrange("p (a b) -> p a b", b=8)

        def f(e, pP=pP, pPv=pPv):
            for su in range(6):
                e.transpose(out=pPv[:, su, :], in_=Ls[:, su * 128:(su + 1) * 128], identity=ident[0:8, 0:8])
            return e.transpose(out=pP[0:8, 64:72], in_=Ls[:, 768:776], identity=ident[0:8, 0:8])
        P.op("pe", f, reads=[Ls.r, ident.r], writes=[pP.r])

        def f(e, pP=pP, pPv=pPv):
            yield e.tensor_copy(out=pTs.ap, in_=pPv)
            yield e.tensor_copy(out=pTo.ap, in_=pP[0:8, 64:72])
        P.op("dve", f, reads=[pP.r], writes=[pTs.r, pTo.r])
        pO = nps()

        def f(e, pO=pO, h=h):
            ins = None
            for q in range(8):
                e.matmul(pO[:, q:q + 1], lhsT=vsf[:, h, :], rhs=pTo[:, q:q + 1], start=True, stop=False)
                for su in range(6):
                    ins = e.matmul(pO[:, q:q + 1], lhsT=kvsel[:, q * 3 + su // 2, su % 2, 128:256], rhs=pTs[:, su, q:q + 1], start=False, stop=(su == 5))
            return ins
        P.op("pe", f, reads=[vsf.r, pTo.r, kvsel.r, pTs.r], writes=[pO.r])
        evac_copy("act", mixT[:, MH + h, 128 * NO:128 * NO + 8], pO[:, 0:8], [pO.r], [mixT.r.k(MH + h)])

    for h in range(AH):
        do_shead(h)

    if STOP == "p1c":
        P.emit()
        st.close()
        return nc, hc
    R1.barrier()
    R3.barrier()
    wslots.clear()
    h1 = R1.get([128, NOT, D], F32, "h1")
    otl = [(128 * i, 128) for i in range(NO)] + [(128 * NO, 8)]
    for i, (o0, r) in enumerate(otl):
        P.dma("sp", "h1l", lambda e, i=i, o0=o0, r=r: e.dma_start(out=h1[0:r, i, :], in_=x_all[TOK_O0 + o0:TOK_O0 + o0 + r, :]), writes=[h1.r.k(i)])
    for cg in range(D // 512):
        w = wload("o", 2, R3, [128, max(KM, KC), 512], [(0, w_out[:, cg * 512:(cg + 1) * 512])])
        for i, (o0, r) in enumerate(otl):
            p = nps()
            mm_tok(p, r, 512, mixT, o0, KM, w, 0, [mixT.r])
            P.op("dve", lambda e, p=p, i=i, r=r, cg=cg: e.tensor_tensor(out=h1[0:r, i, cg * 512:(cg + 1) * 512], in0=p[0:r, :], in1=h1[0:r, i, cg * 512:(cg + 1) * 512], op=ALU.add),
                 reads=[p.r, h1.r.k(i)], writes=[h1.r.k(i)])
    def norm_stage(grow):
        R3.barrier()
        wslots.clear()
        g = load_g(0, grow)
        xnb = R3.get([128, D], BF16, "xnb")
        junk3 = R3.get([128, D], BF16, "junk3")
        R2.barrier()
        xT = R2.get([128, KC, NOW], BF16, "xT")
        for i, (o0, r) in enumerate(otl):
            norm_tile(h1[0:r, i, :], h1.r.k(i), r, g, xnb[0:r, :], xnb.r, ssq, rstd, junk3)
            transpose_into(xnb, r, KC, xT, o0, xT.r)
        R3.barrier()
        wslots.clear()
        return xT
    xT2 = norm_stage(1)
    aT = R3.get([128, 4, NOW], BF16, "aT")
    rl = [R3.get([128, 512], F32, "rl") for _ in range(2)]
    ch_o = chunks(0, 128 * NO) + [(128 * NO, 8)]
    nrl = 0
    for fg in range(DFF // 512):
        wu = wload("u", 2, R3, [128, KC, 512], [(0, w_up[:, fg * 512:(fg + 1) * 512])])
        wd = wload("d", 2, R3, [128, 4, D], [(0, w_down[fg * 512:(fg + 1) * 512, :])])
        for f_ in range(4):
            for (a, n) in ch_o:
                p = nps()
                mm_feat(p, 128, n, wu, f_ * 128, KC, xT2, a, [xT2.r])
                rt = rl[nrl % 2]
                nrl += 1
                P.op("act", lambda e, p=p, n=n, rt=rt: e.activation(out=rt[:, 0:n], in_=p[:, 0:n], func=AF.Relu), reads=[p.r], writes=[rt.r])
                P.op("pool", lambda e, rt=rt, f_=f_, a=a, n=n: e.tensor_tensor(out=aT[:, f_, a:a + n], in0=rt[:, 0:n], in1=rt[:, 0:n], op=ALU.mult), reads=[rt.r], writes=[aT.r])
        for i, (o0, r) in enumerate(otl):
            for cg in range(D // 512):
                p = nps()

                def f(e, p=p, o0=o0, r=r, cg=cg, wd=wd):
                    ins = None
                    for f_ in range(4):
                        ins = e.matmul(p[0:r, :], lhsT=aT[:, f_, o0:o0 + r], rhs=wd[:, f_, cg * 512:(cg + 1) * 512], start=(f_ == 0), stop=(f_ == 3))
                    return ins
                P.op("pe", f, reads=[aT.r, wd.r], writes=[p.r])
                P.op("dve", lambda e, p=p, i=i, r=r, cg=cg: e.tensor_tensor(out=h1[0:r, i, cg * 512:(cg + 1) * 512], in0=p[0:r, :], in1=h1[0:r, i, cg * 512:(cg + 1) * 512], op=ALU.add),
                     reads=[p.r, h1.r.k(i)], writes=[h1.r.k(i)])
    xT3 = norm_stage(2)
    KP = PLE // 128
    pT_ = R3.get([128, KP, NOW], BF16, "pT_")
    pst = R3.get([128, PLE], F32, "pst")
    pbf = R3.get([128, PLE], BF16, "pbf")
    for i, (o0, r) in enumerate(otl):
        P.dma("sp", "pst", lambda e, o0=o0, r=r: e.dma_start(out=pst[0:r, :], in_=p_all[o0:o0 + r, :]), writes=[pst.r])
        P.op("dve", lambda e, r=r: e.tensor_copy(out=pbf[0:r, :], in_=pst[0:r, :]), reads=[pst.r], writes=[pbf.r])
        transpose_into(pbf, r, KP, pT_, o0, pT_.r)
    sg = [R3.get([128, 512], F32, "sg") for _ in range(2)]
    for cg in range(D // 512):
        wgt = wload("u", 2, R3, [128, KC, 512], [(0, w_pg[:, cg * 512:(cg + 1) * 512])])
        wpp = wload("p", 2, R3, [128, KP, 512], [(0, w_pp[:, cg * 512:(cg + 1) * 512])])
        for i, (o0, r) in enumerate(otl):
            p1, p2 = nps(), nps()
            mm_tok(p1, r, 512, xT3, o0, KC, wgt, 0, [xT3.r])
            mm_tok(p2, r, 512, pT_, o0, KP, wpp, 0, [pT_.r])
            s_ = sg[i % 2]
            P.op("act", lambda e, p1=p1, r=r, s_=s_: e.activation(out=s_[0:r, :], in_=p1[0:r, :], func=AF.Sigmoid), reads=[p1.r], writes=[s_.r])
            P.op("dve", lambda e, p2=p2, r=r, s_=s_: e.tensor_tensor(out=s_[0:r, :], in0=s_[0:r, :], in1=p2[0:r, :], op=ALU.mult), reads=[p2.r, s_.r], writes=[s_.r])
            P.op("pool", lambda e, i=i, r=r, cg=cg, s_=s_: e.tensor_tensor(out=h1[0:r, i, cg * 512:(cg + 1) * 512], in0=h1[0:r, i, cg * 512:(cg + 1) * 512], in1=s_[0:r, :], op=ALU.add),
                 reads=[s_.r, h1.r.k(i)], writes=[h1.r.k(i)])
    R3.barrier()
    wslots.clear()
    g = load_g(0, 3)
    junk3 = R3.get([128, D], BF16, "junk3")
    yst = [R3.get([128, D], F32, "yst") for _ in range(2)]
    for i, (o0, r) in enumerate(otl):
        y_ = yst[i % 2]
        norm_tile(h1[0:r, i, :], h1.r.k(i), r, g, y_[0:r, :], y_.r, ssq, rstd, junk3)
        P.dma("sp", f"yst{i % 2}", lambda e, y_=y_, o0=o0, r=r: e.dma_start(out=y_o[o0:o0 + r, :], in_=y_[0:r, :]), reads=[y_.r])
    P.emit()
    st.close()
    return nc, hc


def make_in_maps(c, inp, ncores):
    MH, AH = c.MH, c.AH
    half_len = 128 * c.NO
    hc = host_consts(c)
    f32 = np.float32
    shared = {
        "w_in": np.ascontiguousarray(inp["w_in"][0]), "w_out": np.ascontiguousarray(inp["w_out"][0]),
        "w_up": np.ascontiguousarray(inp["w_up"][0]), "w_down": np.ascontiguousarray(inp["w_down"][0]),
        "w_pg": np.ascontiguousarray(inp["w_ple_gate"][0]), "w_pp": np.ascontiguousarray(inp["w_ple_proj"][0]),
        "gvec": np.ascontiguousarray(np.stack([inp["g_mix"][0], inp["g_ffn"][0], inp["g_ple"][0], inp["g_final"]]).astype(f32)),
        "g_mh": np.ascontiguousarray(inp["g_mhead"][0].reshape(1, MH * 128)),
        "b_i": np.ascontiguousarray(inp["b_igate"][0].reshape(MH, 1)), "b_f": np.ascontiguousarray(inp["b_fgate"][0].reshape(MH, 1)),
        "relt": np.ascontiguousarray(np.concatenate([inp["rel_bias_table"], np.full((1, AH), -BIG, f32)], 0).astype(f32)),
        "cache_k": np.ascontiguousarray(inp["cache_k"][0]).reshape(c.NPHYS * 128, AH * 128),
        "cache_kv": np.ascontiguousarray(np.stack([inp["cache_k"][0], inp["cache_v"][0]], axis=3).transpose(0, 2, 1, 3, 4)).reshape(c.NPHYS * AH * 128, 256),
    }
    for k, v in hc.items():
        shared["c_" + k] = v
    maps = []
    for cid in range(ncores):
        b, half = cid // 2, cid % 2
        xp = inp["x_prompt"][b]
        own = xp[half * half_len:(half + 1) * half_len]
        pre = xp[0:half_len] if half == 1 else np.zeros_like(own)
        m = dict(shared)
        m["x_all"] = np.ascontiguousarray(np.concatenate([pre, own, inp["x_sample"][cid]], 0))
        m["p_all"] = np.ascontiguousarray(np.concatenate([inp["p_prompt"][0, b, half * half_len:(half + 1) * half_len], inp["p_sample"][0, cid]], 0))
        m["pt"] = np.ascontiguousarray(inp["page_table"][cid:cid + 1]).astype(np.int32)
        m["sC"] = np.ascontiguousarray(inp["state_C"][0, cid]).reshape(MH * 64, 128)
        m["sn"] = np.ascontiguousarray(inp["state_n"][0, cid]).reshape(MH * 64, 1)
        m["sm"] = np.ascontiguousarray(inp["state_m"][0, cid]).reshape(MH, 1)
        m["flag"] = np.full((1, 1), float(half), f32)
        maps.append(m)
    return maps


def assemble(c, res, ncores):
    MH, AH = c.MH, c.AH
    B = ncores // 2
    hl = 128 * c.NO
    S = 2 * hl
    D = c.D
    f32 = np.float32
    y_p = np.zeros((B, S, D), f32)
    y_s = np.zeros((ncores, 8, D), f32)
    k_p = np.zeros((1, B, S, AH, 128), f32)
    v_p = np.zeros((1, B, S, AH, 128), f32)
    C_p = np.zeros((1, B, MH, 64, 128), f32)
    n_p = np.zeros((1, B, MH, 64), f32)
    m_p = np.zeros((1, B, MH), f32)
    k_s = np.zeros((1, ncores, 8, AH, 128), f32)
    v_s = np.zeros((1, ncores, 8, AH, 128), f32)
    C_s = np.zeros((1, ncores, MH, 64, 128), f32)
    n_s = np.zeros((1, ncores, MH, 64), f32)
    m_s = np.zeros((1, ncores, MH), f32)
    for cid in range(ncores):
        r = res[cid]
        b, half = cid // 2, cid % 2
        y_p[b, half * hl:(half + 1) * hl] = r["y_o"][0:hl]
        y_s[cid] = r["y_o"][hl:hl + 8]
        k_p[0, b, half * hl:(half + 1) * hl] = r["k_o"][0:hl].reshape(hl, AH, 128)
        v_p[0, b, half * hl:(half + 1) * hl] = r["v_o"][0:hl].reshape(hl, AH, 128)
        k_s[0, cid] = r["k_o"][hl:hl + 8].reshape(8, AH, 128)
        v_s[0, cid] = r["v_o"][hl:hl + 8].reshape(8, AH, 128)
        if half == 1:
            C_p[0, b] = r["Cp_o"].reshape(MH, 64, 128)
            n_p[0, b] = r["np_o"].reshape(MH, 64)
            m_p[0, b] = r["mp_o"].reshape(MH)
        C_s[0, cid] = r["Cs_o"].reshape(MH, 64, 128)
        n_s[0, cid] = r["ns_o"].reshape(MH, 64)
        m_s[0, cid] = r["ms_o"].reshape(MH)
    return (y_p, y_s, k_p, v_p, C_p, n_p, m_p, k_s, v_s, C_s, n_s, m_s)


def kernel(**inputs):
    c = Cfg()
    ncores = 8
    inp = {k: np.asarray(v) for k, v in inputs.items()}
    nc, _ = build(c)
    maps = make_in_maps(c, inp, ncores)
    res = run_bass_kernel_spmd(nc, maps, core_ids=list(range(ncores)))
    return assemble(c, res.results, ncores)
```

```python
import numpy as np
import concourse.bass as bass
import concourse.mybir as mybir

F32 = mybir.dt.float32
BF16 = mybir.dt.bfloat16
I32 = mybir.dt.int32
AF = mybir.ActivationFunctionType
ALU = mybir.AluOpType
AX = mybir.AxisListType

ENGS = ("pe", "act", "dve", "pool", "sp")


class Res:
    __slots__ = ("name", "parent", "kids", "lw", "rd", "excl")

    def __init__(self, name, parent=None):
        self.excl = False
        self.name = name
        self.parent = parent
        self.kids = {}
        self.lw = None
        self.rd = []

    def k(self, key):
        r = self.kids.get(key)
        if r is None:
            r = Res(f"{self.name}.{key}", self)
            self.kids[key] = r
        return r


class Op:
    __slots__ = ("eng", "fn", "deps", "idx", "sig", "cnt", "chan", "isdma", "dw", "ny")

    def __init__(self, eng, fn):
        self.eng = eng
        self.fn = fn
        self.deps = set()
        self.dw = {}
        self.sig = False
        self.cnt = 0
        self.chan = None
        self.isdma = False


class Chan:
    def __init__(self, name):
        self.name = name
        self.res = Res("chan_" + name)
        self.n = 0
        self.sem = None


class Prog:
    def __init__(self, nc, same_engine_sync=True):
        self.nc = nc
        self.ops = []
        self.chans = {}
        self.same_engine_sync = same_engine_sync

    def chan(self, name):
        c = self.chans.get(name)
        if c is None:
            c = Chan(name)
            self.chans[name] = c
        return c

    def _desc(self, r, out):
        for kk in r.kids.values():
            out.append(kk)
            if kk.kids:
                self._desc(kk, out)

    def _related(self, r):
        out = [r]
        p = r.parent
        while p is not None:
            out.append(p)
            p = p.parent
        if r.kids:
            self._desc(r, out)
        return out

    def _add(self, op, reads, writes):
        for r in reads:
            if r.excl and r not in writes:
                writes = writes + [r]
        deps = op.deps
        for r in reads:
            for x in self._related(r):
                if x.lw is not None:
                    deps.add(x.lw)
        for w in writes:
            for x in self._related(w):
                if x.lw is not None:
                    deps.add(x.lw)
                for o in x.rd:
                    deps.add(o)
        for d in list(deps):
            if d.isdma:
                c = d.chan
                if op.dw.get(c, 0) < 16 * c.n:
                    op.dw[c] = 16 * c.n
                if not (op.isdma and op.chan is c):
                    c.res.rd.append(op)
                deps.discard(d)
        for r in reads:
            r.rd.append(op)
        for w in writes:
            w.lw = op
            w.rd = []
            if w.kids:
                dd = []
                self._desc(w, dd)
                for kk in dd:
                    kk.lw = op
                    kk.rd = []
        deps.discard(op)
        op.idx = len(self.ops)
        self.ops.append(op)

    frozen = False

    def op(self, eng, fn, reads=(), writes=()):
        if self.frozen:
            return None
        o = Op(eng, fn)
        self._add(o, list(reads), list(writes))
        return o

    def dma(self, eng, chan, fn, reads=(), writes=()):
        if self.frozen:
            return None
        c = self.chan(chan) if isinstance(chan, str) else chan
        o = Op(eng, fn)
        o.isdma = True
        o.chan = c
        assert getattr(c, "eng", eng) == eng
        c.eng = eng
        for x in c.res.rd:
            o.deps.add(x)
        c.res.rd = []
        self._add(o, list(reads), list(writes))
        c.n += 1
        o.cnt = 16 * c.n
        return o

    def emit(self, final_wait_eng="sp"):
        nc = self.nc
        ops = self.ops
        for o in ops:
            for d in o.deps:
                if d.isdma:
                    continue
                if d.eng == o.eng:
                    if o.eng == "pe" or not self.same_engine_sync:
                        continue
                d.sig = True
        import inspect

        class _FI:
            def then_inc(self, *a, **k):
                return self

        class _FE:
            def __getattr__(self, n):
                return lambda *a, **k: _FI()
        counts = {e: 0 for e in ENGS}
        for o in ops:
            o.ny = 0
            if o.isdma:
                continue
            if inspect.isgeneratorfunction(o.fn):
                o.ny = sum(1 for _ in o.fn(_FE()))
                counts[o.eng] += o.ny
                o.cnt = counts[o.eng]
            elif o.sig:
                counts[o.eng] += 1
                o.cnt = counts[o.eng]
        import contextlib
        with contextlib.ExitStack() as st:
            esem = {e: st.enter_context(nc.semaphore("s_" + e)) for e in ENGS if e != "sp"}
            for c in self.chans.values():
                c.sem = st.enter_context(nc.semaphore("c_" + c.name))
            per = {e: [o for o in ops if o.eng == e] for e in ENGS}

            block = st.enter_context(nc.Block())

            def run(eng_name, eng):
                waited = {}
                for o in per[eng_name]:
                    need = {}
                    for c, v in o.dw.items():
                        need[id(c.sem)] = (c.sem, v)
                    for d in o.deps:
                        if d.eng == o.eng and (o.eng == "pe" or not self.same_engine_sync):
                            continue
                        s, v = esem[d.eng], d.cnt
                        key = id(s)
                        if need.get(key, (None, 0))[1] < v:
                            need[key] = (s, v)
                    for key, (s, v) in need.items():
                        if waited.get(key, 0) < v:
                            eng.wait_ge(s, v)
                            waited[key] = v
                    if o.ny:
                        gen = o.fn(eng)
                        base = o.cnt - o.ny
                        for gi, cur in enumerate(gen):
                            cur.then_inc(esem[eng_name], 1)
                            if gi < o.ny - 1:
                                eng.wait_ge(esem[eng_name], base + gi + 1)
                                waited[id(esem[eng_name])] = base + gi + 1
                        continue
                    ins = o.fn(eng)
                    if o.isdma:
                        ins.then_inc(o.chan.sem, 16)
                    elif o.sig:
                        ins.then_inc(esem[o.eng], 1)
                if eng_name == final_wait_eng:
                    for c in self.chans.values():
                        if c.n:
                            eng.wait_ge(c.sem, 16 * c.n)

            @block.tensor
            def _(e):
                run("pe", e)

            @block.scalar
            def _(e):
                run("act", e)

            @block.vector
            def _(e):
                run("dve", e)

            @block.gpsimd
            def _(e):
                run("pool", e)

            @block.sync
            def _(e):
                run("sp", e)

import math
import contextlib
from concourse.bass_utils import run_bass_kernel_spmd

BIG = 30000.0
EPS = 1e-6


class Cfg:
    def __init__(s, D=2048, NP=8, NO=8, MH=8, AH=8, DFF=8192, PLE=256, NPAGES=128, NPHYS=1280):
        s.D, s.NP, s.NO, s.MH, s.AH, s.DFF, s.PLE, s.NPAGES, s.NPHYS = D, NP, NO, MH, AH, DFF, PLE, NPAGES, NPHYS
        s.KC = D // 128
        s.DK, s.DV, s.DH, s.TS = 64, 128, 128, 8
        s.c_mq = 0
        s.c_mk = MH * 64
        s.c_mv = 2 * MH * 64
        s.c_mo = s.c_mv + MH * 128
        s.c_mi = s.c_mo + MH * 128
        s.c_mf = s.c_mi + MH
        s.c_aq = s.c_mf + MH
        s.c_ak = s.c_aq + AH * 128
        s.c_av = s.c_ak + AH * 128
        s.DIN = s.c_av + AH * 128
        s.MIXW = MH * 128 + AH * 128
        s.KM = s.MIXW // 128
        s.NTOK = 128 * (NP + NO) + 8
        s.NOW = 128 * NO + 8
        s.NPB, s.NOB = NP // 2, NO // 2
        s.NBLK = s.NPB + s.NOB
        s.NB = NPAGES // 2
        s.OHW = 768
        s.NSEL = 8 * AH * 6


class V:
    def __init__(s, ap, r):
        s.ap, s.r = ap, r

    def __getitem__(s, k):
        return s.ap[k]


def t5_bucket_np(rel):
    n = np.maximum(rel, 0)
    nf = np.maximum(n, 1).astype(np.float32)
    large = 16 + (np.log(nf / np.float32(16)) / np.float32(math.log(128 / 16)) * np.float32(16)).astype(np.int32)
    large = np.minimum(large, 31)
    return np.where(n < 16, n, large)


def host_consts(c):
    k = {}
    k["ident"] = np.eye(128, dtype=np.float32)
    s_ = np.arange(128)
    k["maskle"] = (s_[:, None] <= s_[None, :]).astype(np.float32)
    rel = np.arange(c.OHW) - 255
    b = t5_bucket_np(rel)
    oh = np.zeros((33, c.OHW), np.float32)
    for i in range(c.OHW):
        if rel[i] < 0:
            oh[32, i] = 1.0
        else:
            oh[b[i], i] = 1.0
    k["ohlong"] = oh
    bs = np.full((c.NOB, 8), -BIG, np.float32)
    for v in range(c.NOB):
        bs[v, : c.NPB + v] = 0.0
    k["bstruct"] = bs.reshape(1, c.NOB * 8)
    pm = np.zeros((1, 8), np.float32)
    pm[0, : c.NPB] = 1.0
    k["prefmask"] = pm
    nhp = c.MH // 2
    sel = np.zeros((c.MH, nhp * 128), np.float32)
    for hp in range(nhp):
        for p in range(128):
            sel[2 * hp + p // 64, hp * 128 + p] = 1.0
    k["sel"] = sel
    addc = np.zeros((128, c.NSEL), np.float32)
    col = 0
    for q in range(8):
        for h in range(c.AH):
            for s in range(3):
                for u in range(2):
                    addc[:, col] = np.arange(128) * c.AH + h
                    col += 1
    k["addc"] = addc
    ew = np.zeros((128, 2 * c.NB - 1), np.float32)
    ew[:, c.NB - 1] = 1.0
    k["ewin"] = ew
    addc2 = np.zeros((128, c.NSEL // 2), np.float32)
    col = 0
    for q in range(8):
        for h in range(c.AH):
            for s in range(3):
                addc2[:, col] = h * 64 + (np.arange(128) % 64)
                col += 1
    k["addc2"] = addc2
    k["iotap"] = np.arange(128, dtype=np.float32).reshape(128, 1)
    k["iotab"] = np.broadcast_to(np.arange(c.NB, dtype=np.float32), (8, c.NB)).copy()
    e8 = np.zeros((8, 8, c.AH * 6), np.float32)
    for q in range(8):
        e8[q, q, :] = 1.0
    k["eye8x"] = e8.reshape(8, 8 * c.AH * 6)
    return k


def build(c):
    import os as _os
    STOP = _os.environ.get("MK_STOP", "")

    def chk(name):
        if STOP == name:
            P.frozen = True
    nc = bass.Bass("TRN2", target_bir_lowering=False)
    P = Prog(nc)
    st = contextlib.ExitStack()
    D, KC, NP, NO, MH, AH, DFF, PLE = c.D, c.KC, c.NP, c.NO, c.MH, c.AH, c.DFF, c.PLE
    NTOK, NOW, KM = c.NTOK, c.NOW, c.KM
    NT = NP + NO + 1
    NB, NPG, NSEL = c.NB, c.NPAGES, c.NSEL
    NOT = NO + 1

    def din(name, shape, dt=F32):
        return nc.dram_tensor(name, list(shape), dt, kind="ExternalInput").ap()

    def dout(name, shape, dt=F32):
        return nc.dram_tensor(name, list(shape), dt, kind="ExternalOutput").ap()

    hc = host_consts(c)
    x_all = din("x_all", [NTOK, D])
    p_all = din("p_all", [NOW, PLE])
    w_in = din("w_in", [D, c.DIN])
    w_out = din("w_out", [c.MIXW, D])
    w_up = din("w_up", [D, DFF])
    w_down = din("w_down", [DFF, D])
    w_pg = din("w_pg", [D, D])
    w_pp = din("w_pp", [PLE, D])
    gvec = din("gvec", [4, D])
    g_mh = din("g_mh", [1, MH * 128])
    b_i = din("b_i", [MH, 1])
    b_f = din("b_f", [MH, 1])
    relt = din("relt", [33, AH])
    cache_k = din("cache_k", [c.NPHYS * 128, AH * 128])
    cache_kv = din("cache_kv", [c.NPHYS * AH * 128, 256])
    pt = din("pt", [1, c.NPAGES], I32)
    sC = din("sC", [MH * 64, 128])
    sn = din("sn", [MH * 64, 1])
    sm = din("sm", [MH, 1])
    flag = din("flag", [1, 1])
    cin = {k: din("c_" + k, v.shape) for k, v in hc.items()}

    y_o = dout("y_o", [NOW, D])
    k_o = dout("k_o", [NOW, AH * 128])
    v_o = dout("v_o", [NOW, AH * 128])
    Cp_o = dout("Cp_o", [MH * 64, 128])
    np_o = dout("np_o", [MH * 64, 1])
    mp_o = dout("mp_o", [MH, 1])
    Cs_o = dout("Cs_o", [MH * 64, 128])
    ns_o = dout("ns_o", [MH * 64, 1])
    ms_o = dout("ms_o", [MH, 1])

    _cnt = [0]

    def sbt(shape, dt=F32, name=None):
        _cnt[0] += 1
        nm = name or f"t{_cnt[0]}"
        t = st.enter_context(nc.sbuf_tensor(nm, list(shape), dt))
        return V(t[:], Res(nm))

    class Region:
        def __init__(s, nbytes, name):
            s.t = st.enter_context(nc.sbuf_tensor(name, [128, nbytes // 4], F32))
            s.r = Res(name)
            s.off = 0
            s.n = nbytes
            s.name = name

        def reset(s):
            s.off = 0

        def barrier(s):
            P.op("pool", lambda e: e.memset(s.t[0:1, 0:1], 0.0), writes=[s.r])
            s.off = 0

        def get(s, shape, dt=F32, name="v"):
            esz = 4 if dt in (F32, I32) else 2
            per = int(np.prod(shape[1:])) * esz
            per4 = (per + 3) // 4
            assert s.off + per4 * 4 <= s.n, (s.name, name, s.off, per4 * 4, s.n)
            ap = s.t[:, s.off // 4: s.off // 4 + per4]
            s.off += per4 * 4
            if dt != F32:
                ap = ap.bitcast(dt)
            n = int(np.prod(shape[1:]))
            ap = ap[:, 0:n]
            if len(shape) == 3:
                ap = ap.rearrange("p (a b) -> p a b", b=shape[2])
            elif len(shape) == 4:
                ap = ap.rearrange("p (a b c) -> p a b c", b=shape[2], c=shape[3])
            ap = ap[0:shape[0]]
            _cnt[0] += 1
            return V(ap, s.r.k(f"{name}{_cnt[0]}"))

    R1B = max(KC * NTOK * 2, NOT * D * 4)
    R2B = max(KM, KC) * NOW * 2
    R1 = Region(R1B, "R1")
    R2 = Region(R2B, "R2")
    R3B = 88 * 1024
    R3 = Region(R3B, "R3")

    psb = []
    for i in range(8):
        t = st.enter_context(nc.psum_tensor(f"ps{i}", [128, 512], F32))
        psb.append(V(t[:], Res(f"ps{i}")))
        psb[-1].r.excl = True
    _psi = [0]
    reserved = set()

    def nps():
        while True:
            i = _psi[0] % 8
            _psi[0] += 1
            if i not in reserved:
                return psb[i]

    def psbf(p):
        return p.ap.bitcast(BF16)

    ident = sbt([128, 128], F32, "ident")
    identb = sbt([128, 128], BF16, "identb")
    maskle = sbt([128, 128], F32, "maskle")
    reltt = sbt([33, AH], F32, "reltt")
    bstruct = sbt([128, c.NOB * 8], F32, "bstruct")
    prefmask = sbt([128, 8], F32, "prefmask")
    flagc = sbt([128, 1], F32, "flagc")
    iotap = sbt([128, 1], F32, "iotap")
    iotab = sbt([8, c.NB], F32, "iotab")
    onesf = sbt([128, 8], F32, "onesf")
    onesb = sbt([128, 8], BF16, "onesb")
    bi_t = sbt([MH, 1], F32, "bi_t")
    bf_t = sbt([MH, 1], F32, "bf_t")
    nbf_t = sbt([MH, 1], F32, "nbf_t")
    sm_t = sbt([MH, 1], F32, "sm_t")
    t31bc = sbt([128, AH], F32, "t31bc")
    bbv = sbt([128, c.NOB * 8], F32, "bbv")
    qTs = sbt([128, AH, 8], BF16, "qTs")
    kTs = sbt([128, AH, 8], BF16, "kTs")
    T256s = sbt([8, AH, 128], F32, "T256s")
    T0s = sbt([8, AH, 8], F32, "T0s")
    epT = sbt([128, NT, MH], F32, "epT")
    flT = sbt([128, NT, MH], F32, "flT")
    wpb = sbt([128, MH // 2, NT + 1], F32, "wpb")
    mouts = sbt([MH, 2], F32, "mouts")

    def cload(dst, src, eng="sp"):
        P.dma(eng, "const", lambda e, d=dst, s_=src: e.dma_start(out=d.ap, in_=s_), writes=[dst.r])

    cload(ident, cin["ident"])
    cload(maskle, cin["maskle"])
    cload(reltt, relt)
    cload(bstruct, cin["bstruct"].partition_broadcast(128))
    cload(prefmask, cin["prefmask"].partition_broadcast(128))
    cload(flagc, flag.partition_broadcast(128))
    cload(iotap, cin["iotap"])
    cload(iotab, cin["iotab"])
    cload(bi_t, b_i)
    cload(bf_t, b_f)
    cload(sm_t, sm)
    P.op("dve", lambda e: e.tensor_copy(out=identb.ap, in_=ident.ap), reads=[ident.r], writes=[identb.r])
    P.op("pool", lambda e: e.memset(onesf.ap, 1.0), writes=[onesf.r])
    epsD = sbt([128, 1], F32, "epsD")
    epsV = sbt([128, 1], F32, "epsV")
    one1 = sbt([128, 1], F32, "one1")
    P.op("pool", lambda e: e.memset(epsD.ap, float(D * EPS)), writes=[epsD.r])
    P.op("pool", lambda e: e.memset(epsV.ap, float(128 * EPS)), writes=[epsV.r])
    P.op("pool", lambda e: e.memset(one1.ap, 1.0), writes=[one1.r])
    P.op("pool", lambda e: e.memset(onesb.ap, 1.0), writes=[onesb.r])
    P.op("dve", lambda e: e.tensor_scalar(out=nbf_t.ap, in0=bf_t.ap, scalar1=-1.0, scalar2=None, op0=ALU.mult),
         reads=[bf_t.r], writes=[nbf_t.r])
    fm1 = sbt([128, 1], F32, "fm1")
    P.op("dve", lambda e: e.tensor_scalar(out=fm1.ap, in0=flagc.ap, scalar1=-1.0, scalar2=BIG, op0=ALU.add, op1=ALU.mult),
         reads=[flagc.r], writes=[fm1.r])
    for v in range(c.NOB):
        P.op("dve", lambda e, v=v: e.scalar_tensor_tensor(out=bbv[:, v * 8:(v + 1) * 8], in0=prefmask.ap, scalar=fm1[:, 0:1],
                                                          in1=bstruct[:, v * 8:(v + 1) * 8], op0=ALU.mult, op1=ALU.add),
             reads=[prefmask.r, fm1.r, bstruct.r], writes=[bbv.r])

    def load_g(slot, row):
        g = R3.get([128, D], F32, "gbc")
        P.dma("sp", f"g{slot}", lambda e: e.dma_start(out=g.ap, in_=gvec[row:row + 1, :].partition_broadcast(128)), writes=[g.r])
        P.op("pool", lambda e: e.tensor_scalar(out=g.ap, in0=g.ap, scalar1=float(math.sqrt(D)), scalar2=None, op0=ALU.mult),
             reads=[g.r], writes=[g.r])
        return g

    def evac_copy(eng, out_ap, in_ap, rd, wr):
        if eng == "act":
            P.op("act", lambda e: e.copy(out=out_ap, in_=in_ap), reads=rd, writes=wr)
        else:
            P.op(eng, lambda e: e.tensor_copy(out=out_ap, in_=in_ap), reads=rd, writes=wr)

    _alt = [0]

    def alt_eng():
        _alt[0] += 1
        return "act" if _alt[0] % 2 else "dve"

    def norm_tile(src_ap, src_r, r, g, xn_ap, xn_r, ssq, rstd, junk):
        P.op("act", lambda e: e.activation(out=junk[0:r, :], in_=src_ap, func=AF.Square, accum_out=ssq[0:r, 0:1]),
             reads=[src_r], writes=[junk.r, ssq.r])
        P.op("act", lambda e: e.activation(out=rstd[0:r, 0:1], in_=ssq[0:r, 0:1], func=AF.Ln, bias=epsD[0:r, 0:1], scale=1.0), reads=[ssq.r, epsD.r], writes=[rstd.r])
        P.op("act", lambda e: e.activation(out=rstd[0:r, 0:1], in_=rstd[0:r, 0:1], func=AF.Exp, scale=-0.5), reads=[rstd.r], writes=[rstd.r])
        P.op("dve", lambda e: e.scalar_tensor_tensor(out=xn_ap, in0=src_ap, scalar=rstd[0:r, 0:1], in1=g[0:r, :],
                                                     op0=ALU.mult, op1=ALU.mult), reads=[src_r, rstd.r, g.r], writes=[xn_r])

    def transpose_into(xn, r, nch, dstT, tok0, dst_r):
        j0 = 0
        while j0 < nch:
            n = min(8, nch - j0)
            p = nps()
            pb = psbf(p).rearrange("p (a b) -> p a b", b=128)

            def f(e, j0=j0, n=n, pb=pb):
                ins = None
                for j in range(n):
                    ins = e.transpose(out=pb[:, j, 0:r], in_=xn[0:r, (j0 + j) * 128:(j0 + j + 1) * 128], identity=identb[0:r, 0:r])
                return ins
            P.op("pe", f, reads=[xn.r, identb.r], writes=[p.r])
            evac_copy(alt_eng(), dstT[:, j0:j0 + n, tok0:tok0 + r], pb[:, 0:n, 0:r], [p.r], [dst_r])
            j0 += n

    wslots = {}

    def wload(pool_name, nslots, region, shape, pieces):
        key = pool_name
        if key not in wslots:
            wslots[key] = [[region.get(shape, BF16, name=f"w{pool_name}{i}") for i in range(nslots)], 0]
        sl = wslots[key]
        w = sl[0][sl[1] % nslots]
        ch = f"w{pool_name}{sl[1] % nslots}"
        sl[1] += 1
        for pi_, (c0, src) in enumerate(pieces):
            ncol = src.shape[1]
            nk_ = src.shape[0] // 128
            P.dma("pool", ch, lambda e, c0=c0, src=src, ncol=ncol, nk_=nk_: e.dma_start(
                out=w[:, 0:nk_, c0:c0 + ncol], in_=src.rearrange("(k p) c -> p k c", p=128)), writes=[w.r.k(pi_)] if len(pieces) > 1 else [w.r])
        return w

    def mm_tok(p, r, ncol, xT, tok0, nk, w, c0, rd):
        def f(e):
            ins = None
            for k in range(nk):
                ins = e.matmul(p[0:r, 0:ncol], lhsT=xT[:, k, tok0:tok0 + r], rhs=w[:, k, c0:c0 + ncol], start=(k == 0), stop=(k == nk - 1))
            return ins
        P.op("pe", f, reads=rd + [w.r], writes=[p.r])

    def mm_feat(p, m, n, w, c0, nk, xT, tok0, rd):
        def f(e):
            ins = None
            for k in range(nk):
                ins = e.matmul(p[0:m, 0:n], lhsT=w[:, k, c0:c0 + m], rhs=xT[:, k, tok0:tok0 + n], start=(k == 0), stop=(k == nk - 1))
            return ins
        P.op("pe", f, reads=rd + [w.r], writes=[p.r])

    tiles = [(128 * t, 128) for t in range(NP + NO)] + [(128 * (NP + NO), 8)]
    own_tiles = list(range(NP, NP + NO + 1))
    def chunks(t0, t1):
        out = []
        a = t0
        while a < t1:
            n = min(512, t1 - a)
            out.append((a, n))
            a += n
        return out
    TOK_P0, TOK_O0, TOK_S0 = 0, 128 * NP, 128 * (NP + NO)
    ch_all = chunks(0, TOK_O0) + chunks(TOK_O0, TOK_S0) + [(TOK_S0, 8)]
    ch_own = chunks(TOK_O0, TOK_S0) + [(TOK_S0, 8)]

    xnT = R1.get([128, KC, NTOK], BF16, "xnT")
    xnT_k = [xnT.r.k(t) for t in range(NT)]
    vsf = R1.get([8, AH, 128], F32, "vsf")
    g0 = load_g(0, 0)
    xs_ = [R3.get([128, D], F32, "xs") for _ in range(2)]
    xn_ = [R3.get([128, D], BF16, "xn") for _ in range(2)]
    junk = R3.get([128, D], BF16, "junk")
    ssq = sbt([128, 2], F32, "ssq")
    rstd = sbt([128, 2], F32, "rstd")
    for t, (tok0, r) in enumerate(tiles):
        xs, xn = xs_[t % 2], xn_[t % 2]
        P.dma("sp", f"xs{t % 2}", lambda e, xs=xs, tok0=tok0, r=r: e.dma_start(out=xs[0:r, :], in_=x_all[tok0:tok0 + r, :]), writes=[xs.r])
        norm_tile(xs[0:r, :], xs.r, r, g0, xn[0:r, :], xn.r, ssq, rstd, junk)
        transpose_into(xn, r, KC, xnT, tok0, xnT_k[t])

    R3.barrier()
    if STOP == "p0":
        P.emit()
        st.close()
        return nc, hc
    selc = R3.get([MH, (MH // 2) * 128], F32, "selc")
    cload(selc, cin["sel"])
    wg = wload("g", 1, R3, [128, KC, 2 * MH], [(0, w_in[:, c.c_mi:c.c_mi + 2 * MH])])
    NTK = NTOK
    li = R3.get([MH, NTK], F32, "li")
    nb = R3.get([MH, NTK], F32, "nb")
    nb2 = R3.get([MH, NTK], F32, "nb2")
    G = R3.get([MH, NTK], F32, "G")
    ep = R3.get([MH, NTK], F32, "ep")
    fl = R3.get([MH, NTK], F32, "fl")
    for (a, n) in ch_all:
        pi, pf = nps(), nps()
        mm_feat(pi, MH, n, wg, 0, KC, xnT, a, [xnT.r])
        mm_feat(pf, MH, n, wg, MH, KC, xnT, a, [xnT.r])
        P.op("act", lambda e, pi=pi, a=a, n=n: e.activation(out=li[:, a:a + n], in_=pi[0:MH, 0:n], func=AF.Identity, bias=bi_t[:, 0:1], scale=1.0),
             reads=[pi.r, bi_t.r], writes=[li.r])
        P.op("act", lambda e, pf=pf, a=a, n=n: e.activation(out=nb[:, a:a + n], in_=pf[0:MH, 0:n], func=AF.Exp, bias=nbf_t[:, 0:1], scale=-1.0),
             reads=[pf.r, nbf_t.r], writes=[nb.r])
    P.op("act", lambda e: e.activation(out=nb.ap, in_=nb.ap, func=AF.Ln, bias=one1[0:MH, 0:1], scale=1.0), reads=[nb.r, one1.r], writes=[nb.r])
    seqs = [(TOK_P0, 128 * NP), (TOK_O0, 128 * NO), (TOK_S0, 8)]
    cur, oth = nb, nb2
    kk = 1
    maxlen = max(128 * NP, 128 * NO)
    while kk < maxlen:
        def f(e, cur=cur, oth=oth, kk=kk):
            for (a, n) in seqs:
                if kk < n:
                    yield e.tensor_copy(out=oth[:, a:a + kk], in_=cur[:, a:a + kk])
                    yield e.tensor_tensor(out=oth[:, a + kk:a + n], in0=cur[:, a + kk:a + n], in1=cur[:, a:a + n - kk], op=ALU.add)
                else:
                    yield e.tensor_copy(out=oth[:, a:a + n], in_=cur[:, a:a + n])
        P.op("dve", f, reads=[cur.r], writes=[oth.r])
        cur, oth = oth, cur
        kk *= 2
    NBt = cur
    P.op("dve", lambda e: e.tensor_tensor(out=G.ap, in0=li.ap, in1=NBt.ap, op=ALU.add), reads=[li.r, NBt.r], writes=[G.r])
    cm = sbt([MH, NT], F32, "cm")
    Rext = sbt([MH, NT + 3], F32, "Rext")
    negR = sbt([MH, NT + 3], F32, "negR")
    negRl = sbt([MH, NT + 3], F32, "negRl")
    wprev = sbt([MH, NT + 1], F32, "wprev")
    seq_tiles = [(0, NP), (NP, NO), (NP + NO, 1)]
    rofs = [0, NP + 1, NP + NO + 2]
    for si, (t0, ntl) in enumerate(seq_tiles):
        a, n = seqs[si]
        if n >= 128:
            P.op("dve", lambda e, t0=t0, ntl=ntl, a=a, n=n: e.tensor_reduce(out=cm[:, t0:t0 + ntl], in_=G[:, a:a + n].rearrange("p (c l) -> p c l", l=128),
                                                                    axis=AX.X, op=ALU.max), reads=[G.r], writes=[cm.r])
        else:
            P.op("dve", lambda e, t0=t0, a=a, n=n: e.tensor_reduce(out=cm[:, t0:t0 + 1], in_=G[:, a:a + n], axis=AX.X, op=ALU.max),
                 reads=[G.r], writes=[cm.r])
        ro = rofs[si]
        if si == 0:
            P.op("dve", lambda e, ro=ro: e.memset(Rext[:, ro:ro + 1], 0.0), writes=[Rext.r])
        elif si == 1:
            def f(e, ro=ro):
                yield e.tensor_tensor(out=Rext[:, ro:ro + 1], in0=Rext[:, ro - 1:ro], in1=NBt[:, TOK_O0 - 1:TOK_O0], op=ALU.subtract)
                yield e.tensor_scalar(out=Rext[:, ro:ro + 1], in0=Rext[:, ro:ro + 1], scalar1=flagc[0:MH, 0:1], scalar2=None, op0=ALU.mult)
            P.op("dve", f, reads=[Rext.r, NBt.r, flagc.r], writes=[Rext.r])
        else:
            P.op("dve", lambda e, ro=ro: e.tensor_copy(out=Rext[:, ro:ro + 1], in_=sm_t.ap), reads=[sm_t.r], writes=[Rext.r])

        def f(e, ro=ro, t0=t0, ntl=ntl):
            for j in range(ntl):
                yield e.tensor_tensor(out=Rext[:, ro + 1 + j:ro + 2 + j], in0=Rext[:, ro + j:ro + 1 + j], in1=cm[:, t0 + j:t0 + j + 1], op=ALU.max)
        P.op("dve", f, reads=[Rext.r, cm.r], writes=[Rext.r])
        P.op("dve", lambda e, ro=ro, t0=t0, ntl=ntl: e.tensor_tensor(out=wprev[:, t0:t0 + ntl], in0=Rext[:, ro:ro + ntl], in1=Rext[:, ro + 1:ro + 1 + ntl], op=ALU.subtract),
             reads=[Rext.r], writes=[wprev.r])
    P.op("act", lambda e: e.activation(out=wprev[:, 0:NT], in_=wprev[:, 0:NT], func=AF.Exp), reads=[wprev.r], writes=[wprev.r])
    P.op("dve", lambda e: e.tensor_scalar(out=negR.ap, in0=Rext.ap, scalar1=-1.0, scalar2=None, op0=ALU.mult), reads=[Rext.r], writes=[negR.r])
    P.op("dve", lambda e: e.tensor_scalar(out=negRl.ap, in0=Rext.ap, scalar1=-1.0, scalar2=float(math.log(0.125)), op0=ALU.mult, op1=ALU.add),
         reads=[Rext.r], writes=[negRl.r])
    def f(e):
        yield e.tensor_tensor(out=mouts[:, 0:1], in0=Rext[:, rofs[1] + NO:rofs[1] + NO + 1], in1=NBt[:, TOK_S0 - 1:TOK_S0], op=ALU.subtract)
        yield e.tensor_tensor(out=mouts[:, 1:2], in0=Rext[:, rofs[2] + 1:rofs[2] + 2], in1=NBt[:, TOK_S0 + 7:TOK_S0 + 8], op=ALU.subtract)
    P.op("dve", f, reads=[Rext.r, NBt.r], writes=[mouts.r])
    P.dma("sp", "mo", lambda e: e.dma_start(out=mp_o, in_=mouts[:, 0:1]), reads=[mouts.r])
    P.dma("sp", "mo", lambda e: e.dma_start(out=ms_o, in_=mouts[:, 1:2]), reads=[mouts.r])
    for si, (t0, ntl) in enumerate(seq_tiles):
        ro = rofs[si]
        for j in range(ntl):
            tok0, r = tiles[t0 + j]
            P.op("act", lambda e, tok0=tok0, r=r, ro=ro, j=j: e.activation(out=ep[:, tok0:tok0 + r], in_=G[:, tok0:tok0 + r], func=AF.Exp,
                                                                        bias=negRl[:, ro + 1 + j:ro + 2 + j], scale=1.0), reads=[G.r, negRl.r], writes=[ep.r])
            P.op("act", lambda e, tok0=tok0, r=r, ro=ro, j=j: e.activation(out=fl[:, tok0:tok0 + r], in_=NBt[:, tok0:tok0 + r], func=AF.Exp,
                                                                        bias=negR[:, ro + 1 + j:ro + 2 + j], scale=1.0), reads=[NBt.r, negR.r], writes=[fl.r])
    for (src, dst) in ((ep, epT), (fl, flT)):
        p = nps()
        pv = p[:, 0:NT * MH].rearrange("p (t h) -> p t h", h=MH)

        def f(e, src=src, pv=pv):
            ins = None
            for t, (tok0, r) in enumerate(tiles):
                ins = e.transpose(out=pv[0:r, t, :], in_=src[:, tok0:tok0 + r], identity=ident[0:MH, 0:MH])
            return ins
        P.op("pe", f, reads=[src.r, ident.r], writes=[p.r])
        P.op("dve", lambda e, dst=dst, pv=pv: e.tensor_copy(out=dst[:, 0:NT - 1, :], in_=pv[:, 0:NT - 1, :]), reads=[p.r], writes=[dst.r])
        P.op("dve", lambda e, dst=dst, pv=pv: e.tensor_copy(out=dst[0:8, NT - 1, :], in_=pv[0:8, NT - 1, :]), reads=[p.r], writes=[dst.r])
    for hp in range(MH // 2):
        p = nps()
        P.op("pe", lambda e, p=p, hp=hp: e.matmul(p[:, 0:NT], lhsT=selc[:, hp * 128:(hp + 1) * 128], rhs=wprev[:, 0:NT], start=True, stop=True),
             reads=[selc.r, wprev.r], writes=[p.r])
        evac_copy("dve", wpb[:, hp, 0:NT], p[:, 0:NT], [p.r], [wpb.r])
    P.op("pool", lambda e: e.memset(wpb[:, :, NT:NT + 1], 1.0), writes=[wpb.r])

    if STOP == "gates":
        P.emit()
        st.close()
        return nc, hc
    mixT = R2.get([128, KM, NOW], BF16, "mixT")
    R3.barrier()
    wslots.clear()
    gmh = R3.get([128, MH * 128], F32, "gmh")
    P.dma("sp", "gmh", lambda e: e.dma_start(out=gmh.ap, in_=g_mh.partition_broadcast(128)), writes=[gmh.r])
    P.op("pool", lambda e: e.tensor_scalar(out=gmh.ap, in0=gmh.ap, scalar1=float(math.sqrt(128.0)), scalar2=None, op0=ALU.mult),
         reads=[gmh.r], writes=[gmh.r])
    qTm = R3.get([128, NOW], BF16, "qTm")
    kTz = R3.get([128, 2, NOW], BF16, "kTz")
    Kt = R3.get([128, NT, 128], BF16, "Kt")
    Va = R3.get([128, NT, 2, 129], BF16, "Va")
    gsig = R3.get([128, NOT, 256], F32, "gsig")
    Cf = R3.get([128, 129], F32, "Cf")
    Cbz = R3.get([128, 2, 129], BF16, "Cbz")
    StT = [R3.get([128, 2, 128], BF16, "StT") for _ in range(2)]
    omb = [R3.get([128, 256], BF16, "omb") for _ in range(2)]
    sml = [R3.get([128, 16], F32, "sml") for _ in range(2)]
    junk2 = R3.get([128, 128], F32, "junk2")
    Cin = R3.get([128, 129], F32, "Cin")
    P.op("pool", lambda e: e.memset(Va[:, :, :, 128:129], 1.0), writes=[Va.r])
    P.op("pool", lambda e: e.memset(kTz.ap, 0.0), writes=[kTz.r])
    P.op("pool", lambda e: e.memset(Cbz.ap, 0.0), writes=[Cbz.r])
    own_off = lambda t: 128 * (t - NP)

    ptb = sbt([128, NPG], I32, "ptb")
    ptf = sbt([128, NPG], F32, "ptf")
    idxp = sbt([128, NPG], I32, "idxp")
    kmsh = sbt([128, AH * NB], BF16, "kmsh")
    kmsl = sbt([128, AH * NB], BF16, "kmsl")
    kmst = R3.get([128, AH * NB], F32, "kmst")
    kpg = [R3.get([128, AH * 128], F32, "kpg") for _ in range(2)]
    kpb = [R3.get([128, AH * 128], BF16, "kpb") for _ in range(2)]
    kmrow = R3.get([NB, AH * 128], F32, "kmrow")
    ewf = R3.get([128, 2 * NB - 1], F32, "ewf")
    ewb = R3.get([128, 2 * NB - 1], BF16, "ewb")
    cload(ewf, cin["ewin"])

    def pass1():
        P.dma("sp", "ptl", lambda e: e.dma_start(out=ptb.ap, in_=pt.partition_broadcast(128)), writes=[ptb.r])

        def f(e):
            yield e.tensor_copy(out=ptf.ap, in_=ptb.ap)
            yield e.tensor_scalar(out=ptf.ap, in0=ptf.ap, scalar1=128.0, scalar2=iotap[:, 0:1], op0=ALU.mult, op1=ALU.add)
            yield e.tensor_copy(out=idxp.ap, in_=ptf.ap)
        P.op("dve", f, reads=[ptb.r, iotap.r], writes=[ptf.r, idxp.r])
        P.op("dve", lambda e: e.tensor_copy(out=ptf.ap, in_=ptb.ap), reads=[ptb.r, idxp.r], writes=[ptf.r])
        pKa, pKb = nps(), nps()
        rs_i = [psb.index(pKa), psb.index(pKb)]
        reserved.update(rs_i)
        HW_ = AH * 128
        H1 = min(512, HW_)
        def page_dma(j):
            kp = kpg[j % 2]
            P.dma("pool", f"kpg{j % 2}", lambda e, kp=kp, j=j: e.indirect_dma_start(out=kp.ap, out_offset=None, in_=cache_k,
                                                                        in_offset=bass.IndirectOffsetOnAxis(ap=idxp[:, j:j + 1], axis=0)),
                  reads=[idxp.r], writes=[kp.r])
        P.op("dve", lambda e: e.tensor_copy(out=ewb.ap, in_=ewf.ap), reads=[ewf.r], writes=[ewb.r])
        page_dma(0)
        for j in range(NPG):
            kp = kpg[j % 2]
            kb = kpb[j % 2]
            if j + 1 < NPG:
                page_dma(j + 1)
            if j % 2 == 0:
                P.op("act", lambda e, kp=kp, kb=kb: e.copy(out=kb.ap, in_=kp.ap), reads=[kp.r], writes=[kb.r])
            else:
                P.op("dve", lambda e, kp=kp, kb=kb: e.tensor_copy(out=kb.ap, in_=kp.ap), reads=[kp.r], writes=[kb.r])
            b_ = j // 2

            def f(e, kb=kb, j=j, b_=b_):
                ins = e.matmul(pKa[0:NB, 0:H1], lhsT=ewb[:, NB - 1 - b_:2 * NB - 1 - b_], rhs=kb[:, 0:H1], start=(j == 0), stop=(j == NPG - 1))
                if HW_ > 512:
                    ins = e.matmul(pKb[0:NB, 0:HW_ - 512], lhsT=ewb[:, NB - 1 - b_:2 * NB - 1 - b_], rhs=kb[:, 512:HW_], start=(j == 0), stop=(j == NPG - 1))
                return ins
            P.op("pe", f, reads=[kb.r, ewb.r], writes=[pKa.r, pKb.r])
            yield

        P.op("dve", lambda e: e.tensor_copy(out=kmrow[:, 0:H1], in_=pKa[0:NB, 0:H1]), reads=[pKa.r], writes=[kmrow.r])
        if HW_ > 512:
            P.op("act", lambda e: e.copy(out=kmrow[:, 512:HW_], in_=pKb[0:NB, 0:HW_ - 512]), reads=[pKb.r], writes=[kmrow.r])
        pKt = nps()

        def f(e):
            ins = None
            for h in range(AH):
                ins = e.transpose(out=pKt[:, h * NB:(h + 1) * NB], in_=kmrow[:, 128 * h:128 * h + 128], identity=ident[0:NB, 0:NB])
            return ins
        P.op("pe", f, reads=[kmrow.r, ident.r], writes=[pKt.r])

        def f(e):
            yield e.tensor_scalar(out=kmst.ap, in0=pKt[:, 0:AH * NB], scalar1=1.0 / 256.0, scalar2=None, op0=ALU.mult)
            yield e.tensor_copy(out=kmsh.ap, in_=kmst.ap)
            yield e.tensor_tensor(out=kmst.ap, in0=kmst.ap, in1=kmsh.ap, op=ALU.subtract)
            yield e.tensor_copy(out=kmsl.ap, in_=kmst.ap)
        P.op("dve", f, reads=[pKt.r], writes=[kmst.r, kmsh.r, kmsl.r])
        for x_ in rs_i:
            reserved.discard(x_)

    p1 = pass1()

    _p1c = [0]

    def p1step(n=1):
        if n == 1:
            _p1c[0] += 1
            if _p1c[0] % 2:
                return
        for _ in range(n):
            try:
                next(p1)
            except StopIteration:
                return

    def do_pair(hp):
        wA = wload("m", 2, R3, [128, KC, 256], [(0, w_in[:, c.c_mq + 128 * hp:c.c_mq + 128 * hp + 128]),
                                                 (128, w_in[:, c.c_mk + 128 * hp:c.c_mk + 128 * hp + 128])])
        for (a, n) in ch_own:
            p = nps()
            mm_feat(p, 128, n, wA, 0, KC, xnT, a, [xnT.r])
            evac_copy(alt_eng(), qTm[:, a - TOK_O0:a - TOK_O0 + n], p[:, 0:n], [p.r], [qTm.r])
            p = nps()
            mm_feat(p, 128, n, wA, 128, KC, xnT, a, [xnT.r])
            evac_copy("act", kTz[0:64, 0, a - TOK_O0:a - TOK_O0 + n], p[0:64, 0:n], [p.r], [kTz.r])
            evac_copy("dve", kTz[64:128, 1, a - TOK_O0:a - TOK_O0 + n], p[64:128, 0:n], [p.r], [kTz.r])
        chk("a1")
        for t, (tok0, r) in enumerate(tiles):
            p = nps()
            mm_tok(p, r, 128, xnT, tok0, KC, wA, 128, [xnT_k[t]])
            p1step()
            P.op("dve", lambda e, p=p, t=t, r=r, hp=hp: e.tensor_tensor(
                out=Kt[0:r, t, :].rearrange("p (j d) -> p j d", d=64), in0=p[0:r, 0:128].rearrange("p (j d) -> p j d", d=64),
                in1=epT[0:r, t, 2 * hp:2 * hp + 2].unsqueeze(2).to_broadcast([r, 2, 64]), op=ALU.mult),
                reads=[p.r, epT.r], writes=[Kt.r])
        chk("a2")
        wB = wload("m", 2, R3, [128, KC, 256], [(0, w_in[:, c.c_mv + 256 * hp:c.c_mv + 256 * hp + 256])])
        for t, (tok0, r) in enumerate(tiles):
            p = nps()
            mm_tok(p, r, 256, xnT, tok0, KC, wB, 0, [xnT_k[t]])
            p1step()
            evac_copy(alt_eng(), Va[0:r, t, :, 0:128], p[0:r, 0:256].rearrange("p (j d) -> p j d", d=128), [p.r], [Va.r])
        chk("a3")
        wC = wload("m", 2, R3, [128, KC, 256], [(0, w_in[:, c.c_mo + 256 * hp:c.c_mo + 256 * hp + 256])])
        for t in own_tiles:
            tok0, r = tiles[t]
            p = nps()
            mm_tok(p, r, 256, xnT, tok0, KC, wC, 0, [xnT_k[t]])
            P.op("act", lambda e, p=p, t=t, r=r: e.activation(out=gsig[0:r, t - NP, :], in_=p[0:r, 0:256], func=AF.Sigmoid), reads=[p.r], writes=[gsig.r])
            P.op("pool", lambda e, t=t, r=r, hp=hp: e.tensor_tensor(out=gsig[0:r, t - NP, :], in0=gsig[0:r, t - NP, :], in1=gmh[0:r, 256 * hp:256 * hp + 256], op=ALU.mult),
                 reads=[gsig.r, gmh.r], writes=[gsig.r])
            p1step()

        chk("a4")

        def state_update(t, r, last):
            p = nps()

            def f(e, p=p, t=t, r=r):
                ins = None
                for j in range(2):
                    ins = e.matmul(p[64 * j:64 * j + 64, 0:129], lhsT=Kt[0:r, t, 64 * j:64 * j + 64], rhs=Va[0:r, t, j, :], start=True, stop=True)
                return ins
            P.op("pe", f, reads=[Kt.r, Va.r], writes=[p.r])
            P.op("dve", lambda e, p=p, t=t: e.scalar_tensor_tensor(out=Cf.ap, in0=Cf.ap, scalar=wpb[:, hp, t:t + 1], in1=p[:, 0:129], op0=ALU.mult, op1=ALU.add),
                 reads=[Cf.r, wpb.r, p.r], writes=[Cf.r])
            if not last:
                P.op("act", lambda e, t=t: e.activation(out=Cbz[0:64, 0, :], in_=Cf[0:64, :], func=AF.Identity, scale=wpb[0:64, hp, t + 1:t + 2]), reads=[Cf.r, wpb.r], writes=[Cbz.r])
                P.op("act", lambda e, t=t: e.activation(out=Cbz[64:128, 1, :], in_=Cf[64:128, :], func=AF.Identity, scale=wpb[64:128, hp, t + 1:t + 2]), reads=[Cf.r, wpb.r], writes=[Cbz.r])

        def chunk(t, r, ci):
            tok0 = tiles[t][0]
            oo = own_off(t)
            S, om, sm_ = StT[ci % 2], omb[ci % 2], sml[ci % 2]
            pS = nps()

            def f(e, pS=pS):
                ins = None
                for j in range(2):
                    ins = e.matmul(pS[0:r, j * 128:j * 128 + r], lhsT=kTz[:, j, oo:oo + r], rhs=qTm[:, oo:oo + r], start=True, stop=True)
                return ins
            P.op("pe", f, reads=[kTz.r, qTm.r], writes=[pS.r])
            chk("c1")
            for j in range(2):
                P.op("dve", lambda e, j=j, pS=pS, S=S: e.scalar_tensor_tensor(out=S[0:r, j, 0:r], in0=pS[0:r, j * 128:j * 128 + r], scalar=epT[0:r, t, 2 * hp + j:2 * hp + j + 1],
                                                                          in1=maskle[0:r, 0:r], op0=ALU.mult, op1=ALU.mult), reads=[pS.r, epT.r, maskle.r], writes=[S.r])
            chk("c2")
            pX = nps()
            pXv = pX[:, 0:512].rearrange("p (j d) -> p j d", d=256)

            def f(e, pXv=pXv, S=S):
                ins = None
                for j in range(2):
                    e.matmul(pXv[0:r, j, 0:129], lhsT=qTm[:, oo:oo + r], rhs=Cbz[:, j, :], start=True, stop=False)
                    ins = e.matmul(pXv[0:r, j, 0:129], lhsT=S[0:r, j, 0:r], rhs=Va[0:r, t, j, :], start=False, stop=True)
                return ins
            P.op("pe", f, reads=[qTm.r, Cbz.r, S.r, Va.r], writes=[pX.r])
            chk("c3")
            def f(e, pXv=pXv, sm_=sm_):
                yield e.tensor_scalar(out=sm_[0:r, 0:2], in0=pXv[0:r, :, 128], scalar1=-1.0, scalar2=None, op0=ALU.mult)
                yield e.tensor_tensor(out=sm_[0:r, 0:2], in0=sm_[0:r, 0:2], in1=pXv[0:r, :, 128], op=ALU.max)
                yield e.tensor_tensor(out=sm_[0:r, 0:2], in0=sm_[0:r, 0:2], in1=flT[0:r, t, 2 * hp:2 * hp + 2], op=ALU.max)
                yield e.reciprocal(out=sm_[0:r, 2:4], in_=sm_[0:r, 0:2])
            P.op("dve", f, reads=[pX.r, flT.r], writes=[sm_.r.k("a")])
            chk("c4")
            for j in range(2):
                P.op("act", lambda e, j=j, pXv=pXv, sm_=sm_: e.activation(out=junk2[0:r, :], in_=pXv[0:r, j, 0:128], func=AF.Square, scale=sm_[0:r, 2 + j:3 + j],
                                                                     accum_out=sm_[0:r, 4 + j:5 + j]), reads=[pX.r, sm_.r.k("a")], writes=[junk2.r, sm_.r.k(f"b{j}")])

            chk("c5")
            P.op("act", lambda e, sm_=sm_: e.activation(out=sm_[0:r, 6:8], in_=sm_[0:r, 4:6], func=AF.Ln, bias=epsV[0:r, 0:1], scale=1.0),
                 reads=[sm_.r.k("b0"), sm_.r.k("b1"), epsV.r], writes=[sm_.r.k("c0")])
            P.op("act", lambda e, sm_=sm_: e.activation(out=sm_[0:r, 6:8], in_=sm_[0:r, 6:8], func=AF.Exp, scale=-0.5),
                 reads=[sm_.r.k("c0")], writes=[sm_.r.k("c0")])
            P.op("dve", lambda e, sm_=sm_: e.tensor_tensor(out=sm_[0:r, 8:10], in0=sm_[0:r, 6:8], in1=sm_[0:r, 2:4], op=ALU.mult),
                 reads=[sm_.r.k("a"), sm_.r.k("c0")], writes=[sm_.r.k("c")])
            for j in range(2):
                P.op("dve", lambda e, j=j, pXv=pXv, sm_=sm_, om=om: e.scalar_tensor_tensor(out=om[0:r, j * 128:(j + 1) * 128], in0=pXv[0:r, j, 0:128], scalar=sm_[0:r, 8 + j:9 + j],
                                                                                in1=gsig[0:r, t - NP, j * 128:(j + 1) * 128], op0=ALU.mult, op1=ALU.mult),
                     reads=[pX.r, sm_.r.k("c"), gsig.r], writes=[om.r])
            chk("c7")
            pT = nps()
            pTb = psbf(pT).rearrange("p (a b) -> p a b", b=128)

            def f(e, pTb=pTb, om=om):
                ins = None
                for j in range(2):
                    ins = e.transpose(out=pTb[:, j, 0:r], in_=om[0:r, j * 128:(j + 1) * 128], identity=identb[0:r, 0:r])
                return ins
            P.op("pe", f, reads=[om.r, identb.r], writes=[pT.r])
            evac_copy("act", mixT[:, 2 * hp:2 * hp + 2, oo:oo + r], pTb[:, 0:2, 0:r], [pT.r], [mixT.r.k(2 * hp)])

        P.op("pool", lambda e: e.memset(Cf.ap, 0.0), writes=[Cf.r])
        for t in range(NP):
            state_update(t, 128, True)
            p1step()
        chk("a5")
        P.op("dve", lambda e: e.tensor_scalar(out=Cf.ap, in0=Cf.ap, scalar1=flagc[:, 0:1], scalar2=None, op0=ALU.mult), reads=[Cf.r, flagc.r], writes=[Cf.r])
        P.op("act", lambda e: e.activation(out=Cbz[0:64, 0, :], in_=Cf[0:64, :], func=AF.Identity, scale=wpb[0:64, hp, NP:NP + 1]), reads=[Cf.r, wpb.r], writes=[Cbz.r])
        P.op("act", lambda e: e.activation(out=Cbz[64:128, 1, :], in_=Cf[64:128, :], func=AF.Identity, scale=wpb[64:128, hp, NP:NP + 1]), reads=[Cf.r, wpb.r], writes=[Cbz.r])
        chk("a6")
        for i in range(NO):
            t = NP + i
            chunk(t, 128, i)
            if i == 0:
                chk("a7")
            state_update(t, 128, i == NO - 1)
            p1step()
            if i == 0:
                chk("a8")
        chk("a9")
        for j in range(2):
            h = 2 * hp + j
            P.dma("sp", "co", lambda e, j=j, h=h: e.dma_start(out=Cp_o[64 * h:64 * h + 64, :], in_=Cf[64 * j:64 * j + 64, 0:128]), reads=[Cf.r])
            P.dma("sp", "co", lambda e, j=j, h=h: e.dma_start(out=np_o[64 * h:64 * h + 64, :], in_=Cf[64 * j:64 * j + 64, 128:129]), reads=[Cf.r])
        chk("a10")
        P.dma("sp", "cin", lambda e: e.dma_start(out=Cin[:, 0:128], in_=sC[128 * hp:128 * hp + 128, :]), writes=[Cin.r])
        P.dma("sp", "cin", lambda e: e.dma_start(out=Cin[:, 128:129], in_=sn[128 * hp:128 * hp + 128, :]), writes=[Cin.r])
        P.op("dve", lambda e: e.tensor_copy(out=Cf.ap, in_=Cin.ap), reads=[Cin.r], writes=[Cf.r])
        ts_ = NP + NO
        P.op("act", lambda e: e.activation(out=Cbz[0:64, 0, :], in_=Cf[0:64, :], func=AF.Identity, scale=wpb[0:64, hp, ts_:ts_ + 1]), reads=[Cf.r, wpb.r], writes=[Cbz.r])
        P.op("act", lambda e: e.activation(out=Cbz[64:128, 1, :], in_=Cf[64:128, :], func=AF.Identity, scale=wpb[64:128, hp, ts_:ts_ + 1]), reads=[Cf.r, wpb.r], writes=[Cbz.r])
        chunk(ts_, 8, 0)
        state_update(ts_, 8, True)
        for j in range(2):
            h = 2 * hp + j
            P.dma("sp", "co", lambda e, j=j, h=h: e.dma_start(out=Cs_o[64 * h:64 * h + 64, :], in_=Cf[64 * j:64 * j + 64, 0:128]), reads=[Cf.r])
            P.dma("sp", "co", lambda e, j=j, h=h: e.dma_start(out=ns_o[64 * h:64 * h + 64, :], in_=Cf[64 * j:64 * j + 64, 128:129]), reads=[Cf.r])

    for hp in range(MH // 2):
        do_pair(hp)
    p1step(10 ** 6)

    if STOP == "p1a":
        P.emit()
        st.close()
        return nc, hc
    R3.barrier()
    wslots.clear()
    ohl = R3.get([33, c.OHW], F32, "ohl")
    cload(ohl, cin["ohlong"])
    Tb = {d: R3.get([128, AH, 256], F32, f"Tb{d}") for d in (0, 128, 256)}
    ohb = R3.get([33, c.OHW], BF16, "ohb")
    r2 = sbt([33, 2 * AH], BF16, "r2")
    rtmp = sbt([33, AH], F32, "rtmp")
    P.op("dve", lambda e: e.tensor_copy(out=ohb.ap, in_=ohl.ap), reads=[ohl.r], writes=[ohb.r])

    def f(e):
        yield e.tensor_copy(out=r2[:, 0:AH], in_=reltt.ap)
        yield e.tensor_tensor(out=rtmp.ap, in0=reltt.ap, in1=r2[:, 0:AH], op=ALU.subtract)
        yield e.tensor_copy(out=r2[:, AH:2 * AH], in_=rtmp.ap)
    P.op("dve", f, reads=[reltt.r], writes=[r2.r, rtmp.r])
    p = nps()
    P.op("pe", lambda e, p=p: e.matmul(p[:, 0:2 * AH], lhsT=ohb[:, 600:728], rhs=r2.ap, start=True, stop=True), reads=[ohb.r, r2.r], writes=[p.r])

    def f(e, p=p):
        yield e.tensor_copy(out=t31bc.ap, in_=p[:, 0:AH])
        yield e.tensor_tensor(out=t31bc.ap, in0=t31bc.ap, in1=p[:, AH:2 * AH], op=ALU.add)
    P.op("dve", f, reads=[p.r], writes=[t31bc.r])
    P.op("pool", lambda e: e.memset(Tb[0][:, :, 128:256], -BIG), writes=[Tb[0].r])
    P.op("dve", lambda e: e.tensor_copy(out=Tb[256][:, :, 0:128], in_=t31bc.ap.unsqueeze(2).to_broadcast([128, AH, 128])), reads=[t31bc.r], writes=[Tb[256].r])
    for d in (0, 128, 256):
        for k0 in range(0, 256, 32):
            if (d == 0 and k0 >= 128) or (d == 256 and k0 < 128):
                continue
            p = nps()
            pv = p[:, 0:32 * 2 * AH].rearrange("p (k h) -> p k h", h=2 * AH)

            def f(e, d=d, k0=k0, pv=pv):
                ins = None
                for kk in range(32):
                    stt = d - (k0 + kk) + 255
                    ins = e.matmul(pv[:, kk, :], lhsT=ohb[:, stt:stt + 128], rhs=r2.ap, start=True, stop=True)
                return ins
            P.op("pe", f, reads=[ohb.r, r2.r], writes=[p.r])
            dst = Tb[d][:, :, k0:k0 + 32]

            def f(e, pv=pv, dst=dst):
                yield e.tensor_copy(out=dst, in_=pv[:, :, 0:AH].rearrange("p k h -> p h k"))
                yield e.tensor_tensor(out=dst, in0=dst, in1=pv[:, :, AH:2 * AH].rearrange("p k h -> p h k"), op=ALU.add)
            P.op("dve", f, reads=[p.r], writes=[Tb[d].r.k(k0)])
    P.op("dve", lambda e: e.tensor_tensor(out=Tb[256].ap, in0=Tb[256].ap, in1=t31bc.ap.unsqueeze(2).to_broadcast([128, AH, 256]), op=ALU.subtract),
         reads=[Tb[256].r, t31bc.r], writes=[Tb[256].r])
    P.op("dve", lambda e: e.tensor_copy(out=T256s.ap, in_=Tb[256][0:8, :, 128:256]), reads=[Tb[256].r], writes=[T256s.r])
    P.op("dve", lambda e: e.tensor_copy(out=T0s.ap, in_=Tb[0][0:8, :, 0:8]), reads=[Tb[0].r], writes=[T0s.r])

    chk("b0")
    NBLK, NPB = c.NBLK, c.NPB
    qTa = R3.get([128, NOW], BF16, "qTa")
    kTa = R3.get([128, NTOK], BF16, "kTa")
    Vt = R3.get([128, NT, 128], BF16, "Vt")
    Lg2 = [R3.get([128, NBLK * 256], F32, "Lg") for _ in range(2)]
    Pb2 = [R3.get([128, NBLK * 256], BF16, "Pb") for _ in range(2)]
    PT2 = [R3.get([128, NBLK * 2, 128], BF16, "PT") for _ in range(2)]
    kst = [R3.get([128, 128], F32, "kst") for _ in range(2)]
    vst = [R3.get([128, 128], F32, "vst") for _ in range(2)]
    ksum = sbt([128, 8], F32, "ksum")
    kmh = sbt([128, 8], BF16, "kmh")
    kml = sbt([128, 8], BF16, "kml")
    kmt = sbt([128, 8], F32, "kmt")
    gm2 = [sbt([128, 8], F32, f"gm{i}") for i in range(2)]
    top82 = [sbt([128, 8], F32, f"top8{i}") for i in range(2)]
    fbt2 = [sbt([128, 8], F32, f"fbt{i}") for i in range(2)]
    cst = sbt([128, c.NOB * 8], F32, "cst")
    mxs2 = [sbt([128, 4], F32, f"mxs{i}") for i in range(2)]
    obf2 = [sbt([128, 128], BF16, f"obf{i}") for i in range(2)]
    SCL = float(128 ** -0.5)
    def do_head(h):
        wq = wload("a", 1, R3, [128, KC, 384], [(0, w_in[:, c.c_aq + 128 * h:c.c_aq + 128 * h + 128]),
                                                 (128, w_in[:, c.c_ak + 128 * h:c.c_ak + 128 * h + 128]),
                                                 (256, w_in[:, c.c_av + 128 * h:c.c_av + 128 * h + 128])])
        for (a, n) in ch_own:
            p = nps()
            mm_feat(p, 128, n, wq, 0, KC, xnT, a, [xnT.r])
            P.op("act", lambda e, p=p, a=a, n=n: e.activation(out=qTa[:, a - TOK_O0:a - TOK_O0 + n], in_=p[:, 0:n], func=AF.Identity, scale=SCL), reads=[p.r], writes=[qTa.r])
        P.op("dve", lambda e, h=h: e.tensor_copy(out=qTs[:, h, :], in_=qTa[:, 128 * NO:128 * NO + 8]), reads=[qTa.r], writes=[qTs.r])
        chk("b1a")
        P.op("pool", lambda e: e.memset(ksum.ap, 0.0), writes=[ksum.r])
        for (a, n) in ch_all:
            p = nps()
            mm_feat(p, 128, n, wq, 128, KC, xnT, a, [xnT.r])
            if n == 8:
                evac_copy("act", kTa[:, a:a + n], p[:, 0:n], [p.r], [kTa.r])
            else:
                for b0 in range(0, n, 256):
                    blk = (a + b0) // 256
                    P.op("act", lambda e, p=p, a=a, b0=b0, blk=blk: e.activation(out=kTa[:, a + b0:a + b0 + 256], in_=p[:, b0:b0 + 256], func=AF.Identity,
                                                                              accum_out=ksum[:, blk:blk + 1]), reads=[p.r], writes=[kTa.r, ksum.r])
        P.op("dve", lambda e, h=h: e.tensor_copy(out=kTs[:, h, :], in_=kTa[:, TOK_S0:TOK_S0 + 8]), reads=[kTa.r], writes=[kTs.r])
        chk("b1b")
        def f(e):
            yield e.tensor_scalar(out=kmt.ap, in0=ksum.ap, scalar1=1.0 / 256.0, scalar2=None, op0=ALU.mult)
            yield e.tensor_copy(out=kmh.ap, in_=kmt.ap)
            yield e.tensor_tensor(out=kmt.ap, in0=kmt.ap, in1=kmh.ap, op=ALU.subtract)
            yield e.tensor_copy(out=kml.ap, in_=kmt.ap)
        P.op("dve", f, reads=[ksum.r], writes=[kmt.r, kmh.r, kml.r])
        chk("b1c")
        for t, (tok0, r) in enumerate(tiles):
            p = nps()
            mm_tok(p, r, 128, xnT, tok0, KC, wq, 256, [xnT_k[t]])
            evac_copy("act", Vt[0:r, t, :], p[0:r, 0:128], [p.r], [Vt.r])
            if t == NP - 1:
                chk("b1d")
            if t >= NP:
                oo = own_off(t)
                vs_ = vst[t % 2]
                evac_copy("act", vs_[0:r, :], p[0:r, 0:128], [p.r], [vs_.r])
                if t == NP:
                    chk("b1e")
                P.dma("sp", f"vst{t % 2}", lambda e, vs_=vs_, oo=oo, r=r, h=h: e.dma_start(out=v_o[oo:oo + r, 128 * h:128 * h + 128], in_=vs_[0:r, :]), reads=[vs_.r])
                if t == NP:
                    chk("b1f")
                if r == 8:
                    P.op("dve", lambda e, p=p, h=h: e.tensor_copy(out=vsf[:, h, :], in_=p[0:8, 0:128]), reads=[p.r], writes=[vsf.r])
                p2 = nps()
                mm_tok(p2, r, 128, xnT, tok0, KC, wq, 128, [xnT_k[t]])
                ks_ = kst[t % 2]
                evac_copy("act", ks_[0:r, :], p2[0:r, 0:128], [p2.r], [ks_.r])
                P.dma("sp", f"kst{t % 2}", lambda e, ks_=ks_, oo=oo, r=r, h=h: e.dma_start(out=k_o[oo:oo + r, 128 * h:128 * h + 128], in_=ks_[0:r, :]), reads=[ks_.r])
                if t == NP:
                    chk("b1g")
                if t == NP + NO - 1:
                    chk("b1h")
        chk("b1")
        P.op("dve", lambda e, h=h: e.tensor_scalar(out=cst.ap, in0=bbv.ap, scalar1=t31bc[:, h:h + 1], scalar2=None, op0=ALU.add),
             reads=[bbv.r, t31bc.r], writes=[cst.r])
        def do_tileA(i, Lg, Pb, PT, gm, top8, fbt, mxs, obf):
            v = i // 2
            ob = NPB + v
            oo = 128 * i
            nk = (ob + 1) * 256
            pg = nps()

            def f(e, pg=pg, oo=oo):
                e.matmul(pg[:, 0:8], lhsT=qTa[:, oo:oo + 128], rhs=kmh.ap, start=True, stop=False)
                return e.matmul(pg[:, 0:8], lhsT=qTa[:, oo:oo + 128], rhs=kml.ap, start=False, stop=True)
            P.op("pe", f, reads=[qTa.r, kmh.r, kml.r], writes=[pg.r])

            def f(e, pg=pg, v=v):
                yield e.tensor_tensor(out=gm.ap, in0=pg[:, 0:8], in1=bbv[:, v * 8:v * 8 + 8], op=ALU.add)
                yield e.max(out=top8.ap, in_=gm.ap)
                yield e.tensor_scalar(out=fbt.ap, in0=gm.ap, scalar1=top8[:, 2:3], scalar2=1.0, op0=ALU.is_ge, op1=ALU.subtract)
                yield e.scalar_tensor_tensor(out=fbt.ap, in0=fbt.ap, scalar=BIG, in1=cst[:, v * 8:v * 8 + 8], op0=ALU.mult, op1=ALU.add)
            P.op("dve", f, reads=[pg.r, bbv.r, cst.r], writes=[gm.r, top8.r, fbt.r])
            for c0 in range(0, nk, 512):
                n = min(512, nk - c0)
                pS = nps()
                P.op("pe", lambda e, pS=pS, c0=c0, n=n, oo=oo: e.matmul(pS[:, 0:n], lhsT=qTa[:, oo:oo + 128], rhs=kTa[:, c0:c0 + n], start=True, stop=True),
                     reads=[qTa.r, kTa.r], writes=[pS.r])
                for b0 in range(0, n, 256):
                    blk = (c0 + b0) // 256
                    if blk == ob:
                        dlt = 0 if i % 2 == 0 else 128
                        P.op("dve", lambda e, pS=pS, b0=b0, blk=blk, dlt=dlt, h=h: e.tensor_tensor(out=Lg[:, blk * 256:blk * 256 + 256], in0=pS[:, b0:b0 + 256],
                                                                                              in1=Tb[dlt][:, h, :], op=ALU.add), reads=[pS.r, Tb[dlt].r], writes=[Lg.r.k(blk)])
                    elif blk == ob - 1 and i % 2 == 0:
                        P.op("dve", lambda e, pS=pS, b0=b0, blk=blk, h=h: e.scalar_tensor_tensor(out=Lg[:, blk * 256:blk * 256 + 256], in0=pS[:, b0:b0 + 256], scalar=fbt[:, blk:blk + 1],
                                                                                            in1=Tb[256][:, h, :], op0=ALU.add, op1=ALU.add), reads=[pS.r, fbt.r, Tb[256].r], writes=[Lg.r.k(blk)])
                    else:
                        P.op("act", lambda e, pS=pS, b0=b0, blk=blk: e.activation(out=Lg[:, blk * 256:blk * 256 + 256], in_=pS[:, b0:b0 + 256], func=AF.Identity,
                                                                              bias=fbt[:, blk:blk + 1], scale=1.0), reads=[pS.r, fbt.r], writes=[Lg.r.k(blk)])
            def f(e, nk=nk):
                yield e.reduce_max(out=mxs[:, 0:1], in_=Lg[:, 0:nk], axis=AX.X)
                yield e.tensor_scalar(out=mxs[:, 1:2], in0=mxs[:, 0:1], scalar1=-1.0, scalar2=None, op0=ALU.mult)
            P.op("dve", f, reads=[Lg.r], writes=[mxs.r.k("m")])

        def do_tileA2(i, Lg, Pb, PT, gm, top8, fbt, mxs, obf):
            v = i // 2
            ob = NPB + v
            nk = (ob + 1) * 256
            P.op("act", lambda e, nk=nk: e.activation(out=Pb[:, 0:nk], in_=Lg[:, 0:nk], func=AF.Exp, bias=mxs[:, 1:2], scale=1.0, accum_out=mxs[:, 2:3]),
                 reads=[Lg.r, mxs.r.k("m")], writes=[Pb.r, mxs.r.k("s")])
            P.op("dve", lambda e: e.reciprocal(out=mxs[:, 3:4], in_=mxs[:, 2:3]), reads=[mxs.r.k("s")], writes=[mxs.r.k("r")])

        def do_tileB(i, Lg, Pb, PT, gm, top8, fbt, mxs, obf):
            v = i // 2
            ob = NPB + v
            oo = 128 * i
            nk = (ob + 1) * 256
            nch = nk // 128
            j0 = 0
            while j0 < nch:
                n = min(8, nch - j0)
                pT = nps()
                pTb = psbf(pT).rearrange("p (a b) -> p a b", b=128)

                def f(e, pTb=pTb, j0=j0, n=n):
                    ins = None
                    for j in range(n):
                        ins = e.transpose(out=pTb[:, j, :], in_=Pb[:, (j0 + j) * 128:(j0 + j + 1) * 128], identity=identb.ap)
                    return ins
                P.op("pe", f, reads=[Pb.r, identb.r], writes=[pT.r])
                evac_copy(alt_eng(), PT[:, j0:j0 + n, :], pTb[:, 0:n, :], [pT.r], [PT.r])
                j0 += n
            pO = nps()

            def f(e, pO=pO, nch=nch):
                ins = None
                for j in range(nch):
                    ins = e.matmul(pO[:, 0:128], lhsT=PT[:, j, :], rhs=Vt[:, j, :], start=(j == 0), stop=(j == nch - 1))
                return ins
            P.op("pe", f, reads=[PT.r, Vt.r], writes=[pO.r])
            P.op("act", lambda e, pO=pO: e.activation(out=obf.ap, in_=pO[:, 0:128], func=AF.Identity, scale=mxs[:, 3:4]), reads=[pO.r, mxs.r.k("r")], writes=[obf.r])
            pT = nps()
            pTb = psbf(pT).rearrange("p (a b) -> p a b", b=128)
            P.op("pe", lambda e, pTb=pTb: e.transpose(out=pTb[:, 0, :], in_=obf.ap, identity=identb.ap), reads=[obf.r, identb.r], writes=[pT.r])
            evac_copy("dve", mixT[:, MH + h, oo:oo + 128], pTb[:, 0, :], [pT.r], [mixT.r.k(MH + h)])
            chk("b2")
            if i == NO - 1:
                chk("b3")

        def bufs(i):
            return (Lg2[i % 2], Pb2[i % 2], PT2[i % 2], gm2[i % 2], top82[i % 2], fbt2[i % 2], mxs2[i % 2], obf2[i % 2])
        do_tileA(0, *bufs(0))
        do_tileA2(0, *bufs(0))
        for i in range(1, NO):
            do_tileA(i, *bufs(i))
            do_tileB(i - 1, *bufs(i - 1))
            do_tileA2(i, *bufs(i))
        do_tileB(NO - 1, *bufs(NO - 1))

    for h in range(AH):
        do_head(h)

    if STOP == "p1b":
        P.emit()
        st.close()
        return nc, hc
    R3.barrier()
    wslots.clear()
    kvsel = R3.get([128, 24, 2, 256], F32, "kvsel")
    kselT = R3.get([128, 48, 128], BF16, "kselT")
    gts = R3.get([8, AH, NB], F32, "gts")
    tp8 = R3.get([8, AH, 8], F32, "tp8")
    OH = R3.get([8, AH, NB], F32, "OH")
    OHt = R3.get([8, AH, NB], F32, "OHt")
    selp = R3.get([8, AH, 3, 2], F32, "selp")
    is63 = R3.get([8, AH, 3], F32, "is63")
    Xd = R3.get([8, 8, AH * 6], F32, "Xd")
    addc2 = R3.get([128, NSEL // 2], F32, "addc2")
    idxs = R3.get([128, NSEL // 2], I32, "idxs")
    sTs = R3.get([128, 48], F32, "sTs")
    Ls = R3.get([8, 784], F32, "Ls")
    tmpb = R3.get([8, 256], F32, "tmpb")
    pTs = R3.get([128, 6, 8], F32, "pTs")
    pTo = R3.get([8, 8], F32, "pTo")
    sms = R3.get([8, 4], F32, "sms")
    cload(addc2, cin["addc2"])
    eye8x = R3.get([8, 8 * AH * 6], F32, "eye8x")
    ones8 = R3.get([8, 128], F32, "ones8")
    cload(eye8x, cin["eye8x"])
    P.op("pool", lambda e: e.memset(ones8.ap, 1.0), writes=[ones8.r])
    pg = nps()

    def f(e):
        ins = None
        for h in range(AH):
            e.matmul(pg[0:8, h * NB:(h + 1) * NB], lhsT=qTs[:, h, :], rhs=kmsh[:, h * NB:(h + 1) * NB], start=True, stop=False)
            ins = e.matmul(pg[0:8, h * NB:(h + 1) * NB], lhsT=qTs[:, h, :], rhs=kmsl[:, h * NB:(h + 1) * NB], start=False, stop=True)
        return ins
    P.op("pe", f, reads=[qTs.r, kmsh.r, kmsl.r], writes=[pg.r])

    def f(e):
        yield e.tensor_copy(out=gts.ap, in_=pg[0:8, 0:AH * NB].rearrange("p (h n) -> p h n", n=NB))
        for h in range(AH):
            yield e.max(out=tp8[:, h, :], in_=gts[:, h, :])
    P.op("dve", f, reads=[pg.r], writes=[gts.r, tp8.r])
    ptv = ptf[0:8, :].rearrange("p (n u) -> p u n", u=2)
    for s_ in range(3):
        def f(e, s_=s_):
            yield e.tensor_tensor(out=OH.ap, in0=gts.ap, in1=tp8[:, :, s_:s_ + 1].to_broadcast([8, AH, NB]), op=ALU.is_equal)
            yield e.tensor_copy(out=is63[:, :, s_], in_=OH[:, :, NB - 1])
            for u in range(2):
                yield e.tensor_tensor(out=OHt.ap, in0=OH.ap, in1=ptv[:, u:u + 1, :].to_broadcast([8, AH, NB]), op=ALU.mult)
                yield e.tensor_reduce(out=selp[:, :, s_, u], in_=OHt.ap, axis=AX.X, op=ALU.add)
        P.op("dve", f, reads=[gts.r, tp8.r, ptf.r], writes=[OH.r, OHt.r, selp.r, is63.r])
    P.op("dve", lambda e: e.tensor_tensor(out=Xd.ap, in0=eye8x.ap.rearrange("p (q x) -> p q x", q=8),
                                          in1=selp.ap.rearrange("p h s u -> p (h s u)").unsqueeze(1).to_broadcast([8, 8, AH * 6]), op=ALU.mult),
         reads=[eye8x.r, selp.r], writes=[Xd.r])
    pB = nps()
    P.op("pe", lambda e, pB=pB: e.matmul(pB[:, 0:NSEL], lhsT=ones8.ap, rhs=Xd.ap.rearrange("p q x -> p (q x)"), start=True, stop=True),
         reads=[ones8.r, Xd.r], writes=[pB.r])

    pBv2 = pB[:, 0:NSEL].rearrange("p (x u) -> p x u", u=2)

    def f(e):
        yield e.scalar_tensor_tensor(out=addc2[0:64, :], in0=pBv2[0:64, :, 0], scalar=float(64 * AH), in1=addc2[0:64, :], op0=ALU.mult, op1=ALU.add)
        yield e.scalar_tensor_tensor(out=addc2[64:128, :], in0=pBv2[64:128, :, 1], scalar=float(64 * AH), in1=addc2[64:128, :], op0=ALU.mult, op1=ALU.add)
        yield e.tensor_copy(out=idxs.ap, in_=addc2.ap)
    P.op("dve", f, reads=[pB.r, addc2.r], writes=[addc2.r, idxs.r])
    ckv_rows = cache_kv.rearrange("(r two) x -> r (two x)", two=2)
    if R1.n >= 24 * 2 * 256 * 4 + 4096 and (KC * NTOK * 2) % 4 == 0:
        R1.barrier()
        kvsel_b = R1.get([128, 24, 2, 256], F32, "kvselb")
        kvs = [kvsel, kvsel_b]
    else:
        kvs = [kvsel, kvsel]

    def sgather(h, kvsel):
        for q in range(8):
            for s_ in range(3):
                un = q * 3 + s_
                col = (q * AH + h) * 3 + s_
                P.dma("pool", f"kvsel{h % 2}", lambda e, un=un, col=col: e.indirect_dma_start(out=kvsel[:, un, :, :].rearrange("p e x -> p (e x)"), out_offset=None, in_=ckv_rows,
                                                                                    in_offset=bass.IndirectOffsetOnAxis(ap=idxs[:, col:col + 1], axis=0)),
                      reads=[idxs.r], writes=[kvsel.r.k(un)])

    def do_shead(h, kvsel):
        for u0 in range(0, 48, 4):
            pT = nps()
            pTv = pT.ap.rearrange("p (a b) -> p a b", b=128)

            def f(e, pTv=pTv, u0=u0):
                ins = None
                for j in range(4):
                    ins = e.transpose(out=pTv[:, j, :], in_=kvsel[:, (u0 + j) // 2, (u0 + j) % 2, 0:128], identity=ident.ap)
                return ins
            P.op("pe", f, reads=[kvsel.r, ident.r], writes=[pT.r])
            evac_copy(alt_eng(), kselT[:, u0:u0 + 4, :], pTv[:, 0:4, :], [pT.r], [kselT.r])
        pS = nps()

        def f(e, pS=pS, h=h):
            ins = None
            for q in range(8):
                for su in range(6):
                    un = q * 6 + su
                    ins = e.matmul(pS[:, un:un + 1], lhsT=kselT[:, un, :], rhs=qTs[:, h, q:q + 1], start=True, stop=True)
            return ins
        P.op("pe", f, reads=[kselT.r, qTs.r], writes=[pS.r])
        evac_copy("dve", sTs.ap, pS[:, 0:48], [pS.r], [sTs.r])
        sTv = sTs.ap.rearrange("p (q x) -> p x q", x=6)
        pA, pBk = nps(), nps()
        pAv = pA.ap.rearrange("p (a b) -> p a b", b=128)
        pBv = pBk.ap.rearrange("p (a b) -> p a b", b=128)

        def f(e, pAv=pAv, pBv=pBv, pBk=pBk, h=h):
            for su in range(4):
                e.transpose(out=pAv[0:8, su, :], in_=sTv[:, su, :], identity=ident.ap)
            for su in range(4, 6):
                e.transpose(out=pBv[0:8, su - 4, :], in_=sTv[:, su, :], identity=ident.ap)
            return e.matmul(pBk[0:8, 256:264], lhsT=qTs[:, h, :], rhs=kTs[:, h, :], start=True, stop=True)
        P.op("pe", f, reads=[sTs.r, ident.r, qTs.r, kTs.r], writes=[pA.r, pBk.r])

        def f(e, pAv=pAv, pBv=pBv, pBk=pBk, h=h):
            for s_ in range(3):
                yield e.memset(tmpb[:, 0:128], 0.0)
                yield e.tensor_scalar(out=tmpb[:, 128:256], in0=T256s[:, h, :], scalar1=is63[:, h, s_:s_ + 1], scalar2=None, op0=ALU.mult)
                yield e.tensor_scalar(out=tmpb.ap, in0=tmpb.ap, scalar1=t31bc[0:8, h:h + 1], scalar2=None, op0=ALU.add)
                src = pAv[0:8, 2 * s_:2 * s_ + 2, :] if s_ < 2 else pBv[0:8, 0:2, :]
                yield e.tensor_tensor(out=Ls[:, s_ * 256:(s_ + 1) * 256].rearrange("p (a u b) -> p a u b", u=2, b=64), in0=src.rearrange("p a (u b) -> p a u b", u=2),
                                in1=tmpb.ap.rearrange("q (u p e) -> q e u p", u=2, e=2), op=ALU.add)
            yield e.tensor_tensor(out=Ls[:, 768:776], in0=pBk[0:8, 256:264], in1=T0s[:, h, :], op=ALU.add)
            yield e.reduce_max(out=sms[:, 0:1], in_=Ls[:, 0:776], axis=AX.X)
            yield e.tensor_scalar(out=sms[:, 1:2], in0=sms[:, 0:1], scalar1=-1.0, scalar2=None, op0=ALU.mult)
        P.op("dve", f, reads=[pA.r, pBk.r, T256s.r, is63.r, t31bc.r, T0s.r], writes=[tmpb.r, Ls.r, sms.r.k("m")])
        P.op("act", lambda e: e.activation(out=Ls[:, 0:776], in_=Ls[:, 0:776], func=AF.Exp, bias=sms[:, 1:2], scale=1.0, accum_out=sms[:, 2:3]),
             reads=[Ls.r, sms.r.k("m")], writes=[Ls.r, sms.r.k("s")])

        def f(e):
            yield e.reciprocal(out=sms[:, 3:4], in_=sms[:, 2:3])
            yield e.tensor_scalar(out=Ls[:, 0:776], in0=Ls[:, 0:776], scalar1=sms[:, 3:4], scalar2=None, op0=ALU.mult)
        P.op("dve", f, reads=[Ls.r, sms.r.k("s")], writes=[Ls.r, sms.r.k("r")])
        pP = nps()
        pPv = pP[:, 0:48].rearrange("p (a b) -> p a b", b=8)

        def f(e, pP=pP, pPv=pPv):
            for su in range(6):
                e.transpose(out=pPv[:, su, :], in_=Ls[:, su * 128:(su + 1) * 128], identity=ident[0:8, 0:8])
            return e.transpose(out=pP[0:8, 64:72], in_=Ls[:, 768:776], identity=ident[0:8, 0:8])
        P.op("pe", f, reads=[Ls.r, ident.r], writes=[pP.r])

        def f(e, pP=pP, pPv=pPv):
            yield e.tensor_copy(out=pTs.ap, in_=pPv)
            yield e.tensor_copy(out=pTo.ap, in_=pP[0:8, 64:72])
        P.op("dve", f, reads=[pP.r], writes=[pTs.r, pTo.r])
        pO = nps()

        def f(e, pO=pO, h=h):
            ins = None
            for q in range(8):
                e.matmul(pO[:, q:q + 1], lhsT=vsf[:, h, :], rhs=pTo[:, q:q + 1], start=True, stop=False)
                for su in range(6):
                    ins = e.matmul(pO[:, q:q + 1], lhsT=kvsel[:, q * 3 + su // 2, su % 2, 128:256], rhs=pTs[:, su, q:q + 1], start=False, stop=(su == 5))
            return ins
        P.op("pe", f, reads=[vsf.r, pTo.r, kvsel.r, pTs.r], writes=[pO.r])
        evac_copy("act", mixT[:, MH + h, 128 * NO:128 * NO + 8], pO[:, 0:8], [pO.r], [mixT.r.k(MH + h)])

    if kvs[0] is kvs[1]:
        for h in range(AH):
            sgather(h, kvs[0])
            do_shead(h, kvs[0])
    else:
        sgather(0, kvs[0])
        for h in range(AH):
            if h + 1 < AH:
                sgather(h + 1, kvs[(h + 1) % 2])
            do_shead(h, kvs[h % 2])

    if STOP == "p1c":
        P.emit()
        st.close()
        return nc, hc
    R1.barrier()
    R3.barrier()
    wslots.clear()
    h1 = R1.get([128, NOT, D], F32, "h1")
    otl = [(128 * i, 128) for i in range(NO)] + [(128 * NO, 8)]
    for i, (o0, r) in enumerate(otl):
        P.dma("sp", "h1l", lambda e, i=i, o0=o0, r=r: e.dma_start(out=h1[0:r, i, :], in_=x_all[TOK_O0 + o0:TOK_O0 + o0 + r, :]), writes=[h1.r.k(i)])
    for cg in range(D // 512):
        w = wload("o", 2, R3, [128, max(KM, KC), 512], [(0, w_out[:, cg * 512:(cg + 1) * 512])])
        for i, (o0, r) in enumerate(otl):
            p = nps()
            mm_tok(p, r, 512, mixT, o0, KM, w, 0, [mixT.r])
            P.op("dve", lambda e, p=p, i=i, r=r, cg=cg: e.tensor_tensor(out=h1[0:r, i, cg * 512:(cg + 1) * 512], in0=p[0:r, :], in1=h1[0:r, i, cg * 512:(cg + 1) * 512], op=ALU.add),
                 reads=[p.r, h1.r.k(i)], writes=[h1.r.k(i)])
    def norm_stage(grow):
        R3.barrier()
        wslots.clear()
        g = load_g(0, grow)
        xnb = R3.get([128, D], BF16, "xnb")
        junk3 = R3.get([128, D], BF16, "junk3")
        R2.barrier()
        xT = R2.get([128, KC, NOW], BF16, "xT")
        for i, (o0, r) in enumerate(otl):
            norm_tile(h1[0:r, i, :], h1.r.k(i), r, g, xnb[0:r, :], xnb.r, ssq, rstd, junk3)
            transpose_into(xnb, r, KC, xT, o0, xT.r)
        R3.barrier()
        wslots.clear()
        return xT
    xT2 = norm_stage(1)
    aT = R3.get([128, 4, NOW], BF16, "aT")
    rl = [R3.get([128, 512], F32, "rl") for _ in range(2)]
    ch_o = chunks(0, 128 * NO) + [(128 * NO, 8)]
    nrl = 0
    for fg in range(DFF // 512):
        wu = wload("u", 2, R3, [128, KC, 512], [(0, w_up[:, fg * 512:(fg + 1) * 512])])
        wd = wload("d", 2, R3, [128, 4, D], [(0, w_down[fg * 512:(fg + 1) * 512, :])])
        for f_ in range(4):
            for (a, n) in ch_o:
                p = nps()
                mm_feat(p, 128, n, wu, f_ * 128, KC, xT2, a, [xT2.r])
                rt = rl[nrl % 2]
                nrl += 1
                P.op("act", lambda e, p=p, n=n, rt=rt: e.activation(out=rt[:, 0:n], in_=p[:, 0:n], func=AF.Relu), reads=[p.r], writes=[rt.r])
                P.op("pool", lambda e, rt=rt, f_=f_, a=a, n=n: e.tensor_tensor(out=aT[:, f_, a:a + n], in0=rt[:, 0:n], in1=rt[:, 0:n], op=ALU.mult), reads=[rt.r], writes=[aT.r])
        for i, (o0, r) in enumerate(otl):
            for cg in range(D // 512):
                p = nps()

                def f(e, p=p, o0=o0, r=r, cg=cg, wd=wd):
                    ins = None
                    for f_ in range(4):
                        ins = e.matmul(p[0:r, :], lhsT=aT[:, f_, o0:o0 + r], rhs=wd[:, f_, cg * 512:(cg + 1) * 512], start=(f_ == 0), stop=(f_ == 3))
                    return ins
                P.op("pe", f, reads=[aT.r, wd.r], writes=[p.r])
                P.op("dve", lambda e, p=p, i=i, r=r, cg=cg: e.tensor_tensor(out=h1[0:r, i, cg * 512:(cg + 1) * 512], in0=p[0:r, :], in1=h1[0:r, i, cg * 512:(cg + 1) * 512], op=ALU.add),
                     reads=[p.r, h1.r.k(i)], writes=[h1.r.k(i)])
    xT3 = norm_stage(2)
    KP = PLE // 128
    pT_ = R3.get([128, KP, NOW], BF16, "pT_")
    pst = R3.get([128, PLE], F32, "pst")
    pbf = R3.get([128, PLE], BF16, "pbf")
    for i, (o0, r) in enumerate(otl):
        P.dma("sp", "pst", lambda e, o0=o0, r=r: e.dma_start(out=pst[0:r, :], in_=p_all[o0:o0 + r, :]), writes=[pst.r])
        P.op("dve", lambda e, r=r: e.tensor_copy(out=pbf[0:r, :], in_=pst[0:r, :]), reads=[pst.r], writes=[pbf.r])
        transpose_into(pbf, r, KP, pT_, o0, pT_.r)
    sg = [R3.get([128, 512], F32, "sg") for _ in range(2)]
    for cg in range(D // 512):
        wgt = wload("u", 2, R3, [128, KC, 512], [(0, w_pg[:, cg * 512:(cg + 1) * 512])])
        wpp = wload("p", 2, R3, [128, KP, 512], [(0, w_pp[:, cg * 512:(cg + 1) * 512])])
        for i, (o0, r) in enumerate(otl):
            p1, p2 = nps(), nps()
            mm_tok(p1, r, 512, xT3, o0, KC, wgt, 0, [xT3.r])
            mm_tok(p2, r, 512, pT_, o0, KP, wpp, 0, [pT_.r])
            s_ = sg[i % 2]
            P.op("act", lambda e, p1=p1, r=r, s_=s_: e.activation(out=s_[0:r, :], in_=p1[0:r, :], func=AF.Sigmoid), reads=[p1.r], writes=[s_.r])
            P.op("dve", lambda e, p2=p2, r=r, s_=s_: e.tensor_tensor(out=s_[0:r, :], in0=s_[0:r, :], in1=p2[0:r, :], op=ALU.mult), reads=[p2.r, s_.r], writes=[s_.r])
            P.op("pool", lambda e, i=i, r=r, cg=cg, s_=s_: e.tensor_tensor(out=h1[0:r, i, cg * 512:(cg + 1) * 512], in0=h1[0:r, i, cg * 512:(cg + 1) * 512], in1=s_[0:r, :], op=ALU.add),
                 reads=[s_.r, h1.r.k(i)], writes=[h1.r.k(i)])
    R3.barrier()
    wslots.clear()
    g = load_g(0, 3)
    junk3 = R3.get([128, D], BF16, "junk3")
    yst = [R3.get([128, D], F32, "yst") for _ in range(2)]
    for i, (o0, r) in enumerate(otl):
        y_ = yst[i % 2]
        norm_tile(h1[0:r, i, :], h1.r.k(i), r, g, y_[0:r, :], y_.r, ssq, rstd, junk3)
        P.dma("sp", f"yst{i % 2}", lambda e, y_=y_, o0=o0, r=r: e.dma_start(out=y_o[o0:o0 + r, :], in_=y_[0:r, :]), reads=[y_.r])
    P.emit()
    st.close()
    return nc, hc


def make_in_maps(c, inp, ncores):
    MH, AH = c.MH, c.AH
    half_len = 128 * c.NO
    hc = host_consts(c)
    f32 = np.float32
    shared = {
        "w_in": np.ascontiguousarray(inp["w_in"][0]), "w_out": np.ascontiguousarray(inp["w_out"][0]),
        "w_up": np.ascontiguousarray(inp["w_up"][0]), "w_down": np.ascontiguousarray(inp["w_down"][0]),
        "w_pg": np.ascontiguousarray(inp["w_ple_gate"][0]), "w_pp": np.ascontiguousarray(inp["w_ple_proj"][0]),
        "gvec": np.ascontiguousarray(np.stack([inp["g_mix"][0], inp["g_ffn"][0], inp["g_ple"][0], inp["g_final"]]).astype(f32)),
        "g_mh": np.ascontiguousarray(inp["g_mhead"][0].reshape(1, MH * 128)),
        "b_i": np.ascontiguousarray(inp["b_igate"][0].reshape(MH, 1)), "b_f": np.ascontiguousarray(inp["b_fgate"][0].reshape(MH, 1)),
        "relt": np.ascontiguousarray(np.concatenate([inp["rel_bias_table"], np.full((1, AH), -BIG, f32)], 0).astype(f32)),
        "cache_k": np.ascontiguousarray(inp["cache_k"][0]).reshape(c.NPHYS * 128, AH * 128),
        "cache_kv": np.ascontiguousarray(np.stack([inp["cache_k"][0], inp["cache_v"][0]], axis=3).transpose(0, 2, 1, 3, 4)).reshape(c.NPHYS * AH * 128, 256),
    }
    for k, v in hc.items():
        shared["c_" + k] = v
    maps = []
    for cid in range(ncores):
        b, half = cid // 2, cid % 2
        xp = inp["x_prompt"][b]
        own = xp[half * half_len:(half + 1) * half_len]
        pre = xp[0:half_len] if half == 1 else np.zeros_like(own)
        m = dict(shared)
        m["x_all"] = np.ascontiguousarray(np.concatenate([pre, own, inp["x_sample"][cid]], 0))
        m["p_all"] = np.ascontiguousarray(np.concatenate([inp["p_prompt"][0, b, half * half_len:(half + 1) * half_len], inp["p_sample"][0, cid]], 0))
        m["pt"] = np.ascontiguousarray(inp["page_table"][cid:cid + 1]).astype(np.int32)
        m["sC"] = np.ascontiguousarray(inp["state_C"][0, cid]).reshape(MH * 64, 128)
        m["sn"] = np.ascontiguousarray(inp["state_n"][0, cid]).reshape(MH * 64, 1)
        m["sm"] = np.ascontiguousarray(inp["state_m"][0, cid]).reshape(MH, 1)
        m["flag"] = np.full((1, 1), float(half), f32)
        maps.append(m)
    return maps


def assemble(c, res, ncores):
    MH, AH = c.MH, c.AH
    B = ncores // 2
    hl = 128 * c.NO
    S = 2 * hl
    D = c.D
    f32 = np.float32
    y_p = np.zeros((B, S, D), f32)
    y_s = np.zeros((ncores, 8, D), f32)
    k_p = np.zeros((1, B, S, AH, 128), f32)
    v_p = np.zeros((1, B, S, AH, 128), f32)
    C_p = np.zeros((1, B, MH, 64, 128), f32)
    n_p = np.zeros((1, B, MH, 64), f32)
    m_p = np.zeros((1, B, MH), f32)
    k_s = np.zeros((1, ncores, 8, AH, 128), f32)
    v_s = np.zeros((1, ncores, 8, AH, 128), f32)
    C_s = np.zeros((1, ncores, MH, 64, 128), f32)
    n_s = np.zeros((1, ncores, MH, 64), f32)
    m_s = np.zeros((1, ncores, MH), f32)
    for cid in range(ncores):
        r = res[cid]
        b, half = cid // 2, cid % 2
        y_p[b, half * hl:(half + 1) * hl] = r["y_o"][0:hl]
        y_s[cid] = r["y_o"][hl:hl + 8]
        k_p[0, b, half * hl:(half + 1) * hl] = r["k_o"][0:hl].reshape(hl, AH, 128)
        v_p[0, b, half * hl:(half + 1) * hl] = r["v_o"][0:hl].reshape(hl, AH, 128)
        k_s[0, cid] = r["k_o"][hl:hl + 8].reshape(8, AH, 128)
        v_s[0, cid] = r["v_o"][hl:hl + 8].reshape(8, AH, 128)
        if half == 1:
            C_p[0, b] = r["Cp_o"].reshape(MH, 64, 128)
            n_p[0, b] = r["np_o"].reshape(MH, 64)
            m_p[0, b] = r["mp_o"].reshape(MH)
        C_s[0, cid] = r["Cs_o"].reshape(MH, 64, 128)
        n_s[0, cid] = r["ns_o"].reshape(MH, 64)
        m_s[0, cid] = r["ms_o"].reshape(MH)
    return (y_p, y_s, k_p, v_p, C_p, n_p, m_p, k_s, v_s, C_s, n_s, m_s)


def kernel(**inputs):
    c = Cfg()
    ncores = 8
    inp = {k: np.asarray(v) for k, v in inputs.items()}
    nc, _ = build(c)
    maps = make_in_maps(c, inp, ncores)
    res = run_bass_kernel_spmd(nc, maps, core_ids=list(range(ncores)))
    return assemble(c, res.results, ncores)
```
